# Optimizing a Trainium2 kernel written in Bass

```python
import math
import jax, jax.numpy as jnp
from jax import lax
import numpy as np

D_MODEL = 1024
BATCH = 8
SEQ = 4096
DEPTH = 1

GRID_W = 64
CTX_LEN = 256
EPS = 1e-6
RG_WIDTH = D_MODEL
RG_BLOCKS = 16
RG_BLOCK = RG_WIDTH // RG_BLOCKS
RG_C = 8.0
CONV_W = 4
CONV_PAD_LEFT = 2
ML_WIDTH = D_MODEL
ML_HEADS = 4
ML_HEAD_DIM = ML_WIDTH // ML_HEADS
ML_QKV_BLOCK = 4
ML_CHUNK = 64
N_IN = 2 * RG_WIDTH + 2 * ML_WIDTH + 2 * D_MODEL
D_FF = int(math.ceil(8 * D_MODEL / 3 / 256)) * 256

kernel_name = "hybrid_rglru_mlstm_dit_block"


def _rmsnorm(x, g):
    xf = x.astype(jnp.float32)
    y = xf * lax.rsqrt(jnp.mean(xf * xf, axis=-1, keepdims=True) + EPS)
    return (y * g.astype(jnp.float32)).astype(x.dtype)


def _modulate(x, shift, scale):
    return x * (1 + scale) + shift


def _conv_centred(x, w, b):
    L = x.shape[1]
    xp = jnp.pad(x, ((0, 0), (CONV_PAD_LEFT, CONV_W - 1 - CONV_PAD_LEFT), (0, 0)))
    y = b
    for j in range(CONV_W):
        y = y + w[j] * xp[:, j:j + L]
    return y


def _block_diag(x, w):
    nb, bi, bo = w.shape
    xs = x.reshape(x.shape[:-1] + (nb, bi))
    return jnp.einsum('...ni,nio->...no', xs, w).reshape(x.shape[:-1] + (nb * bo,))


def _to_col_major(t, rows):
    B, L, C = t.shape
    return t.reshape(B, rows, GRID_W, C).transpose(0, 2, 1, 3).reshape(B, L, C)


def _to_row_major(t, rows):
    B, L, C = t.shape
    return t.reshape(B, GRID_W, rows, C).transpose(0, 2, 1, 3).reshape(B, L, C)


def _rglru_coeffs(xc, wa, ba, wx, bx, lam):
    r = jax.nn.sigmoid(_block_diag(xc, wa) + ba).astype(jnp.float32)
    i = jax.nn.sigmoid(_block_diag(xc, wx) + bx).astype(jnp.float32)
    log_a = -RG_C * jax.nn.softplus(-lam.astype(jnp.float32)) * r
    a = jnp.exp(log_a)
    b = jnp.sqrt(-jnp.expm1(2.0 * log_a)) * (i * xc.astype(jnp.float32))
    return a, b


def _linear_scan(a, b, h0):
    def comb(l, r):
        return l[0] * r[0], r[0] * l[1] + r[1]
    A, H = lax.associative_scan(comb, (a, b), axis=1)
    return H + A * h0[:, None, :]


def _rglru_bidir(xc_lat, xc_ctx, wa, ba, wx, bx, lam):
    outs_l, outs_c = [], []
    for d in range(2):
        rev = d == 1
        seq_c = jnp.flip(xc_ctx, 1) if rev else xc_ctx
        seq_l = jnp.flip(xc_lat, 1) if rev else xc_lat
        a_c, b_c = _rglru_coeffs(seq_c, wa[d], ba[d], wx[d], bx[d], lam[d])
        h_c = _linear_scan(a_c, b_c, jnp.zeros_like(b_c[:, 0]))
        a_l, b_l = _rglru_coeffs(seq_l, wa[d], ba[d], wx[d], bx[d], lam[d])
        h_l = _linear_scan(a_l, b_l, h_c[:, -1])
        if rev:
            h_c, h_l = jnp.flip(h_c, 1), jnp.flip(h_l, 1)
        outs_c.append(h_c)
        outs_l.append(h_l)
    return outs_l[0] + outs_l[1], outs_c[0] + outs_c[1]


def _mlstm_inputs(u, conv_w, conv_b, wq, wk, wv):
    xc = jax.nn.silu(_conv_centred(u, conv_w, conv_b))
    q = _block_diag(xc, wq)
    k = _block_diag(xc, wk) * (ML_HEAD_DIM ** -0.5)
    v = _block_diag(u, wv)
    gin = jnp.concatenate([q, k, v], axis=-1)
    return xc, q, k, v, gin


def _heads(t):
    B, L, _ = t.shape
    return t.astype(jnp.float32).reshape(B, L, ML_HEADS, ML_HEAD_DIM).transpose(0, 2, 1, 3)


def _mlstm_gates(gin, wi, bi, wf, bf):
    log_i = (gin @ wi + bi).astype(jnp.float32)
    log_f = jax.nn.log_sigmoid((gin @ wf + bf).astype(jnp.float32))
    return log_i.transpose(0, 2, 1), log_f.transpose(0, 2, 1)


def _mlstm_chunkwise(q, k, v, log_i, log_f, state):
    B, H, L, dh = q.shape
    nc = L // ML_CHUNK
    to_c = lambda t: jnp.moveaxis(t.reshape((B, H, nc, ML_CHUNK) + t.shape[3:]), 2, 0)
    causal = jnp.tril(jnp.ones((ML_CHUNK, ML_CHUNK), dtype=bool))

    def step(carry, inp):
        C, n, m = carry
        qc, kc, vc, lic, lfc = inp
        bcum = jnp.cumsum(lfc, axis=-1)
        dmat = bcum[..., :, None] - bcum[..., None, :] + lic[..., None, :]
        dmat = jnp.where(causal, dmat, -jnp.inf)
        inter = bcum + m[..., None]
        m_t = jnp.maximum(inter, jnp.max(dmat, axis=-1))
        wmat = jnp.exp(dmat - m_t[..., None])
        s_inter = jnp.exp(inter - m_t)
        s = jnp.einsum('bhtd,bhsd->bhts', qc, kc) * wmat
        num = jnp.einsum('bhts,bhsv->bhtv', s, vc) + s_inter[..., None] * jnp.einsum('bhvk,bhtk->bhtv', C, qc)
        den = jnp.sum(s, axis=-1) + s_inter * jnp.einsum('bhk,bhtk->bht', n, qc)
        h = num / jnp.maximum(jnp.abs(den), jnp.exp(-m_t))[..., None]
        bL = bcum[..., -1]
        g = bL[..., None] - bcum + lic
        m_new = jnp.maximum(bL + m, jnp.max(g, axis=-1))
        decay = jnp.exp(bL + m - m_new)
        ws = jnp.exp(g - m_new[..., None])
        C_new = decay[..., None, None] * C + jnp.einsum('bhs,bhsv,bhsk->bhvk', ws, vc, kc)
        n_new = decay[..., None] * n + jnp.einsum('bhs,bhsk->bhk', ws, kc)
        return (C_new, n_new, m_new), h

    state, hs = lax.scan(step, state, (to_c(q), to_c(k), to_c(v), to_c(log_i), to_c(log_f)))
    h = jnp.moveaxis(hs, 0, 2).reshape(B, H, L, dh)
    return h, state


def _mlstm_bidir(q_l, k_l, v_l, gin_l, q_c, k_c, v_c, gin_c, wi, bi, wf, bf):
    B = q_l.shape[0]
    outs_l, outs_c = [], []
    for d in range(2):
        rev = d == 1
        fl = (lambda t: jnp.flip(t, axis=2)) if rev else (lambda t: t)
        li_c, lf_c = _mlstm_gates(gin_c, wi[d], bi[d], wf[d], bf[d])
        li_l, lf_l = _mlstm_gates(gin_l, wi[d], bi[d], wf[d], bf[d])
        st0 = (jnp.zeros((B, ML_HEADS, ML_HEAD_DIM, ML_HEAD_DIM), jnp.float32),
               jnp.zeros((B, ML_HEADS, ML_HEAD_DIM), jnp.float32),
               jnp.zeros((B, ML_HEADS), jnp.float32))
        h_c, st = _mlstm_chunkwise(fl(q_c), fl(k_c), fl(v_c), fl(li_c), fl(lf_c), st0)
        h_l, _ = _mlstm_chunkwise(fl(q_l), fl(k_l), fl(v_l), fl(li_l), fl(lf_l), st)
        outs_c.append(fl(h_c))
        outs_l.append(fl(h_l))
    return outs_l[0] + outs_l[1], outs_c[0] + outs_c[1]


def _mlstm_out(h, xc, norm_g, skip):
    mu = jnp.mean(h, axis=-1, keepdims=True)
    var = jnp.mean(jnp.square(h - mu), axis=-1, keepdims=True)
    hn = (h - mu) * lax.rsqrt(var + EPS)
    B, H, L, dh = h.shape
    hn = hn.transpose(0, 2, 1, 3).reshape(B, L, H * dh) * norm_g.astype(jnp.float32)
    return (hn + skip.astype(jnp.float32) * xc.astype(jnp.float32)).astype(xc.dtype)


def _merge(y_rg, y_ml, g_rg, g_ml, w_branch_rg, w_branch_ml, w_out):
    mix = jax.nn.sigmoid(g_rg) * (y_rg @ w_branch_rg) + jax.nn.sigmoid(g_ml) * (y_ml @ w_branch_ml)
    return mix @ w_out


def _mixer(u_lat, u_ctx, need_ctx, w_in, rg_conv_w, rg_conv_b, rg_wa, rg_ba, rg_wx, rg_bx, rg_lambda,
           ml_conv_w, ml_conv_b, ml_wq, ml_wk, ml_wv, ml_wi, ml_bi, ml_wf, ml_bf, ml_norm_g, ml_skip,
           w_branch_rg, w_branch_ml, w_out):
    rows = u_lat.shape[1] // GRID_W
    splits = [RG_WIDTH, 2 * RG_WIDTH, 2 * RG_WIDTH + ML_WIDTH, 2 * RG_WIDTH + 2 * ML_WIDTH,
              2 * RG_WIDTH + 2 * ML_WIDTH + D_MODEL]
    rgx_l, rgg_l, mlx_l, mlo_l, gr_l, gm_l = jnp.split(u_lat @ w_in, splits, axis=-1)
    rgx_c, rgg_c, mlx_c, mlo_c, gr_c, gm_c = jnp.split(u_ctx @ w_in, splits, axis=-1)

    xc_l = _conv_centred(rgx_l, rg_conv_w, rg_conv_b)
    xc_c = _conv_centred(rgx_c, rg_conv_w, rg_conv_b)
    h_rg_l, h_rg_c = _rglru_bidir(xc_l, xc_c, rg_wa, rg_ba, rg_wx, rg_bx, rg_lambda)
    y_rg_l = h_rg_l.astype(u_lat.dtype) * jax.nn.gelu(rgg_l)

    xm_l, q_l, k_l, v_l, gin_l = _mlstm_inputs(_to_col_major(mlx_l, rows), ml_conv_w, ml_conv_b, ml_wq, ml_wk, ml_wv)
    xm_c, q_c, k_c, v_c, gin_c = _mlstm_inputs(mlx_c, ml_conv_w, ml_conv_b, ml_wq, ml_wk, ml_wv)
    h_ml_l, h_ml_c = _mlstm_bidir(_heads(q_l), _heads(k_l), _heads(v_l), gin_l,
                                  _heads(q_c), _heads(k_c), _heads(v_c), gin_c,
                                  ml_wi, ml_bi, ml_wf, ml_bf)
    y_ml_l = _to_row_major(_mlstm_out(h_ml_l, xm_l, ml_norm_g, ml_skip), rows) * jax.nn.sigmoid(mlo_l)

    y_lat = _merge(y_rg_l, y_ml_l, gr_l, gm_l, w_branch_rg, w_branch_ml, w_out)
    y_ctx = None
    if need_ctx:
        y_rg_c = h_rg_c.astype(u_ctx.dtype) * jax.nn.gelu(rgg_c)
        y_ml_c = _mlstm_out(h_ml_c, xm_c, ml_norm_g, ml_skip) * jax.nn.sigmoid(mlo_c)
        y_ctx = _merge(y_rg_c, y_ml_c, gr_c, gm_c, w_branch_rg, w_branch_ml, w_out)
    return y_lat, y_ctx


def _swiglu(h, w_ffn_in, w_ffn_out):
    gate, up = jnp.split(h @ w_ffn_in, 2, axis=-1)
    return (jax.nn.silu(gate) * up) @ w_ffn_out


def setup_inputs(seed: int = 0) -> dict:
    key = jax.random.key(seed)
    ks = iter(jax.random.split(key, 48))
    nrm = lambda shape, s: jax.random.normal(next(ks), shape, jnp.float32) * s
    L = DEPTH
    u = jax.random.uniform(next(ks), (L, 2, RG_WIDTH), jnp.float32, minval=0.9, maxval=0.999)
    a0 = u ** (1.0 / RG_C)
    return {
        "x": nrm((BATCH, SEQ, D_MODEL), 1.0),
        "c": nrm((BATCH, D_MODEL), 1.0),
        "ctx": nrm((BATCH, CTX_LEN, D_MODEL), 1.0),
        "c_ctx": nrm((D_MODEL,), 1.0),
        "w_mod": nrm((L, D_MODEL, 6 * D_MODEL), 0.5 * D_MODEL ** -0.5),
        "b_mod": nrm((L, 6 * D_MODEL), 0.02),
        "norm1_g": 1.0 + nrm((L, D_MODEL), 0.02),
        "norm2_g": 1.0 + nrm((L, D_MODEL), 0.02),
        "w_in": nrm((L, D_MODEL, N_IN), D_MODEL ** -0.5),
        "rg_conv_w": nrm((L, CONV_W, RG_WIDTH), CONV_W ** -0.5),
        "rg_conv_b": nrm((L, RG_WIDTH), 0.02),
        "rg_wa": nrm((L, 2, RG_BLOCKS, RG_BLOCK, RG_BLOCK), RG_BLOCK ** -0.5),
        "rg_ba": nrm((L, 2, RG_WIDTH), 0.02),
        "rg_wx": nrm((L, 2, RG_BLOCKS, RG_BLOCK, RG_BLOCK), RG_BLOCK ** -0.5),
        "rg_bx": nrm((L, 2, RG_WIDTH), 0.02),
        "rg_lambda": jnp.log(a0) - jnp.log1p(-a0),
        "ml_conv_w": nrm((L, CONV_W, ML_WIDTH), CONV_W ** -0.5),
        "ml_conv_b": nrm((L, ML_WIDTH), 0.02),
        "ml_wq": nrm((L, ML_WIDTH // ML_QKV_BLOCK, ML_QKV_BLOCK, ML_QKV_BLOCK), ML_QKV_BLOCK ** -0.5),
        "ml_wk": nrm((L, ML_WIDTH // ML_QKV_BLOCK, ML_QKV_BLOCK, ML_QKV_BLOCK), ML_QKV_BLOCK ** -0.5),
        "ml_wv": nrm((L, ML_WIDTH // ML_QKV_BLOCK, ML_QKV_BLOCK, ML_QKV_BLOCK), ML_QKV_BLOCK ** -0.5),
        "ml_wi": nrm((L, 2, 3 * ML_WIDTH, ML_HEADS), (3 * ML_WIDTH) ** -0.5),
        "ml_bi": nrm((L, 2, ML_HEADS), 0.1),
        "ml_wf": nrm((L, 2, 3 * ML_WIDTH, ML_HEADS), (3 * ML_WIDTH) ** -0.5),
        "ml_bf": jnp.linspace(3.0, 6.0, ML_HEADS, dtype=jnp.float32)[None, None, :] + nrm((L, 2, ML_HEADS), 0.1),
        "ml_norm_g": 1.0 + nrm((L, ML_WIDTH), 0.02),
        "ml_skip": 1.0 + nrm((L, ML_WIDTH), 0.02),
        "w_branch_rg": nrm((L, RG_WIDTH, D_MODEL), RG_WIDTH ** -0.5),
        "w_branch_ml": nrm((L, ML_WIDTH, D_MODEL), ML_WIDTH ** -0.5),
        "w_out": nrm((L, D_MODEL, D_MODEL), D_MODEL ** -0.5),
        "w_ffn_in": nrm((L, D_MODEL, 2 * D_FF), D_MODEL ** -0.5),
        "w_ffn_out": nrm((L, D_FF, D_MODEL), D_FF ** -0.5),
        "final_norm_g": 1.0 + nrm((D_MODEL,), 0.02),
    }


def reference(x, c, ctx, c_ctx, w_mod, b_mod, norm1_g, norm2_g, w_in, rg_conv_w, rg_conv_b, rg_wa, rg_ba,
              rg_wx, rg_bx, rg_lambda, ml_conv_w, ml_conv_b, ml_wq, ml_wk, ml_wv, ml_wi, ml_bi, ml_wf, ml_bf,
              ml_norm_g, ml_skip, w_branch_rg, w_branch_ml, w_out, w_ffn_in, w_ffn_out, final_norm_g):
    s_lat = jax.nn.silu(c)
    s_ctx = jax.nn.silu(c_ctx)
    for l in range(DEPTH):
        last = l == DEPTH - 1
        mod_l = (s_lat @ w_mod[l] + b_mod[l])[:, None, :]
        mod_c = s_ctx @ w_mod[l] + b_mod[l]
        sh1, sc1, g1, sh2, sc2, g2 = jnp.split(mod_l, 6, axis=-1)
        csh1, csc1, cg1, csh2, csc2, cg2 = jnp.split(mod_c, 6, axis=-1)

        u_lat = _modulate(_rmsnorm(x, norm1_g[l]), sh1, sc1)
        u_ctx = _modulate(_rmsnorm(ctx, norm1_g[l]), csh1, csc1)
        y_lat, y_ctx = _mixer(u_lat, u_ctx, not last, w_in[l], rg_conv_w[l], rg_conv_b[l], rg_wa[l], rg_ba[l],
                              rg_wx[l], rg_bx[l], rg_lambda[l], ml_conv_w[l], ml_conv_b[l], ml_wq[l], ml_wk[l],
                              ml_wv[l], ml_wi[l], ml_bi[l], ml_wf[l], ml_bf[l], ml_norm_g[l], ml_skip[l],
                              w_branch_rg[l], w_branch_ml[l], w_out[l])
        x = x + g1 * y_lat
        h = _modulate(_rmsnorm(x, norm2_g[l]), sh2, sc2)
        x = x + g2 * _swiglu(h, w_ffn_in[l], w_ffn_out[l])
        if not last:
            ctx = ctx + cg1 * y_ctx
            hc = _modulate(_rmsnorm(ctx, norm2_g[l]), csh2, csc2)
            ctx = ctx + cg2 * _swiglu(hc, w_ffn_in[l], w_ffn_out[l])
    return _rmsnorm(x, final_norm_g)
```

```python
import numpy as np
from contextlib import ExitStack
import concourse.bass as bass
import concourse.mybir as mybir
from concourse.bass_utils import run_bass_kernel_spmd

F32 = mybir.dt.float32
BF16 = mybir.dt.bfloat16
AF = mybir.ActivationFunctionType
ALU = mybir.AluOpType
AX = mybir.AxisListType

ENGS = ["pe", "act", "dve", "pool", "sp"]
EPS = 1e-6
NT = 4352
NCTX = 256
NLAT = 4096


class Trk:
    __slots__ = ("name", "w", "r", "dsem", "dcnt", "excl")

    def __init__(self, name=""):
        self.name = name
        self.w = None
        self.r = {}
        self.dsem = None
        self.dcnt = 0
        self.excl = name.startswith("PS_")


class Prog:
    def __init__(self, nc, stack):
        self.nc = nc
        self.stack = stack
        self.ops = {e: [] for e in ENGS}
        self.seq = {e: 0 for e in ENGS}
        self.known = {e: {} for e in ENGS}
        self.esem = {e: stack.enter_context(nc.semaphore("s_" + e)) for e in ENGS}
        self.nsem = len(ENGS)
        self.out_toks = []
        self.dtrks = []
        self.sem_pool = {"sp": [], "pool": []}

    def new_dsem(self, name):
        s = self.stack.enter_context(self.nc.semaphore("d%d_%s" % (self.nsem, name)))
        self.nsem += 1
        return s

    def _need(self, eng, waits, dep):
        sem, val = dep
        if eng == "pe" and sem is self.esem["pe"]:
            return
        k = id(sem)
        if self.known[eng].get(k, 0) >= val:
            return
        self.known[eng][k] = val
        waits[k] = (sem, val)

    def _deps(self, eng, reads, writes):
        waits = {}
        for t in reads:
            if t.w is not None:
                self._need(eng, waits, t.w)
            if t.excl:
                for dep in t.r.values():
                    self._need(eng, waits, dep)
        for t in writes:
            if t.w is not None:
                self._need(eng, waits, t.w)
            for dep in t.r.values():
                self._need(eng, waits, dep)
        return waits

    def _record(self, tok, reads, writes):
        for t in reads:
            t.r[id(tok[0])] = tok
        for t in writes:
            t.w = tok
            t.r = {}

    def op(self, eng, fn, reads=(), writes=()):
        waits = self._deps(eng, reads, writes)
        self.seq[eng] += 1
        tok = (self.esem[eng], self.seq[eng])
        self._record(tok, reads, writes)
        self.ops[eng].append((list(waits.values()), fn, (self.esem[eng], 1)))

    def dma(self, eng, fn, reads=(), writes=(), semtrk=None):
        if semtrk is None:
            semtrk = writes[0] if writes else reads[0]
        if semtrk.dsem is None:
            if self.sem_pool[eng]:
                semtrk.dsem, semtrk.dcnt = self.sem_pool[eng].pop()
            else:
                semtrk.dsem = self.new_dsem(semtrk.name)
            self.dtrks.append((semtrk, eng))
        waits = self._deps(eng, reads, writes)
        if semtrk.dcnt > 0:
            self._need(eng, waits, (semtrk.dsem, semtrk.dcnt))
        semtrk.dcnt += 16
        tok = (semtrk.dsem, semtrk.dcnt)
        self._record(tok, reads, writes)
        self.ops[eng].append((list(waits.values()), fn, (semtrk.dsem, 16)))
        return tok

    def wait_all(self, eng, toks):
        waits = {}
        for t in toks:
            self._need(eng, waits, t)
        self.ops[eng].append((list(waits.values()), None, None))

    def barrier(self):
        waits = {}
        for e in ENGS:
            if e != "sp" and self.seq[e] > 0:
                self._need("sp", waits, (self.esem[e], self.seq[e]))
        for t, _e in self.dtrks:
            if t.dcnt > 0:
                self._need("sp", waits, (t.dsem, t.dcnt))
        self.seq["sp"] += 1
        self.ops["sp"].append((list(waits.values()), lambda e: e.nop(), (self.esem["sp"], 1)))
        for t, e_ in self.dtrks:
            self.sem_pool[e_].append((t.dsem, t.dcnt))
            t.dsem = None
            t.dcnt = 0
        self.dtrks = []
        for e in ENGS:
            if e != "sp":
                w = {}
                self._need(e, w, (self.esem["sp"], self.seq["sp"]))
                self.ops[e].append((list(w.values()), None, None))

    def emit(self):
        nc = self.nc
        ops = self.ops
        self.ops = {e: [] for e in ENGS}
        with nc.Block() as block:
            def run(e, lst):
                for waits, fn, inc in lst:
                    for sem, val in waits:
                        e.wait_ge(sem, val)
                    if fn is not None:
                        fn(e).then_inc(inc[0], inc[1])

            @block.tensor
            def _(e):
                run(e, ops["pe"])

            @block.scalar
            def _(e):
                run(e, ops["act"])

            @block.vector
            def _(e):
                run(e, ops["dve"])

            @block.gpsimd
            def _(e):
                run(e, ops["pool"])

            @block.sync
            def _(e):
                run(e, ops["sp"])


class Rot:
    def __init__(self, alloc, name, n, shape, dt):
        self.bufs = [(alloc("%s%d" % (name, i), shape, dt), Trk("%s%d" % (name, i))) for i in range(n)]
        self.i = 0

    def next(self):
        b = self.bufs[self.i % len(self.bufs)]
        self.i += 1
        return b


FVCOLS = {}
_off = 0
for _n, _w in [("n1g", 8), ("n2g", 8), ("rgcw", 32), ("rgcb", 8), ("rgba", 16), ("rgbx", 16), ("rglam", 16),
               ("mlcw", 32), ("mlcb", 8), ("mlng", 8), ("mlsk", 8)]:
    FVCOLS[_n] = (_off, _w)
    _off += _w
NV = _off


def fm(vec):
    v = np.asarray(vec, np.float32)
    return np.ascontiguousarray(v.reshape(-1, 128).T)


def colchunks(w):
    K, N = w.shape
    a = w.reshape(K // 128, 128, N // 128, 128)
    a = a.transpose(2, 1, 0, 3)
    return np.ascontiguousarray(a.reshape(N // 128, 128, (K // 128) * 128))


def rowchunks(w):
    K, N = w.shape
    return np.ascontiguousarray(w.reshape(K // 128, 128, N).transpose(1, 0, 2))


def blockdiag128(blocks):
    nb, bi, bo = blocks.shape
    per = 128 // bi
    out = np.zeros((nb // per, 128, 128), np.float32)
    for b in range(nb):
        c, q = divmod(b, per)
        out[c, q * bi:(q + 1) * bi, q * bo:(q + 1) * bo] = blocks[b]
    return out


def prep_shared(inp):
    f = lambda k: np.asarray(inp[k], np.float32)
    sh = {}
    w_mod = f("w_mod")[0]
    sh["wmod"] = colchunks(w_mod)
    b_mod = f("b_mod")[0]
    sh["bmod2"] = np.ascontiguousarray(np.repeat(fm(b_mod)[:, :, None], 2, axis=2).reshape(128, 96))
    sh["wmodg"] = np.stack([rowchunks(w_mod[:, 2048:3072]), rowchunks(w_mod[:, 5120:6144])], 0)
    sh["bmodg"] = np.ascontiguousarray(np.broadcast_to(
        np.concatenate([b_mod[2048:3072], b_mod[5120:6144]])[None, :], (128, 2048)))
    fv = np.zeros((128, NV), np.float32)

    def put(name, arr):
        o, w = FVCOLS[name]
        assert arr.shape == (128, w), (name, arr.shape)
        fv[:, o:o + w] = arr
    put("n1g", fm(f("norm1_g")[0]))
    put("n2g", fm(f("norm2_g")[0]))
    cw = f("rg_conv_w")[0]
    put("rgcw", np.stack([fm(cw[j]) for j in range(4)], 2).reshape(128, 32))
    put("rgcb", fm(f("rg_conv_b")[0]))
    put("rgba", np.concatenate([fm(f("rg_ba")[0][d]) for d in range(2)], 1))
    put("rgbx", np.concatenate([fm(f("rg_bx")[0][d]) for d in range(2)], 1))
    put("rglam", np.concatenate([fm(f("rg_lambda")[0][d]) for d in range(2)], 1))
    cw = f("ml_conv_w")[0]
    put("mlcw", np.stack([fm(cw[j]) for j in range(4)], 2).reshape(128, 32))
    put("mlcb", fm(f("ml_conv_b")[0]))
    put("mlng", fm(f("ml_norm_g")[0]))
    put("mlsk", fm(f("ml_skip")[0]))
    sh["fv"] = fv
    sh["win"] = colchunks(f("w_in")[0])
    rg = []
    for d in range(2):
        for w in (f("rg_wa")[0][d], f("rg_wx")[0][d]):
            rg.append(blockdiag128(w).transpose(1, 0, 2))
    sh["rgbd"] = np.ascontiguousarray(np.concatenate(rg, 1))
    ml = [blockdiag128(f(k)[0]).transpose(1, 0, 2) for k in ("ml_wq", "ml_wk", "ml_wv")]
    sh["mlbd"] = np.ascontiguousarray(np.concatenate(ml, 1))
    wi, wf = f("ml_wi")[0], f("ml_wf")[0]
    wg = np.concatenate([wi[0], wf[0], wi[1], wf[1]], 1)
    sh["wgt"] = np.ascontiguousarray(wg.reshape(24, 128, 16).transpose(1, 0, 2))
    bi, bf = f("ml_bi")[0], f("ml_bf")[0]
    gb = np.stack([np.tile(np.concatenate([bi[d], bf[d]]), 2) for d in range(2)], 0)
    sh["gbias"] = np.ascontiguousarray(np.broadcast_to(gb[None], (128, 2, 16)))
    tri = np.zeros((2, 128, 128), np.float32)
    ii = np.arange(128)
    tri[0] = (ii[:, None] <= ii[None, :])
    tri[1] = (ii[:, None] >= ii[None, :])
    sh["tri"] = np.ascontiguousarray(tri.transpose(1, 0, 2))
    ident = np.eye(128, dtype=np.float32)
    sh["ident"] = ident
    sh["wbrg"] = colchunks(f("w_branch_rg")[0])
    sh["wbml"] = colchunks(f("w_branch_ml")[0])
    sh["wout"] = rowchunks(f("w_out")[0])
    sh["wffi"] = colchunks(f("w_ffn_in")[0])
    sh["wffo"] = rowchunks(f("w_ffn_out")[0])
    sh["fgrow"] = np.ascontiguousarray(np.broadcast_to(f("final_norm_g")[None, :], (128, 1024)))
    return sh


def build(debug=(), stop_after=None):
    nc = bass.Bass("TRN2", target_bir_lowering=False)
    din = lambda name, shape, dt=F32: nc.dram_tensor(name, list(shape), dt, kind="ExternalInput").ap()
    x_d = din("x", [NLAT, 1024])
    ctx_d = din("ctx", [NCTX, 1024])
    cv_d = din("cv", [128, 16])
    wmod_d = din("wmod", [48, 128, 1024])
    bmod2_d = din("bmod2", [128, 96])
    wmodg_d = din("wmodg", [2, 128, 8, 1024])
    bmodg_d = din("bmodg", [128, 2048])
    fv_d = din("fv", [128, NV])
    win_d = din("win", [48, 128, 1024])
    ident_d = din("ident", [128, 128])
    rgbd_d = din("rgbd", [128, 32, 128])
    mlbd_d = din("mlbd", [128, 24, 128])
    wgt_d = din("wgt", [128, 24, 16])
    gbias_d = din("gbias", [128, 2, 16])
    tri_d = din("tri", [128, 2, 128])
    wbrg_d = din("wbrg", [8, 128, 1024])
    wbml_d = din("wbml", [8, 128, 1024])
    wout_d = din("wout", [128, 8, 1024])
    wffi_d = din("wffi", [44, 128, 1024])
    wffo_d = din("wffo", [128, 22, 1024])
    fgrow_d = din("fgrow", [128, 1024])
    SPQ_d = nc.dram_tensor("SPQ", [17, 4, 128, 8, 256], BF16).ap()
    SPX_d = nc.dram_tensor("SPX", [16, 128, 8, 256], F32).ap()
    X1_d = nc.dram_tensor("X1", [NLAT, 1024], F32).ap()
    H2_d = nc.dram_tensor("H2", [16, 128, 8, 256], BF16).ap()
    HF_d = nc.dram_tensor("HF", [32, 128, 1024], F32).ap()
    if "yml" in debug:
        YML_d = nc.dram_tensor("dbg_yml", [8, 128, NLAT], BF16, kind="ExternalOutput").ap()
    else:
        YML_d = nc.dram_tensor("YML", [8, 128, NLAT], BF16).ap()
    YRG_d = nc.dram_tensor("YRG", [8, 128, NLAT], BF16).ap()
    out_d = nc.dram_tensor("out", [NLAT, 1024], F32, kind="ExternalOutput").ap()
    dbg_d = {}

    with ExitStack() as gst:
        P = Prog(nc, gst)
        galloc = lambda name, shape, dt: gst.enter_context(nc.sbuf_tensor(name, list(shape), dt))

        def dump(name, ap, trk, shape, dt=F32):
            if name not in debug:
                return
            d = nc.dram_tensor("dbg_" + name, list(shape), dt, kind="ExternalOutput").ap()
            dbg_d[name] = d
            P.out_toks.append(P.dma("sp", lambda e: e.dma_start(out=d, in_=ap), reads=[trk], semtrk=Trk("dbg" + name)))

        t_uT = [Trk("uT%d" % i) for i in range(34)]
        FV = galloc("FV", [128, NV], F32); t_FV = Trk("FV")
        MODT = galloc("MODT", [128, 96], F32); t_MODT = Trk("MODT")
        A1 = galloc("A1", [128, 16], F32); t_A1 = Trk("A1")
        A2 = galloc("A2", [128, 8], F32); t_A2 = Trk("A2")
        GROW = galloc("GROW", [128, 2, 1024], F32); t_GROW = Trk("GROW")
        KD = galloc("KD", [128, 16], F32); t_KD = Trk("KD")
        IDB = galloc("IDB", [128, 128], BF16); t_IDB = Trk("IDB")
        IDF = galloc("IDF", [128, 128], F32); t_IDF = Trk("IDF")
        t_YRG = [Trk("YRG%d" % c) for c in range(8)]
        ust = ExitStack()
        uT = ust.enter_context(nc.sbuf_tensor("uT", [128, 8, NT], BF16))

        def fvc(name, i=0, n=1):
            o, w = FVCOLS[name]
            return FV[:, o + i:o + i + n]

        with ExitStack() as st:
            sb = lambda name, shape, dt: st.enter_context(nc.sbuf_tensor(name, list(shape), dt))
            ps = lambda name, shape, dt: st.enter_context(nc.psum_tensor(name, list(shape), dt))
            CV = sb("CV", [128, 16], F32); t_CV = Trk("CV")
            S2 = sb("S2", [128, 16], F32); t_S2 = Trk("S2")
            SREP = sb("SREP", [128, 8, 128], F32); t_SREP = Trk("SREP")
            BM2 = sb("BM2", [128, 96], F32); t_BM2 = Trk("BM2")
            BMG = sb("BMG", [128, 2048], F32); t_BMG = Trk("BMG")
            TMPA = sb("TMPA", [128, 16], F32); t_TMPA = Trk("TMPA")
            TMPB = sb("TMPB", [128, 16], F32); t_TMPB = Trk("TMPB")
            WM = Rot(sb, "WM", 3, [128, 1024], F32)
            WGm = sb("WGm", [128, 8, 1024], F32); t_WGm = Trk("WGm")
            MODP = ps("MODP", [128, 512], F32); t_MODP = Trk("PS_MODP")
            GRP = ps("GRP", [128, 1024], F32); t_GRP = Trk("PS_GRP")

            P.dma("sp", lambda e: e.dma_start(out=CV[:], in_=cv_d), writes=[t_CV])
            P.dma("sp", lambda e: e.dma_start(out=FV[:], in_=fv_d), writes=[t_FV])
            P.dma("sp", lambda e: e.dma_start(out=BM2[:], in_=bmod2_d), writes=[t_BM2])
            P.dma("sp", lambda e: e.dma_start(out=BMG[:], in_=bmodg_d), writes=[t_BMG])
            P.dma("sp", lambda e: e.dma_start(out=IDF[:], in_=ident_d), writes=[t_IDF])
            P.dma("pool", lambda e: e.dma_start(out=IDB[:], in_=ident_d), writes=[t_IDB])
            P.op("act", lambda e: e.activation(out=S2[:], in_=CV[:], func=AF.Silu), reads=[t_CV], writes=[t_S2])
            for kc in range(8):
                P.op("dve", (lambda kc: lambda e: e.tensor_copy(
                    out=SREP[:, kc, :], in_=S2[:, 2 * kc:2 * kc + 1].to_broadcast([128, 128])))(kc),
                    reads=[t_S2], writes=[t_SREP])
            for n in range(48):
                wm, t_wm = WM.next()
                P.dma("sp", (lambda wm, n: lambda e: e.dma_start(out=wm[:], in_=wmod_d[n]))(wm, n), writes=[t_wm])
                for kc in range(8):
                    P.op("pe", (lambda wm, n, kc: lambda e: e.matmul(
                        MODP[:, 2 * n:2 * n + 2], lhsT=wm[:, kc * 128:(kc + 1) * 128], rhs=S2[:, 2 * kc:2 * kc + 2],
                        start=(kc == 0), stop=(kc == 7)))(wm, n, kc), reads=[t_wm, t_S2], writes=[t_MODP])
            P.op("dve", lambda e: e.tensor_tensor(out=MODT[:], in0=MODP[:, 0:96], in1=BM2[:], op=ALU.add),
                 reads=[t_MODP, t_BM2], writes=[t_MODT])
            P.op("dve", lambda e: e.tensor_scalar_add(out=TMPA[:], in0=MODT[:, 16:32], scalar1=1.0),
                 reads=[t_MODT], writes=[t_TMPA])
            for j in range(2):
                P.op("dve", (lambda j: lambda e: e.tensor_tensor(
                    out=A1[:, j:16:2], in0=TMPA[:, j:16:2], in1=fvc("n1g", 0, 8), op=ALU.mult))(j),
                    reads=[t_TMPA, t_FV], writes=[t_A1])
            P.op("dve", lambda e: e.tensor_scalar_add(out=TMPB[:, 0:8], in0=MODT[:, 64:80:2], scalar1=1.0),
                 reads=[t_MODT], writes=[t_TMPB])
            P.op("dve", lambda e: e.tensor_tensor(out=A2[:], in0=TMPB[:, 0:8], in1=fvc("n2g", 0, 8), op=ALU.mult),
                 reads=[t_TMPB, t_FV], writes=[t_A2])
            for g in range(2):
                P.dma("sp", (lambda g: lambda e: e.dma_start(out=WGm[:], in_=wmodg_d[g]))(g), writes=[t_WGm])
                for half in range(2):
                    for kc in range(8):
                        P.op("pe", (lambda half, kc: lambda e: e.matmul(
                            GRP[:, half * 512:(half + 1) * 512], lhsT=SREP[:, kc, :],
                            rhs=WGm[:, kc, half * 512:(half + 1) * 512], start=(kc == 0), stop=(kc == 7)))(half, kc),
                            reads=[t_SREP, t_WGm], writes=[t_GRP])
                P.op("dve", (lambda g: lambda e: e.tensor_tensor(
                    out=GROW[:, g, :], in0=GRP[:], in1=BMG[:, g * 1024:(g + 1) * 1024], op=ALU.add))(g),
                    reads=[t_GRP, t_BMG], writes=[t_GROW])
            P.op("act", lambda e: e.activation(out=TMPA[:], in_=fvc("rglam", 0, 16), func=AF.Exp, scale=-1.0),
                 reads=[t_FV], writes=[t_TMPA])
            P.op("act", lambda e: e.activation(out=TMPB[:], in_=TMPA[:], func=AF.Ln, bias=1.0),
                 reads=[t_TMPA], writes=[t_TMPB])
            P.op("dve", lambda e: e.tensor_scalar(out=KD[:], in0=TMPB[:], scalar1=-8.0, scalar2=None, op0=ALU.mult),
                 reads=[t_TMPB], writes=[t_KD])
            dump("modT", MODT[:], t_MODT, [128, 96])
            dump("grow", GROW[:], t_GROW, [128, 2, 1024])
            dump("kd", KD[:], t_KD, [128, 16])
            P.barrier()
            P.emit()

        with ExitStack() as st:
            sb = lambda name, shape, dt: st.enter_context(nc.sbuf_tensor(name, list(shape), dt))
            ps = lambda name, shape, dt: st.enter_context(nc.psum_tensor(name, list(shape), dt))
            XT = Rot(sb, "XT", 3, [128, 1024], F32)
            XN = Rot(sb, "XN", 2, [128, 1024], BF16)
            TP = Rot(ps, "PS_TP", 2, [128, 8, 128], BF16)
            ST = sb("ST", [128, 34, 4], F32)
            for i in range(34):
                t_st = Trk("st%d" % i)
                xt, t_xt = XT.next()
                xn, t_xn = XN.next()
                tp, t_tp = TP.next()
                src = ctx_d[i * 128:(i + 1) * 128, :] if i < 2 else x_d[(i - 2) * 128:(i - 1) * 128, :]
                j = 1 if i < 2 else 0
                P.dma("sp", (lambda xt, src: lambda e: e.dma_start(out=xt[:], in_=src))(xt, src), writes=[t_xt])
                P.op("act", (lambda xt, xn, i: lambda e: e.activation(
                    out=xn[:], in_=xt[:], func=AF.Square, accum_out=ST[:, i, 0:1]))(xt, xn, i),
                    reads=[t_xt], writes=[t_xn, t_st])
                P.op("act", (lambda i: lambda e: e.activation(
                    out=ST[:, i, 1:2], in_=ST[:, i, 0:1], func=AF.Sqrt, scale=1.0 / 1024.0, bias=EPS))(i),
                    reads=[t_st], writes=[t_st])
                P.op("dve", (lambda i: lambda e: e.reciprocal(out=ST[:, i, 2:3], in_=ST[:, i, 1:2]))(i),
                     reads=[t_st], writes=[t_st])
                P.op("act", (lambda xt, xn, i: lambda e: e.activation(
                    out=xn[:], in_=xt[:], func=AF.Copy, scale=ST[:, i, 2:3]))(xt, xn, i),
                    reads=[t_xt, t_st], writes=[t_xn])
                for c in range(8):
                    P.op("pe", (lambda xn, tp, c: lambda e: e.transpose(
                        out=tp[:, c, :], in_=xn[:, c * 128:(c + 1) * 128], identity=IDB[:]))(xn, tp, c),
                        reads=[t_xn, t_IDB], writes=[t_tp])
                for c in range(8):
                    if i < 2:
                        dst = uT[:, c, i * 128:(i + 1) * 128]
                        src_tp = tp[:, c, :]
                    else:
                        r0 = 2 * (i - 2)
                        dst = uT[:, c, 256:NT].rearrange("p (j r) -> p r j", r=64)[:, r0:r0 + 2, :]
                        src_tp = tp[:, c, :].rearrange("p (r j) -> p r j", j=64)
                    if c % 2 == 0:
                        P.op("dve", (lambda tp, c, dst, j: lambda e: e.tensor_scalar(
                            out=dst, in0=tp, scalar1=A1[:, 2 * c + j:2 * c + j + 1],
                            scalar2=MODT[:, 2 * c + j:2 * c + j + 1], op0=ALU.mult, op1=ALU.add))(src_tp, c, dst, j),
                            reads=[t_tp, t_A1, t_MODT], writes=[t_uT[i]])
                    else:
                        P.op("act", (lambda tp, c, dst, j: lambda e: e.activation(
                            out=dst, in_=tp, func=AF.Identity, scale=A1[:, 2 * c + j:2 * c + j + 1],
                            bias=MODT[:, 2 * c + j:2 * c + j + 1]))(src_tp, c, dst, j),
                            reads=[t_tp, t_A1, t_MODT], writes=[t_uT[i]])
            if "uT" in debug:
                UD = sb("UD", [128, 8, 512], F32); t_UD = Trk("UD")
                P.op("dve", lambda e: e.tensor_copy(out=UD[:], in_=uT[:, :, 128:640]), reads=t_uT[1:5], writes=[t_UD])
                dump("uT", UD[:], t_UD, [128, 8, 512])
            P.barrier()
            P.emit()


        CT0, LT0, PEND = 2, 261, 4357
        with ExitStack() as st:
            sb = lambda name, shape, dt: st.enter_context(nc.sbuf_tensor(name, list(shape), dt))
            ps = lambda name, shape, dt: st.enter_context(nc.psum_tensor(name, list(shape), dt))
            RGBD = sb("RGBD", [128, 32, 128], BF16); t_RGBD = Trk("RGBD")
            P.dma("pool", lambda e: e.dma_start(out=RGBD[:], in_=rgbd_d, max_dma_last_dim=4096), writes=[t_RGBD])
            RX = sb("RX", [128, 4360], F32); t_RX = Trk("RX")
            XC = sb("XC", [128, 4360], F32); t_XC = Trk("XC")
            XCB = sb("XCB", [128, 4360], BF16); t_XCB = Trk("XCB")
            TMP = sb("TMP", [128, 2180], F32); t_TMP = Trk("TMP")
            BB = [sb("B0", [128, 4360], F32), sb("B1", [128, 4360], F32)]; t_BB = [Trk("B0"), Trk("B1")]
            WR = Rot(sb, "WR", 4, [128, 8, 128], BF16)
            PJ = Rot(ps, "PS_PJ", 3, [128, 512], F32)
            GA = Rot(ps, "PS_GA", 4, [128, 512], F32)
            GL = Rot(sb, "GL", 2, [128, 512], F32)
            TS = Rot(sb, "TS", 2, [128, 512], F32)
            YS = Rot(sb, "YS", 1, [128, NLAT], BF16)
            for (a, b) in ((0, 2), (258, 261), (4357, 4360)):
                P.op("dve", (lambda a, b: lambda e: e.memset(RX[:, a:b], 0.0))(a, b), writes=[t_RX])
            for d in range(2):
                P.op("pool", (lambda d: lambda e: e.memset(BB[d][:, 256:264], 0.0))(d), writes=[t_BB[d]])
            blocks = [(0, 256, CT0)] + [(256 + 512 * b, 512, LT0 + 512 * b) for b in range(8)]

            def ut_trks(t0, n):
                return t_uT[t0 // 128:(t0 + n - 1) // 128 + 1]

            def ut_nat(kc, t0, n):
                if t0 < 256:
                    return uT[:, kc, t0:t0 + n]
                r0 = (t0 - 256) // 64
                return uT[:, kc, 256:NT].rearrange("p (j r) -> p r j", r=64)[:, r0:r0 + n // 64, :]

            def rev(ap2d):
                n = ap2d.shape[1]
                return bass.AP(ap2d.tensor, ap2d.offset + (n - 1), [list(ap2d.ap[0]), [-1, n]])

            def load_wr(c):
                wrx, t_wrx = WR.next()
                wrg, t_wrg = WR.next()
                P.dma("pool", (lambda w, c: lambda e: e.dma_start(
                    out=w[:], in_=win_d[c].rearrange("p (k j) -> p k j", j=128)))(wrx, c), writes=[t_wrx])
                P.dma("pool", (lambda w, c: lambda e: e.dma_start(
                    out=w[:], in_=win_d[8 + c].rearrange("p (k j) -> p k j", j=128)))(wrg, c), writes=[t_wrg])
                return wrx, t_wrx, wrg, t_wrg

            wr_next = load_wr(0)
            for c in range(8):
                wrx, t_wrx, wrg, t_wrg = wr_next
                if c + 1 < 8:
                    wr_next = load_wr(c + 1)
                if c > 0:
                    P.op("dve", lambda e: e.memset(RX[:, 258:261], 0.0), writes=[t_RX])
                pj, t_pj = PJ.next()
                for kc in range(8):
                    P.op("pe", (lambda pj, wrx, kc: lambda e: e.matmul(
                        pj[:, 0:256], lhsT=wrx[:, kc, :], rhs=uT[:, kc, 0:256], start=(kc == 0), stop=(kc == 7)))(
                        pj, wrx, kc), reads=[t_wrx] + t_uT[0:2], writes=[t_pj])
                P.op("act", (lambda pj: lambda e: e.activation(
                    out=RX[:, CT0:CT0 + 256], in_=pj[:, 0:256], func=AF.Copy))(pj), reads=[t_pj], writes=[t_RX])
                for b in range(8):
                    pj, t_pj = PJ.next()
                    for kc in range(8):
                        P.op("pe", (lambda pj, wrx, kc, b: lambda e: e.matmul(
                            pj[:], lhsT=wrx[:, kc, :], rhs=uT[:, kc, 256 + b * 512:256 + (b + 1) * 512], start=(kc == 0), stop=(kc == 7)))(
                            pj, wrx, kc, b), reads=[t_wrx] + t_uT[2:34], writes=[t_pj])
                    P.op("act", (lambda pj, b: lambda e: e.activation(
                        out=RX[:, LT0:LT0 + 4096].rearrange("p (r j) -> p j r", j=64)[:, 8 * b:8 * b + 8, :],
                        in_=pj[:].rearrange("p (j r) -> p j r", r=64), func=AF.Copy))(pj, b), reads=[t_pj], writes=[t_RX])
                L = PEND - 2
                P.op("dve", (lambda c: lambda e: e.tensor_scalar(
                    out=XC[:, 2:PEND], in0=RX[:, 0:L], scalar1=fvc("rgcw", c * 4), scalar2=fvc("rgcb", c),
                    op0=ALU.mult, op1=ALU.add))(c), reads=[t_RX, t_FV], writes=[t_XC])
                for j in range(1, 4):
                    P.op("dve", (lambda c, j: lambda e: e.scalar_tensor_tensor(
                        out=XC[:, 2:PEND], in0=RX[:, j:j + L], scalar=fvc("rgcw", c * 4 + j), in1=XC[:, 2:PEND],
                        op0=ALU.mult, op1=ALU.add))(c, j), reads=[t_RX, t_FV, t_XC], writes=[t_XC])
                P.op("act", lambda e: e.activation(out=XCB[:, 2:PEND], in_=XC[:, 2:PEND], func=AF.Copy),
                     reads=[t_XC], writes=[t_XCB])
                if c == 3:
                    dump("xc", XC[:], t_XC, [128, 4360])
                for d in range(2):
                    Bd, t_Bd = BB[d], t_BB[d]
                    for (t0, n, pos) in blocks:
                        ga, t_ga = GA.next()
                        gx, t_gx = GA.next()
                        P.op("pe", (lambda ga, d, c, n, pos: lambda e: e.matmul(
                            ga[:, 0:n], lhsT=RGBD[:, (d * 2) * 8 + c, :], rhs=XCB[:, pos:pos + n], start=True, stop=True))(
                            ga, d, c, n, pos), reads=[t_RGBD, t_XCB], writes=[t_ga])
                        P.op("pe", (lambda gx, d, c, n, pos: lambda e: e.matmul(
                            gx[:, 0:n], lhsT=RGBD[:, (d * 2 + 1) * 8 + c, :], rhs=XCB[:, pos:pos + n], start=True, stop=True))(
                            gx, d, c, n, pos), reads=[t_RGBD, t_XCB], writes=[t_gx])
                        P.op("act", (lambda ga, d, c, n, pos: lambda e: e.activation(
                            out=RX[:, pos:pos + n], in_=ga[:, 0:n], func=AF.Sigmoid, bias=fvc("rgba", d * 8 + c)))(
                            ga, d, c, n, pos), reads=[t_ga, t_FV], writes=[t_RX])
                        P.op("act", (lambda gx, Bd, d, c, n, pos: lambda e: e.activation(
                            out=Bd[:, pos:pos + n], in_=gx[:, 0:n], func=AF.Sigmoid, bias=fvc("rgbx", d * 8 + c)))(
                            gx, Bd, d, c, n, pos), reads=[t_gx, t_FV], writes=[t_Bd])
                    P.op("act", (lambda d, c: lambda e: e.activation(
                        out=RX[:, 2:PEND], in_=RX[:, 2:PEND], func=AF.Exp, scale=KD[:, d * 8 + c:d * 8 + c + 1]))(d, c),
                        reads=[t_RX, t_KD], writes=[t_RX])
                    P.op("dve", (lambda Bd: lambda e: e.tensor_tensor(
                        out=Bd[:, 2:PEND], in0=Bd[:, 2:PEND], in1=XC[:, 2:PEND], op=ALU.mult))(Bd),
                        reads=[t_Bd, t_XC], writes=[t_Bd])
                    for (ra, rb) in ((2, 2180), (2180, PEND)):
                        P.op("act", (lambda ra, rb: lambda e: e.activation(
                            out=TMP[:, 0:rb - ra], in_=RX[:, ra:rb], func=AF.Square))(ra, rb), reads=[t_RX], writes=[t_TMP])
                        P.op("act", (lambda ra, rb: lambda e: e.activation(
                            out=TMP[:, 0:rb - ra], in_=TMP[:, 0:rb - ra], func=AF.Sqrt, scale=-1.0, bias=1.0))(ra, rb),
                            reads=[t_TMP], writes=[t_TMP])
                        P.op("dve", (lambda Bd, ra, rb: lambda e: e.tensor_tensor(
                            out=Bd[:, ra:rb], in0=Bd[:, ra:rb], in1=TMP[:, 0:rb - ra], op=ALU.mult))(Bd, ra, rb),
                            reads=[t_Bd, t_TMP], writes=[t_Bd])
                    f_ = (lambda ap: ap) if d == 0 else rev
                    c0, c1 = CT0, CT0 + 256
                    l0, l1 = LT0, LT0 + 4096
                    P.op("dve", (lambda Bd, f_: lambda e: e.tensor_tensor_scan(
                        out=f_(Bd[:, c0:c1]), data0=f_(RX[:, c0:c1]), data1=f_(Bd[:, c0:c1]), initial=0.0,
                        op0=ALU.mult, op1=ALU.add))(Bd, f_), reads=[t_RX, t_Bd], writes=[t_Bd])
                    ini = (c1 - 1) if d == 0 else c0
                    P.op("dve", (lambda Bd, f_, ini: lambda e: e.tensor_tensor_scan(
                        out=f_(Bd[:, l0:l1]), data0=f_(RX[:, l0:l1]), data1=f_(Bd[:, l0:l1]), initial=Bd[:, ini:ini + 1],
                        op0=ALU.mult, op1=ALU.add))(Bd, f_, ini), reads=[t_RX, t_Bd], writes=[t_Bd])
                ys, t_ys = YS.next()
                for b in range(8):
                    pj, t_pj = PJ.next()
                    gl, t_gl = GL.next()
                    ts, t_ts = TS.next()
                    for kc in range(8):
                        P.op("pe", (lambda pj, wrg, kc, b: lambda e: e.matmul(
                            pj[:], lhsT=wrg[:, kc, :], rhs=uT[:, kc, 256 + b * 512:256 + (b + 1) * 512], start=(kc == 0), stop=(kc == 7)))(
                            pj, wrg, kc, b), reads=[t_wrg] + t_uT[2:34], writes=[t_pj])
                    P.op("act", (lambda pj, gl: lambda e: e.activation(out=gl[:], in_=pj[:], func=AF.Gelu))(pj, gl),
                         reads=[t_pj], writes=[t_gl])
                    P.op("dve", (lambda ts, b: lambda e: e.tensor_tensor(
                        out=ts[:].rearrange("p (j r) -> p j r", r=64),
                        in0=BB[0][:, LT0:LT0 + 4096].rearrange("p (r j) -> p j r", j=64)[:, 8 * b:8 * b + 8, :],
                        in1=BB[1][:, LT0:LT0 + 4096].rearrange("p (r j) -> p j r", j=64)[:, 8 * b:8 * b + 8, :], op=ALU.add))(ts, b),
                        reads=t_BB, writes=[t_ts])
                    P.op("dve", (lambda ys, ts, gl, b: lambda e: e.tensor_tensor(
                        out=ys[:, b * 512:(b + 1) * 512], in0=ts[:], in1=gl[:], op=ALU.mult))(ys, ts, gl, b),
                        reads=[t_ts, t_gl], writes=[t_ys])
                P.dma("sp", (lambda ys, c: lambda e: e.dma_start(out=YRG_d[c], in_=ys[:]))(ys, c), reads=[t_ys], writes=[t_YRG[c]],
                      semtrk=t_ys)
                if c == 3:
                    if "hrg" in debug:
                        dump("hrg", HD[:], t_HD, [128, NLAT])
                    dump("yrg", ys[:], t_ys, [128, NLAT], BF16)
            P.barrier()
            P.emit()
        if stop_after == "rg":
            P.wait_all("sp", P.out_toks)
            P.barrier()
            P.emit()
            ust.close()
            return nc, dbg_d

        t_HF = [Trk("HF%d" % i) for i in range(32)]
        t_YML = [Trk("YML%d" % i) for i in range(16)]
        t_SPQ = [[Trk("SPQ%d_%d" % (g, i)) for i in range(4)] for g in range(17)]
        t_SPX = [Trk("SPX%d" % g) for g in range(16)]
        t_spd = [Trk("spd%d" % i) for i in range(5)]
        with ExitStack() as st:
            sb = lambda name, shape, dt: st.enter_context(nc.sbuf_tensor(name, list(shape), dt))
            ps = lambda name, shape, dt: st.enter_context(nc.psum_tensor(name, list(shape), dt))
            MLBD = sb("MLBD", [128, 24, 128], BF16); t_MLBD = Trk("MLBD")
            WGT = sb("WGT", [128, 24, 16], BF16); t_WGT = Trk("WGT")
            GBI = sb("GBI", [128, 2, 16], F32); t_GBI = Trk("GBI")
            TRI = sb("TRI", [128, 2, 128], F32); t_TRI = Trk("TRI")
            ONES = sb("ONES", [128, 128], F32); t_ONES = Trk("ONES")
            P.dma("pool", lambda e: e.dma_start(out=MLBD[:], in_=mlbd_d, max_dma_last_dim=4096), writes=[t_MLBD])
            P.dma("pool", lambda e: e.dma_start(out=WGT[:], in_=wgt_d), writes=[t_WGT])
            P.dma("sp", lambda e: e.dma_start(out=GBI[:], in_=gbias_d), writes=[t_GBI])
            P.dma("sp", lambda e: e.dma_start(out=TRI[:], in_=tri_d), writes=[t_TRI])
            P.op("dve", lambda e: e.memset(ONES[:], 1.0), writes=[t_ONES])
            B_PM = ps("B_PM", [128, 512], F32); t_PM = Trk("PS_PM")
            B_QK = ps("B_QK", [128, 512], F32); t_PQ = Trk("PS_PQ")
            B_VO = ps("B_VO", [128, 512], F32); t_PV = Trk("PS_PV")
            B_GP = ps("B_GP", [128, 512], F32); t_GP = Trk("PS_GP")
            B_N = ps("B_N", [128, 4, 512], F32); t_N = [Trk("PS_N%d" % i) for i in range(4)]
            BU = [B_PM, B_QK, B_VO, B_GP]; t_BU = [t_PM, t_PQ, t_PV, t_GP]
            VT = Rot(sb, "VT", 2, [128, 256], BF16)
            XM = sb("XM", [128, 8, 256], F32); t_XM = [Trk("XM%d" % c) for c in range(8)]
            XMB = sb("XMB", [128, 8, 256], BF16); t_XMB = [Trk("XMB%d" % c) for c in range(8)]
            UMB = sb("UMB", [128, 8, 256], BF16); t_UMB = [Trk("UMB%d" % c) for c in range(8)]
            QT = sb("QT", [128, 8, 256], BF16); t_QT = [Trk("QT%d" % c) for c in range(8)]
            KT = sb("KT", [128, 8, 256], BF16); t_KT = [Trk("KT%d" % c) for c in range(8)]
            GF = sb("GF", [8, 256], F32); t_GF = Trk("GF")
            GG = sb("GG", [128, 16], F32); t_GG = Trk("GG")
            GE = sb("GE", [128, 8], F32); t_GE = Trk("GE")
            GLn = sb("GLn", [128, 8], F32); t_GLn = Trk("GLn")
            GT2 = sb("GT2", [128, 8], F32); t_GT2 = Trk("GT2")
            EB = sb("EB", [128, 8], F32); t_EB = Trk("EB")
            WS = sb("WS", [128, 8], F32); t_WS = Trk("WS")
            EBL = sb("EBL", [128, 8], F32); t_EBL = Trk("EBL")
            DS = sb("DS", [128, 4, 2, 257], F32); t_DS = [Trk("DS%d" % h) for h in range(4)]
            DB = sb("DB", [128, 4, 2, 258], BF16); t_DB = [Trk("DB%d" % h) for h in range(4)]
            VX = sb("VX", [128, 2, 4, 258], BF16); t_VX = [Trk("VX%d" % i) for i in range(2)]
            KTM = sb("KTM", [128, 2, 1024], BF16); t_KTM = [Trk("KTM%d" % i) for i in range(2)]
            STt = sb("STt", [128, 2, 4, 128], BF16); t_STt = [Trk("STt%d" % i) for i in range(2)]
            E1 = sb("E1", [128, 8, 4], F32); t_E1 = Trk("E1")
            HH = Rot(sb, "HH", 1, [128, 1024], F32)

            def grp_rhs(kc, g):
                if g == 0:
                    return uT[:, kc, 0:256]
                gi = g - 1
                return uT[:, kc, 256 + gi * 256:256 + (gi + 1) * 256]

            def grp_trks(g):
                return t_uT[0:2] if g == 0 else t_uT[2:34]

            def gates_post(d):
                P.op("act", lambda e: e.activation(out=GF[:], in_=B_GP[0:8, 0:256], func=AF.Copy), reads=[t_GP], writes=[t_GF])
                for ch in range(2):
                    P.op("pe", (lambda ch: lambda e: e.transpose(
                        out=B_GP[:, 256 + ch * 8:256 + ch * 8 + 8], in_=GF[0:8, ch * 128:(ch + 1) * 128], identity=IDF[0:8, 0:8]))(ch),
                        reads=[t_GF, t_IDF], writes=[t_GP])
                P.op("dve", (lambda d: lambda e: e.tensor_tensor(
                    out=GG[:], in0=B_GP[:, 256:272], in1=GBI[:, d, :], op=ALU.add))(d), reads=[t_GP, t_GBI], writes=[t_GG])
                GGv = GG[:].rearrange("t (c k) -> t c k", k=8)
                P.op("act", lambda e: e.activation(
                    out=GE[:].rearrange("t (c h) -> t c h", h=4), in_=GGv[:, :, 4:8], func=AF.Exp, scale=-1.0),
                    reads=[t_GG], writes=[t_GE])
                P.op("act", lambda e: e.activation(out=GLn[:], in_=GE[:], func=AF.Ln, bias=1.0), reads=[t_GE], writes=[t_GLn])
                P.op("pe", (lambda d: lambda e: e.matmul(
                    B_GP[:, 288:296], lhsT=TRI[:, d, :], rhs=GLn[:], start=True, stop=True))(d),
                    reads=[t_TRI, t_GLn], writes=[t_GP])
                P.op("pe", lambda e: e.matmul(B_GP[:, 304:312], lhsT=ONES[:], rhs=GLn[:], start=True, stop=True),
                     reads=[t_ONES, t_GLn], writes=[t_GP])
                P.op("act", lambda e: e.activation(out=EB[:], in_=B_GP[:, 288:296], func=AF.Exp, scale=-1.0),
                     reads=[t_GP], writes=[t_EB])
                P.op("dve", lambda e: e.tensor_tensor(
                    out=GT2[:].rearrange("t (c h) -> t c h", h=4), in0=B_GP[:, 288:296].rearrange("t (c h) -> t c h", h=4),
                    in1=GGv[:, :, 0:4], op=ALU.add), reads=[t_GP, t_GG], writes=[t_GT2])
                P.op("act", lambda e: e.activation(out=WS[:], in_=GT2[:], func=AF.Exp), reads=[t_GT2], writes=[t_WS])
                P.op("act", lambda e: e.activation(out=EBL[:], in_=B_GP[:, 304:312], func=AF.Exp, scale=-1.0),
                     reads=[t_GP], writes=[t_EBL])
            def rec_group(g, d, QT, KT, XMB, UMB, t_QT, t_KT, t_XMB, t_UMB, XM, t_XM, hs):
                lat = g > 0
                gi = g - 1
                have_state = hs[0]
                for ch in range(2):
                    cols = slice(ch * 128, (ch + 1) * 128)
                    for c in range(8):
                        h, half = divmod(c, 2)
                        bk = h // 2
                        off = (h % 2) * 256 + half * 128
                        P.op("pe", (lambda c, bk, off, cols: lambda e: e.matmul(
                            B_N[:, bk, off:off + 128], lhsT=UMB[:, c, cols], rhs=MLBD[:, 16 + c, :], start=True, stop=True))(c, bk, off, cols),
                            reads=[t_UMB[c], t_MLBD], writes=[t_N[bk]])
                        P.op("pe", (lambda c, bk, off, cols: lambda e: e.matmul(
                            B_N[:, 2 + bk, off:off + 128], lhsT=XMB[:, c, cols], rhs=MLBD[:, 8 + c, :], start=True, stop=True))(c, bk, off, cols),
                            reads=[t_XMB[c], t_MLBD], writes=[t_N[2 + bk]])
                    for h in range(4):
                        bk = h // 2
                        off = (h % 2) * 256
                        P.op("dve", (lambda ch, h, bk, off: lambda e: e.tensor_scalar(
                            out=VX[:, ch, h, 0:256], in0=B_N[:, bk, off:off + 256], scalar1=WS[:, ch * 4 + h:ch * 4 + h + 1],
                            scalar2=None, op0=ALU.mult))(ch, h, bk, off), reads=[t_N[bk], t_WS], writes=[t_VX[ch]])
                    P.op("act", (lambda ch: lambda e: e.activation(
                        out=VX[:, ch, :, 256:257], in_=WS[:, ch * 4:ch * 4 + 4].unsqueeze(2), func=AF.Copy))(ch), reads=[t_WS], writes=[t_VX[ch]])
                    for bk in range(2):
                        P.op("act", (lambda ch, bk: lambda e: e.activation(
                            out=KTM[:, ch, bk * 512:(bk + 1) * 512], in_=B_N[:, 2 + bk, :], func=AF.Copy, scale=1.0 / 16.0))(ch, bk),
                            reads=[t_N[2 + bk]], writes=[t_KTM[ch]])
                    for h in range(4):
                        for half in range(2):
                            c = 2 * h + half
                            P.op("pe", (lambda c, h, half, cols: lambda e: e.matmul(
                                B_GP[:, h * 128:(h + 1) * 128], lhsT=KT[:, c, cols], rhs=QT[:, c, cols], start=(half == 0), stop=(half == 1)))(c, h, half, cols),
                                reads=[t_KT[c], t_QT[c]], writes=[t_GP])
                    P.op("dve", (lambda ch, d: lambda e: e.tensor_tensor(
                        out=STt[:, ch], in0=B_GP[:].rearrange("p (h t) -> p h t", t=128),
                        in1=TRI[:, d, :].unsqueeze(1).to_broadcast([128, 4, 128]), op=ALU.mult))(ch, d),
                        reads=[t_GP, t_TRI], writes=[t_STt[ch]])
                chs = (0, 1) if d == 0 else (1, 0)
                if d == 1 and lat:
                    yg, t_yg = YG.next()
                for ch in chs:
                    cols = slice(ch * 128, (ch + 1) * 128)
                    if d == 1 and lat:
                        for cq in (gi * 2 + ch, gi * 2 + ch - 1):
                            if cq >= 0 and cq not in hft_map:
                                hb, t_hb = HFt.next()
                                P.dma("sp", (lambda hb, cq: lambda e: e.dma_start(out=hb[:], in_=HF_d[cq]))(hb, cq),
                                      reads=[t_HF[cq]], writes=[t_hb])
                                hft_map[cq] = (hb, t_hb)
                    last_chunk = (d == 0 and g == 16 and ch == 1) or (d == 1 and g == 1 and ch == 0)
                    if not last_chunk:
                        for r in range(2):
                            for hh2 in range(2):
                                h = 2 * r + hh2
                                for half in range(2):
                                    bi_ = hh2 * 2 + half
                                    P.op("pe", (lambda ch, h, half, bi_: lambda e: e.matmul(
                                        BU[bi_][:, 0:257], lhsT=KTM[:, ch, h * 256 + half * 128:h * 256 + (half + 1) * 128],
                                        rhs=VX[:, ch, h, 0:257], start=True, stop=True))(ch, h, half, bi_),
                                        reads=[t_KTM[ch], t_VX[ch]], writes=[t_BU[bi_]])
                            for hh2 in range(2):
                                h = 2 * r + hh2
                                for half in range(2):
                                    bi_ = hh2 * 2 + half
                                    if have_state:
                                        P.op("dve", (lambda h, half, bi_: lambda e: e.tensor_tensor(
                                            out=DS[:, h, half, :], in0=DS[:, h, half, :], in1=BU[bi_][:, 0:257], op=ALU.add))(h, half, bi_),
                                            reads=[t_DS[h], t_BU[bi_]], writes=[t_DS[h]])
                                    else:
                                        P.op("dve", (lambda h, half, bi_: lambda e: e.tensor_copy(
                                            out=DS[:, h, half, :], in_=BU[bi_][:, 0:257]))(h, half, bi_),
                                            reads=[t_BU[bi_]], writes=[t_DS[h]])
                    if lat:
                        hh, t_hh = HH.next()
                        for h in range(4):
                            P.op("pe", (lambda ch, h, hs: lambda e: e.matmul(
                                B_N[:, h, 0:257], lhsT=STt[:, ch, h, :], rhs=VX[:, ch, h, 0:257], start=True, stop=(not hs)))(ch, h, have_state),
                                reads=[t_STt[ch], t_VX[ch]], writes=[t_N[h]])
                            if have_state:
                                for half in range(2):
                                    c = 2 * h + half
                                    P.op("pe", (lambda c, h, half, cols: lambda e: e.matmul(
                                        B_N[:, h, 0:257], lhsT=QT[:, c, cols], rhs=DB[:, h, half, 0:257], start=False, stop=(half == 1)))(c, h, half, cols),
                                        reads=[t_QT[c], t_DB[h]], writes=[t_N[h]])
                        e0 = ch * 4
                        P.op("dve", (lambda ch: lambda e: e.tensor_tensor(
                            out=E1[:, 0:4, 0], in0=B_N[:, :, 256], in1=EB[:, ch * 4:ch * 4 + 4], op=ALU.mult))(ch),
                            reads=t_N + [t_EB], writes=[t_E1])
                        P.op("dve", lambda e: e.tensor_scalar(
                            out=E1[:, 0:4, 1], in0=E1[:, 0:4, 0], scalar1=-1.0, scalar2=1.0, op0=ALU.mult, op1=ALU.max),
                            reads=[t_E1], writes=[t_E1])
                        P.op("dve", lambda e: e.scalar_tensor_tensor(
                            out=E1[:, 0:4, 2], in0=E1[:, 0:4, 0], scalar=1.0, in1=E1[:, 0:4, 1], op0=ALU.max, op1=ALU.max),
                            reads=[t_E1], writes=[t_E1])
                        P.op("dve", lambda e: e.reciprocal(out=E1[:, 0:4, 3], in_=E1[:, 0:4, 2]), reads=[t_E1], writes=[t_E1])
                        P.op("dve", (lambda ch: lambda e: e.tensor_tensor(
                            out=E1[:, 4:8, 0], in0=E1[:, 0:4, 3], in1=EB[:, ch * 4:ch * 4 + 4], op=ALU.mult))(ch),
                            reads=[t_E1, t_EB], writes=[t_E1])
                        for h in range(4):
                            P.op("act", (lambda hh, h: lambda e: e.activation(
                                out=hh[:, h * 256:(h + 1) * 256], in_=B_N[:, h, 0:256], func=AF.Copy, scale=E1[:, 4 + h, 0:1]))(hh, h),
                                reads=[t_N[h], t_E1], writes=[t_hh])
                    if not last_chunk:
                        for h in range(4):
                            idx = ch * 4 + h
                            P.op("act", (lambda h, idx: lambda e: e.activation(
                                out=DB[:, h, :, 0:257], in_=DS[:, h, :, :], func=AF.Copy, scale=EBL[:, idx:idx + 1]))(h, idx),
                                reads=[t_DS[h], t_EBL], writes=[t_DB[h]])
                            P.op("dve", (lambda h, idx: lambda e: e.tensor_scalar(
                                out=DS[:, h, :, :], in0=DS[:, h, :, :], scalar1=EBL[:, idx:idx + 1], scalar2=None, op0=ALU.mult))(h, idx),
                                reads=[t_DS[h], t_EBL], writes=[t_DS[h]])
                    have_state = True; hs[0] = True
                    if not lat:
                        continue
                    cg = gi * 2 + ch
                    if d == 0:
                        P.dma("sp", (lambda hh, cg: lambda e: e.dma_start(out=HF_d[cg], in_=hh[:]))(hh, cg),
                              reads=[t_hh], writes=[t_HF[cg]], semtrk=t_hh)
                        continue
                    hft, t_hft = hft_map.pop(cg)
                    P.op("dve", (lambda hh, hft: lambda e: e.tensor_tensor(out=hh[:], in0=hh[:], in1=hft[:], op=ALU.add))(hh, hft),
                         reads=[t_hh, t_hft], writes=[t_hh])
                    for h in range(4):
                        P.op("dve", (lambda hh, h: lambda e: e.bn_stats(out=BS[:, h, :], in_=hh[:, h * 256:(h + 1) * 256]))(hh, h),
                             reads=[t_hh], writes=[t_BS])
                        P.op("dve", (lambda h: lambda e: e.bn_aggr(out=MV[:, h, :], in_=BS[:, h, :]))(h), reads=[t_BS], writes=[t_MV])
                    P.op("act", lambda e: e.activation(out=SD[:, 0:4], in_=MV[:, :, 1], func=AF.Sqrt, bias=EPS), reads=[t_MV], writes=[t_SD])
                    P.op("dve", lambda e: e.reciprocal(out=SD[:, 4:8], in_=SD[:, 0:4]), reads=[t_SD], writes=[t_SD])
                    for h in range(4):
                        P.op("dve", (lambda hh, h: lambda e: e.tensor_scalar(
                            out=HN[:, h * 256:(h + 1) * 256], in0=hh[:, h * 256:(h + 1) * 256], scalar1=MV[:, h, 0:1],
                            scalar2=SD[:, 4 + h:5 + h], op0=ALU.subtract, op1=ALU.mult))(hh, h),
                            reads=[t_hh, t_MV, t_SD], writes=[t_HN])
                    for c in range(8):
                        P.op("pe", (lambda c: lambda e: e.transpose(
                            out=B_N[:, c // 4, (c % 4) * 128:(c % 4 + 1) * 128], in_=HN[:, c * 128:(c + 1) * 128], identity=IDF[:]))(c),
                            reads=[t_HN, t_IDF], writes=[t_N[c // 4]])
                    o_, w_ = FVCOLS["mlng"]
                    for b2 in range(2):
                        P.op("dve", (lambda b2: lambda e: e.tensor_tensor(
                            out=Y1[:, 4 * b2:4 * b2 + 4, :], in0=B_N[:, b2, :].rearrange("p (c t) -> p c t", t=128),
                            in1=FV[:, o_ + 4 * b2:o_ + 4 * b2 + 4].unsqueeze(2).to_broadcast([128, 4, 128]), op=ALU.mult))(b2),
                            reads=[t_N[b2], t_FV], writes=[t_Y1])
                    P.op("dve", (lambda cols: lambda e: e.tensor_tensor(out=Y1[:], in0=Y1[:], in1=XM[:, :, cols], op=ALU.add))(cols),
                         reads=[t_Y1] + t_XM, writes=[t_Y1])
                    P.op("dve", (lambda yg, cols: lambda e: e.tensor_tensor(out=yg[:, :, cols], in0=Y1[:], in1=SIG[:, :, cols], op=ALU.mult))(yg, cols),
                         reads=[t_Y1] + t_SIG, writes=[t_yg])
                if d == 1 and lat:
                    P.dma("sp", (lambda yg, gi: lambda e: e.dma_start(
                        out=YML_d[:, :, gi * 256:(gi + 1) * 256].rearrange("c p t -> p c t"), in_=yg[:]))(yg, gi),
                        reads=[t_yg], writes=[t_YML[gi]], semtrk=t_yg)
            for d in range(2):
                with ExitStack() as st2:
                    sb2 = lambda name, shape, dt: st2.enter_context(nc.sbuf_tensor(name, list(shape), dt))
                    if d == 0:
                        WMX = sb2("WMX", [128, 8, 8, 128], BF16); t_WMX = [Trk("WMX%d" % c) for c in range(8)]
                        for c in range(8):
                            P.dma("pool", (lambda c: lambda e: e.dma_start(
                                out=WMX[:, c], in_=win_d[16 + c].rearrange("p (k j) -> p k j", j=128)))(c), writes=[t_WMX[c]])
                        UMF = Rot(sb2, "UMF", 2, [128, 260], F32)
                        HALO = sb2("HALO", [128, 8, 2], F32); t_HALO = [Trk("HALO%d" % c) for c in range(8)]
                        XCV = Rot(sb2, "XCV", 2, [128, 256], F32)
                    if d == 1:
                        WMO = sb2("WMO", [128, 8, 8, 128], BF16); t_WMO = [Trk("WMO%d" % c) for c in range(8)]
                        for c in range(8):
                            P.dma("pool", (lambda c: lambda e: e.dma_start(
                                out=WMO[:, c], in_=win_d[24 + c].rearrange("p (k j) -> p k j", j=128)))(c), writes=[t_WMO[c]])
                        SIG = sb2("SIG", [128, 8, 256], F32); t_SIG = [Trk("SIG%d" % c) for c in range(8)]
                        HFt = Rot(sb2, "HFt", 2, [128, 1024], F32)
                        hft_map = {}
                        HN = sb2("HN", [128, 1024], F32); t_HN = Trk("HN")
                        BS = sb2("BS", [128, 4, 6], F32); t_BS = Trk("BS")
                        MV = sb2("MV", [128, 4, 2], F32); t_MV = Trk("MV")
                        SD = sb2("SD", [128, 8], F32); t_SD = Trk("SD")
                        Y1 = sb2("Y1", [128, 8, 128], F32); t_Y1 = Trk("Y1")
                        YG = Rot(sb2, "YG", 2, [128, 8, 256], BF16)
                        GS1 = [sb2("QT1", [128, 8, 256], BF16), sb2("KT1", [128, 8, 256], BF16),
                               sb2("XMB1", [128, 8, 256], BF16), sb2("UMB1", [128, 8, 256], BF16)]
                        GSETS = [((QT, KT, XMB, UMB), [Trk("gs0_%d" % i) for i in range(4)]),
                                 (tuple(GS1), [Trk("gs1_%d" % i) for i in range(4)])]
                        t_XMl = Trk("XMl")
                    hs = [False]
                    order = [0] + (list(range(1, 17)) if d == 0 else list(range(16, 0, -1)))
                    for gpos, g in enumerate(order):
                        lat = g > 0
                        gi = g - 1
                        if d == 0:
                            import os as _os2
                            PB = [B_N[:, 0, :], B_N[:, 1, :]]
                            t_PB = [t_N[0], t_N[1]]
                            if _os2.environ.get('PBPM'):
                                PB = [B_PM[:], B_PM[:]]; t_PB = [t_PM, t_PM]
                            hi = 259 if (lat and gi <= 14) else 258
                            lo = 0 if (lat and gi >= 1) else 2

                            def emit_proj(c):
                                pb, t_pb = PB[c % 2], t_PB[c % 2]
                                for kc in range(8):
                                    P.op("pe", (lambda pb, c, kc, g: lambda e: e.matmul(
                                        pb[:, 2:258], lhsT=WMX[:, c, kc, :], rhs=grp_rhs(kc, g), start=(kc == 0), stop=(kc == 7)))(pb, c, kc, g),
                                        reads=[t_WMX[c]] + grp_trks(g), writes=[t_pb])
                                if hi == 259:
                                    b0 = 256 + (gi + 1) * 256
                                    for kc in range(8):
                                        P.op("pe", (lambda pb, c, kc, b0: lambda e: e.matmul(
                                            pb[:, 258:259], lhsT=WMX[:, c, kc, :], rhs=uT[:, kc, b0:b0 + 1], start=(kc == 0), stop=(kc == 7)))(pb, c, kc, b0),
                                            reads=[t_WMX[c]] + grp_trks(g), writes=[t_pb])

                            bufs_c = {}

                            def emit_evac_act(c):
                                pb, t_pb = PB[c % 2], t_PB[c % 2]
                                umf, t_umf = UMF.next()
                                xcv, t_xcv = XCV.next()
                                bufs_c[c] = (umf, t_umf, xcv, t_xcv)
                                P.op("act", (lambda umf, pb, hi: lambda e: e.activation(
                                    out=umf[:, 2:hi], in_=pb[:, 2:hi], func=AF.Copy))(umf, pb, hi), reads=[t_pb], writes=[t_umf])
                                P.op("act", (lambda c, pb: lambda e: e.activation(out=UMB[:, c, :], in_=pb[:, 2:258], func=AF.Copy))(c, pb),
                                     reads=[t_pb], writes=[t_UMB[c]])

                            def emit_conv(c):
                                umf, t_umf, xcv, t_xcv = bufs_c[c]
                                if lo == 0:
                                    P.op("dve", (lambda umf, c: lambda e: e.tensor_copy(out=umf[:, 0:2], in_=HALO[:, c, :]))(umf, c),
                                         reads=[t_HALO[c]], writes=[t_umf])
                                else:
                                    P.op("dve", (lambda umf: lambda e: e.memset(umf[:, 0:2], 0.0))(umf), writes=[t_umf])
                                if hi == 258:
                                    P.op("dve", (lambda umf: lambda e: e.memset(umf[:, 258:259], 0.0))(umf), writes=[t_umf])
                                if lat and gi <= 14:
                                    P.op("dve", (lambda umf, c: lambda e: e.tensor_copy(out=HALO[:, c, :], in_=umf[:, 256:258]))(umf, c),
                                         reads=[t_umf], writes=[t_HALO[c]])
                                P.op("dve", (lambda umf, xcv, c: lambda e: e.tensor_scalar(
                                    out=xcv[:], in0=umf[:, 0:256], scalar1=fvc("mlcw", c * 4), scalar2=fvc("mlcb", c),
                                    op0=ALU.mult, op1=ALU.add))(umf, xcv, c), reads=[t_umf, t_FV], writes=[t_xcv])
                                for j in range(1, 4):
                                    P.op("dve", (lambda umf, xcv, c, j: lambda e: e.scalar_tensor_tensor(
                                        out=xcv[:], in0=umf[:, j:j + 256], scalar=fvc("mlcw", c * 4 + j), in1=xcv[:],
                                        op0=ALU.mult, op1=ALU.add))(umf, xcv, c, j), reads=[t_umf, t_FV, t_xcv], writes=[t_xcv])
                                P.op("act", (lambda xcv, c: lambda e: e.activation(out=XM[:, c, :], in_=xcv[:], func=AF.Silu))(xcv, c),
                                     reads=[t_xcv], writes=[t_XM[c]])
                                P.op("act", (lambda xcv, c: lambda e: e.activation(out=XMB[:, c, :], in_=xcv[:], func=AF.Silu))(xcv, c),
                                     reads=[t_xcv], writes=[t_XMB[c]])

                            vts = {}

                            def emit_qkv(c):
                                vt, t_vt = VT.next()
                                vts[c] = (vt, t_vt)
                                P.op("pe", (lambda c: lambda e: e.matmul(
                                    B_QK[:, 0:256], lhsT=MLBD[:, c, :], rhs=XMB[:, c, :], start=True, stop=True))(c),
                                    reads=[t_MLBD, t_XMB[c]], writes=[t_PQ])
                                P.op("pe", (lambda c: lambda e: e.matmul(
                                    B_QK[:, 256:512], lhsT=MLBD[:, 8 + c, :], rhs=XMB[:, c, :], start=True, stop=True))(c),
                                    reads=[t_MLBD, t_XMB[c]], writes=[t_PQ])
                                P.op("pe", (lambda c: lambda e: e.matmul(
                                    B_VO[:, 0:256], lhsT=MLBD[:, 16 + c, :], rhs=UMB[:, c, :], start=True, stop=True))(c),
                                    reads=[t_MLBD, t_UMB[c]], writes=[t_PV])
                                P.op("act", (lambda c: lambda e: e.activation(out=QT[:, c, :], in_=B_QK[:, 0:256], func=AF.Copy))(c),
                                     reads=[t_PQ], writes=[t_QT[c]])
                                P.op("act", (lambda c: lambda e: e.activation(
                                    out=KT[:, c, :], in_=B_QK[:, 256:512], func=AF.Copy, scale=1.0 / 16.0))(c),
                                    reads=[t_PQ], writes=[t_KT[c]])
                                P.op("dve", (lambda vt: lambda e: e.tensor_copy(out=vt[:], in_=B_VO[:, 0:256]))(vt),
                                     reads=[t_PV], writes=[t_vt])

                            def emit_gates(c):
                                vt, t_vt = vts[c]
                                for ti, (src, t_src) in enumerate(((QT[:, c, :], t_QT[c]), (KT[:, c, :], t_KT[c]), (vt[:], t_vt))):
                                    P.op("pe", (lambda c, ti, src, d: lambda e: e.matmul(
                                        B_GP[0:8, 0:256], lhsT=WGT[:, ti * 8 + c, d * 8:(d + 1) * 8], rhs=src,
                                        start=(c == 0 and ti == 0), stop=(c == 7 and ti == 2)))(c, ti, src, d),
                                        reads=[t_WGT, t_src], writes=[t_GP])

                            if _os2.environ.get("NOPIPE"):
                                for c in range(8):
                                    emit_proj(c)
                                    emit_evac_act(c)
                                    emit_conv(c)
                                    emit_qkv(c)
                                    emit_gates(c)
                            else:
                                emit_proj(0)
                                emit_evac_act(0)
                                for c in range(8):
                                    if c + 1 < 8:
                                        emit_proj(c + 1)
                                        emit_evac_act(c + 1)
                                    emit_conv(c)
                                    emit_qkv(c)
                                    if c > 0:
                                        emit_gates(c - 1)
                                emit_gates(7)
                            for wi_, (arr, trs) in enumerate(((QT, t_QT), (KT, t_KT), (XMB, t_XMB), (UMB, t_UMB))):
                                P.dma("sp", (lambda arr, g, wi_: lambda e: e.dma_start(out=SPQ_d[g, wi_], in_=arr[:]))(arr, g, wi_),
                                      reads=trs, writes=[t_SPQ[g][wi_]], semtrk=t_spd[wi_])
                            if lat:
                                P.dma("sp", (lambda gi: lambda e: e.dma_start(out=SPX_d[gi], in_=XM[:]))(gi),
                                      reads=t_XM, writes=[t_SPX[gi]], semtrk=t_spd[4])
                            gates_post(d)
                            rec_group(g, d, QT, KT, XMB, UMB, t_QT, t_KT, t_XMB, t_UMB, XM, t_XM, hs)
                            continue
                        def load_set(gq, si):
                            arrs, trs = GSETS[si]
                            for wi_ in range(4):
                                P.dma("sp", (lambda arrs, gq, wi_: lambda e: e.dma_start(out=arrs[wi_][:], in_=SPQ_d[gq, wi_]))(arrs, gq, wi_),
                                      reads=[t_SPQ[gq][wi_]], writes=[trs[wi_]])
                        if gpos == 0:
                            load_set(g, 0)
                        if gpos + 1 < len(order):
                            load_set(order[gpos + 1], (gpos + 1) % 2)
                        (QTg, KTg, XMBg, UMBg), trs = GSETS[gpos % 2]
                        if lat:
                            P.dma("sp", (lambda gi: lambda e: e.dma_start(out=XM[:], in_=SPX_d[gi]))(gi), reads=[t_SPX[gi]], writes=[t_XMl])
                        for c in range(8):
                            vt, t_vt = VT.next()
                            P.op("pe", (lambda c, UMBg: lambda e: e.matmul(
                                B_VO[:, 0:256], lhsT=MLBD[:, 16 + c, :], rhs=UMBg[:, c, :], start=True, stop=True))(c, UMBg),
                                reads=[t_MLBD, trs[3]], writes=[t_PV])
                            P.op("dve", (lambda vt: lambda e: e.tensor_copy(out=vt[:], in_=B_VO[:, 0:256]))(vt),
                                 reads=[t_PV], writes=[t_vt])
                            if lat:
                                for kc in range(8):
                                    P.op("pe", (lambda c, kc, g: lambda e: e.matmul(
                                        B_PM[:, 0:256], lhsT=WMO[:, c, kc, :], rhs=grp_rhs(kc, g), start=(kc == 0), stop=(kc == 7)))(c, kc, g),
                                        reads=[t_WMO[c]] + grp_trks(g), writes=[t_PM])
                                P.op("act", (lambda c: lambda e: e.activation(out=SIG[:, c, :], in_=B_PM[:, 0:256], func=AF.Sigmoid))(c),
                                     reads=[t_PM], writes=[t_SIG[c]])
                            for ti, (src, t_src) in enumerate(((QTg[:, c, :], trs[0]), (KTg[:, c, :], trs[1]), (vt[:], t_vt))):
                                P.op("pe", (lambda c, ti, src, d: lambda e: e.matmul(
                                    B_GP[0:8, 0:256], lhsT=WGT[:, ti * 8 + c, d * 8:(d + 1) * 8], rhs=src,
                                    start=(c == 0 and ti == 0), stop=(c == 7 and ti == 2)))(c, ti, src, d),
                                    reads=[t_WGT, t_src], writes=[t_GP])
                        if lat:
                            o2_, w2_ = FVCOLS["mlsk"]
                            P.op("dve", lambda e: e.tensor_tensor(
                                out=XM[:], in0=XM[:], in1=FV[:, o2_:o2_ + 8].unsqueeze(2).to_broadcast([128, 8, 256]), op=ALU.mult),
                                reads=[t_XMl, t_FV], writes=[t_XMl])
                        gates_post(d)
                        rec_group(g, d, QTg, KTg, XMBg, UMBg, [trs[0]] * 8, [trs[1]] * 8, [trs[2]] * 8, [trs[3]] * 8, XM, [t_XMl] * 8, hs)
                    P.barrier()
                    P.emit()
        if "yml" in debug:
            dbg_d["yml"] = YML_d
        if stop_after == "ml":
            P.wait_all("sp", P.out_toks)
            P.barrier()
            P.emit()
            ust.close()
            return nc, dbg_d

        x_cm = x_d.rearrange("(r j) d -> j r d", j=64)
        out_cm = out_d.rearrange("(r j) d -> j r d", j=64)
        t_X1 = [Trk("X1_%d" % i) for i in range(32)]
        t_H2 = [Trk("H2_%d" % i) for i in range(16)]
        with ExitStack() as st:
            sb = lambda name, shape, dt: st.enter_context(nc.sbuf_tensor(name, list(shape), dt))
            ps = lambda name, shape, dt: st.enter_context(nc.psum_tensor(name, list(shape), dt))
            WGR = sb("WGR", [128, 8, 8, 128], BF16); WGM = sb("WGM", [128, 8, 8, 128], BF16)
            WBR = sb("WBR", [128, 8, 8, 128], BF16); WBM = sb("WBM", [128, 8, 8, 128], BF16)
            WOUT = sb("WOUT", [128, 8, 1024], BF16)
            t_WGR = [Trk("WGR%d" % i) for i in range(8)]; t_WGM = [Trk("WGM%d" % i) for i in range(8)]
            t_WBR = [Trk("WBR%d" % i) for i in range(8)]; t_WBM = [Trk("WBM%d" % i) for i in range(8)]
            t_WOUT = [Trk("WOUT%d" % i) for i in range(8)]
            for oc in range(8):
                for (W, t_W, src) in ((WGR, t_WGR, win_d[32 + oc]), (WGM, t_WGM, win_d[40 + oc]),
                                      (WBR, t_WBR, wbrg_d[oc]), (WBM, t_WBM, wbml_d[oc])):
                    P.dma("pool", (lambda W, oc, src: lambda e: e.dma_start(
                        out=W[:, oc], in_=src.rearrange("p (k j) -> p k j", j=128)))(W, oc, src), writes=[t_W[oc]])
            for kc in range(8):
                P.dma("pool", (lambda kc: lambda e: e.dma_start(out=WOUT[:, kc, :], in_=wout_d[:, kc, :]))(kc), writes=[t_WOUT[kc]])
            YRt = Rot(sb, "YRt", 1, [128, 8, 512], BF16)
            YMt = Rot(sb, "YMt", 1, [128, 8, 512], BF16)
            SG = Rot(sb, "SG", 1, [128, 1024], F32)
            MIX = Rot(sb, "MIX", 1, [128, 8, 512], BF16)
            XT = Rot(sb, "XTc", 2, [128, 1024], F32)
            X1t = Rot(sb, "X1t", 1, [128, 1024], F32)
            XN = Rot(sb, "XNc", 1, [128, 1024], BF16)
            H2s = Rot(sb, "H2s", 1, [128, 8, 256], BF16)
            STc = sb("STc", [128, 32, 4], F32)
            BA0 = ps("BA0", [128, 512], F32); t_BA0 = Trk("PS_BA0")
            BA1 = ps("BA1", [128, 512], F32); t_BA1 = Trk("PS_BA1")
            BB0 = ps("BB0", [128, 512], F32); t_BB0 = Trk("PS_BB0")
            BB1 = ps("BB1", [128, 512], F32); t_BB1 = Trk("PS_BB1")
            BY = ps("BY", [128, 1024], F32); t_BY = Trk("PS_BY")
            BT = ps("BT", [128, 8, 128], BF16); t_BT = Trk("PS_BT")
            def load_y(T):
                yr, t_yr = YRt.next()
                ym, t_ym = YMt.next()
                P.dma("sp", (lambda yr, T: lambda e: e.dma_start(
                    out=yr[:], in_=YRG_d[:, :, T * 512:(T + 1) * 512].rearrange("c p t -> p c t")))(yr, T), reads=t_YRG, writes=[t_yr])
                P.dma("sp", (lambda ym, T: lambda e: e.dma_start(
                    out=ym[:], in_=YML_d[:, :, T * 512:(T + 1) * 512].rearrange("c p t -> p c t")))(ym, T),
                    reads=t_YML[2 * T:2 * T + 2], writes=[t_ym])
                return yr, t_yr, ym, t_ym

            def load_x(ti):
                xt, t_xt = XT.next()
                for jj in range(2):
                    P.dma("sp", (lambda xt, jj, ti: lambda e: e.dma_start(
                        out=xt[jj * 64:(jj + 1) * 64, :], in_=x_cm[2 * ti + jj]))(xt, jj, ti), writes=[t_xt])
                return xt, t_xt

            ynext = load_y(0)
            xnext = load_x(0)
            for T in range(8):
                yr, t_yr, ym, t_ym = ynext
                mix, t_mix = MIX.next()
                for oc in range(8):
                    sg, t_sg = SG.next()
                    for (W, t_W, bank, t_bank) in ((WGR, t_WGR, BA0, t_BA0), (WGM, t_WGM, BA1, t_BA1)):
                        for kc in range(8):
                            P.op("pe", (lambda W, bank, oc, kc, T: lambda e: e.matmul(
                                bank[:], lhsT=W[:, oc, kc, :],
                                rhs=uT[:, kc, 256 + T * 512:256 + (T + 1) * 512],
                                start=(kc == 0), stop=(kc == 7)))(W, bank, oc, kc, T), reads=[t_W[oc]] + t_uT[2:34], writes=[t_bank])
                    for (W, t_W, src, t_src, bank, t_bank) in ((WBR, t_WBR, yr, t_yr, BB0, t_BB0), (WBM, t_WBM, ym, t_ym, BB1, t_BB1)):
                        for kc in range(8):
                            P.op("pe", (lambda W, bank, oc, kc, src: lambda e: e.matmul(
                                bank[:], lhsT=W[:, oc, kc, :], rhs=src[:, kc, :],
                                start=(kc == 0), stop=(kc == 7)))(W, bank, oc, kc, src), reads=[t_W[oc], t_src], writes=[t_bank])
                    for (i_, bank, t_bank) in ((0, BA0, t_BA0), (1, BA1, t_BA1)):
                        P.op("act", (lambda sg, bank, i_: lambda e: e.activation(
                            out=sg[:, i_ * 512:(i_ + 1) * 512], in_=bank[:], func=AF.Sigmoid))(sg, bank, i_), reads=[t_bank], writes=[t_sg])
                    for (i_, bank, t_bank) in ((0, BB0, t_BB0), (1, BB1, t_BB1)):
                        P.op("dve", (lambda sg, bank, i_: lambda e: e.tensor_tensor(
                            out=sg[:, i_ * 512:(i_ + 1) * 512], in0=bank[:], in1=sg[:, i_ * 512:(i_ + 1) * 512], op=ALU.mult))(sg, bank, i_),
                            reads=[t_bank, t_sg], writes=[t_sg])
                    P.op("dve", (lambda mix, sg, oc: lambda e: e.tensor_tensor(
                        out=mix[:, oc, :], in0=sg[:, 0:512], in1=sg[:, 512:1024], op=ALU.add))(mix, sg, oc),
                        reads=[t_sg], writes=[t_mix])
                if T + 1 < 8:
                    ynext = load_y(T + 1)
                for s_ in range(4):
                    ti = T * 4 + s_
                    if s_ % 2 == 0:
                        h2s, t_h2s = H2s.next()
                    xt, t_xt = xnext
                    if ti + 1 < 32:
                        xnext = load_x(ti + 1)
                    x1, t_x1 = X1t.next()
                    xn, t_xn = XN.next()
                    t_st = Trk("stc%d" % ti)
                    for half in range(2):
                        for kc in range(8):
                            P.op("pe", (lambda mix, half, kc, s_: lambda e: e.matmul(
                                BY[:, half * 512:(half + 1) * 512], lhsT=mix[:, kc, s_ * 128:(s_ + 1) * 128],
                                rhs=WOUT[:, kc, half * 512:(half + 1) * 512], start=(kc == 0), stop=(kc == 7)))(mix, half, kc, s_),
                                reads=[t_mix, t_WOUT[kc]], writes=[t_BY])
                    P.op("dve", (lambda x1: lambda e: e.tensor_tensor(out=x1[:], in0=BY[:], in1=GROW[:, 0, :], op=ALU.mult))(x1),
                         reads=[t_BY, t_GROW], writes=[t_x1])
                    P.op("dve", (lambda x1, xt: lambda e: e.tensor_tensor(out=x1[:], in0=x1[:], in1=xt[:], op=ALU.add))(x1, xt),
                         reads=[t_x1, t_xt], writes=[t_x1])
                    P.dma("sp", (lambda x1, ti: lambda e: e.dma_start(out=X1_d[ti * 128:(ti + 1) * 128, :], in_=x1[:]))(x1, ti),
                          reads=[t_x1], writes=[t_X1[ti]], semtrk=t_x1)
                    P.op("act", (lambda x1, xn, ti: lambda e: e.activation(
                        out=xn[:], in_=x1[:], func=AF.Square, accum_out=STc[:, ti, 0:1]))(x1, xn, ti), reads=[t_x1], writes=[t_xn, t_st])
                    P.op("act", (lambda ti: lambda e: e.activation(
                        out=STc[:, ti, 1:2], in_=STc[:, ti, 0:1], func=AF.Sqrt, scale=1.0 / 1024.0, bias=EPS))(ti), reads=[t_st], writes=[t_st])
                    P.op("dve", (lambda ti: lambda e: e.reciprocal(out=STc[:, ti, 2:3], in_=STc[:, ti, 1:2]))(ti), reads=[t_st], writes=[t_st])
                    P.op("act", (lambda x1, xn, ti: lambda e: e.activation(
                        out=xn[:], in_=x1[:], func=AF.Copy, scale=STc[:, ti, 2:3]))(x1, xn, ti), reads=[t_x1, t_st], writes=[t_xn])
                    for c in range(8):
                        P.op("pe", (lambda xn, c: lambda e: e.transpose(
                            out=BT[:, c, :], in_=xn[:, c * 128:(c + 1) * 128], identity=IDB[:]))(xn, c), reads=[t_xn, t_IDB], writes=[t_BT])
                    for c in range(8):
                        dst = h2s[:, c, (s_ % 2) * 128:(s_ % 2 + 1) * 128]
                        if c % 2 == 0:
                            P.op("dve", (lambda c, dst: lambda e: e.tensor_scalar(
                                out=dst, in0=BT[:, c, :], scalar1=A2[:, c:c + 1], scalar2=MODT[:, 48 + 2 * c:49 + 2 * c],
                                op0=ALU.mult, op1=ALU.add))(c, dst), reads=[t_BT, t_A2, t_MODT], writes=[t_h2s])
                        else:
                            P.op("act", (lambda c, dst: lambda e: e.activation(
                                out=dst, in_=BT[:, c, :], func=AF.Identity, scale=A2[:, c:c + 1], bias=MODT[:, 48 + 2 * c:49 + 2 * c]))(c, dst),
                                reads=[t_BT, t_A2, t_MODT], writes=[t_h2s])
                    if s_ % 2 == 1:
                        Tq = ti // 2
                        P.dma("sp", (lambda h2s, Tq: lambda e: e.dma_start(out=H2_d[Tq], in_=h2s[:]))(h2s, Tq),
                              reads=[t_h2s], writes=[t_H2[Tq]], semtrk=t_h2s)
            P.barrier()
            P.emit()
        ust.close()
        if "x1" in debug:
            dbg_d["x1"] = X1_d

        with ExitStack() as st:
            sb = lambda name, shape, dt: st.enter_context(nc.sbuf_tensor(name, list(shape), dt))
            ps = lambda name, shape, dt: st.enter_context(nc.psum_tensor(name, list(shape), dt))
            WFI = sb("WFI", [128, 44, 8, 128], BF16); t_WFI = [Trk("WFI%d" % i) for i in range(44)]
            WFO = sb("WFO", [128, 22, 1024], BF16); t_WFO = [Trk("WFO%d" % i) for i in range(22)]
            FGR = sb("FGR", [128, 1024], F32); t_FGR = Trk("FGR")
            P.dma("sp", lambda e: e.dma_start(out=FGR[:], in_=fgrow_d), writes=[t_FGR])
            for f_ in range(22):
                for ci in (f_, 22 + f_):
                    P.dma("pool", (lambda ci: lambda e: e.dma_start(
                        out=WFI[:, ci], in_=wffi_d[ci].rearrange("p (k j) -> p k j", j=128)))(ci), writes=[t_WFI[ci]])
                P.dma("pool", (lambda f_: lambda e: e.dma_start(out=WFO[:, f_, :], in_=wffo_d[:, f_, :]))(f_), writes=[t_WFO[f_]])
            H2t = Rot(sb, "H2t", 2, [128, 8, 512], BF16)
            SGt = Rot(sb, "SGt", 2, [128, 512], F32)
            HID = Rot(sb, "HID", 1, [128, 22, 512], BF16)
            X1r = Rot(sb, "X1r", 2, [128, 1024], F32)
            T2 = Rot(sb, "T2", 2, [128, 1024], F32)
            JK = sb("JK", [128, 1024], BF16); t_JK = Trk("JK")
            STf = sb("STf", [128, 32, 4], F32)
            BG0 = Rot(ps, "PS_BG0", 2, [128, 512], F32)
            BG1 = Rot(ps, "PS_BG1", 2, [128, 512], F32)
            BO = Rot(ps, "PS_BO", 2, [128, 1024], F32)
            h2_map = {}
            x1_map = {}
            for T in range(8):
                for Tq in (T, T + 1):
                    if Tq < 8 and Tq not in h2_map:
                        hb, t_hb = H2t.next()
                        for q_ in range(2):
                            P.dma("sp", (lambda hb, Tq, q_: lambda e: e.dma_start(out=hb[:, :, q_ * 256:(q_ + 1) * 256], in_=H2_d[2 * Tq + q_]))(hb, Tq, q_),
                                  reads=[t_H2[2 * Tq + q_]], writes=[t_hb])
                        h2_map[Tq] = (hb, t_hb)
                h2, t_h2 = h2_map.pop(T)
                hid, t_hid = HID.next()
                for f_ in range(22):
                    g0, t_g0 = BG0.next()
                    g1, t_g1 = BG1.next()
                    sg, t_sg = SGt.next()
                    for (ci, bank, t_bank) in ((f_, g0, t_g0), (22 + f_, g1, t_g1)):
                        for kc in range(8):
                            P.op("pe", (lambda bank, ci, kc, h2: lambda e: e.matmul(
                                bank[:], lhsT=WFI[:, ci, kc, :], rhs=h2[:, kc, :], start=(kc == 0), stop=(kc == 7)))(bank, ci, kc, h2),
                                reads=[t_WFI[ci], t_h2], writes=[t_bank])
                    P.op("act", (lambda sg, g0: lambda e: e.activation(out=sg[:], in_=g0[:], func=AF.Silu))(sg, g0),
                         reads=[t_g0], writes=[t_sg])
                    P.op("dve", (lambda hid, f_, sg, g1: lambda e: e.tensor_tensor(
                        out=hid[:, f_, :], in0=g1[:], in1=sg[:], op=ALU.mult))(hid, f_, sg, g1), reads=[t_g1, t_sg], writes=[t_hid])
                for s_ in range(4):
                    ti = T * 4 + s_
                    bo, t_bo = BO.next()
                    for tq in (ti, ti + 1):
                        if tq < 32 and tq not in x1_map:
                            xb, t_xb = X1r.next()
                            P.dma("sp", (lambda xb, tq: lambda e: e.dma_start(out=xb[:], in_=X1_d[tq * 128:(tq + 1) * 128, :]))(xb, tq),
                                  reads=[t_X1[tq]], writes=[t_xb])
                            x1_map[tq] = (xb, t_xb)
                    x1, t_x1 = x1_map.pop(ti)
                    t2, t_t2 = T2.next()
                    t_st = Trk("stf%d" % ti)
                    for half in range(2):
                        for f_ in range(22):
                            P.op("pe", (lambda bo, hid, half, f_, s_: lambda e: e.matmul(
                                bo[:, half * 512:(half + 1) * 512], lhsT=hid[:, f_, s_ * 128:(s_ + 1) * 128],
                                rhs=WFO[:, f_, half * 512:(half + 1) * 512], start=(f_ == 0), stop=(f_ == 21)))(bo, hid, half, f_, s_),
                                reads=[t_hid, t_WFO[f_]], writes=[t_bo])
                    P.op("dve", (lambda t2, bo: lambda e: e.tensor_tensor(out=t2[:], in0=bo[:], in1=GROW[:, 1, :], op=ALU.mult))(t2, bo),
                         reads=[t_bo, t_GROW], writes=[t_t2])
                    P.op("dve", (lambda t2, x1: lambda e: e.tensor_tensor(out=t2[:], in0=t2[:], in1=x1[:], op=ALU.add))(t2, x1),
                         reads=[t_t2, t_x1], writes=[t_t2])
                    P.op("act", (lambda t2, ti: lambda e: e.activation(
                        out=JK[:], in_=t2[:], func=AF.Square, accum_out=STf[:, ti, 0:1]))(t2, ti), reads=[t_t2], writes=[t_JK, t_st])
                    P.op("act", (lambda ti: lambda e: e.activation(
                        out=STf[:, ti, 1:2], in_=STf[:, ti, 0:1], func=AF.Sqrt, scale=1.0 / 1024.0, bias=EPS))(ti), reads=[t_st], writes=[t_st])
                    P.op("dve", (lambda ti: lambda e: e.reciprocal(out=STf[:, ti, 2:3], in_=STf[:, ti, 1:2]))(ti), reads=[t_st], writes=[t_st])
                    P.op("act", (lambda t2, ti: lambda e: e.activation(
                        out=t2[:], in_=t2[:], func=AF.Copy, scale=STf[:, ti, 2:3]))(t2, ti), reads=[t_t2, t_st], writes=[t_t2])
                    P.op("dve", (lambda t2: lambda e: e.tensor_tensor(out=t2[:], in0=t2[:], in1=FGR[:], op=ALU.mult))(t2),
                         reads=[t_t2, t_FGR], writes=[t_t2])
                    for jj in range(2):
                        P.out_toks.append(P.dma("sp", (lambda t2, jj, ti: lambda e: e.dma_start(
                            out=out_cm[2 * ti + jj], in_=t2[jj * 64:(jj + 1) * 64, :]))(t2, jj, ti), reads=[t_t2], semtrk=t_t2))
            P.barrier()
            P.emit()

        P.wait_all("sp", P.out_toks)
        P.emit()
    return nc, dbg_d


def make_in_maps(inputs):
    sh = prep_shared(inputs)
    x = np.asarray(inputs["x"], np.float32)
    c = np.asarray(inputs["c"], np.float32)
    ctx = np.asarray(inputs["ctx"], np.float32)
    c_ctx = np.asarray(inputs["c_ctx"], np.float32)
    maps = []
    for b in range(8):
        m = dict(sh)
        m["x"] = np.ascontiguousarray(x[b])
        m["ctx"] = np.ascontiguousarray(ctx[b])
        m["cv"] = np.ascontiguousarray(np.stack([fm(c[b]), fm(c_ctx)], 2).reshape(128, 16))
        maps.append(m)
    return maps


def kernel(**inputs):
    nc, _ = build()
    maps = make_in_maps(inputs)
    res = run_bass_kernel_spmd(nc, maps, core_ids=list(range(8)))
    return np.stack([r["out"] for r in res.results], 0)
```

```python
import numpy as np
from contextlib import ExitStack
import concourse.bass as bass
import concourse.mybir as mybir
from concourse.bass_utils import run_bass_kernel_spmd

F32 = mybir.dt.float32
BF16 = mybir.dt.bfloat16
AF = mybir.ActivationFunctionType
ALU = mybir.AluOpType
AX = mybir.AxisListType

ENGS = ["pe", "act", "dve", "pool", "sp"]
EPS = 1e-6
NT = 4352
NCTX = 256
NLAT = 4096


class Trk:
    __slots__ = ("name", "w", "r", "dsem", "dcnt", "excl")

    def __init__(self, name=""):
        self.name = name
        self.w = None
        self.r = {}
        self.dsem = None
        self.dcnt = 0
        self.excl = name.startswith("PS_")


class Prog:
    def __init__(self, nc, stack):
        self.nc = nc
        self.stack = stack
        self.ops = {e: [] for e in ENGS}
        self.seq = {e: 0 for e in ENGS}
        self.known = {e: {} for e in ENGS}
        self.esem = {e: stack.enter_context(nc.semaphore("s_" + e)) for e in ENGS}
        self.nsem = len(ENGS)
        self.out_toks = []
        self.dtrks = []
        self.sem_pool = {"sp": [], "pool": []}

    def new_dsem(self, name):
        s = self.stack.enter_context(self.nc.semaphore("d%d_%s" % (self.nsem, name)))
        self.nsem += 1
        return s

    def _need(self, eng, waits, dep):
        sem, val = dep
        if eng == "pe" and sem is self.esem["pe"]:
            return
        k = id(sem)
        if self.known[eng].get(k, 0) >= val:
            return
        self.known[eng][k] = val
        waits[k] = (sem, val)

    def _deps(self, eng, reads, writes):
        waits = {}
        for t in reads:
            if t.w is not None:
                self._need(eng, waits, t.w)
            if t.excl:
                for dep in t.r.values():
                    self._need(eng, waits, dep)
        for t in writes:
            if t.w is not None:
                self._need(eng, waits, t.w)
            for dep in t.r.values():
                self._need(eng, waits, dep)
        return waits

    def _record(self, tok, reads, writes):
        for t in reads:
            t.r[id(tok[0])] = tok
        for t in writes:
            t.w = tok
            t.r = {}

    def op(self, eng, fn, reads=(), writes=()):
        waits = self._deps(eng, reads, writes)
        self.seq[eng] += 1
        tok = (self.esem[eng], self.seq[eng])
        self._record(tok, reads, writes)
        self.ops[eng].append((list(waits.values()), fn, (self.esem[eng], 1)))

    def dma(self, eng, fn, reads=(), writes=(), semtrk=None):
        if semtrk is None:
            semtrk = writes[0] if writes else reads[0]
        if semtrk.dsem is None:
            if self.sem_pool[eng]:
                semtrk.dsem, semtrk.dcnt = self.sem_pool[eng].pop()
            else:
                semtrk.dsem = self.new_dsem(semtrk.name)
            self.dtrks.append((semtrk, eng))
        waits = self._deps(eng, reads, writes)
        if semtrk.dcnt > 0:
            self._need(eng, waits, (semtrk.dsem, semtrk.dcnt))
        semtrk.dcnt += 16
        tok = (semtrk.dsem, semtrk.dcnt)
        self._record(tok, reads, writes)
        self.ops[eng].append((list(waits.values()), fn, (semtrk.dsem, 16)))
        return tok

    def wait_all(self, eng, toks):
        waits = {}
        for t in toks:
            self._need(eng, waits, t)
        self.ops[eng].append((list(waits.values()), None, None))

    def barrier(self):
        waits = {}
        for e in ENGS:
            if e != "sp" and self.seq[e] > 0:
                self._need("sp", waits, (self.esem[e], self.seq[e]))
        for t, _e in self.dtrks:
            if t.dcnt > 0:
                self._need("sp", waits, (t.dsem, t.dcnt))
        self.seq["sp"] += 1
        self.ops["sp"].append((list(waits.values()), lambda e: e.nop(), (self.esem["sp"], 1)))
        for t, e_ in self.dtrks:
            self.sem_pool[e_].append((t.dsem, t.dcnt))
            t.dsem = None
            t.dcnt = 0
        self.dtrks = []
        for e in ENGS:
            if e != "sp":
                w = {}
                self._need(e, w, (self.esem["sp"], self.seq["sp"]))
                self.ops[e].append((list(w.values()), None, None))

    def emit(self):
        nc = self.nc
        ops = self.ops
        self.ops = {e: [] for e in ENGS}
        with nc.Block() as block:
            def run(e, lst):
                for waits, fn, inc in lst:
                    for sem, val in waits:
                        e.wait_ge(sem, val)
                    if fn is not None:
                        fn(e).then_inc(inc[0], inc[1])

            @block.tensor
            def _(e):
                run(e, ops["pe"])

            @block.scalar
            def _(e):
                run(e, ops["act"])

            @block.vector
            def _(e):
                run(e, ops["dve"])

            @block.gpsimd
            def _(e):
                run(e, ops["pool"])

            @block.sync
            def _(e):
                run(e, ops["sp"])


class Rot:
    def __init__(self, alloc, name, n, shape, dt):
        self.bufs = [(alloc("%s%d" % (name, i), shape, dt), Trk("%s%d" % (name, i))) for i in range(n)]
        self.i = 0

    def next(self):
        b = self.bufs[self.i % len(self.bufs)]
        self.i += 1
        return b


FVCOLS = {}
_off = 0
for _n, _w in [("n1g", 8), ("n2g", 8), ("rgcw", 32), ("rgcb", 8), ("rgba", 16), ("rgbx", 16), ("rglam", 16),
               ("mlcw", 32), ("mlcb", 8), ("mlng", 8), ("mlsk", 8)]:
    FVCOLS[_n] = (_off, _w)
    _off += _w
NV = _off


def fm(vec):
    v = np.asarray(vec, np.float32)
    return np.ascontiguousarray(v.reshape(-1, 128).T)


def colchunks(w):
    K, N = w.shape
    a = w.reshape(K // 128, 128, N // 128, 128)
    a = a.transpose(2, 1, 0, 3)
    return np.ascontiguousarray(a.reshape(N // 128, 128, (K // 128) * 128))


def rowchunks(w):
    K, N = w.shape
    return np.ascontiguousarray(w.reshape(K // 128, 128, N).transpose(1, 0, 2))


def blockdiag128(blocks):
    nb, bi, bo = blocks.shape
    per = 128 // bi
    out = np.zeros((nb // per, 128, 128), np.float32)
    for b in range(nb):
        c, q = divmod(b, per)
        out[c, q * bi:(q + 1) * bi, q * bo:(q + 1) * bo] = blocks[b]
    return out


def prep_shared(inp):
    f = lambda k: np.asarray(inp[k], np.float32)
    sh = {}
    w_mod = f("w_mod")[0]
    sh["wmod"] = colchunks(w_mod)
    b_mod = f("b_mod")[0]
    sh["bmod2"] = np.ascontiguousarray(np.repeat(fm(b_mod)[:, :, None], 2, axis=2).reshape(128, 96))
    sh["wmodg"] = np.stack([rowchunks(w_mod[:, 2048:3072]), rowchunks(w_mod[:, 5120:6144])], 0)
    sh["bmodg"] = np.ascontiguousarray(np.broadcast_to(
        np.concatenate([b_mod[2048:3072], b_mod[5120:6144]])[None, :], (128, 2048)))
    fv = np.zeros((128, NV), np.float32)

    def put(name, arr):
        o, w = FVCOLS[name]
        assert arr.shape == (128, w), (name, arr.shape)
        fv[:, o:o + w] = arr
    put("n1g", fm(f("norm1_g")[0]))
    put("n2g", fm(f("norm2_g")[0]))
    cw = f("rg_conv_w")[0]
    put("rgcw", np.stack([fm(cw[j]) for j in range(4)], 2).reshape(128, 32))
    put("rgcb", fm(f("rg_conv_b")[0]))
    put("rgba", np.concatenate([fm(f("rg_ba")[0][d]) for d in range(2)], 1))
    put("rgbx", np.concatenate([fm(f("rg_bx")[0][d]) for d in range(2)], 1))
    put("rglam", np.concatenate([fm(f("rg_lambda")[0][d]) for d in range(2)], 1))
    cw = f("ml_conv_w")[0]
    put("mlcw", np.stack([fm(cw[j]) for j in range(4)], 2).reshape(128, 32))
    put("mlcb", fm(f("ml_conv_b")[0]))
    put("mlng", fm(f("ml_norm_g")[0]))
    put("mlsk", fm(f("ml_skip")[0]))
    sh["fv"] = fv
    sh["win"] = colchunks(f("w_in")[0])
    rg = []
    for d in range(2):
        for w in (f("rg_wa")[0][d], f("rg_wx")[0][d]):
            rg.append(blockdiag128(w).transpose(1, 0, 2))
    sh["rgbd"] = np.ascontiguousarray(np.concatenate(rg, 1))
    ml = [blockdiag128(f(k)[0]).transpose(1, 0, 2) for k in ("ml_wq", "ml_wk", "ml_wv")]
    sh["mlbd"] = np.ascontiguousarray(np.concatenate(ml, 1))
    wi, wf = f("ml_wi")[0], f("ml_wf")[0]
    wg = np.concatenate([wi[0], wf[0], wi[1], wf[1]], 1)
    sh["wgt"] = np.ascontiguousarray(wg.reshape(24, 128, 16).transpose(1, 0, 2))
    bi, bf = f("ml_bi")[0], f("ml_bf")[0]
    gb = np.stack([np.tile(np.concatenate([bi[d], bf[d]]), 2) for d in range(2)], 0)
    sh["gbias"] = np.ascontiguousarray(np.broadcast_to(gb[None], (128, 2, 16)))
    tri = np.zeros((2, 128, 128), np.float32)
    ii = np.arange(128)
    tri[0] = (ii[:, None] <= ii[None, :])
    tri[1] = (ii[:, None] >= ii[None, :])
    sh["tri"] = np.ascontiguousarray(tri.transpose(1, 0, 2))
    ident = np.eye(128, dtype=np.float32)
    sh["ident"] = ident
    sh["wbrg"] = colchunks(f("w_branch_rg")[0])
    sh["wbml"] = colchunks(f("w_branch_ml")[0])
    sh["wout"] = rowchunks(f("w_out")[0])
    sh["wffi"] = colchunks(f("w_ffn_in")[0])
    sh["wffo"] = rowchunks(f("w_ffn_out")[0])
    sh["fgrow"] = np.ascontiguousarray(np.broadcast_to(f("final_norm_g")[None, :], (128, 1024)))
    return sh


def build(debug=(), stop_after=None):
    nc = bass.Bass("TRN2", target_bir_lowering=False)
    din = lambda name, shape, dt=F32: nc.dram_tensor(name, list(shape), dt, kind="ExternalInput").ap()
    x_d = din("x", [NLAT, 1024])
    ctx_d = din("ctx", [NCTX, 1024])
    cv_d = din("cv", [128, 16])
    wmod_d = din("wmod", [48, 128, 1024])
    bmod2_d = din("bmod2", [128, 96])
    wmodg_d = din("wmodg", [2, 128, 8, 1024])
    bmodg_d = din("bmodg", [128, 2048])
    fv_d = din("fv", [128, NV])
    win_d = din("win", [48, 128, 1024])
    ident_d = din("ident", [128, 128])
    rgbd_d = din("rgbd", [128, 32, 128])
    mlbd_d = din("mlbd", [128, 24, 128])
    wgt_d = din("wgt", [128, 24, 16])
    gbias_d = din("gbias", [128, 2, 16])
    tri_d = din("tri", [128, 2, 128])
    wbrg_d = din("wbrg", [8, 128, 1024])
    wbml_d = din("wbml", [8, 128, 1024])
    wout_d = din("wout", [128, 8, 1024])
    wffi_d = din("wffi", [44, 128, 1024])
    wffo_d = din("wffo", [128, 22, 1024])
    fgrow_d = din("fgrow", [128, 1024])
    SPQ_d = nc.dram_tensor("SPQ", [17, 4, 128, 8, 256], BF16).ap()
    SPX_d = nc.dram_tensor("SPX", [16, 128, 8, 256], F32).ap()
    X1_d = nc.dram_tensor("X1", [NLAT, 1024], F32).ap()
    H2_d = nc.dram_tensor("H2", [16, 128, 8, 256], BF16).ap()
    HF_d = nc.dram_tensor("HF", [32, 128, 1024], F32).ap()
    if "yml" in debug:
        YML_d = nc.dram_tensor("dbg_yml", [8, 128, NLAT], BF16, kind="ExternalOutput").ap()
    else:
        YML_d = nc.dram_tensor("YML", [8, 128, NLAT], BF16).ap()
    YRG_d = nc.dram_tensor("YRG", [8, 128, NLAT], BF16).ap()
    out_d = nc.dram_tensor("out", [NLAT, 1024], F32, kind="ExternalOutput").ap()
    dbg_d = {}

    with ExitStack() as gst:
        P = Prog(nc, gst)
        galloc = lambda name, shape, dt: gst.enter_context(nc.sbuf_tensor(name, list(shape), dt))

        def dump(name, ap, trk, shape, dt=F32):
            if name not in debug:
                return
            d = nc.dram_tensor("dbg_" + name, list(shape), dt, kind="ExternalOutput").ap()
            dbg_d[name] = d
            P.out_toks.append(P.dma("sp", lambda e: e.dma_start(out=d, in_=ap), reads=[trk], semtrk=Trk("dbg" + name)))

        t_uT = [Trk("uT%d" % i) for i in range(34)]
        FV = galloc("FV", [128, NV], F32); t_FV = Trk("FV")
        MODT = galloc("MODT", [128, 96], F32); t_MODT = Trk("MODT")
        A1 = galloc("A1", [128, 16], F32); t_A1 = Trk("A1")
        A2 = galloc("A2", [128, 8], F32); t_A2 = Trk("A2")
        GROW = galloc("GROW", [128, 2, 1024], F32); t_GROW = Trk("GROW")
        KD = galloc("KD", [128, 16], F32); t_KD = Trk("KD")
        IDB = galloc("IDB", [128, 128], BF16); t_IDB = Trk("IDB")
        IDF = galloc("IDF", [128, 128], F32); t_IDF = Trk("IDF")
        t_YRG = [Trk("YRG%d" % c) for c in range(8)]
        ust = ExitStack()
        uT = ust.enter_context(nc.sbuf_tensor("uT", [128, 8, NT], BF16))

        def fvc(name, i=0, n=1):
            o, w = FVCOLS[name]
            return FV[:, o + i:o + i + n]

        with ExitStack() as st:
            sb = lambda name, shape, dt: st.enter_context(nc.sbuf_tensor(name, list(shape), dt))
            ps = lambda name, shape, dt: st.enter_context(nc.psum_tensor(name, list(shape), dt))
            CV = sb("CV", [128, 16], F32); t_CV = Trk("CV")
            S2 = sb("S2", [128, 16], F32); t_S2 = Trk("S2")
            SREP = sb("SREP", [128, 8, 128], F32); t_SREP = Trk("SREP")
            BM2 = sb("BM2", [128, 96], F32); t_BM2 = Trk("BM2")
            BMG = sb("BMG", [128, 2048], F32); t_BMG = Trk("BMG")
            TMPA = sb("TMPA", [128, 16], F32); t_TMPA = Trk("TMPA")
            TMPB = sb("TMPB", [128, 16], F32); t_TMPB = Trk("TMPB")
            WM = Rot(sb, "WM", 3, [128, 1024], F32)
            WGm = sb("WGm", [128, 8, 1024], F32); t_WGm = Trk("WGm")
            MODP = ps("MODP", [128, 512], F32); t_MODP = Trk("PS_MODP")
            GRP = ps("GRP", [128, 1024], F32); t_GRP = Trk("PS_GRP")

            P.dma("sp", lambda e: e.dma_start(out=CV[:], in_=cv_d), writes=[t_CV])
            P.dma("sp", lambda e: e.dma_start(out=FV[:], in_=fv_d), writes=[t_FV])
            P.dma("sp", lambda e: e.dma_start(out=BM2[:], in_=bmod2_d), writes=[t_BM2])
            P.dma("sp", lambda e: e.dma_start(out=BMG[:], in_=bmodg_d), writes=[t_BMG])
            P.dma("sp", lambda e: e.dma_start(out=IDF[:], in_=ident_d), writes=[t_IDF])
            P.dma("pool", lambda e: e.dma_start(out=IDB[:], in_=ident_d), writes=[t_IDB])
            P.op("act", lambda e: e.activation(out=S2[:], in_=CV[:], func=AF.Silu), reads=[t_CV], writes=[t_S2])
            for kc in range(8):
                P.op("dve", (lambda kc: lambda e: e.tensor_copy(
                    out=SREP[:, kc, :], in_=S2[:, 2 * kc:2 * kc + 1].to_broadcast([128, 128])))(kc),
                    reads=[t_S2], writes=[t_SREP])
            for n in range(48):
                wm, t_wm = WM.next()
                P.dma("sp", (lambda wm, n: lambda e: e.dma_start(out=wm[:], in_=wmod_d[n]))(wm, n), writes=[t_wm])
                for kc in range(8):
                    P.op("pe", (lambda wm, n, kc: lambda e: e.matmul(
                        MODP[:, 2 * n:2 * n + 2], lhsT=wm[:, kc * 128:(kc + 1) * 128], rhs=S2[:, 2 * kc:2 * kc + 2],
                        start=(kc == 0), stop=(kc == 7)))(wm, n, kc), reads=[t_wm, t_S2], writes=[t_MODP])
            P.op("dve", lambda e: e.tensor_tensor(out=MODT[:], in0=MODP[:, 0:96], in1=BM2[:], op=ALU.add),
                 reads=[t_MODP, t_BM2], writes=[t_MODT])
            P.op("dve", lambda e: e.tensor_scalar_add(out=TMPA[:], in0=MODT[:, 16:32], scalar1=1.0),
                 reads=[t_MODT], writes=[t_TMPA])
            for j in range(2):
                P.op("dve", (lambda j: lambda e: e.tensor_tensor(
                    out=A1[:, j:16:2], in0=TMPA[:, j:16:2], in1=fvc("n1g", 0, 8), op=ALU.mult))(j),
                    reads=[t_TMPA, t_FV], writes=[t_A1])
            P.op("dve", lambda e: e.tensor_scalar_add(out=TMPB[:, 0:8], in0=MODT[:, 64:80:2], scalar1=1.0),
                 reads=[t_MODT], writes=[t_TMPB])
            P.op("dve", lambda e: e.tensor_tensor(out=A2[:], in0=TMPB[:, 0:8], in1=fvc("n2g", 0, 8), op=ALU.mult),
                 reads=[t_TMPB, t_FV], writes=[t_A2])
            for g in range(2):
                P.dma("sp", (lambda g: lambda e: e.dma_start(out=WGm[:], in_=wmodg_d[g]))(g), writes=[t_WGm])
                for half in range(2):
                    for kc in range(8):
                        P.op("pe", (lambda half, kc: lambda e: e.matmul(
                            GRP[:, half * 512:(half + 1) * 512], lhsT=SREP[:, kc, :],
                            rhs=WGm[:, kc, half * 512:(half + 1) * 512], start=(kc == 0), stop=(kc == 7)))(half, kc),
                            reads=[t_SREP, t_WGm], writes=[t_GRP])
                P.op("dve", (lambda g: lambda e: e.tensor_tensor(
                    out=GROW[:, g, :], in0=GRP[:], in1=BMG[:, g * 1024:(g + 1) * 1024], op=ALU.add))(g),
                    reads=[t_GRP, t_BMG], writes=[t_GROW])
            P.op("act", lambda e: e.activation(out=TMPA[:], in_=fvc("rglam", 0, 16), func=AF.Exp, scale=-1.0),
                 reads=[t_FV], writes=[t_TMPA])
            P.op("act", lambda e: e.activation(out=TMPB[:], in_=TMPA[:], func=AF.Ln, bias=1.0),
                 reads=[t_TMPA], writes=[t_TMPB])
            P.op("dve", lambda e: e.tensor_scalar(out=KD[:], in0=TMPB[:], scalar1=-8.0, scalar2=None, op0=ALU.mult),
                 reads=[t_TMPB], writes=[t_KD])
            dump("modT", MODT[:], t_MODT, [128, 96])
            dump("grow", GROW[:], t_GROW, [128, 2, 1024])
            dump("kd", KD[:], t_KD, [128, 16])
            P.barrier()
            P.emit()

        with ExitStack() as st:
            sb = lambda name, shape, dt: st.enter_context(nc.sbuf_tensor(name, list(shape), dt))
            ps = lambda name, shape, dt: st.enter_context(nc.psum_tensor(name, list(shape), dt))
            XT = Rot(sb, "XT", 3, [128, 1024], F32)
            XN = Rot(sb, "XN", 2, [128, 1024], BF16)
            TP = Rot(ps, "PS_TP", 2, [128, 8, 128], BF16)
            ST = sb("ST", [128, 34, 4], F32)
            for i in range(34):
                t_st = Trk("st%d" % i)
                xt, t_xt = XT.next()
                xn, t_xn = XN.next()
                tp, t_tp = TP.next()
                src = ctx_d[i * 128:(i + 1) * 128, :] if i < 2 else x_d[(i - 2) * 128:(i - 1) * 128, :]
                j = 1 if i < 2 else 0
                P.dma("sp", (lambda xt, src: lambda e: e.dma_start(out=xt[:], in_=src))(xt, src), writes=[t_xt])
                P.op("act", (lambda xt, xn, i: lambda e: e.activation(
                    out=xn[:], in_=xt[:], func=AF.Square, accum_out=ST[:, i, 0:1]))(xt, xn, i),
                    reads=[t_xt], writes=[t_xn, t_st])
                P.op("act", (lambda i: lambda e: e.activation(
                    out=ST[:, i, 1:2], in_=ST[:, i, 0:1], func=AF.Sqrt, scale=1.0 / 1024.0, bias=EPS))(i),
                    reads=[t_st], writes=[t_st])
                P.op("dve", (lambda i: lambda e: e.reciprocal(out=ST[:, i, 2:3], in_=ST[:, i, 1:2]))(i),
                     reads=[t_st], writes=[t_st])
                P.op("act", (lambda xt, xn, i: lambda e: e.activation(
                    out=xn[:], in_=xt[:], func=AF.Copy, scale=ST[:, i, 2:3]))(xt, xn, i),
                    reads=[t_xt, t_st], writes=[t_xn])
                for c in range(8):
                    P.op("pe", (lambda xn, tp, c: lambda e: e.transpose(
                        out=tp[:, c, :], in_=xn[:, c * 128:(c + 1) * 128], identity=IDB[:]))(xn, tp, c),
                        reads=[t_xn, t_IDB], writes=[t_tp])
                for c in range(8):
                    if i < 2:
                        dst = uT[:, c, i * 128:(i + 1) * 128]
                        src_tp = tp[:, c, :]
                    else:
                        r0 = 2 * (i - 2)
                        dst = uT[:, c, 256:NT].rearrange("p (j r) -> p r j", r=64)[:, r0:r0 + 2, :]
                        src_tp = tp[:, c, :].rearrange("p (r j) -> p r j", j=64)
                    if c % 2 == 0:
                        P.op("dve", (lambda tp, c, dst, j: lambda e: e.tensor_scalar(
                            out=dst, in0=tp, scalar1=A1[:, 2 * c + j:2 * c + j + 1],
                            scalar2=MODT[:, 2 * c + j:2 * c + j + 1], op0=ALU.mult, op1=ALU.add))(src_tp, c, dst, j),
                            reads=[t_tp, t_A1, t_MODT], writes=[t_uT[i]])
                    else:
                        P.op("act", (lambda tp, c, dst, j: lambda e: e.activation(
                            out=dst, in_=tp, func=AF.Identity, scale=A1[:, 2 * c + j:2 * c + j + 1],
                            bias=MODT[:, 2 * c + j:2 * c + j + 1]))(src_tp, c, dst, j),
                            reads=[t_tp, t_A1, t_MODT], writes=[t_uT[i]])
            if "uT" in debug:
                UD = sb("UD", [128, 8, 512], F32); t_UD = Trk("UD")
                P.op("dve", lambda e: e.tensor_copy(out=UD[:], in_=uT[:, :, 128:640]), reads=t_uT[1:5], writes=[t_UD])
                dump("uT", UD[:], t_UD, [128, 8, 512])
            P.barrier()
            P.emit()


        CT0, LT0, PEND = 2, 261, 4357
        with ExitStack() as st:
            sb = lambda name, shape, dt: st.enter_context(nc.sbuf_tensor(name, list(shape), dt))
            ps = lambda name, shape, dt: st.enter_context(nc.psum_tensor(name, list(shape), dt))
            RGBD = sb("RGBD", [128, 32, 128], BF16); t_RGBD = Trk("RGBD")
            P.dma("pool", lambda e: e.dma_start(out=RGBD[:], in_=rgbd_d, max_dma_last_dim=4096), writes=[t_RGBD])
            RX = sb("RX", [128, 4360], F32); t_RX = Trk("RX")
            XC = sb("XC", [128, 4360], F32); t_XC = Trk("XC")
            XCB = sb("XCB", [128, 4360], BF16); t_XCB = Trk("XCB")
            TMP = sb("TMP", [128, 2180], F32); t_TMP = Trk("TMP")
            BB = [sb("B0", [128, 4360], F32), sb("B1", [128, 4360], F32)]; t_BB = [Trk("B0"), Trk("B1")]
            WR = Rot(sb, "WR", 4, [128, 8, 128], BF16)
            PJ = Rot(ps, "PS_PJ", 3, [128, 512], F32)
            GA = Rot(ps, "PS_GA", 4, [128, 512], F32)
            GL = Rot(sb, "GL", 2, [128, 512], F32)
            TS = Rot(sb, "TS", 2, [128, 512], F32)
            YS = Rot(sb, "YS", 1, [128, NLAT], BF16)
            for (a, b) in ((0, 2), (258, 261), (4357, 4360)):
                P.op("dve", (lambda a, b: lambda e: e.memset(RX[:, a:b], 0.0))(a, b), writes=[t_RX])
            for d in range(2):
                P.op("pool", (lambda d: lambda e: e.memset(BB[d][:, 256:264], 0.0))(d), writes=[t_BB[d]])
            blocks = [(0, 256, CT0)] + [(256 + 512 * b, 512, LT0 + 512 * b) for b in range(8)]

            def ut_trks(t0, n):
                return t_uT[t0 // 128:(t0 + n - 1) // 128 + 1]

            def ut_nat(kc, t0, n):
                if t0 < 256:
                    return uT[:, kc, t0:t0 + n]
                r0 = (t0 - 256) // 64
                return uT[:, kc, 256:NT].rearrange("p (j r) -> p r j", r=64)[:, r0:r0 + n // 64, :]

            def rev(ap2d):
                n = ap2d.shape[1]
                return bass.AP(ap2d.tensor, ap2d.offset + (n - 1), [list(ap2d.ap[0]), [-1, n]])

            def load_wr(c):
                wrx, t_wrx = WR.next()
                wrg, t_wrg = WR.next()
                P.dma("pool", (lambda w, c: lambda e: e.dma_start(
                    out=w[:], in_=win_d[c].rearrange("p (k j) -> p k j", j=128)))(wrx, c), writes=[t_wrx])
                P.dma("pool", (lambda w, c: lambda e: e.dma_start(
                    out=w[:], in_=win_d[8 + c].rearrange("p (k j) -> p k j", j=128)))(wrg, c), writes=[t_wrg])
                return wrx, t_wrx, wrg, t_wrg

            wr_next = load_wr(0)
            for c in range(8):
                wrx, t_wrx, wrg, t_wrg = wr_next
                if c + 1 < 8:
                    wr_next = load_wr(c + 1)
                if c > 0:
                    P.op("dve", lambda e: e.memset(RX[:, 258:261], 0.0), writes=[t_RX])
                pj, t_pj = PJ.next()
                for kc in range(8):
                    P.op("pe", (lambda pj, wrx, kc: lambda e: e.matmul(
                        pj[:, 0:256], lhsT=wrx[:, kc, :], rhs=uT[:, kc, 0:256], start=(kc == 0), stop=(kc == 7)))(
                        pj, wrx, kc), reads=[t_wrx] + t_uT[0:2], writes=[t_pj])
                P.op("act", (lambda pj: lambda e: e.activation(
                    out=RX[:, CT0:CT0 + 256], in_=pj[:, 0:256], func=AF.Copy))(pj), reads=[t_pj], writes=[t_RX])
                for b in range(8):
                    pj, t_pj = PJ.next()
                    for kc in range(8):
                        P.op("pe", (lambda pj, wrx, kc, b: lambda e: e.matmul(
                            pj[:], lhsT=wrx[:, kc, :], rhs=uT[:, kc, 256 + b * 512:256 + (b + 1) * 512], start=(kc == 0), stop=(kc == 7)))(
                            pj, wrx, kc, b), reads=[t_wrx] + t_uT[2:34], writes=[t_pj])
                    P.op("act", (lambda pj, b: lambda e: e.activation(
                        out=RX[:, LT0:LT0 + 4096].rearrange("p (r j) -> p j r", j=64)[:, 8 * b:8 * b + 8, :],
                        in_=pj[:].rearrange("p (j r) -> p j r", r=64), func=AF.Copy))(pj, b), reads=[t_pj], writes=[t_RX])
                L = PEND - 2
                P.op("dve", (lambda c: lambda e: e.tensor_scalar(
                    out=XC[:, 2:PEND], in0=RX[:, 0:L], scalar1=fvc("rgcw", c * 4), scalar2=fvc("rgcb", c),
                    op0=ALU.mult, op1=ALU.add))(c), reads=[t_RX, t_FV], writes=[t_XC])
                for j in range(1, 4):
                    P.op("dve", (lambda c, j: lambda e: e.scalar_tensor_tensor(
                        out=XC[:, 2:PEND], in0=RX[:, j:j + L], scalar=fvc("rgcw", c * 4 + j), in1=XC[:, 2:PEND],
                        op0=ALU.mult, op1=ALU.add))(c, j), reads=[t_RX, t_FV, t_XC], writes=[t_XC])
                P.op("act", lambda e: e.activation(out=XCB[:, 2:PEND], in_=XC[:, 2:PEND], func=AF.Copy),
                     reads=[t_XC], writes=[t_XCB])
                if c == 3:
                    dump("xc", XC[:], t_XC, [128, 4360])
                for d in range(2):
                    Bd, t_Bd = BB[d], t_BB[d]
                    for (t0, n, pos) in blocks:
                        ga, t_ga = GA.next()
                        gx, t_gx = GA.next()
                        P.op("pe", (lambda ga, d, c, n, pos: lambda e: e.matmul(
                            ga[:, 0:n], lhsT=RGBD[:, (d * 2) * 8 + c, :], rhs=XCB[:, pos:pos + n], start=True, stop=True))(
                            ga, d, c, n, pos), reads=[t_RGBD, t_XCB], writes=[t_ga])
                        P.op("pe", (lambda gx, d, c, n, pos: lambda e: e.matmul(
                            gx[:, 0:n], lhsT=RGBD[:, (d * 2 + 1) * 8 + c, :], rhs=XCB[:, pos:pos + n], start=True, stop=True))(
                            gx, d, c, n, pos), reads=[t_RGBD, t_XCB], writes=[t_gx])
                        P.op("act", (lambda ga, d, c, n, pos: lambda e: e.activation(
                            out=RX[:, pos:pos + n], in_=ga[:, 0:n], func=AF.Sigmoid, bias=fvc("rgba", d * 8 + c)))(
                            ga, d, c, n, pos), reads=[t_ga, t_FV], writes=[t_RX])
                        P.op("act", (lambda gx, Bd, d, c, n, pos: lambda e: e.activation(
                            out=Bd[:, pos:pos + n], in_=gx[:, 0:n], func=AF.Sigmoid, bias=fvc("rgbx", d * 8 + c)))(
                            gx, Bd, d, c, n, pos), reads=[t_gx, t_FV], writes=[t_Bd])
                    P.op("act", (lambda d, c: lambda e: e.activation(
                        out=RX[:, 2:PEND], in_=RX[:, 2:PEND], func=AF.Exp, scale=KD[:, d * 8 + c:d * 8 + c + 1]))(d, c),
                        reads=[t_RX, t_KD], writes=[t_RX])
                    P.op("dve", (lambda Bd: lambda e: e.tensor_tensor(
                        out=Bd[:, 2:PEND], in0=Bd[:, 2:PEND], in1=XC[:, 2:PEND], op=ALU.mult))(Bd),
                        reads=[t_Bd, t_XC], writes=[t_Bd])
                    for (ra, rb) in ((2, 2180), (2180, PEND)):
                        P.op("act", (lambda ra, rb: lambda e: e.activation(
                            out=TMP[:, 0:rb - ra], in_=RX[:, ra:rb], func=AF.Square))(ra, rb), reads=[t_RX], writes=[t_TMP])
                        P.op("act", (lambda ra, rb: lambda e: e.activation(
                            out=TMP[:, 0:rb - ra], in_=TMP[:, 0:rb - ra], func=AF.Sqrt, scale=-1.0, bias=1.0))(ra, rb),
                            reads=[t_TMP], writes=[t_TMP])
                        P.op("dve", (lambda Bd, ra, rb: lambda e: e.tensor_tensor(
                            out=Bd[:, ra:rb], in0=Bd[:, ra:rb], in1=TMP[:, 0:rb - ra], op=ALU.mult))(Bd, ra, rb),
                            reads=[t_Bd, t_TMP], writes=[t_Bd])
                    f_ = (lambda ap: ap) if d == 0 else rev
                    c0, c1 = CT0, CT0 + 256
                    l0, l1 = LT0, LT0 + 4096
                    P.op("dve", (lambda Bd, f_: lambda e: e.tensor_tensor_scan(
                        out=f_(Bd[:, c0:c1]), data0=f_(RX[:, c0:c1]), data1=f_(Bd[:, c0:c1]), initial=0.0,
                        op0=ALU.mult, op1=ALU.add))(Bd, f_), reads=[t_RX, t_Bd], writes=[t_Bd])
                    ini = (c1 - 1) if d == 0 else c0
                    P.op("dve", (lambda Bd, f_, ini: lambda e: e.tensor_tensor_scan(
                        out=f_(Bd[:, l0:l1]), data0=f_(RX[:, l0:l1]), data1=f_(Bd[:, l0:l1]), initial=Bd[:, ini:ini + 1],
                        op0=ALU.mult, op1=ALU.add))(Bd, f_, ini), reads=[t_RX, t_Bd], writes=[t_Bd])
                ys, t_ys = YS.next()
                for b in range(8):
                    pj, t_pj = PJ.next()
                    gl, t_gl = GL.next()
                    ts, t_ts = TS.next()
                    for kc in range(8):
                        P.op("pe", (lambda pj, wrg, kc, b: lambda e: e.matmul(
                            pj[:], lhsT=wrg[:, kc, :], rhs=uT[:, kc, 256 + b * 512:256 + (b + 1) * 512], start=(kc == 0), stop=(kc == 7)))(
                            pj, wrg, kc, b), reads=[t_wrg] + t_uT[2:34], writes=[t_pj])
                    P.op("act", (lambda pj, gl: lambda e: e.activation(out=gl[:], in_=pj[:], func=AF.Gelu))(pj, gl),
                         reads=[t_pj], writes=[t_gl])
                    P.op("dve", (lambda ts, b: lambda e: e.tensor_tensor(
                        out=ts[:].rearrange("p (j r) -> p j r", r=64),
                        in0=BB[0][:, LT0:LT0 + 4096].rearrange("p (r j) -> p j r", j=64)[:, 8 * b:8 * b + 8, :],
                        in1=BB[1][:, LT0:LT0 + 4096].rearrange("p (r j) -> p j r", j=64)[:, 8 * b:8 * b + 8, :], op=ALU.add))(ts, b),
                        reads=t_BB, writes=[t_ts])
                    P.op("dve", (lambda ys, ts, gl, b: lambda e: e.tensor_tensor(
                        out=ys[:, b * 512:(b + 1) * 512], in0=ts[:], in1=gl[:], op=ALU.mult))(ys, ts, gl, b),
                        reads=[t_ts, t_gl], writes=[t_ys])
                P.dma("sp", (lambda ys, c: lambda e: e.dma_start(out=YRG_d[c], in_=ys[:]))(ys, c), reads=[t_ys], writes=[t_YRG[c]],
                      semtrk=t_ys)
                if c == 3:
                    if "hrg" in debug:
                        dump("hrg", HD[:], t_HD, [128, NLAT])
                    dump("yrg", ys[:], t_ys, [128, NLAT], BF16)
            P.barrier()
            P.emit()
        if stop_after == "rg":
            P.wait_all("sp", P.out_toks)
            P.barrier()
            P.emit()
            ust.close()
            return nc, dbg_d

        t_HF = [Trk("HF%d" % i) for i in range(32)]
        t_YML = [Trk("YML%d" % i) for i in range(16)]
        t_SPQ = [[Trk("SPQ%d_%d" % (g, i)) for i in range(4)] for g in range(17)]
        t_SPX = [Trk("SPX%d" % g) for g in range(16)]
        t_spd = [Trk("spd%d" % i) for i in range(5)]
        with ExitStack() as st:
            sb = lambda name, shape, dt: st.enter_context(nc.sbuf_tensor(name, list(shape), dt))
            ps = lambda name, shape, dt: st.enter_context(nc.psum_tensor(name, list(shape), dt))
            MLBD = sb("MLBD", [128, 24, 128], BF16); t_MLBD = Trk("MLBD")
            WGT = sb("WGT", [128, 24, 16], BF16); t_WGT = Trk("WGT")
            GBI = sb("GBI", [128, 2, 16], F32); t_GBI = Trk("GBI")
            TRI = sb("TRI", [128, 2, 128], F32); t_TRI = Trk("TRI")
            ONES = sb("ONES", [128, 128], F32); t_ONES = Trk("ONES")
            P.dma("pool", lambda e: e.dma_start(out=MLBD[:], in_=mlbd_d, max_dma_last_dim=4096), writes=[t_MLBD])
            P.dma("pool", lambda e: e.dma_start(out=WGT[:], in_=wgt_d), writes=[t_WGT])
            P.dma("sp", lambda e: e.dma_start(out=GBI[:], in_=gbias_d), writes=[t_GBI])
            P.dma("sp", lambda e: e.dma_start(out=TRI[:], in_=tri_d), writes=[t_TRI])
            P.op("dve", lambda e: e.memset(ONES[:], 1.0), writes=[t_ONES])
            B_PM = ps("B_PM", [128, 512], F32); t_PM = Trk("PS_PM")
            B_QK = ps("B_QK", [128, 512], F32); t_PQ = Trk("PS_PQ")
            B_VO = ps("B_VO", [128, 512], F32); t_PV = Trk("PS_PV")
            B_GP = ps("B_GP", [128, 512], F32); t_GP = Trk("PS_GP")
            B_N = ps("B_N", [128, 4, 512], F32); t_N = [Trk("PS_N%d" % i) for i in range(4)]
            BU = [B_PM, B_QK, B_VO, B_GP]; t_BU = [t_PM, t_PQ, t_PV, t_GP]
            VT = Rot(sb, "VT", 2, [128, 256], BF16)
            XM = sb("XM", [128, 8, 256], F32); t_XM = [Trk("XM%d" % c) for c in range(8)]
            XMB = sb("XMB", [128, 8, 256], BF16); t_XMB = [Trk("XMB%d" % c) for c in range(8)]
            UMB = sb("UMB", [128, 8, 256], BF16); t_UMB = [Trk("UMB%d" % c) for c in range(8)]
            QT = sb("QT", [128, 8, 256], BF16); t_QT = [Trk("QT%d" % c) for c in range(8)]
            KT = sb("KT", [128, 8, 256], BF16); t_KT = [Trk("KT%d" % c) for c in range(8)]
            GF = sb("GF", [8, 256], F32); t_GF = Trk("GF")
            GG = sb("GG", [128, 16], F32); t_GG = Trk("GG")
            GE = sb("GE", [128, 8], F32); t_GE = Trk("GE")
            GLn = sb("GLn", [128, 8], F32); t_GLn = Trk("GLn")
            GT2 = sb("GT2", [128, 8], F32); t_GT2 = Trk("GT2")
            EB = sb("EB", [128, 8], F32); t_EB = Trk("EB")
            WS = sb("WS", [128, 8], F32); t_WS = Trk("WS")
            EBL = sb("EBL", [128, 8], F32); t_EBL = Trk("EBL")
            DS = sb("DS", [128, 4, 2, 257], F32); t_DS = [Trk("DS%d" % h) for h in range(4)]
            DB = sb("DB", [128, 4, 2, 258], BF16); t_DB = [Trk("DB%d" % h) for h in range(4)]
            VX = sb("VX", [128, 2, 4, 258], BF16); t_VX = [Trk("VX%d" % i) for i in range(2)]
            KTM = sb("KTM", [128, 2, 1024], BF16); t_KTM = [Trk("KTM%d" % i) for i in range(2)]
            STt = sb("STt", [128, 2, 4, 128], BF16); t_STt = [Trk("STt%d" % i) for i in range(2)]
            E1 = sb("E1", [128, 8, 4], F32); t_E1 = Trk("E1")
            HH = Rot(sb, "HH", 1, [128, 1024], F32)

            def grp_rhs(kc, g):
                if g == 0:
                    return uT[:, kc, 0:256]
                gi = g - 1
                return uT[:, kc, 256 + gi * 256:256 + (gi + 1) * 256]

            def grp_trks(g):
                return t_uT[0:2] if g == 0 else t_uT[2:34]

            def gates_post(d):
                P.op("act", lambda e: e.activation(out=GF[:], in_=B_GP[0:8, 0:256], func=AF.Copy), reads=[t_GP], writes=[t_GF])
                for ch in range(2):
                    P.op("pe", (lambda ch: lambda e: e.transpose(
                        out=B_GP[:, 256 + ch * 8:256 + ch * 8 + 8], in_=GF[0:8, ch * 128:(ch + 1) * 128], identity=IDF[0:8, 0:8]))(ch),
                        reads=[t_GF, t_IDF], writes=[t_GP])
                P.op("dve", (lambda d: lambda e: e.tensor_tensor(
                    out=GG[:], in0=B_GP[:, 256:272], in1=GBI[:, d, :], op=ALU.add))(d), reads=[t_GP, t_GBI], writes=[t_GG])
                GGv = GG[:].rearrange("t (c k) -> t c k", k=8)
                P.op("act", lambda e: e.activation(
                    out=GE[:].rearrange("t (c h) -> t c h", h=4), in_=GGv[:, :, 4:8], func=AF.Exp, scale=-1.0),
                    reads=[t_GG], writes=[t_GE])
                P.op("act", lambda e: e.activation(out=GLn[:], in_=GE[:], func=AF.Ln, bias=1.0), reads=[t_GE], writes=[t_GLn])
                P.op("pe", (lambda d: lambda e: e.matmul(
                    B_GP[:, 288:296], lhsT=TRI[:, d, :], rhs=GLn[:], start=True, stop=True))(d),
                    reads=[t_TRI, t_GLn], writes=[t_GP])
                P.op("pe", lambda e: e.matmul(B_GP[:, 304:312], lhsT=ONES[:], rhs=GLn[:], start=True, stop=True),
                     reads=[t_ONES, t_GLn], writes=[t_GP])
                P.op("act", lambda e: e.activation(out=EB[:], in_=B_GP[:, 288:296], func=AF.Exp, scale=-1.0),
                     reads=[t_GP], writes=[t_EB])
                P.op("dve", lambda e: e.tensor_tensor(
                    out=GT2[:].rearrange("t (c h) -> t c h", h=4), in0=B_GP[:, 288:296].rearrange("t (c h) -> t c h", h=4),
                    in1=GGv[:, :, 0:4], op=ALU.add), reads=[t_GP, t_GG], writes=[t_GT2])
                P.op("act", lambda e: e.activation(out=WS[:], in_=GT2[:], func=AF.Exp), reads=[t_GT2], writes=[t_WS])
                P.op("act", lambda e: e.activation(out=EBL[:], in_=B_GP[:, 304:312], func=AF.Exp, scale=-1.0),
                     reads=[t_GP], writes=[t_EBL])
            def rec_group(g, d, QT, KT, XMB, UMB, t_QT, t_KT, t_XMB, t_UMB, XM, t_XM, hs):
                lat = g > 0
                gi = g - 1
                have_state = hs[0]
                for ch in range(2):
                    cols = slice(ch * 128, (ch + 1) * 128)
                    for c in range(8):
                        h, half = divmod(c, 2)
                        bk = h // 2
                        off = (h % 2) * 256 + half * 128
                        P.op("pe", (lambda c, bk, off, cols: lambda e: e.matmul(
                            B_N[:, bk, off:off + 128], lhsT=UMB[:, c, cols], rhs=MLBD[:, 16 + c, :], start=True, stop=True))(c, bk, off, cols),
                            reads=[t_UMB[c], t_MLBD], writes=[t_N[bk]])
                        P.op("pe", (lambda c, bk, off, cols: lambda e: e.matmul(
                            B_N[:, 2 + bk, off:off + 128], lhsT=XMB[:, c, cols], rhs=MLBD[:, 8 + c, :], start=True, stop=True))(c, bk, off, cols),
                            reads=[t_XMB[c], t_MLBD], writes=[t_N[2 + bk]])
                    for h in range(4):
                        bk = h // 2
                        off = (h % 2) * 256
                        P.op("dve", (lambda ch, h, bk, off: lambda e: e.tensor_scalar(
                            out=VX[:, ch, h, 0:256], in0=B_N[:, bk, off:off + 256], scalar1=WS[:, ch * 4 + h:ch * 4 + h + 1],
                            scalar2=None, op0=ALU.mult))(ch, h, bk, off), reads=[t_N[bk], t_WS], writes=[t_VX[ch]])
                    P.op("act", (lambda ch: lambda e: e.activation(
                        out=VX[:, ch, :, 256:257], in_=WS[:, ch * 4:ch * 4 + 4].unsqueeze(2), func=AF.Copy))(ch), reads=[t_WS], writes=[t_VX[ch]])
                    for bk in range(2):
                        P.op("act", (lambda ch, bk: lambda e: e.activation(
                            out=KTM[:, ch, bk * 512:(bk + 1) * 512], in_=B_N[:, 2 + bk, :], func=AF.Copy, scale=1.0 / 16.0))(ch, bk),
                            reads=[t_N[2 + bk]], writes=[t_KTM[ch]])
                    for h in range(4):
                        for half in range(2):
                            c = 2 * h + half
                            P.op("pe", (lambda c, h, half, cols: lambda e: e.matmul(
                                B_GP[:, h * 128:(h + 1) * 128], lhsT=KT[:, c, cols], rhs=QT[:, c, cols], start=(half == 0), stop=(half == 1)))(c, h, half, cols),
                                reads=[t_KT[c], t_QT[c]], writes=[t_GP])
                    P.op("dve", (lambda ch, d: lambda e: e.tensor_tensor(
                        out=STt[:, ch], in0=B_GP[:].rearrange("p (h t) -> p h t", t=128),
                        in1=TRI[:, d, :].unsqueeze(1).to_broadcast([128, 4, 128]), op=ALU.mult))(ch, d),
                        reads=[t_GP, t_TRI], writes=[t_STt[ch]])
                chs = (0, 1) if d == 0 else (1, 0)
                if d == 1 and lat:
                    yg, t_yg = YG.next()
                for ch in chs:
                    cols = slice(ch * 128, (ch + 1) * 128)
                    if d == 1 and lat:
                        for cq in (gi * 2 + ch, gi * 2 + ch - 1):
                            if cq >= 0 and cq not in hft_map:
                                hb, t_hb = HFt.next()
                                P.dma("sp", (lambda hb, cq: lambda e: e.dma_start(out=hb[:], in_=HF_d[cq]))(hb, cq),
                                      reads=[t_HF[cq]], writes=[t_hb])
                                hft_map[cq] = (hb, t_hb)
                    last_chunk = (d == 0 and g == 16 and ch == 1) or (d == 1 and g == 1 and ch == 0)
                    if not last_chunk:
                        for r in range(2):
                            for hh2 in range(2):
                                h = 2 * r + hh2
                                for half in range(2):
                                    bi_ = hh2 * 2 + half
                                    P.op("pe", (lambda ch, h, half, bi_: lambda e: e.matmul(
                                        BU[bi_][:, 0:257], lhsT=KTM[:, ch, h * 256 + half * 128:h * 256 + (half + 1) * 128],
                                        rhs=VX[:, ch, h, 0:257], start=True, stop=True))(ch, h, half, bi_),
                                        reads=[t_KTM[ch], t_VX[ch]], writes=[t_BU[bi_]])
                            for hh2 in range(2):
                                h = 2 * r + hh2
                                for half in range(2):
                                    bi_ = hh2 * 2 + half
                                    if have_state:
                                        P.op("dve", (lambda h, half, bi_: lambda e: e.tensor_tensor(
                                            out=DS[:, h, half, :], in0=DS[:, h, half, :], in1=BU[bi_][:, 0:257], op=ALU.add))(h, half, bi_),
                                            reads=[t_DS[h], t_BU[bi_]], writes=[t_DS[h]])
                                    else:
                                        P.op("dve", (lambda h, half, bi_: lambda e: e.tensor_copy(
                                            out=DS[:, h, half, :], in_=BU[bi_][:, 0:257]))(h, half, bi_),
                                            reads=[t_BU[bi_]], writes=[t_DS[h]])
                    if lat:
                        hh, t_hh = HH.next()
                        for h in range(4):
                            P.op("pe", (lambda ch, h, hs: lambda e: e.matmul(
                                B_N[:, h, 0:257], lhsT=STt[:, ch, h, :], rhs=VX[:, ch, h, 0:257], start=True, stop=(not hs)))(ch, h, have_state),
                                reads=[t_STt[ch], t_VX[ch]], writes=[t_N[h]])
                            if have_state:
                                for half in range(2):
                                    c = 2 * h + half
                                    P.op("pe", (lambda c, h, half, cols: lambda e: e.matmul(
                                        B_N[:, h, 0:257], lhsT=QT[:, c, cols], rhs=DB[:, h, half, 0:257], start=False, stop=(half == 1)))(c, h, half, cols),
                                        reads=[t_QT[c], t_DB[h]], writes=[t_N[h]])
                        e0 = ch * 4
                        P.op("dve", (lambda ch: lambda e: e.tensor_tensor(
                            out=E1[:, 0:4, 0], in0=B_N[:, :, 256], in1=EB[:, ch * 4:ch * 4 + 4], op=ALU.mult))(ch),
                            reads=t_N + [t_EB], writes=[t_E1])
                        P.op("dve", lambda e: e.tensor_scalar(
                            out=E1[:, 0:4, 1], in0=E1[:, 0:4, 0], scalar1=-1.0, scalar2=1.0, op0=ALU.mult, op1=ALU.max),
                            reads=[t_E1], writes=[t_E1])
                        P.op("dve", lambda e: e.scalar_tensor_tensor(
                            out=E1[:, 0:4, 2], in0=E1[:, 0:4, 0], scalar=1.0, in1=E1[:, 0:4, 1], op0=ALU.max, op1=ALU.max),
                            reads=[t_E1], writes=[t_E1])
                        P.op("dve", lambda e: e.reciprocal(out=E1[:, 0:4, 3], in_=E1[:, 0:4, 2]), reads=[t_E1], writes=[t_E1])
                        P.op("dve", (lambda ch: lambda e: e.tensor_tensor(
                            out=E1[:, 4:8, 0], in0=E1[:, 0:4, 3], in1=EB[:, ch * 4:ch * 4 + 4], op=ALU.mult))(ch),
                            reads=[t_E1, t_EB], writes=[t_E1])
                        for h in range(4):
                            P.op("act", (lambda hh, h: lambda e: e.activation(
                                out=hh[:, h * 256:(h + 1) * 256], in_=B_N[:, h, 0:256], func=AF.Copy, scale=E1[:, 4 + h, 0:1]))(hh, h),
                                reads=[t_N[h], t_E1], writes=[t_hh])
                    if not last_chunk:
                        for h in range(4):
                            idx = ch * 4 + h
                            P.op("act", (lambda h, idx: lambda e: e.activation(
                                out=DB[:, h, :, 0:257], in_=DS[:, h, :, :], func=AF.Copy, scale=EBL[:, idx:idx + 1]))(h, idx),
                                reads=[t_DS[h], t_EBL], writes=[t_DB[h]])
                            P.op("dve", (lambda h, idx: lambda e: e.tensor_scalar(
                                out=DS[:, h, :, :], in0=DS[:, h, :, :], scalar1=EBL[:, idx:idx + 1], scalar2=None, op0=ALU.mult))(h, idx),
                                reads=[t_DS[h], t_EBL], writes=[t_DS[h]])
                    have_state = True; hs[0] = True
                    if not lat:
                        continue
                    cg = gi * 2 + ch
                    if d == 0:
                        P.dma("sp", (lambda hh, cg: lambda e: e.dma_start(out=HF_d[cg], in_=hh[:]))(hh, cg),
                              reads=[t_hh], writes=[t_HF[cg]], semtrk=t_hh)
                        continue
                    hft, t_hft = hft_map.pop(cg)
                    P.op("dve", (lambda hh, hft: lambda e: e.tensor_tensor(out=hh[:], in0=hh[:], in1=hft[:], op=ALU.add))(hh, hft),
                         reads=[t_hh, t_hft], writes=[t_hh])
                    for h in range(4):
                        P.op("dve", (lambda hh, h: lambda e: e.bn_stats(out=BS[:, h, :], in_=hh[:, h * 256:(h + 1) * 256]))(hh, h),
                             reads=[t_hh], writes=[t_BS])
                        P.op("dve", (lambda h: lambda e: e.bn_aggr(out=MV[:, h, :], in_=BS[:, h, :]))(h), reads=[t_BS], writes=[t_MV])
                    P.op("act", lambda e: e.activation(out=SD[:, 0:4], in_=MV[:, :, 1], func=AF.Sqrt, bias=EPS), reads=[t_MV], writes=[t_SD])
                    P.op("dve", lambda e: e.reciprocal(out=SD[:, 4:8], in_=SD[:, 0:4]), reads=[t_SD], writes=[t_SD])
                    for h in range(4):
                        P.op("dve", (lambda hh, h: lambda e: e.tensor_scalar(
                            out=HN[:, h * 256:(h + 1) * 256], in0=hh[:, h * 256:(h + 1) * 256], scalar1=MV[:, h, 0:1],
                            scalar2=SD[:, 4 + h:5 + h], op0=ALU.subtract, op1=ALU.mult))(hh, h),
                            reads=[t_hh, t_MV, t_SD], writes=[t_HN])
                    for c in range(8):
                        P.op("pe", (lambda c: lambda e: e.transpose(
                            out=B_N[:, c // 4, (c % 4) * 128:(c % 4 + 1) * 128], in_=HN[:, c * 128:(c + 1) * 128], identity=IDF[:]))(c),
                            reads=[t_HN, t_IDF], writes=[t_N[c // 4]])
                    o_, w_ = FVCOLS["mlng"]
                    for b2 in range(2):
                        P.op("dve", (lambda b2: lambda e: e.tensor_tensor(
                            out=Y1[:, 4 * b2:4 * b2 + 4, :], in0=B_N[:, b2, :].rearrange("p (c t) -> p c t", t=128),
                            in1=FV[:, o_ + 4 * b2:o_ + 4 * b2 + 4].unsqueeze(2).to_broadcast([128, 4, 128]), op=ALU.mult))(b2),
                            reads=[t_N[b2], t_FV], writes=[t_Y1])
                    P.op("dve", (lambda cols: lambda e: e.tensor_tensor(out=Y1[:], in0=Y1[:], in1=XM[:, :, cols], op=ALU.add))(cols),
                         reads=[t_Y1] + t_XM, writes=[t_Y1])
                    P.op("dve", (lambda yg, cols: lambda e: e.tensor_tensor(out=yg[:, :, cols], in0=Y1[:], in1=SIG[:, :, cols], op=ALU.mult))(yg, cols),
                         reads=[t_Y1] + t_SIG, writes=[t_yg])
                if d == 1 and lat:
                    P.dma("sp", (lambda yg, gi: lambda e: e.dma_start(
                        out=YML_d[:, :, gi * 256:(gi + 1) * 256].rearrange("c p t -> p c t"), in_=yg[:]))(yg, gi),
                        reads=[t_yg], writes=[t_YML[gi]], semtrk=t_yg)
            for d in range(2):
                with ExitStack() as st2:
                    sb2 = lambda name, shape, dt: st2.enter_context(nc.sbuf_tensor(name, list(shape), dt))
                    if d == 0:
                        WMX = sb2("WMX", [128, 8, 8, 128], BF16); t_WMX = [Trk("WMX%d" % c) for c in range(8)]
                        for c in range(8):
                            P.dma("pool", (lambda c: lambda e: e.dma_start(
                                out=WMX[:, c], in_=win_d[16 + c].rearrange("p (k j) -> p k j", j=128)))(c), writes=[t_WMX[c]])
                        UMF = Rot(sb2, "UMF", 2, [128, 260], F32)
                        HALO = sb2("HALO", [128, 8, 2], F32); t_HALO = [Trk("HALO%d" % c) for c in range(8)]
                        XCV = Rot(sb2, "XCV", 2, [128, 256], F32)
                    if d == 1:
                        WMO = sb2("WMO", [128, 8, 8, 128], BF16); t_WMO = [Trk("WMO%d" % c) for c in range(8)]
                        for c in range(8):
                            P.dma("pool", (lambda c: lambda e: e.dma_start(
                                out=WMO[:, c], in_=win_d[24 + c].rearrange("p (k j) -> p k j", j=128)))(c), writes=[t_WMO[c]])
                        SIG = sb2("SIG", [128, 8, 256], F32); t_SIG = [Trk("SIG%d" % c) for c in range(8)]
                        HFt = Rot(sb2, "HFt", 2, [128, 1024], F32)
                        hft_map = {}
                        HN = sb2("HN", [128, 1024], F32); t_HN = Trk("HN")
                        BS = sb2("BS", [128, 4, 6], F32); t_BS = Trk("BS")
                        MV = sb2("MV", [128, 4, 2], F32); t_MV = Trk("MV")
                        SD = sb2("SD", [128, 8], F32); t_SD = Trk("SD")
                        Y1 = sb2("Y1", [128, 8, 128], F32); t_Y1 = Trk("Y1")
                        YG = Rot(sb2, "YG", 2, [128, 8, 256], BF16)
                        GS1 = [sb2("QT1", [128, 8, 256], BF16), sb2("KT1", [128, 8, 256], BF16),
                               sb2("XMB1", [128, 8, 256], BF16), sb2("UMB1", [128, 8, 256], BF16)]
                        GSETS = [((QT, KT, XMB, UMB), [Trk("gs0_%d" % i) for i in range(4)]),
                                 (tuple(GS1), [Trk("gs1_%d" % i) for i in range(4)])]
                        t_XMl = Trk("XMl")
                    hs = [False]
                    order = [0] + (list(range(1, 17)) if d == 0 else list(range(16, 0, -1)))
                    for gpos, g in enumerate(order):
                        lat = g > 0
                        gi = g - 1
                        if d == 0:
                            import os as _os2
                            PB = [B_N[:, 0, :], B_N[:, 1, :]]
                            t_PB = [t_N[0], t_N[1]]
                            if _os2.environ.get('PBPM'):
                                PB = [B_PM[:], B_PM[:]]; t_PB = [t_PM, t_PM]
                            hi = 259 if (lat and gi <= 14) else 258
                            lo = 0 if (lat and gi >= 1) else 2

                            def emit_proj(c):
                                pb, t_pb = PB[c % 2], t_PB[c % 2]
                                for kc in range(8):
                                    P.op("pe", (lambda pb, c, kc, g: lambda e: e.matmul(
                                        pb[:, 2:258], lhsT=WMX[:, c, kc, :], rhs=grp_rhs(kc, g), start=(kc == 0), stop=(kc == 7)))(pb, c, kc, g),
                                        reads=[t_WMX[c]] + grp_trks(g), writes=[t_pb])
                                if hi == 259:
                                    b0 = 256 + (gi + 1) * 256
                                    for kc in range(8):
                                        P.op("pe", (lambda pb, c, kc, b0: lambda e: e.matmul(
                                            pb[:, 258:259], lhsT=WMX[:, c, kc, :], rhs=uT[:, kc, b0:b0 + 1], start=(kc == 0), stop=(kc == 7)))(pb, c, kc, b0),
                                            reads=[t_WMX[c]] + grp_trks(g), writes=[t_pb])

                            bufs_c = {}

                            def emit_evac_act(c):
                                pb, t_pb = PB[c % 2], t_PB[c % 2]
                                umf, t_umf = UMF.next()
                                xcv, t_xcv = XCV.next()
                                bufs_c[c] = (umf, t_umf, xcv, t_xcv)
                                P.op("act", (lambda umf, pb, hi: lambda e: e.activation(
                                    out=umf[:, 2:hi], in_=pb[:, 2:hi], func=AF.Copy))(umf, pb, hi), reads=[t_pb], writes=[t_umf])
                                P.op("act", (lambda c, pb: lambda e: e.activation(out=UMB[:, c, :], in_=pb[:, 2:258], func=AF.Copy))(c, pb),
                                     reads=[t_pb], writes=[t_UMB[c]])

                            def emit_conv(c):
                                umf, t_umf, xcv, t_xcv = bufs_c[c]
                                if lo == 0:
                                    P.op("dve", (lambda umf, c: lambda e: e.tensor_copy(out=umf[:, 0:2], in_=HALO[:, c, :]))(umf, c),
                                         reads=[t_HALO[c]], writes=[t_umf])
                                else:
                                    P.op("dve", (lambda umf: lambda e: e.memset(umf[:, 0:2], 0.0))(umf), writes=[t_umf])
                                if hi == 258:
                                    P.op("dve", (lambda umf: lambda e: e.memset(umf[:, 258:259], 0.0))(umf), writes=[t_umf])
                                if lat and gi <= 14:
                                    P.op("dve", (lambda umf, c: lambda e: e.tensor_copy(out=HALO[:, c, :], in_=umf[:, 256:258]))(umf, c),
                                         reads=[t_umf], writes=[t_HALO[c]])
                                P.op("dve", (lambda umf, xcv, c: lambda e: e.tensor_scalar(
                                    out=xcv[:], in0=umf[:, 0:256], scalar1=fvc("mlcw", c * 4), scalar2=fvc("mlcb", c),
                                    op0=ALU.mult, op1=ALU.add))(umf, xcv, c), reads=[t_umf, t_FV], writes=[t_xcv])
                                for j in range(1, 4):
                                    P.op("dve", (lambda umf, xcv, c, j: lambda e: e.scalar_tensor_tensor(
                                        out=xcv[:], in0=umf[:, j:j + 256], scalar=fvc("mlcw", c * 4 + j), in1=xcv[:],
                                        op0=ALU.mult, op1=ALU.add))(umf, xcv, c, j), reads=[t_umf, t_FV, t_xcv], writes=[t_xcv])
                                P.op("act", (lambda xcv, c: lambda e: e.activation(out=XM[:, c, :], in_=xcv[:], func=AF.Silu))(xcv, c),
                                     reads=[t_xcv], writes=[t_XM[c]])
                                P.op("act", (lambda xcv, c: lambda e: e.activation(out=XMB[:, c, :], in_=xcv[:], func=AF.Silu))(xcv, c),
                                     reads=[t_xcv], writes=[t_XMB[c]])

                            vts = {}

                            def emit_qkv(c):
                                vt, t_vt = VT.next()
                                vts[c] = (vt, t_vt)
                                P.op("pe", (lambda c: lambda e: e.matmul(
                                    B_QK[:, 0:256], lhsT=MLBD[:, c, :], rhs=XMB[:, c, :], start=True, stop=True))(c),
                                    reads=[t_MLBD, t_XMB[c]], writes=[t_PQ])
                                P.op("pe", (lambda c: lambda e: e.matmul(
                                    B_QK[:, 256:512], lhsT=MLBD[:, 8 + c, :], rhs=XMB[:, c, :], start=True, stop=True))(c),
                                    reads=[t_MLBD, t_XMB[c]], writes=[t_PQ])
                                P.op("pe", (lambda c: lambda e: e.matmul(
                                    B_VO[:, 0:256], lhsT=MLBD[:, 16 + c, :], rhs=UMB[:, c, :], start=True, stop=True))(c),
                                    reads=[t_MLBD, t_UMB[c]], writes=[t_PV])
                                P.op("act", (lambda c: lambda e: e.activation(out=QT[:, c, :], in_=B_QK[:, 0:256], func=AF.Copy))(c),
                                     reads=[t_PQ], writes=[t_QT[c]])
                                P.op("act", (lambda c: lambda e: e.activation(
                                    out=KT[:, c, :], in_=B_QK[:, 256:512], func=AF.Copy, scale=1.0 / 16.0))(c),
                                    reads=[t_PQ], writes=[t_KT[c]])
                                P.op("dve", (lambda vt: lambda e: e.tensor_copy(out=vt[:], in_=B_VO[:, 0:256]))(vt),
                                     reads=[t_PV], writes=[t_vt])

                            def emit_gates(c):
                                vt, t_vt = vts[c]
                                for ti, (src, t_src) in enumerate(((QT[:, c, :], t_QT[c]), (KT[:, c, :], t_KT[c]), (vt[:], t_vt))):
                                    P.op("pe", (lambda c, ti, src, d: lambda e: e.matmul(
                                        B_GP[0:8, 0:256], lhsT=WGT[:, ti * 8 + c, d * 8:(d + 1) * 8], rhs=src,
                                        start=(c == 0 and ti == 0), stop=(c == 7 and ti == 2)))(c, ti, src, d),
                                        reads=[t_WGT, t_src], writes=[t_GP])

                            if _os2.environ.get("NOPIPE"):
                                for c in range(8):
                                    emit_proj(c)
                                    emit_evac_act(c)
                                    emit_conv(c)
                                    emit_qkv(c)
                                    emit_gates(c)
                            else:
                                emit_proj(0)
                                emit_proj(1)
                                emit_evac_act(0)
                                for c in range(8):
                                    if c + 2 < 8:
                                        emit_proj(c + 2)
                                    if c + 1 < 8:
                                        emit_evac_act(c + 1)
                                    emit_conv(c)
                                    emit_qkv(c)
                                    if c > 0:
                                        emit_gates(c - 1)
                                emit_gates(7)
                            for wi_, (arr, trs) in enumerate(((QT, t_QT), (KT, t_KT), (XMB, t_XMB), (UMB, t_UMB))):
                                P.dma("sp", (lambda arr, g, wi_: lambda e: e.dma_start(out=SPQ_d[g, wi_], in_=arr[:]))(arr, g, wi_),
                                      reads=trs, writes=[t_SPQ[g][wi_]], semtrk=t_spd[wi_])
                            if lat:
                                P.dma("sp", (lambda gi: lambda e: e.dma_start(out=SPX_d[gi], in_=XM[:]))(gi),
                                      reads=t_XM, writes=[t_SPX[gi]], semtrk=t_spd[4])
                            gates_post(d)
                            rec_group(g, d, QT, KT, XMB, UMB, t_QT, t_KT, t_XMB, t_UMB, XM, t_XM, hs)
                            continue
                        def load_set(gq, si):
                            arrs, trs = GSETS[si]
                            for wi_ in range(4):
                                P.dma("sp", (lambda arrs, gq, wi_: lambda e: e.dma_start(out=arrs[wi_][:], in_=SPQ_d[gq, wi_]))(arrs, gq, wi_),
                                      reads=[t_SPQ[gq][wi_]], writes=[trs[wi_]])
                        if gpos == 0:
                            load_set(g, 0)
                        if gpos + 1 < len(order):
                            load_set(order[gpos + 1], (gpos + 1) % 2)
                        (QTg, KTg, XMBg, UMBg), trs = GSETS[gpos % 2]
                        if lat:
                            P.dma("sp", (lambda gi: lambda e: e.dma_start(out=XM[:], in_=SPX_d[gi]))(gi), reads=[t_SPX[gi]], writes=[t_XMl])
                        for c in range(8):
                            vt, t_vt = VT.next()
                            P.op("pe", (lambda c, UMBg: lambda e: e.matmul(
                                B_VO[:, 0:256], lhsT=MLBD[:, 16 + c, :], rhs=UMBg[:, c, :], start=True, stop=True))(c, UMBg),
                                reads=[t_MLBD, trs[3]], writes=[t_PV])
                            P.op("dve", (lambda vt: lambda e: e.tensor_copy(out=vt[:], in_=B_VO[:, 0:256]))(vt),
                                 reads=[t_PV], writes=[t_vt])
                            if lat:
                                for kc in range(8):
                                    P.op("pe", (lambda c, kc, g: lambda e: e.matmul(
                                        B_PM[:, 0:256], lhsT=WMO[:, c, kc, :], rhs=grp_rhs(kc, g), start=(kc == 0), stop=(kc == 7)))(c, kc, g),
                                        reads=[t_WMO[c]] + grp_trks(g), writes=[t_PM])
                                P.op("act", (lambda c: lambda e: e.activation(out=SIG[:, c, :], in_=B_PM[:, 0:256], func=AF.Sigmoid))(c),
                                     reads=[t_PM], writes=[t_SIG[c]])
                            for ti, (src, t_src) in enumerate(((QTg[:, c, :], trs[0]), (KTg[:, c, :], trs[1]), (vt[:], t_vt))):
                                P.op("pe", (lambda c, ti, src, d: lambda e: e.matmul(
                                    B_GP[0:8, 0:256], lhsT=WGT[:, ti * 8 + c, d * 8:(d + 1) * 8], rhs=src,
                                    start=(c == 0 and ti == 0), stop=(c == 7 and ti == 2)))(c, ti, src, d),
                                    reads=[t_WGT, t_src], writes=[t_GP])
                        if lat:
                            o2_, w2_ = FVCOLS["mlsk"]
                            P.op("dve", lambda e: e.tensor_tensor(
                                out=XM[:], in0=XM[:], in1=FV[:, o2_:o2_ + 8].unsqueeze(2).to_broadcast([128, 8, 256]), op=ALU.mult),
                                reads=[t_XMl, t_FV], writes=[t_XMl])
                        gates_post(d)
                        rec_group(g, d, QTg, KTg, XMBg, UMBg, [trs[0]] * 8, [trs[1]] * 8, [trs[2]] * 8, [trs[3]] * 8, XM, [t_XMl] * 8, hs)
                    P.barrier()
                    P.emit()
        if "yml" in debug:
            dbg_d["yml"] = YML_d
        if stop_after == "ml":
            P.wait_all("sp", P.out_toks)
            P.barrier()
            P.emit()
            ust.close()
            return nc, dbg_d

        x_cm = x_d.rearrange("(r j) d -> j r d", j=64)
        out_cm = out_d.rearrange("(r j) d -> j r d", j=64)
        t_X1 = [Trk("X1_%d" % i) for i in range(32)]
        t_H2 = [Trk("H2_%d" % i) for i in range(16)]
        with ExitStack() as st:
            sb = lambda name, shape, dt: st.enter_context(nc.sbuf_tensor(name, list(shape), dt))
            ps = lambda name, shape, dt: st.enter_context(nc.psum_tensor(name, list(shape), dt))
            WGR = sb("WGR", [128, 8, 8, 128], BF16); WGM = sb("WGM", [128, 8, 8, 128], BF16)
            WBR = sb("WBR", [128, 8, 8, 128], BF16); WBM = sb("WBM", [128, 8, 8, 128], BF16)
            WOUT = sb("WOUT", [128, 8, 1024], BF16)
            t_WGR = [Trk("WGR%d" % i) for i in range(8)]; t_WGM = [Trk("WGM%d" % i) for i in range(8)]
            t_WBR = [Trk("WBR%d" % i) for i in range(8)]; t_WBM = [Trk("WBM%d" % i) for i in range(8)]
            t_WOUT = [Trk("WOUT%d" % i) for i in range(8)]
            for oc in range(8):
                for (W, t_W, src) in ((WGR, t_WGR, win_d[32 + oc]), (WGM, t_WGM, win_d[40 + oc]),
                                      (WBR, t_WBR, wbrg_d[oc]), (WBM, t_WBM, wbml_d[oc])):
                    P.dma("pool", (lambda W, oc, src: lambda e: e.dma_start(
                        out=W[:, oc], in_=src.rearrange("p (k j) -> p k j", j=128)))(W, oc, src), writes=[t_W[oc]])
            for kc in range(8):
                P.dma("pool", (lambda kc: lambda e: e.dma_start(out=WOUT[:, kc, :], in_=wout_d[:, kc, :]))(kc), writes=[t_WOUT[kc]])
            YRt = Rot(sb, "YRt", 1, [128, 8, 512], BF16)
            YMt = Rot(sb, "YMt", 1, [128, 8, 512], BF16)
            SG = Rot(sb, "SG", 1, [128, 1024], F32)
            MIX = Rot(sb, "MIX", 1, [128, 8, 512], BF16)
            XT = Rot(sb, "XTc", 2, [128, 1024], F32)
            X1t = Rot(sb, "X1t", 1, [128, 1024], F32)
            XN = Rot(sb, "XNc", 1, [128, 1024], BF16)
            H2s = Rot(sb, "H2s", 1, [128, 8, 256], BF16)
            STc = sb("STc", [128, 32, 4], F32)
            BA0 = ps("BA0", [128, 512], F32); t_BA0 = Trk("PS_BA0")
            BA1 = ps("BA1", [128, 512], F32); t_BA1 = Trk("PS_BA1")
            BB0 = ps("BB0", [128, 512], F32); t_BB0 = Trk("PS_BB0")
            BB1 = ps("BB1", [128, 512], F32); t_BB1 = Trk("PS_BB1")
            BY = ps("BY", [128, 1024], F32); t_BY = Trk("PS_BY")
            BT = ps("BT", [128, 8, 128], BF16); t_BT = Trk("PS_BT")
            def load_y(T):
                yr, t_yr = YRt.next()
                ym, t_ym = YMt.next()
                P.dma("sp", (lambda yr, T: lambda e: e.dma_start(
                    out=yr[:], in_=YRG_d[:, :, T * 512:(T + 1) * 512].rearrange("c p t -> p c t")))(yr, T), reads=t_YRG, writes=[t_yr])
                P.dma("sp", (lambda ym, T: lambda e: e.dma_start(
                    out=ym[:], in_=YML_d[:, :, T * 512:(T + 1) * 512].rearrange("c p t -> p c t")))(ym, T),
                    reads=t_YML[2 * T:2 * T + 2], writes=[t_ym])
                return yr, t_yr, ym, t_ym

            def load_x(ti):
                xt, t_xt = XT.next()
                for jj in range(2):
                    P.dma("sp", (lambda xt, jj, ti: lambda e: e.dma_start(
                        out=xt[jj * 64:(jj + 1) * 64, :], in_=x_cm[2 * ti + jj]))(xt, jj, ti), writes=[t_xt])
                return xt, t_xt

            ynext = load_y(0)
            xnext = load_x(0)
            for T in range(8):
                yr, t_yr, ym, t_ym = ynext
                mix, t_mix = MIX.next()
                for oc in range(8):
                    sg, t_sg = SG.next()
                    for (W, t_W, bank, t_bank) in ((WGR, t_WGR, BA0, t_BA0), (WGM, t_WGM, BA1, t_BA1)):
                        for kc in range(8):
                            P.op("pe", (lambda W, bank, oc, kc, T: lambda e: e.matmul(
                                bank[:], lhsT=W[:, oc, kc, :],
                                rhs=uT[:, kc, 256 + T * 512:256 + (T + 1) * 512],
                                start=(kc == 0), stop=(kc == 7)))(W, bank, oc, kc, T), reads=[t_W[oc]] + t_uT[2:34], writes=[t_bank])
                    for (W, t_W, src, t_src, bank, t_bank) in ((WBR, t_WBR, yr, t_yr, BB0, t_BB0), (WBM, t_WBM, ym, t_ym, BB1, t_BB1)):
                        for kc in range(8):
                            P.op("pe", (lambda W, bank, oc, kc, src: lambda e: e.matmul(
                                bank[:], lhsT=W[:, oc, kc, :], rhs=src[:, kc, :],
                                start=(kc == 0), stop=(kc == 7)))(W, bank, oc, kc, src), reads=[t_W[oc], t_src], writes=[t_bank])
                    for (i_, bank, t_bank) in ((0, BA0, t_BA0), (1, BA1, t_BA1)):
                        P.op("act", (lambda sg, bank, i_: lambda e: e.activation(
                            out=sg[:, i_ * 512:(i_ + 1) * 512], in_=bank[:], func=AF.Sigmoid))(sg, bank, i_), reads=[t_bank], writes=[t_sg])
                    for (i_, bank, t_bank) in ((0, BB0, t_BB0), (1, BB1, t_BB1)):
                        P.op("dve", (lambda sg, bank, i_: lambda e: e.tensor_tensor(
                            out=sg[:, i_ * 512:(i_ + 1) * 512], in0=bank[:], in1=sg[:, i_ * 512:(i_ + 1) * 512], op=ALU.mult))(sg, bank, i_),
                            reads=[t_bank, t_sg], writes=[t_sg])
                    P.op("dve", (lambda mix, sg, oc: lambda e: e.tensor_tensor(
                        out=mix[:, oc, :], in0=sg[:, 0:512], in1=sg[:, 512:1024], op=ALU.add))(mix, sg, oc),
                        reads=[t_sg], writes=[t_mix])
                if T + 1 < 8:
                    ynext = load_y(T + 1)
                for s_ in range(4):
                    ti = T * 4 + s_
                    if s_ % 2 == 0:
                        h2s, t_h2s = H2s.next()
                    xt, t_xt = xnext
                    if ti + 1 < 32:
                        xnext = load_x(ti + 1)
                    x1, t_x1 = X1t.next()
                    xn, t_xn = XN.next()
                    t_st = Trk("stc%d" % ti)
                    for half in range(2):
                        for kc in range(8):
                            P.op("pe", (lambda mix, half, kc, s_: lambda e: e.matmul(
                                BY[:, half * 512:(half + 1) * 512], lhsT=mix[:, kc, s_ * 128:(s_ + 1) * 128],
                                rhs=WOUT[:, kc, half * 512:(half + 1) * 512], start=(kc == 0), stop=(kc == 7)))(mix, half, kc, s_),
                                reads=[t_mix, t_WOUT[kc]], writes=[t_BY])
                    P.op("dve", (lambda x1: lambda e: e.tensor_tensor(out=x1[:], in0=BY[:], in1=GROW[:, 0, :], op=ALU.mult))(x1),
                         reads=[t_BY, t_GROW], writes=[t_x1])
                    P.op("dve", (lambda x1, xt: lambda e: e.tensor_tensor(out=x1[:], in0=x1[:], in1=xt[:], op=ALU.add))(x1, xt),
                         reads=[t_x1, t_xt], writes=[t_x1])
                    P.dma("sp", (lambda x1, ti: lambda e: e.dma_start(out=X1_d[ti * 128:(ti + 1) * 128, :], in_=x1[:]))(x1, ti),
                          reads=[t_x1], writes=[t_X1[ti]], semtrk=t_x1)
                    P.op("act", (lambda x1, xn, ti: lambda e: e.activation(
                        out=xn[:], in_=x1[:], func=AF.Square, accum_out=STc[:, ti, 0:1]))(x1, xn, ti), reads=[t_x1], writes=[t_xn, t_st])
                    P.op("act", (lambda ti: lambda e: e.activation(
                        out=STc[:, ti, 1:2], in_=STc[:, ti, 0:1], func=AF.Sqrt, scale=1.0 / 1024.0, bias=EPS))(ti), reads=[t_st], writes=[t_st])
                    P.op("dve", (lambda ti: lambda e: e.reciprocal(out=STc[:, ti, 2:3], in_=STc[:, ti, 1:2]))(ti), reads=[t_st], writes=[t_st])
                    P.op("act", (lambda x1, xn, ti: lambda e: e.activation(
                        out=xn[:], in_=x1[:], func=AF.Copy, scale=STc[:, ti, 2:3]))(x1, xn, ti), reads=[t_x1, t_st], writes=[t_xn])
                    for c in range(8):
                        P.op("pe", (lambda xn, c: lambda e: e.transpose(
                            out=BT[:, c, :], in_=xn[:, c * 128:(c + 1) * 128], identity=IDB[:]))(xn, c), reads=[t_xn, t_IDB], writes=[t_BT])
                    for c in range(8):
                        dst = h2s[:, c, (s_ % 2) * 128:(s_ % 2 + 1) * 128]
                        if c % 2 == 0:
                            P.op("dve", (lambda c, dst: lambda e: e.tensor_scalar(
                                out=dst, in0=BT[:, c, :], scalar1=A2[:, c:c + 1], scalar2=MODT[:, 48 + 2 * c:49 + 2 * c],
                                op0=ALU.mult, op1=ALU.add))(c, dst), reads=[t_BT, t_A2, t_MODT], writes=[t_h2s])
                        else:
                            P.op("act", (lambda c, dst: lambda e: e.activation(
                                out=dst, in_=BT[:, c, :], func=AF.Identity, scale=A2[:, c:c + 1], bias=MODT[:, 48 + 2 * c:49 + 2 * c]))(c, dst),
                                reads=[t_BT, t_A2, t_MODT], writes=[t_h2s])
                    if s_ % 2 == 1:
                        Tq = ti // 2
                        P.dma("sp", (lambda h2s, Tq: lambda e: e.dma_start(out=H2_d[Tq], in_=h2s[:]))(h2s, Tq),
                              reads=[t_h2s], writes=[t_H2[Tq]], semtrk=t_h2s)
            P.barrier()
            P.emit()
        ust.close()
        if "x1" in debug:
            dbg_d["x1"] = X1_d

        with ExitStack() as st:
            sb = lambda name, shape, dt: st.enter_context(nc.sbuf_tensor(name, list(shape), dt))
            ps = lambda name, shape, dt: st.enter_context(nc.psum_tensor(name, list(shape), dt))
            WFI = sb("WFI", [128, 44, 8, 128], BF16); t_WFI = [Trk("WFI%d" % i) for i in range(44)]
            WFO = sb("WFO", [128, 22, 1024], BF16); t_WFO = [Trk("WFO%d" % i) for i in range(22)]
            FGR = sb("FGR", [128, 1024], F32); t_FGR = Trk("FGR")
            P.dma("sp", lambda e: e.dma_start(out=FGR[:], in_=fgrow_d), writes=[t_FGR])
            for f_ in range(22):
                for ci in (f_, 22 + f_):
                    P.dma("pool", (lambda ci: lambda e: e.dma_start(
                        out=WFI[:, ci], in_=wffi_d[ci].rearrange("p (k j) -> p k j", j=128)))(ci), writes=[t_WFI[ci]])
                P.dma("pool", (lambda f_: lambda e: e.dma_start(out=WFO[:, f_, :], in_=wffo_d[:, f_, :]))(f_), writes=[t_WFO[f_]])
            H2t = Rot(sb, "H2t", 2, [128, 8, 512], BF16)
            SGt = Rot(sb, "SGt", 2, [128, 512], F32)
            HID = Rot(sb, "HID", 1, [128, 22, 512], BF16)
            X1r = Rot(sb, "X1r", 2, [128, 1024], F32)
            T2 = Rot(sb, "T2", 2, [128, 1024], F32)
            JK = sb("JK", [128, 1024], BF16); t_JK = Trk("JK")
            STf = sb("STf", [128, 32, 4], F32)
            BG0 = Rot(ps, "PS_BG0", 2, [128, 512], F32)
            BG1 = Rot(ps, "PS_BG1", 2, [128, 512], F32)
            BO = Rot(ps, "PS_BO", 2, [128, 1024], F32)
            h2_map = {}
            x1_map = {}
            for T in range(8):
                for Tq in (T, T + 1):
                    if Tq < 8 and Tq not in h2_map:
                        hb, t_hb = H2t.next()
                        for q_ in range(2):
                            P.dma("sp", (lambda hb, Tq, q_: lambda e: e.dma_start(out=hb[:, :, q_ * 256:(q_ + 1) * 256], in_=H2_d[2 * Tq + q_]))(hb, Tq, q_),
                                  reads=[t_H2[2 * Tq + q_]], writes=[t_hb])
                        h2_map[Tq] = (hb, t_hb)
                h2, t_h2 = h2_map.pop(T)
                hid, t_hid = HID.next()
                for f_ in range(22):
                    g0, t_g0 = BG0.next()
                    g1, t_g1 = BG1.next()
                    sg, t_sg = SGt.next()
                    for (ci, bank, t_bank) in ((f_, g0, t_g0), (22 + f_, g1, t_g1)):
                        for kc in range(8):
                            P.op("pe", (lambda bank, ci, kc, h2: lambda e: e.matmul(
                                bank[:], lhsT=WFI[:, ci, kc, :], rhs=h2[:, kc, :], start=(kc == 0), stop=(kc == 7)))(bank, ci, kc, h2),
                                reads=[t_WFI[ci], t_h2], writes=[t_bank])
                    P.op("act", (lambda sg, g0: lambda e: e.activation(out=sg[:], in_=g0[:], func=AF.Silu))(sg, g0),
                         reads=[t_g0], writes=[t_sg])
                    P.op("dve", (lambda hid, f_, sg, g1: lambda e: e.tensor_tensor(
                        out=hid[:, f_, :], in0=g1[:], in1=sg[:], op=ALU.mult))(hid, f_, sg, g1), reads=[t_g1, t_sg], writes=[t_hid])
                for s_ in range(4):
                    ti = T * 4 + s_
                    bo, t_bo = BO.next()
                    for tq in (ti, ti + 1):
                        if tq < 32 and tq not in x1_map:
                            xb, t_xb = X1r.next()
                            P.dma("sp", (lambda xb, tq: lambda e: e.dma_start(out=xb[:], in_=X1_d[tq * 128:(tq + 1) * 128, :]))(xb, tq),
                                  reads=[t_X1[tq]], writes=[t_xb])
                            x1_map[tq] = (xb, t_xb)
                    x1, t_x1 = x1_map.pop(ti)
                    t2, t_t2 = T2.next()
                    t_st = Trk("stf%d" % ti)
                    for half in range(2):
                        for f_ in range(22):
                            P.op("pe", (lambda bo, hid, half, f_, s_: lambda e: e.matmul(
                                bo[:, half * 512:(half + 1) * 512], lhsT=hid[:, f_, s_ * 128:(s_ + 1) * 128],
                                rhs=WFO[:, f_, half * 512:(half + 1) * 512], start=(f_ == 0), stop=(f_ == 21)))(bo, hid, half, f_, s_),
                                reads=[t_hid, t_WFO[f_]], writes=[t_bo])
                    P.op("dve", (lambda t2, bo: lambda e: e.tensor_tensor(out=t2[:], in0=bo[:], in1=GROW[:, 1, :], op=ALU.mult))(t2, bo),
                         reads=[t_bo, t_GROW], writes=[t_t2])
                    P.op("dve", (lambda t2, x1: lambda e: e.tensor_tensor(out=t2[:], in0=t2[:], in1=x1[:], op=ALU.add))(t2, x1),
                         reads=[t_t2, t_x1], writes=[t_t2])
                    P.op("act", (lambda t2, ti: lambda e: e.activation(
                        out=JK[:], in_=t2[:], func=AF.Square, accum_out=STf[:, ti, 0:1]))(t2, ti), reads=[t_t2], writes=[t_JK, t_st])
                    P.op("act", (lambda ti: lambda e: e.activation(
                        out=STf[:, ti, 1:2], in_=STf[:, ti, 0:1], func=AF.Sqrt, scale=1.0 / 1024.0, bias=EPS))(ti), reads=[t_st], writes=[t_st])
                    P.op("dve", (lambda ti: lambda e: e.reciprocal(out=STf[:, ti, 2:3], in_=STf[:, ti, 1:2]))(ti), reads=[t_st], writes=[t_st])
                    P.op("act", (lambda t2, ti: lambda e: e.activation(
                        out=t2[:], in_=t2[:], func=AF.Copy, scale=STf[:, ti, 2:3]))(t2, ti), reads=[t_t2, t_st], writes=[t_t2])
                    P.op("dve", (lambda t2: lambda e: e.tensor_tensor(out=t2[:], in0=t2[:], in1=FGR[:], op=ALU.mult))(t2),
                         reads=[t_t2, t_FGR], writes=[t_t2])
                    for jj in range(2):
                        P.out_toks.append(P.dma("sp", (lambda t2, jj, ti: lambda e: e.dma_start(
                            out=out_cm[2 * ti + jj], in_=t2[jj * 64:(jj + 1) * 64, :]))(t2, jj, ti), reads=[t_t2], semtrk=t_t2))
            P.barrier()
            P.emit()

        P.wait_all("sp", P.out_toks)
        P.emit()
    return nc, dbg_d


def make_in_maps(inputs):
    sh = prep_shared(inputs)
    x = np.asarray(inputs["x"], np.float32)
    c = np.asarray(inputs["c"], np.float32)
    ctx = np.asarray(inputs["ctx"], np.float32)
    c_ctx = np.asarray(inputs["c_ctx"], np.float32)
    maps = []
    for b in range(8):
        m = dict(sh)
        m["x"] = np.ascontiguousarray(x[b])
        m["ctx"] = np.ascontiguousarray(ctx[b])
        m["cv"] = np.ascontiguousarray(np.stack([fm(c[b]), fm(c_ctx)], 2).reshape(128, 16))
        maps.append(m)
    return maps


def kernel(**inputs):
    nc, _ = build()
    maps = make_in_maps(inputs)
    res = run_bass_kernel_spmd(nc, maps, core_ids=list(range(8)))
    return np.stack([r["out"] for r in res.results], 0)
```

```python
import numpy as np
from contextlib import ExitStack
import concourse.bass as bass
import concourse.mybir as mybir
from concourse.bass_utils import run_bass_kernel_spmd

F32 = mybir.dt.float32
BF16 = mybir.dt.bfloat16
AF = mybir.ActivationFunctionType
ALU = mybir.AluOpType
AX = mybir.AxisListType

ENGS = ["pe", "act", "dve", "pool", "sp"]
EPS = 1e-6
NT = 4352
NCTX = 256
NLAT = 4096


class Trk:
    __slots__ = ("name", "w", "r", "dsem", "dcnt", "excl")

    def __init__(self, name=""):
        self.name = name
        self.w = None
        self.r = {}
        self.dsem = None
        self.dcnt = 0
        self.excl = name.startswith("PS_")


class Prog:
    def __init__(self, nc, stack):
        self.nc = nc
        self.stack = stack
        self.ops = {e: [] for e in ENGS}
        self.seq = {e: 0 for e in ENGS}
        self.known = {e: {} for e in ENGS}
        self.esem = {e: stack.enter_context(nc.semaphore("s_" + e)) for e in ENGS}
        self.nsem = len(ENGS)
        self.out_toks = []
        self.dtrks = []
        self.sem_pool = {"sp": [], "pool": []}

    def new_dsem(self, name):
        s = self.stack.enter_context(self.nc.semaphore("d%d_%s" % (self.nsem, name)))
        self.nsem += 1
        return s

    def _need(self, eng, waits, dep):
        sem, val = dep
        if eng == "pe" and sem is self.esem["pe"]:
            return
        k = id(sem)
        if self.known[eng].get(k, 0) >= val:
            return
        self.known[eng][k] = val
        waits[k] = (sem, val)

    def _deps(self, eng, reads, writes):
        waits = {}
        for t in reads:
            if t.w is not None:
                self._need(eng, waits, t.w)
            if t.excl:
                for dep in t.r.values():
                    self._need(eng, waits, dep)
        for t in writes:
            if t.w is not None:
                self._need(eng, waits, t.w)
            for dep in t.r.values():
                self._need(eng, waits, dep)
        return waits

    def _record(self, tok, reads, writes):
        for t in reads:
            t.r[id(tok[0])] = tok
        for t in writes:
            t.w = tok
            t.r = {}

    def op(self, eng, fn, reads=(), writes=()):
        waits = self._deps(eng, reads, writes)
        self.seq[eng] += 1
        tok = (self.esem[eng], self.seq[eng])
        self._record(tok, reads, writes)
        self.ops[eng].append((list(waits.values()), fn, (self.esem[eng], 1)))

    def dma(self, eng, fn, reads=(), writes=(), semtrk=None):
        if semtrk is None:
            semtrk = writes[0] if writes else reads[0]
        if semtrk.dsem is None:
            if self.sem_pool[eng]:
                semtrk.dsem, semtrk.dcnt = self.sem_pool[eng].pop()
            else:
                semtrk.dsem = self.new_dsem(semtrk.name)
            self.dtrks.append((semtrk, eng))
        waits = self._deps(eng, reads, writes)
        if semtrk.dcnt > 0:
            self._need(eng, waits, (semtrk.dsem, semtrk.dcnt))
        semtrk.dcnt += 16
        tok = (semtrk.dsem, semtrk.dcnt)
        self._record(tok, reads, writes)
        self.ops[eng].append((list(waits.values()), fn, (semtrk.dsem, 16)))
        return tok

    def wait_all(self, eng, toks):
        waits = {}
        for t in toks:
            self._need(eng, waits, t)
        self.ops[eng].append((list(waits.values()), None, None))

    def barrier(self):
        waits = {}
        for e in ENGS:
            if e != "sp" and self.seq[e] > 0:
                self._need("sp", waits, (self.esem[e], self.seq[e]))
        for t, _e in self.dtrks:
            if t.dcnt > 0:
                self._need("sp", waits, (t.dsem, t.dcnt))
        self.seq["sp"] += 1
        self.ops["sp"].append((list(waits.values()), lambda e: e.nop(), (self.esem["sp"], 1)))
        for t, e_ in self.dtrks:
            self.sem_pool[e_].append((t.dsem, t.dcnt))
            t.dsem = None
            t.dcnt = 0
        self.dtrks = []
        for e in ENGS:
            if e != "sp":
                w = {}
                self._need(e, w, (self.esem["sp"], self.seq["sp"]))
                self.ops[e].append((list(w.values()), None, None))

    def emit(self):
        nc = self.nc
        ops = self.ops
        self.ops = {e: [] for e in ENGS}
        with nc.Block() as block:
            def run(e, lst):
                for waits, fn, inc in lst:
                    for sem, val in waits:
                        e.wait_ge(sem, val)
                    if fn is not None:
                        fn(e).then_inc(inc[0], inc[1])

            @block.tensor
            def _(e):
                run(e, ops["pe"])

            @block.scalar
            def _(e):
                run(e, ops["act"])

            @block.vector
            def _(e):
                run(e, ops["dve"])

            @block.gpsimd
            def _(e):
                run(e, ops["pool"])

            @block.sync
            def _(e):
                run(e, ops["sp"])


class Rot:
    def __init__(self, alloc, name, n, shape, dt):
        self.bufs = [(alloc("%s%d" % (name, i), shape, dt), Trk("%s%d" % (name, i))) for i in range(n)]
        self.i = 0

    def next(self):
        b = self.bufs[self.i % len(self.bufs)]
        self.i += 1
        return b


FVCOLS = {}
_off = 0
for _n, _w in [("n1g", 8), ("n2g", 8), ("rgcw", 32), ("rgcb", 8), ("rgba", 16), ("rgbx", 16), ("rglam", 16),
               ("mlcw", 32), ("mlcb", 8), ("mlng", 8), ("mlsk", 8)]:
    FVCOLS[_n] = (_off, _w)
    _off += _w
NV = _off


def fm(vec):
    v = np.asarray(vec, np.float32)
    return np.ascontiguousarray(v.reshape(-1, 128).T)


def colchunks(w):
    K, N = w.shape
    a = w.reshape(K // 128, 128, N // 128, 128)
    a = a.transpose(2, 1, 0, 3)
    return np.ascontiguousarray(a.reshape(N // 128, 128, (K // 128) * 128))


def rowchunks(w):
    K, N = w.shape
    return np.ascontiguousarray(w.reshape(K // 128, 128, N).transpose(1, 0, 2))


def blockdiag128(blocks):
    nb, bi, bo = blocks.shape
    per = 128 // bi
    out = np.zeros((nb // per, 128, 128), np.float32)
    for b in range(nb):
        c, q = divmod(b, per)
        out[c, q * bi:(q + 1) * bi, q * bo:(q + 1) * bo] = blocks[b]
    return out


def prep_shared(inp):
    f = lambda k: np.asarray(inp[k], np.float32)
    sh = {}
    w_mod = f("w_mod")[0]
    sh["wmod"] = colchunks(w_mod)
    b_mod = f("b_mod")[0]
    sh["bmod2"] = np.ascontiguousarray(np.repeat(fm(b_mod)[:, :, None], 2, axis=2).reshape(128, 96))
    sh["wmodg"] = np.stack([rowchunks(w_mod[:, 2048:3072]), rowchunks(w_mod[:, 5120:6144])], 0)
    sh["bmodg"] = np.ascontiguousarray(np.broadcast_to(
        np.concatenate([b_mod[2048:3072], b_mod[5120:6144]])[None, :], (128, 2048)))
    fv = np.zeros((128, NV), np.float32)

    def put(name, arr):
        o, w = FVCOLS[name]
        assert arr.shape == (128, w), (name, arr.shape)
        fv[:, o:o + w] = arr
    put("n1g", fm(f("norm1_g")[0]))
    put("n2g", fm(f("norm2_g")[0]))
    cw = f("rg_conv_w")[0]
    put("rgcw", np.stack([fm(cw[j]) for j in range(4)], 2).reshape(128, 32))
    put("rgcb", fm(f("rg_conv_b")[0]))
    put("rgba", np.concatenate([fm(f("rg_ba")[0][d]) for d in range(2)], 1))
    put("rgbx", np.concatenate([fm(f("rg_bx")[0][d]) for d in range(2)], 1))
    put("rglam", np.concatenate([fm(f("rg_lambda")[0][d]) for d in range(2)], 1))
    cw = f("ml_conv_w")[0]
    put("mlcw", np.stack([fm(cw[j]) for j in range(4)], 2).reshape(128, 32))
    put("mlcb", fm(f("ml_conv_b")[0]))
    put("mlng", fm(f("ml_norm_g")[0]))
    put("mlsk", fm(f("ml_skip")[0]))
    sh["fv"] = fv
    sh["win"] = colchunks(f("w_in")[0])
    rg = []
    for d in range(2):
        for w in (f("rg_wa")[0][d], f("rg_wx")[0][d]):
            rg.append(blockdiag128(w).transpose(1, 0, 2))
    sh["rgbd"] = np.ascontiguousarray(np.concatenate(rg, 1))
    ml = [blockdiag128(f(k)[0]).transpose(1, 0, 2) for k in ("ml_wq", "ml_wk", "ml_wv")]
    sh["mlbd"] = np.ascontiguousarray(np.concatenate(ml, 1))
    wi, wf = f("ml_wi")[0], f("ml_wf")[0]
    wg = np.concatenate([wi[0], wf[0], wi[1], wf[1]], 1)
    sh["wgt"] = np.ascontiguousarray(wg.reshape(24, 128, 16).transpose(1, 0, 2))
    bi, bf = f("ml_bi")[0], f("ml_bf")[0]
    gb = np.stack([np.tile(np.concatenate([bi[d], bf[d]]), 2) for d in range(2)], 0)
    sh["gbias"] = np.ascontiguousarray(np.broadcast_to(gb[None], (128, 2, 16)))
    tri = np.zeros((2, 128, 128), np.float32)
    ii = np.arange(128)
    tri[0] = (ii[:, None] <= ii[None, :])
    tri[1] = (ii[:, None] >= ii[None, :])
    sh["tri"] = np.ascontiguousarray(tri.transpose(1, 0, 2))
    ident = np.eye(128, dtype=np.float32)
    sh["ident"] = ident
    sh["wbrg"] = colchunks(f("w_branch_rg")[0])
    sh["wbml"] = colchunks(f("w_branch_ml")[0])
    sh["wout"] = rowchunks(f("w_out")[0])
    sh["wffi"] = colchunks(f("w_ffn_in")[0])
    sh["wffo"] = rowchunks(f("w_ffn_out")[0])
    sh["fgrow"] = np.ascontiguousarray(np.broadcast_to(f("final_norm_g")[None, :], (128, 1024)))
    return sh


def build(debug=(), stop_after=None):
    nc = bass.Bass("TRN2", target_bir_lowering=False)
    din = lambda name, shape, dt=F32: nc.dram_tensor(name, list(shape), dt, kind="ExternalInput").ap()
    x_d = din("x", [NLAT, 1024])
    ctx_d = din("ctx", [NCTX, 1024])
    cv_d = din("cv", [128, 16])
    wmod_d = din("wmod", [48, 128, 1024])
    bmod2_d = din("bmod2", [128, 96])
    wmodg_d = din("wmodg", [2, 128, 8, 1024])
    bmodg_d = din("bmodg", [128, 2048])
    fv_d = din("fv", [128, NV])
    win_d = din("win", [48, 128, 1024])
    ident_d = din("ident", [128, 128])
    rgbd_d = din("rgbd", [128, 32, 128])
    mlbd_d = din("mlbd", [128, 24, 128])
    wgt_d = din("wgt", [128, 24, 16])
    gbias_d = din("gbias", [128, 2, 16])
    tri_d = din("tri", [128, 2, 128])
    wbrg_d = din("wbrg", [8, 128, 1024])
    wbml_d = din("wbml", [8, 128, 1024])
    wout_d = din("wout", [128, 8, 1024])
    wffi_d = din("wffi", [44, 128, 1024])
    wffo_d = din("wffo", [128, 22, 1024])
    fgrow_d = din("fgrow", [128, 1024])
    SPQ_d = nc.dram_tensor("SPQ", [17, 4, 128, 8, 256], BF16).ap()
    SPX_d = nc.dram_tensor("SPX", [16, 128, 8, 256], F32).ap()
    X1_d = nc.dram_tensor("X1", [NLAT, 1024], F32).ap()
    H2_d = nc.dram_tensor("H2", [16, 128, 8, 256], BF16).ap()
    HF_d = nc.dram_tensor("HF", [32, 128, 1024], F32).ap()
    if "yml" in debug:
        YML_d = nc.dram_tensor("dbg_yml", [8, 128, NLAT], BF16, kind="ExternalOutput").ap()
    else:
        YML_d = nc.dram_tensor("YML", [8, 128, NLAT], BF16).ap()
    YRG_d = nc.dram_tensor("YRG", [8, 128, NLAT], BF16).ap()
    out_d = nc.dram_tensor("out", [NLAT, 1024], F32, kind="ExternalOutput").ap()
    dbg_d = {}

    with ExitStack() as gst:
        P = Prog(nc, gst)
        galloc = lambda name, shape, dt: gst.enter_context(nc.sbuf_tensor(name, list(shape), dt))

        def dump(name, ap, trk, shape, dt=F32):
            if name not in debug:
                return
            d = nc.dram_tensor("dbg_" + name, list(shape), dt, kind="ExternalOutput").ap()
            dbg_d[name] = d
            P.out_toks.append(P.dma("sp", lambda e: e.dma_start(out=d, in_=ap), reads=[trk], semtrk=Trk("dbg" + name)))

        t_uT = [Trk("uT%d" % i) for i in range(34)]
        FV = galloc("FV", [128, NV], F32); t_FV = Trk("FV")
        MODT = galloc("MODT", [128, 96], F32); t_MODT = Trk("MODT")
        A1 = galloc("A1", [128, 16], F32); t_A1 = Trk("A1")
        A2 = galloc("A2", [128, 8], F32); t_A2 = Trk("A2")
        GROW = galloc("GROW", [128, 2, 1024], F32); t_GROW = Trk("GROW")
        KD = galloc("KD", [128, 16], F32); t_KD = Trk("KD")
        IDB = galloc("IDB", [128, 128], BF16); t_IDB = Trk("IDB")
        IDF = galloc("IDF", [128, 128], F32); t_IDF = Trk("IDF")
        t_YRG = [Trk("YRG%d" % c) for c in range(8)]
        ust = ExitStack()
        uT = ust.enter_context(nc.sbuf_tensor("uT", [128, 8, NT], BF16))

        def fvc(name, i=0, n=1):
            o, w = FVCOLS[name]
            return FV[:, o + i:o + i + n]

        with ExitStack() as st:
            sb = lambda name, shape, dt: st.enter_context(nc.sbuf_tensor(name, list(shape), dt))
            ps = lambda name, shape, dt: st.enter_context(nc.psum_tensor(name, list(shape), dt))
            CV = sb("CV", [128, 16], F32); t_CV = Trk("CV")
            S2 = sb("S2", [128, 16], F32); t_S2 = Trk("S2")
            SREP = sb("SREP", [128, 8, 128], F32); t_SREP = Trk("SREP")
            BM2 = sb("BM2", [128, 96], F32); t_BM2 = Trk("BM2")
            BMG = sb("BMG", [128, 2048], F32); t_BMG = Trk("BMG")
            TMPA = sb("TMPA", [128, 16], F32); t_TMPA = Trk("TMPA")
            TMPB = sb("TMPB", [128, 16], F32); t_TMPB = Trk("TMPB")
            WM = Rot(sb, "WM", 3, [128, 1024], F32)
            WGm = sb("WGm", [128, 8, 1024], F32); t_WGm = Trk("WGm")
            MODP = ps("MODP", [128, 512], F32); t_MODP = Trk("PS_MODP")
            GRP = ps("GRP", [128, 1024], F32); t_GRP = Trk("PS_GRP")

            P.dma("sp", lambda e: e.dma_start(out=CV[:], in_=cv_d), writes=[t_CV])
            P.dma("sp", lambda e: e.dma_start(out=FV[:], in_=fv_d), writes=[t_FV])
            P.dma("sp", lambda e: e.dma_start(out=BM2[:], in_=bmod2_d), writes=[t_BM2])
            P.dma("sp", lambda e: e.dma_start(out=BMG[:], in_=bmodg_d), writes=[t_BMG])
            P.dma("sp", lambda e: e.dma_start(out=IDF[:], in_=ident_d), writes=[t_IDF])
            P.dma("pool", lambda e: e.dma_start(out=IDB[:], in_=ident_d), writes=[t_IDB])
            P.op("act", lambda e: e.activation(out=S2[:], in_=CV[:], func=AF.Silu), reads=[t_CV], writes=[t_S2])
            for kc in range(8):
                P.op("dve", (lambda kc: lambda e: e.tensor_copy(
                    out=SREP[:, kc, :], in_=S2[:, 2 * kc:2 * kc + 1].to_broadcast([128, 128])))(kc),
                    reads=[t_S2], writes=[t_SREP])
            for n in range(48):
                wm, t_wm = WM.next()
                P.dma("sp", (lambda wm, n: lambda e: e.dma_start(out=wm[:], in_=wmod_d[n]))(wm, n), writes=[t_wm])
                for kc in range(8):
                    P.op("pe", (lambda wm, n, kc: lambda e: e.matmul(
                        MODP[:, 2 * n:2 * n + 2], lhsT=wm[:, kc * 128:(kc + 1) * 128], rhs=S2[:, 2 * kc:2 * kc + 2],
                        start=(kc == 0), stop=(kc == 7)))(wm, n, kc), reads=[t_wm, t_S2], writes=[t_MODP])
            P.op("dve", lambda e: e.tensor_tensor(out=MODT[:], in0=MODP[:, 0:96], in1=BM2[:], op=ALU.add),
                 reads=[t_MODP, t_BM2], writes=[t_MODT])
            P.op("dve", lambda e: e.tensor_scalar_add(out=TMPA[:], in0=MODT[:, 16:32], scalar1=1.0),
                 reads=[t_MODT], writes=[t_TMPA])
            for j in range(2):
                P.op("dve", (lambda j: lambda e: e.tensor_tensor(
                    out=A1[:, j:16:2], in0=TMPA[:, j:16:2], in1=fvc("n1g", 0, 8), op=ALU.mult))(j),
                    reads=[t_TMPA, t_FV], writes=[t_A1])
            P.op("dve", lambda e: e.tensor_scalar_add(out=TMPB[:, 0:8], in0=MODT[:, 64:80:2], scalar1=1.0),
                 reads=[t_MODT], writes=[t_TMPB])
            P.op("dve", lambda e: e.tensor_tensor(out=A2[:], in0=TMPB[:, 0:8], in1=fvc("n2g", 0, 8), op=ALU.mult),
                 reads=[t_TMPB, t_FV], writes=[t_A2])
            for g in range(2):
                P.dma("sp", (lambda g: lambda e: e.dma_start(out=WGm[:], in_=wmodg_d[g]))(g), writes=[t_WGm])
                for half in range(2):
                    for kc in range(8):
                        P.op("pe", (lambda half, kc: lambda e: e.matmul(
                            GRP[:, half * 512:(half + 1) * 512], lhsT=SREP[:, kc, :],
                            rhs=WGm[:, kc, half * 512:(half + 1) * 512], start=(kc == 0), stop=(kc == 7)))(half, kc),
                            reads=[t_SREP, t_WGm], writes=[t_GRP])
                P.op("dve", (lambda g: lambda e: e.tensor_tensor(
                    out=GROW[:, g, :], in0=GRP[:], in1=BMG[:, g * 1024:(g + 1) * 1024], op=ALU.add))(g),
                    reads=[t_GRP, t_BMG], writes=[t_GROW])
            P.op("act", lambda e: e.activation(out=TMPA[:], in_=fvc("rglam", 0, 16), func=AF.Exp, scale=-1.0),
                 reads=[t_FV], writes=[t_TMPA])
            P.op("act", lambda e: e.activation(out=TMPB[:], in_=TMPA[:], func=AF.Ln, bias=1.0),
                 reads=[t_TMPA], writes=[t_TMPB])
            P.op("dve", lambda e: e.tensor_scalar(out=KD[:], in0=TMPB[:], scalar1=-8.0, scalar2=None, op0=ALU.mult),
                 reads=[t_TMPB], writes=[t_KD])
            dump("modT", MODT[:], t_MODT, [128, 96])
            dump("grow", GROW[:], t_GROW, [128, 2, 1024])
            dump("kd", KD[:], t_KD, [128, 16])
            P.barrier()
            P.emit()

        with ExitStack() as st:
            sb = lambda name, shape, dt: st.enter_context(nc.sbuf_tensor(name, list(shape), dt))
            ps = lambda name, shape, dt: st.enter_context(nc.psum_tensor(name, list(shape), dt))
            XT = Rot(sb, "XT", 3, [128, 1024], F32)
            XN = Rot(sb, "XN", 2, [128, 1024], BF16)
            TP = Rot(ps, "PS_TP", 2, [128, 8, 128], BF16)
            ST = sb("ST", [128, 34, 4], F32)
            JKA = sb("JKA", [128, 1024], BF16); t_JKA = Trk("JKA")
            stage1 = {}

            def emit_stats(i):
                t_st = Trk("st%d" % i)
                xt, t_xt = XT.next()
                src = ctx_d[i * 128:(i + 1) * 128, :] if i < 2 else x_d[(i - 2) * 128:(i - 1) * 128, :]
                P.dma("sp", (lambda xt, src: lambda e: e.dma_start(out=xt[:], in_=src))(xt, src), writes=[t_xt])
                P.op("act", (lambda xt, i: lambda e: e.activation(
                    out=JKA[:], in_=xt[:], func=AF.Square, accum_out=ST[:, i, 0:1]))(xt, i),
                    reads=[t_xt], writes=[t_JKA, t_st])
                P.op("act", (lambda i: lambda e: e.activation(
                    out=ST[:, i, 1:2], in_=ST[:, i, 0:1], func=AF.Sqrt, scale=1.0 / 1024.0, bias=EPS))(i),
                    reads=[t_st], writes=[t_st])
                P.op("dve", (lambda i: lambda e: e.reciprocal(out=ST[:, i, 2:3], in_=ST[:, i, 1:2]))(i),
                     reads=[t_st], writes=[t_st])
                stage1[i] = (xt, t_xt, t_st)

            emit_stats(0)
            for i in range(34):
                if i + 1 < 34:
                    emit_stats(i + 1)
                xt, t_xt, t_st = stage1.pop(i)
                xn, t_xn = XN.next()
                tp, t_tp = TP.next()
                j = 1 if i < 2 else 0
                P.op("act", (lambda xt, xn, i: lambda e: e.activation(
                    out=xn[:], in_=xt[:], func=AF.Copy, scale=ST[:, i, 2:3]))(xt, xn, i),
                    reads=[t_xt, t_st], writes=[t_xn])
                for c in range(8):
                    P.op("pe", (lambda xn, tp, c: lambda e: e.transpose(
                        out=tp[:, c, :], in_=xn[:, c * 128:(c + 1) * 128], identity=IDB[:]))(xn, tp, c),
                        reads=[t_xn, t_IDB], writes=[t_tp])
                for c in range(8):
                    if i < 2:
                        dst = uT[:, c, i * 128:(i + 1) * 128]
                        src_tp = tp[:, c, :]
                    else:
                        r0 = 2 * (i - 2)
                        dst = uT[:, c, 256:NT].rearrange("p (j r) -> p r j", r=64)[:, r0:r0 + 2, :]
                        src_tp = tp[:, c, :].rearrange("p (r j) -> p r j", j=64)
                    if c % 2 == 0:
                        P.op("dve", (lambda tp, c, dst, j: lambda e: e.tensor_scalar(
                            out=dst, in0=tp, scalar1=A1[:, 2 * c + j:2 * c + j + 1],
                            scalar2=MODT[:, 2 * c + j:2 * c + j + 1], op0=ALU.mult, op1=ALU.add))(src_tp, c, dst, j),
                            reads=[t_tp, t_A1, t_MODT], writes=[t_uT[i]])
                    else:
                        P.op("act", (lambda tp, c, dst, j: lambda e: e.activation(
                            out=dst, in_=tp, func=AF.Identity, scale=A1[:, 2 * c + j:2 * c + j + 1],
                            bias=MODT[:, 2 * c + j:2 * c + j + 1]))(src_tp, c, dst, j),
                            reads=[t_tp, t_A1, t_MODT], writes=[t_uT[i]])
            if "uT" in debug:
                UD = sb("UD", [128, 8, 512], F32); t_UD = Trk("UD")
                P.op("dve", lambda e: e.tensor_copy(out=UD[:], in_=uT[:, :, 128:640]), reads=t_uT[1:5], writes=[t_UD])
                dump("uT", UD[:], t_UD, [128, 8, 512])
            P.barrier()
            P.emit()


        CT0, LT0, PEND = 2, 261, 4357
        with ExitStack() as st:
            sb = lambda name, shape, dt: st.enter_context(nc.sbuf_tensor(name, list(shape), dt))
            ps = lambda name, shape, dt: st.enter_context(nc.psum_tensor(name, list(shape), dt))
            RGBD = sb("RGBD", [128, 32, 128], BF16); t_RGBD = Trk("RGBD")
            P.dma("pool", lambda e: e.dma_start(out=RGBD[:], in_=rgbd_d, max_dma_last_dim=4096), writes=[t_RGBD])
            RX = sb("RX", [128, 4360], F32); t_RX = Trk("RX")
            XC = sb("XC", [128, 4360], F32); t_XC = Trk("XC")
            XCB = sb("XCB", [128, 4360], BF16); t_XCB = Trk("XCB")
            TMP = sb("TMP", [128, 2180], F32); t_TMP = Trk("TMP")
            BB = [sb("B0", [128, 4360], F32), sb("B1", [128, 4360], F32)]; t_BB = [Trk("B0"), Trk("B1")]
            WR = Rot(sb, "WR", 4, [128, 8, 128], BF16)
            PJ = Rot(ps, "PS_PJ", 3, [128, 512], F32)
            GA = Rot(ps, "PS_GA", 4, [128, 512], F32)
            GL = Rot(sb, "GL", 2, [128, 512], F32)
            TS = Rot(sb, "TS", 2, [128, 512], F32)
            YS = Rot(sb, "YS", 1, [128, NLAT], BF16)
            for (a, b) in ((0, 2), (258, 261), (4357, 4360)):
                P.op("dve", (lambda a, b: lambda e: e.memset(RX[:, a:b], 0.0))(a, b), writes=[t_RX])
            for d in range(2):
                P.op("pool", (lambda d: lambda e: e.memset(BB[d][:, 256:264], 0.0))(d), writes=[t_BB[d]])
            blocks = [(0, 256, CT0)] + [(256 + 512 * b, 512, LT0 + 512 * b) for b in range(8)]

            def ut_trks(t0, n):
                return t_uT[t0 // 128:(t0 + n - 1) // 128 + 1]

            def ut_nat(kc, t0, n):
                if t0 < 256:
                    return uT[:, kc, t0:t0 + n]
                r0 = (t0 - 256) // 64
                return uT[:, kc, 256:NT].rearrange("p (j r) -> p r j", r=64)[:, r0:r0 + n // 64, :]

            def rev(ap2d):
                n = ap2d.shape[1]
                return bass.AP(ap2d.tensor, ap2d.offset + (n - 1), [list(ap2d.ap[0]), [-1, n]])

            def load_wr(c):
                wrx, t_wrx = WR.next()
                wrg, t_wrg = WR.next()
                P.dma("pool", (lambda w, c: lambda e: e.dma_start(
                    out=w[:], in_=win_d[c].rearrange("p (k j) -> p k j", j=128)))(wrx, c), writes=[t_wrx])
                P.dma("pool", (lambda w, c: lambda e: e.dma_start(
                    out=w[:], in_=win_d[8 + c].rearrange("p (k j) -> p k j", j=128)))(wrg, c), writes=[t_wrg])
                return wrx, t_wrx, wrg, t_wrg

            wr_next = load_wr(0)
            for c in range(8):
                wrx, t_wrx, wrg, t_wrg = wr_next
                if c + 1 < 8:
                    wr_next = load_wr(c + 1)
                if c > 0:
                    P.op("dve", lambda e: e.memset(RX[:, 258:261], 0.0), writes=[t_RX])
                pj, t_pj = PJ.next()
                for kc in range(8):
                    P.op("pe", (lambda pj, wrx, kc: lambda e: e.matmul(
                        pj[:, 0:256], lhsT=wrx[:, kc, :], rhs=uT[:, kc, 0:256], start=(kc == 0), stop=(kc == 7)))(
                        pj, wrx, kc), reads=[t_wrx] + t_uT[0:2], writes=[t_pj])
                P.op("act", (lambda pj: lambda e: e.activation(
                    out=RX[:, CT0:CT0 + 256], in_=pj[:, 0:256], func=AF.Copy))(pj), reads=[t_pj], writes=[t_RX])
                for b in range(8):
                    pj, t_pj = PJ.next()
                    for kc in range(8):
                        P.op("pe", (lambda pj, wrx, kc, b: lambda e: e.matmul(
                            pj[:], lhsT=wrx[:, kc, :], rhs=uT[:, kc, 256 + b * 512:256 + (b + 1) * 512], start=(kc == 0), stop=(kc == 7)))(
                            pj, wrx, kc, b), reads=[t_wrx] + t_uT[2:34], writes=[t_pj])
                    P.op("act", (lambda pj, b: lambda e: e.activation(
                        out=RX[:, LT0:LT0 + 4096].rearrange("p (r j) -> p j r", j=64)[:, 8 * b:8 * b + 8, :],
                        in_=pj[:].rearrange("p (j r) -> p j r", r=64), func=AF.Copy))(pj, b), reads=[t_pj], writes=[t_RX])
                L = PEND - 2
                P.op("dve", (lambda c: lambda e: e.tensor_scalar(
                    out=XC[:, 2:PEND], in0=RX[:, 0:L], scalar1=fvc("rgcw", c * 4), scalar2=fvc("rgcb", c),
                    op0=ALU.mult, op1=ALU.add))(c), reads=[t_RX, t_FV], writes=[t_XC])
                for j in range(1, 4):
                    P.op("dve", (lambda c, j: lambda e: e.scalar_tensor_tensor(
                        out=XC[:, 2:PEND], in0=RX[:, j:j + L], scalar=fvc("rgcw", c * 4 + j), in1=XC[:, 2:PEND],
                        op0=ALU.mult, op1=ALU.add))(c, j), reads=[t_RX, t_FV, t_XC], writes=[t_XC])
                P.op("act", lambda e: e.activation(out=XCB[:, 2:PEND], in_=XC[:, 2:PEND], func=AF.Copy),
                     reads=[t_XC], writes=[t_XCB])
                if c == 3:
                    dump("xc", XC[:], t_XC, [128, 4360])
                for d in range(2):
                    Bd, t_Bd = BB[d], t_BB[d]
                    for (t0, n, pos) in blocks:
                        ga, t_ga = GA.next()
                        gx, t_gx = GA.next()
                        P.op("pe", (lambda ga, d, c, n, pos: lambda e: e.matmul(
                            ga[:, 0:n], lhsT=RGBD[:, (d * 2) * 8 + c, :], rhs=XCB[:, pos:pos + n], start=True, stop=True))(
                            ga, d, c, n, pos), reads=[t_RGBD, t_XCB], writes=[t_ga])
                        P.op("pe", (lambda gx, d, c, n, pos: lambda e: e.matmul(
                            gx[:, 0:n], lhsT=RGBD[:, (d * 2 + 1) * 8 + c, :], rhs=XCB[:, pos:pos + n], start=True, stop=True))(
                            gx, d, c, n, pos), reads=[t_RGBD, t_XCB], writes=[t_gx])
                        P.op("act", (lambda ga, d, c, n, pos: lambda e: e.activation(
                            out=RX[:, pos:pos + n], in_=ga[:, 0:n], func=AF.Sigmoid, bias=fvc("rgba", d * 8 + c)))(
                            ga, d, c, n, pos), reads=[t_ga, t_FV], writes=[t_RX])
                        P.op("act", (lambda gx, Bd, d, c, n, pos: lambda e: e.activation(
                            out=Bd[:, pos:pos + n], in_=gx[:, 0:n], func=AF.Sigmoid, bias=fvc("rgbx", d * 8 + c)))(
                            gx, Bd, d, c, n, pos), reads=[t_gx, t_FV], writes=[t_Bd])
                    P.op("act", (lambda d, c: lambda e: e.activation(
                        out=RX[:, 2:PEND], in_=RX[:, 2:PEND], func=AF.Exp, scale=KD[:, d * 8 + c:d * 8 + c + 1]))(d, c),
                        reads=[t_RX, t_KD], writes=[t_RX])
                    P.op("dve", (lambda Bd: lambda e: e.tensor_tensor(
                        out=Bd[:, 2:PEND], in0=Bd[:, 2:PEND], in1=XC[:, 2:PEND], op=ALU.mult))(Bd),
                        reads=[t_Bd, t_XC], writes=[t_Bd])
                    for (ra, rb) in ((2, 2180), (2180, PEND)):
                        P.op("act", (lambda ra, rb: lambda e: e.activation(
                            out=TMP[:, 0:rb - ra], in_=RX[:, ra:rb], func=AF.Square))(ra, rb), reads=[t_RX], writes=[t_TMP])
                        P.op("act", (lambda ra, rb: lambda e: e.activation(
                            out=TMP[:, 0:rb - ra], in_=TMP[:, 0:rb - ra], func=AF.Sqrt, scale=-1.0, bias=1.0))(ra, rb),
                            reads=[t_TMP], writes=[t_TMP])
                        P.op("dve", (lambda Bd, ra, rb: lambda e: e.tensor_tensor(
                            out=Bd[:, ra:rb], in0=Bd[:, ra:rb], in1=TMP[:, 0:rb - ra], op=ALU.mult))(Bd, ra, rb),
                            reads=[t_Bd, t_TMP], writes=[t_Bd])
                    f_ = (lambda ap: ap) if d == 0 else rev
                    c0, c1 = CT0, CT0 + 256
                    l0, l1 = LT0, LT0 + 4096
                    P.op("dve", (lambda Bd, f_: lambda e: e.tensor_tensor_scan(
                        out=f_(Bd[:, c0:c1]), data0=f_(RX[:, c0:c1]), data1=f_(Bd[:, c0:c1]), initial=0.0,
                        op0=ALU.mult, op1=ALU.add))(Bd, f_), reads=[t_RX, t_Bd], writes=[t_Bd])
                    ini = (c1 - 1) if d == 0 else c0
                    P.op("dve", (lambda Bd, f_, ini: lambda e: e.tensor_tensor_scan(
                        out=f_(Bd[:, l0:l1]), data0=f_(RX[:, l0:l1]), data1=f_(Bd[:, l0:l1]), initial=Bd[:, ini:ini + 1],
                        op0=ALU.mult, op1=ALU.add))(Bd, f_, ini), reads=[t_RX, t_Bd], writes=[t_Bd])
                ys, t_ys = YS.next()
                for b in range(8):
                    pj, t_pj = PJ.next()
                    gl, t_gl = GL.next()
                    ts, t_ts = TS.next()
                    for kc in range(8):
                        P.op("pe", (lambda pj, wrg, kc, b: lambda e: e.matmul(
                            pj[:], lhsT=wrg[:, kc, :], rhs=uT[:, kc, 256 + b * 512:256 + (b + 1) * 512], start=(kc == 0), stop=(kc == 7)))(
                            pj, wrg, kc, b), reads=[t_wrg] + t_uT[2:34], writes=[t_pj])
                    P.op("act", (lambda pj, gl: lambda e: e.activation(out=gl[:], in_=pj[:], func=AF.Gelu))(pj, gl),
                         reads=[t_pj], writes=[t_gl])
                    P.op("dve", (lambda ts, b: lambda e: e.tensor_tensor(
                        out=ts[:].rearrange("p (j r) -> p j r", r=64),
                        in0=BB[0][:, LT0:LT0 + 4096].rearrange("p (r j) -> p j r", j=64)[:, 8 * b:8 * b + 8, :],
                        in1=BB[1][:, LT0:LT0 + 4096].rearrange("p (r j) -> p j r", j=64)[:, 8 * b:8 * b + 8, :], op=ALU.add))(ts, b),
                        reads=t_BB, writes=[t_ts])
                    P.op("dve", (lambda ys, ts, gl, b: lambda e: e.tensor_tensor(
                        out=ys[:, b * 512:(b + 1) * 512], in0=ts[:], in1=gl[:], op=ALU.mult))(ys, ts, gl, b),
                        reads=[t_ts, t_gl], writes=[t_ys])
                P.dma("sp", (lambda ys, c: lambda e: e.dma_start(out=YRG_d[c], in_=ys[:]))(ys, c), reads=[t_ys], writes=[t_YRG[c]],
                      semtrk=t_ys)
                if c == 3:
                    if "hrg" in debug:
                        dump("hrg", HD[:], t_HD, [128, NLAT])
                    dump("yrg", ys[:], t_ys, [128, NLAT], BF16)
            P.barrier()
            P.emit()
        if stop_after == "rg":
            P.wait_all("sp", P.out_toks)
            P.barrier()
            P.emit()
            ust.close()
            return nc, dbg_d

        t_HF = [Trk("HF%d" % i) for i in range(32)]
        t_YML = [Trk("YML%d" % i) for i in range(16)]
        t_SPQ = [[Trk("SPQ%d_%d" % (g, i)) for i in range(4)] for g in range(17)]
        t_SPX = [Trk("SPX%d" % g) for g in range(16)]
        t_spd = [Trk("spd%d" % i) for i in range(5)]
        with ExitStack() as st:
            sb = lambda name, shape, dt: st.enter_context(nc.sbuf_tensor(name, list(shape), dt))
            ps = lambda name, shape, dt: st.enter_context(nc.psum_tensor(name, list(shape), dt))
            MLBD = sb("MLBD", [128, 24, 128], BF16); t_MLBD = Trk("MLBD")
            WGT = sb("WGT", [128, 24, 16], BF16); t_WGT = Trk("WGT")
            GBI = sb("GBI", [128, 2, 16], F32); t_GBI = Trk("GBI")
            TRI = sb("TRI", [128, 2, 128], F32); t_TRI = Trk("TRI")
            ONES = sb("ONES", [128, 128], F32); t_ONES = Trk("ONES")
            P.dma("pool", lambda e: e.dma_start(out=MLBD[:], in_=mlbd_d, max_dma_last_dim=4096), writes=[t_MLBD])
            P.dma("pool", lambda e: e.dma_start(out=WGT[:], in_=wgt_d), writes=[t_WGT])
            P.dma("sp", lambda e: e.dma_start(out=GBI[:], in_=gbias_d), writes=[t_GBI])
            P.dma("sp", lambda e: e.dma_start(out=TRI[:], in_=tri_d), writes=[t_TRI])
            P.op("dve", lambda e: e.memset(ONES[:], 1.0), writes=[t_ONES])
            B_PM = ps("B_PM", [128, 512], F32); t_PM = Trk("PS_PM")
            B_QK = ps("B_QK", [128, 512], F32); t_PQ = Trk("PS_PQ")
            B_VO = ps("B_VO", [128, 512], F32); t_PV = Trk("PS_PV")
            B_GP = ps("B_GP", [128, 512], F32); t_GP = Trk("PS_GP")
            B_N = ps("B_N", [128, 4, 512], F32); t_N = [Trk("PS_N%d" % i) for i in range(4)]
            BU = [B_PM, B_QK, B_VO, B_GP]; t_BU = [t_PM, t_PQ, t_PV, t_GP]
            VT = Rot(sb, "VT", 2, [128, 256], BF16)
            XM = sb("XM", [128, 8, 256], F32); t_XM = [Trk("XM%d" % c) for c in range(8)]
            XMB = sb("XMB", [128, 8, 256], BF16); t_XMB = [Trk("XMB%d" % c) for c in range(8)]
            UMB = sb("UMB", [128, 8, 256], BF16); t_UMB = [Trk("UMB%d" % c) for c in range(8)]
            QT = sb("QT", [128, 8, 256], BF16); t_QT = [Trk("QT%d" % c) for c in range(8)]
            KT = sb("KT", [128, 8, 256], BF16); t_KT = [Trk("KT%d" % c) for c in range(8)]
            GF = sb("GF", [8, 256], F32); t_GF = Trk("GF")
            GG = sb("GG", [128, 16], F32); t_GG = Trk("GG")
            GE = sb("GE", [128, 8], F32); t_GE = Trk("GE")
            GLn = sb("GLn", [128, 8], F32); t_GLn = Trk("GLn")
            GT2 = sb("GT2", [128, 8], F32); t_GT2 = Trk("GT2")
            EB = sb("EB", [128, 8], F32); t_EB = Trk("EB")
            WS = sb("WS", [128, 8], F32); t_WS = Trk("WS")
            EBL = sb("EBL", [128, 8], F32); t_EBL = Trk("EBL")
            DS = sb("DS", [128, 4, 2, 257], F32); t_DS = [Trk("DS%d" % h) for h in range(4)]
            DB = sb("DB", [128, 4, 2, 258], BF16); t_DB = [Trk("DB%d" % h) for h in range(4)]
            VX = sb("VX", [128, 2, 4, 258], BF16); t_VX = [Trk("VX%d" % i) for i in range(2)]
            KTM = sb("KTM", [128, 2, 1024], BF16); t_KTM = [Trk("KTM%d" % i) for i in range(2)]
            STt = sb("STt", [128, 2, 4, 128], BF16); t_STt = [Trk("STt%d" % i) for i in range(2)]
            E1 = sb("E1", [128, 8, 4], F32); t_E1 = Trk("E1")
            HH = Rot(sb, "HH", 1, [128, 1024], F32)

            def grp_rhs(kc, g):
                if g == 0:
                    return uT[:, kc, 0:256]
                gi = g - 1
                return uT[:, kc, 256 + gi * 256:256 + (gi + 1) * 256]

            def grp_trks(g):
                return t_uT[0:2] if g == 0 else t_uT[2:34]

            def gates_post(d):
                P.op("act", lambda e: e.activation(out=GF[:], in_=B_GP[0:8, 0:256], func=AF.Copy), reads=[t_GP], writes=[t_GF])
                for ch in range(2):
                    P.op("pe", (lambda ch: lambda e: e.transpose(
                        out=B_GP[:, 256 + ch * 8:256 + ch * 8 + 8], in_=GF[0:8, ch * 128:(ch + 1) * 128], identity=IDF[0:8, 0:8]))(ch),
                        reads=[t_GF, t_IDF], writes=[t_GP])
                P.op("dve", (lambda d: lambda e: e.tensor_tensor(
                    out=GG[:], in0=B_GP[:, 256:272], in1=GBI[:, d, :], op=ALU.add))(d), reads=[t_GP, t_GBI], writes=[t_GG])
                GGv = GG[:].rearrange("t (c k) -> t c k", k=8)
                P.op("act", lambda e: e.activation(
                    out=GE[:].rearrange("t (c h) -> t c h", h=4), in_=GGv[:, :, 4:8], func=AF.Exp, scale=-1.0),
                    reads=[t_GG], writes=[t_GE])
                P.op("act", lambda e: e.activation(out=GLn[:], in_=GE[:], func=AF.Ln, bias=1.0), reads=[t_GE], writes=[t_GLn])
                P.op("pe", (lambda d: lambda e: e.matmul(
                    B_GP[:, 288:296], lhsT=TRI[:, d, :], rhs=GLn[:], start=True, stop=True))(d),
                    reads=[t_TRI, t_GLn], writes=[t_GP])
                P.op("pe", lambda e: e.matmul(B_GP[:, 304:312], lhsT=ONES[:], rhs=GLn[:], start=True, stop=True),
                     reads=[t_ONES, t_GLn], writes=[t_GP])
                P.op("act", lambda e: e.activation(out=EB[:], in_=B_GP[:, 288:296], func=AF.Exp, scale=-1.0),
                     reads=[t_GP], writes=[t_EB])
                P.op("dve", lambda e: e.tensor_tensor(
                    out=GT2[:].rearrange("t (c h) -> t c h", h=4), in0=B_GP[:, 288:296].rearrange("t (c h) -> t c h", h=4),
                    in1=GGv[:, :, 0:4], op=ALU.add), reads=[t_GP, t_GG], writes=[t_GT2])
                P.op("act", lambda e: e.activation(out=WS[:], in_=GT2[:], func=AF.Exp), reads=[t_GT2], writes=[t_WS])
                P.op("act", lambda e: e.activation(out=EBL[:], in_=B_GP[:, 304:312], func=AF.Exp, scale=-1.0),
                     reads=[t_GP], writes=[t_EBL])
            def rec_group(g, d, QT, KT, XMB, UMB, t_QT, t_KT, t_XMB, t_UMB, XM, t_XM, hs):
                lat = g > 0
                gi = g - 1
                have_state = hs[0]
                for ch in range(2):
                    cols = slice(ch * 128, (ch + 1) * 128)
                    for c in range(8):
                        h, half = divmod(c, 2)
                        bk = h // 2
                        off = (h % 2) * 256 + half * 128
                        P.op("pe", (lambda c, bk, off, cols: lambda e: e.matmul(
                            B_N[:, bk, off:off + 128], lhsT=UMB[:, c, cols], rhs=MLBD[:, 16 + c, :], start=True, stop=True))(c, bk, off, cols),
                            reads=[t_UMB[c], t_MLBD], writes=[t_N[bk]])
                        P.op("pe", (lambda c, bk, off, cols: lambda e: e.matmul(
                            B_N[:, 2 + bk, off:off + 128], lhsT=XMB[:, c, cols], rhs=MLBD[:, 8 + c, :], start=True, stop=True))(c, bk, off, cols),
                            reads=[t_XMB[c], t_MLBD], writes=[t_N[2 + bk]])
                    for h in range(4):
                        bk = h // 2
                        off = (h % 2) * 256
                        P.op("dve", (lambda ch, h, bk, off: lambda e: e.tensor_scalar(
                            out=VX[:, ch, h, 0:256], in0=B_N[:, bk, off:off + 256], scalar1=WS[:, ch * 4 + h:ch * 4 + h + 1],
                            scalar2=None, op0=ALU.mult))(ch, h, bk, off), reads=[t_N[bk], t_WS], writes=[t_VX[ch]])
                    P.op("act", (lambda ch: lambda e: e.activation(
                        out=VX[:, ch, :, 256:257], in_=WS[:, ch * 4:ch * 4 + 4].unsqueeze(2), func=AF.Copy))(ch), reads=[t_WS], writes=[t_VX[ch]])
                    for bk in range(2):
                        P.op("act", (lambda ch, bk: lambda e: e.activation(
                            out=KTM[:, ch, bk * 512:(bk + 1) * 512], in_=B_N[:, 2 + bk, :], func=AF.Copy, scale=1.0 / 16.0))(ch, bk),
                            reads=[t_N[2 + bk]], writes=[t_KTM[ch]])
                    for h in range(4):
                        for half in range(2):
                            c = 2 * h + half
                            P.op("pe", (lambda c, h, half, cols: lambda e: e.matmul(
                                B_GP[:, h * 128:(h + 1) * 128], lhsT=KT[:, c, cols], rhs=QT[:, c, cols], start=(half == 0), stop=(half == 1)))(c, h, half, cols),
                                reads=[t_KT[c], t_QT[c]], writes=[t_GP])
                    P.op("dve", (lambda ch, d: lambda e: e.tensor_tensor(
                        out=STt[:, ch], in0=B_GP[:].rearrange("p (h t) -> p h t", t=128),
                        in1=TRI[:, d, :].unsqueeze(1).to_broadcast([128, 4, 128]), op=ALU.mult))(ch, d),
                        reads=[t_GP, t_TRI], writes=[t_STt[ch]])
                chs = (0, 1) if d == 0 else (1, 0)
                if d == 1 and lat:
                    yg, t_yg = YG.next()
                for ch in chs:
                    cols = slice(ch * 128, (ch + 1) * 128)
                    if d == 1 and lat:
                        for cq in (gi * 2 + ch, gi * 2 + ch - 1):
                            if cq >= 0 and cq not in hft_map:
                                hb, t_hb = HFt.next()
                                P.dma("sp", (lambda hb, cq: lambda e: e.dma_start(out=hb[:], in_=HF_d[cq]))(hb, cq),
                                      reads=[t_HF[cq]], writes=[t_hb])
                                hft_map[cq] = (hb, t_hb)
                    last_chunk = (d == 0 and g == 16 and ch == 1) or (d == 1 and g == 1 and ch == 0)
                    if not last_chunk:
                        for r in range(2):
                            for hh2 in range(2):
                                h = 2 * r + hh2
                                for half in range(2):
                                    bi_ = hh2 * 2 + half
                                    P.op("pe", (lambda ch, h, half, bi_: lambda e: e.matmul(
                                        BU[bi_][:, 0:257], lhsT=KTM[:, ch, h * 256 + half * 128:h * 256 + (half + 1) * 128],
                                        rhs=VX[:, ch, h, 0:257], start=True, stop=True))(ch, h, half, bi_),
                                        reads=[t_KTM[ch], t_VX[ch]], writes=[t_BU[bi_]])
                            for hh2 in range(2):
                                h = 2 * r + hh2
                                for half in range(2):
                                    bi_ = hh2 * 2 + half
                                    if have_state:
                                        P.op("dve", (lambda h, half, bi_: lambda e: e.tensor_tensor(
                                            out=DS[:, h, half, :], in0=DS[:, h, half, :], in1=BU[bi_][:, 0:257], op=ALU.add))(h, half, bi_),
                                            reads=[t_DS[h], t_BU[bi_]], writes=[t_DS[h]])
                                    else:
                                        P.op("dve", (lambda h, half, bi_: lambda e: e.tensor_copy(
                                            out=DS[:, h, half, :], in_=BU[bi_][:, 0:257]))(h, half, bi_),
                                            reads=[t_BU[bi_]], writes=[t_DS[h]])
                    if lat:
                        hh, t_hh = HH.next()
                        for h in range(4):
                            P.op("pe", (lambda ch, h, hs: lambda e: e.matmul(
                                B_N[:, h, 0:257], lhsT=STt[:, ch, h, :], rhs=VX[:, ch, h, 0:257], start=True, stop=(not hs)))(ch, h, have_state),
                                reads=[t_STt[ch], t_VX[ch]], writes=[t_N[h]])
                            if have_state:
                                for half in range(2):
                                    c = 2 * h + half
                                    P.op("pe", (lambda c, h, half, cols: lambda e: e.matmul(
                                        B_N[:, h, 0:257], lhsT=QT[:, c, cols], rhs=DB[:, h, half, 0:257], start=False, stop=(half == 1)))(c, h, half, cols),
                                        reads=[t_QT[c], t_DB[h]], writes=[t_N[h]])
                        e0 = ch * 4
                        P.op("dve", (lambda ch: lambda e: e.tensor_tensor(
                            out=E1[:, 0:4, 0], in0=B_N[:, :, 256], in1=EB[:, ch * 4:ch * 4 + 4], op=ALU.mult))(ch),
                            reads=t_N + [t_EB], writes=[t_E1])
                        P.op("dve", lambda e: e.tensor_scalar(
                            out=E1[:, 0:4, 1], in0=E1[:, 0:4, 0], scalar1=-1.0, scalar2=1.0, op0=ALU.mult, op1=ALU.max),
                            reads=[t_E1], writes=[t_E1])
                        P.op("dve", lambda e: e.scalar_tensor_tensor(
                            out=E1[:, 0:4, 2], in0=E1[:, 0:4, 0], scalar=1.0, in1=E1[:, 0:4, 1], op0=ALU.max, op1=ALU.max),
                            reads=[t_E1], writes=[t_E1])
                        P.op("dve", lambda e: e.reciprocal(out=E1[:, 0:4, 3], in_=E1[:, 0:4, 2]), reads=[t_E1], writes=[t_E1])
                        P.op("dve", (lambda ch: lambda e: e.tensor_tensor(
                            out=E1[:, 4:8, 0], in0=E1[:, 0:4, 3], in1=EB[:, ch * 4:ch * 4 + 4], op=ALU.mult))(ch),
                            reads=[t_E1, t_EB], writes=[t_E1])
                        for h in range(4):
                            P.op("act", (lambda hh, h: lambda e: e.activation(
                                out=hh[:, h * 256:(h + 1) * 256], in_=B_N[:, h, 0:256], func=AF.Copy, scale=E1[:, 4 + h, 0:1]))(hh, h),
                                reads=[t_N[h], t_E1], writes=[t_hh])
                    if not last_chunk:
                        for h in range(4):
                            idx = ch * 4 + h
                            P.op("act", (lambda h, idx: lambda e: e.activation(
                                out=DB[:, h, :, 0:257], in_=DS[:, h, :, :], func=AF.Copy, scale=EBL[:, idx:idx + 1]))(h, idx),
                                reads=[t_DS[h], t_EBL], writes=[t_DB[h]])
                            P.op("dve", (lambda h, idx: lambda e: e.tensor_scalar(
                                out=DS[:, h, :, :], in0=DS[:, h, :, :], scalar1=EBL[:, idx:idx + 1], scalar2=None, op0=ALU.mult))(h, idx),
                                reads=[t_DS[h], t_EBL], writes=[t_DS[h]])
                    have_state = True; hs[0] = True
                    if not lat:
                        continue
                    cg = gi * 2 + ch
                    if d == 0:
                        P.dma("sp", (lambda hh, cg: lambda e: e.dma_start(out=HF_d[cg], in_=hh[:]))(hh, cg),
                              reads=[t_hh], writes=[t_HF[cg]], semtrk=t_hh)
                        continue
                    hft, t_hft = hft_map.pop(cg)
                    P.op("dve", (lambda hh, hft: lambda e: e.tensor_tensor(out=hh[:], in0=hh[:], in1=hft[:], op=ALU.add))(hh, hft),
                         reads=[t_hh, t_hft], writes=[t_hh])
                    for h in range(4):
                        P.op("dve", (lambda hh, h: lambda e: e.bn_stats(out=BS[:, h, :], in_=hh[:, h * 256:(h + 1) * 256]))(hh, h),
                             reads=[t_hh], writes=[t_BS])
                        P.op("dve", (lambda h: lambda e: e.bn_aggr(out=MV[:, h, :], in_=BS[:, h, :]))(h), reads=[t_BS], writes=[t_MV])
                    P.op("act", lambda e: e.activation(out=SD[:, 0:4], in_=MV[:, :, 1], func=AF.Sqrt, bias=EPS), reads=[t_MV], writes=[t_SD])
                    P.op("dve", lambda e: e.reciprocal(out=SD[:, 4:8], in_=SD[:, 0:4]), reads=[t_SD], writes=[t_SD])
                    for h in range(4):
                        P.op("dve", (lambda hh, h: lambda e: e.tensor_scalar(
                            out=HN[:, h * 256:(h + 1) * 256], in0=hh[:, h * 256:(h + 1) * 256], scalar1=MV[:, h, 0:1],
                            scalar2=SD[:, 4 + h:5 + h], op0=ALU.subtract, op1=ALU.mult))(hh, h),
                            reads=[t_hh, t_MV, t_SD], writes=[t_HN])
                    for c in range(8):
                        P.op("pe", (lambda c: lambda e: e.transpose(
                            out=B_N[:, c // 4, (c % 4) * 128:(c % 4 + 1) * 128], in_=HN[:, c * 128:(c + 1) * 128], identity=IDF[:]))(c),
                            reads=[t_HN, t_IDF], writes=[t_N[c // 4]])
                    o_, w_ = FVCOLS["mlng"]
                    for b2 in range(2):
                        P.op("dve", (lambda b2: lambda e: e.tensor_tensor(
                            out=Y1[:, 4 * b2:4 * b2 + 4, :], in0=B_N[:, b2, :].rearrange("p (c t) -> p c t", t=128),
                            in1=FV[:, o_ + 4 * b2:o_ + 4 * b2 + 4].unsqueeze(2).to_broadcast([128, 4, 128]), op=ALU.mult))(b2),
                            reads=[t_N[b2], t_FV], writes=[t_Y1])
                    P.op("dve", (lambda cols: lambda e: e.tensor_tensor(out=Y1[:], in0=Y1[:], in1=XM[:, :, cols], op=ALU.add))(cols),
                         reads=[t_Y1] + t_XM, writes=[t_Y1])
                    P.op("dve", (lambda yg, cols: lambda e: e.tensor_tensor(out=yg[:, :, cols], in0=Y1[:], in1=SIG[:, :, cols], op=ALU.mult))(yg, cols),
                         reads=[t_Y1] + t_SIG, writes=[t_yg])
                if d == 1 and lat:
                    P.dma("sp", (lambda yg, gi: lambda e: e.dma_start(
                        out=YML_d[:, :, gi * 256:(gi + 1) * 256].rearrange("c p t -> p c t"), in_=yg[:]))(yg, gi),
                        reads=[t_yg], writes=[t_YML[gi]], semtrk=t_yg)
            for d in range(2):
                with ExitStack() as st2:
                    sb2 = lambda name, shape, dt: st2.enter_context(nc.sbuf_tensor(name, list(shape), dt))
                    if d == 0:
                        WMX = sb2("WMX", [128, 8, 8, 128], BF16); t_WMX = [Trk("WMX%d" % c) for c in range(8)]
                        for c in range(8):
                            P.dma("pool", (lambda c: lambda e: e.dma_start(
                                out=WMX[:, c], in_=win_d[16 + c].rearrange("p (k j) -> p k j", j=128)))(c), writes=[t_WMX[c]])
                        UMF = Rot(sb2, "UMF", 2, [128, 260], F32)
                        HALO = sb2("HALO", [128, 8, 2], F32); t_HALO = [Trk("HALO%d" % c) for c in range(8)]
                        XCV = Rot(sb2, "XCV", 2, [128, 256], F32)
                    if d == 1:
                        WMO = sb2("WMO", [128, 8, 8, 128], BF16); t_WMO = [Trk("WMO%d" % c) for c in range(8)]
                        for c in range(8):
                            P.dma("pool", (lambda c: lambda e: e.dma_start(
                                out=WMO[:, c], in_=win_d[24 + c].rearrange("p (k j) -> p k j", j=128)))(c), writes=[t_WMO[c]])
                        SIG = sb2("SIG", [128, 8, 256], F32); t_SIG = [Trk("SIG%d" % c) for c in range(8)]
                        HFt = Rot(sb2, "HFt", 2, [128, 1024], F32)
                        hft_map = {}
                        HN = sb2("HN", [128, 1024], F32); t_HN = Trk("HN")
                        BS = sb2("BS", [128, 4, 6], F32); t_BS = Trk("BS")
                        MV = sb2("MV", [128, 4, 2], F32); t_MV = Trk("MV")
                        SD = sb2("SD", [128, 8], F32); t_SD = Trk("SD")
                        Y1 = sb2("Y1", [128, 8, 128], F32); t_Y1 = Trk("Y1")
                        YG = Rot(sb2, "YG", 2, [128, 8, 256], BF16)
                        GS1 = [sb2("QT1", [128, 8, 256], BF16), sb2("KT1", [128, 8, 256], BF16),
                               sb2("XMB1", [128, 8, 256], BF16), sb2("UMB1", [128, 8, 256], BF16)]
                        GSETS = [((QT, KT, XMB, UMB), [Trk("gs0_%d" % i) for i in range(4)]),
                                 (tuple(GS1), [Trk("gs1_%d" % i) for i in range(4)])]
                        t_XMl = Trk("XMl")
                    hs = [False]
                    order = [0] + (list(range(1, 17)) if d == 0 else list(range(16, 0, -1)))
                    for gpos, g in enumerate(order):
                        lat = g > 0
                        gi = g - 1
                        if d == 0:
                            import os as _os2
                            PB = [B_N[:, 0, :], B_N[:, 1, :]]
                            t_PB = [t_N[0], t_N[1]]
                            if _os2.environ.get('PBPM'):
                                PB = [B_PM[:], B_PM[:]]; t_PB = [t_PM, t_PM]
                            hi = 259 if (lat and gi <= 14) else 258
                            lo = 0 if (lat and gi >= 1) else 2

                            def emit_proj(c):
                                pb, t_pb = PB[c % 2], t_PB[c % 2]
                                for kc in range(8):
                                    P.op("pe", (lambda pb, c, kc, g: lambda e: e.matmul(
                                        pb[:, 2:258], lhsT=WMX[:, c, kc, :], rhs=grp_rhs(kc, g), start=(kc == 0), stop=(kc == 7)))(pb, c, kc, g),
                                        reads=[t_WMX[c]] + grp_trks(g), writes=[t_pb])
                                if hi == 259:
                                    b0 = 256 + (gi + 1) * 256
                                    for kc in range(8):
                                        P.op("pe", (lambda pb, c, kc, b0: lambda e: e.matmul(
                                            pb[:, 258:259], lhsT=WMX[:, c, kc, :], rhs=uT[:, kc, b0:b0 + 1], start=(kc == 0), stop=(kc == 7)))(pb, c, kc, b0),
                                            reads=[t_WMX[c]] + grp_trks(g), writes=[t_pb])

                            bufs_c = {}

                            def emit_evac_act(c):
                                pb, t_pb = PB[c % 2], t_PB[c % 2]
                                umf, t_umf = UMF.next()
                                xcv, t_xcv = XCV.next()
                                bufs_c[c] = (umf, t_umf, xcv, t_xcv)
                                P.op("act", (lambda umf, pb, hi: lambda e: e.activation(
                                    out=umf[:, 2:hi], in_=pb[:, 2:hi], func=AF.Copy))(umf, pb, hi), reads=[t_pb], writes=[t_umf])
                                P.op("act", (lambda c, pb: lambda e: e.activation(out=UMB[:, c, :], in_=pb[:, 2:258], func=AF.Copy))(c, pb),
                                     reads=[t_pb], writes=[t_UMB[c]])

                            def emit_conv(c):
                                umf, t_umf, xcv, t_xcv = bufs_c[c]
                                if lo == 0:
                                    P.op("dve", (lambda umf, c: lambda e: e.tensor_copy(out=umf[:, 0:2], in_=HALO[:, c, :]))(umf, c),
                                         reads=[t_HALO[c]], writes=[t_umf])
                                else:
                                    P.op("dve", (lambda umf: lambda e: e.memset(umf[:, 0:2], 0.0))(umf), writes=[t_umf])
                                if hi == 258:
                                    P.op("dve", (lambda umf: lambda e: e.memset(umf[:, 258:259], 0.0))(umf), writes=[t_umf])
                                if lat and gi <= 14:
                                    P.op("dve", (lambda umf, c: lambda e: e.tensor_copy(out=HALO[:, c, :], in_=umf[:, 256:258]))(umf, c),
                                         reads=[t_umf], writes=[t_HALO[c]])
                                P.op("dve", (lambda umf, xcv, c: lambda e: e.tensor_scalar(
                                    out=xcv[:], in0=umf[:, 0:256], scalar1=fvc("mlcw", c * 4), scalar2=fvc("mlcb", c),
                                    op0=ALU.mult, op1=ALU.add))(umf, xcv, c), reads=[t_umf, t_FV], writes=[t_xcv])
                                for j in range(1, 4):
                                    P.op("dve", (lambda umf, xcv, c, j: lambda e: e.scalar_tensor_tensor(
                                        out=xcv[:], in0=umf[:, j:j + 256], scalar=fvc("mlcw", c * 4 + j), in1=xcv[:],
                                        op0=ALU.mult, op1=ALU.add))(umf, xcv, c, j), reads=[t_umf, t_FV, t_xcv], writes=[t_xcv])
                                P.op("act", (lambda xcv, c: lambda e: e.activation(out=XM[:, c, :], in_=xcv[:], func=AF.Silu))(xcv, c),
                                     reads=[t_xcv], writes=[t_XM[c]])
                                P.op("act", (lambda xcv, c: lambda e: e.activation(out=XMB[:, c, :], in_=xcv[:], func=AF.Silu))(xcv, c),
                                     reads=[t_xcv], writes=[t_XMB[c]])

                            vts = {}

                            def emit_qkv(c):
                                vt, t_vt = VT.next()
                                vts[c] = (vt, t_vt)
                                P.op("pe", (lambda c: lambda e: e.matmul(
                                    B_QK[:, 0:256], lhsT=MLBD[:, c, :], rhs=XMB[:, c, :], start=True, stop=True))(c),
                                    reads=[t_MLBD, t_XMB[c]], writes=[t_PQ])
                                P.op("pe", (lambda c: lambda e: e.matmul(
                                    B_QK[:, 256:512], lhsT=MLBD[:, 8 + c, :], rhs=XMB[:, c, :], start=True, stop=True))(c),
                                    reads=[t_MLBD, t_XMB[c]], writes=[t_PQ])
                                P.op("pe", (lambda c: lambda e: e.matmul(
                                    B_VO[:, 0:256], lhsT=MLBD[:, 16 + c, :], rhs=UMB[:, c, :], start=True, stop=True))(c),
                                    reads=[t_MLBD, t_UMB[c]], writes=[t_PV])
                                P.op("act", (lambda c: lambda e: e.activation(out=QT[:, c, :], in_=B_QK[:, 0:256], func=AF.Copy))(c),
                                     reads=[t_PQ], writes=[t_QT[c]])
                                P.op("act", (lambda c: lambda e: e.activation(
                                    out=KT[:, c, :], in_=B_QK[:, 256:512], func=AF.Copy, scale=1.0 / 16.0))(c),
                                    reads=[t_PQ], writes=[t_KT[c]])
                                P.op("dve", (lambda vt: lambda e: e.tensor_copy(out=vt[:], in_=B_VO[:, 0:256]))(vt),
                                     reads=[t_PV], writes=[t_vt])

                            def emit_gates(c):
                                vt, t_vt = vts[c]
                                for ti, (src, t_src) in enumerate(((QT[:, c, :], t_QT[c]), (KT[:, c, :], t_KT[c]), (vt[:], t_vt))):
                                    P.op("pe", (lambda c, ti, src, d: lambda e: e.matmul(
                                        B_GP[0:8, 0:256], lhsT=WGT[:, ti * 8 + c, d * 8:(d + 1) * 8], rhs=src,
                                        start=(c == 0 and ti == 0), stop=(c == 7 and ti == 2)))(c, ti, src, d),
                                        reads=[t_WGT, t_src], writes=[t_GP])

                            if _os2.environ.get("NOPIPE"):
                                for c in range(8):
                                    emit_proj(c)
                                    emit_evac_act(c)
                                    emit_conv(c)
                                    emit_qkv(c)
                                    emit_gates(c)
                            else:
                                emit_proj(0)
                                emit_proj(1)
                                emit_evac_act(0)
                                for c in range(8):
                                    if c + 2 < 8:
                                        emit_proj(c + 2)
                                    if c + 1 < 8:
                                        emit_evac_act(c + 1)
                                    emit_conv(c)
                                    emit_qkv(c)
                                    if c > 0:
                                        emit_gates(c - 1)
                                emit_gates(7)
                            for wi_, (arr, trs) in enumerate(((QT, t_QT), (KT, t_KT), (XMB, t_XMB), (UMB, t_UMB))):
                                P.dma("sp", (lambda arr, g, wi_: lambda e: e.dma_start(out=SPQ_d[g, wi_], in_=arr[:]))(arr, g, wi_),
                                      reads=trs, writes=[t_SPQ[g][wi_]], semtrk=t_spd[wi_])
                            if lat:
                                P.dma("sp", (lambda gi: lambda e: e.dma_start(out=SPX_d[gi], in_=XM[:]))(gi),
                                      reads=t_XM, writes=[t_SPX[gi]], semtrk=t_spd[4])
                            gates_post(d)
                            rec_group(g, d, QT, KT, XMB, UMB, t_QT, t_KT, t_XMB, t_UMB, XM, t_XM, hs)
                            continue
                        def load_set(gq, si):
                            arrs, trs = GSETS[si]
                            for wi_ in range(4):
                                P.dma("sp", (lambda arrs, gq, wi_: lambda e: e.dma_start(out=arrs[wi_][:], in_=SPQ_d[gq, wi_]))(arrs, gq, wi_),
                                      reads=[t_SPQ[gq][wi_]], writes=[trs[wi_]])
                        if gpos == 0:
                            load_set(g, 0)
                        if gpos + 1 < len(order):
                            load_set(order[gpos + 1], (gpos + 1) % 2)
                        (QTg, KTg, XMBg, UMBg), trs = GSETS[gpos % 2]
                        if lat:
                            P.dma("sp", (lambda gi: lambda e: e.dma_start(out=XM[:], in_=SPX_d[gi]))(gi), reads=[t_SPX[gi]], writes=[t_XMl])
                        for c in range(8):
                            vt, t_vt = VT.next()
                            P.op("pe", (lambda c, UMBg: lambda e: e.matmul(
                                B_VO[:, 0:256], lhsT=MLBD[:, 16 + c, :], rhs=UMBg[:, c, :], start=True, stop=True))(c, UMBg),
                                reads=[t_MLBD, trs[3]], writes=[t_PV])
                            P.op("dve", (lambda vt: lambda e: e.tensor_copy(out=vt[:], in_=B_VO[:, 0:256]))(vt),
                                 reads=[t_PV], writes=[t_vt])
                            if lat:
                                for kc in range(8):
                                    P.op("pe", (lambda c, kc, g: lambda e: e.matmul(
                                        B_PM[:, 0:256], lhsT=WMO[:, c, kc, :], rhs=grp_rhs(kc, g), start=(kc == 0), stop=(kc == 7)))(c, kc, g),
                                        reads=[t_WMO[c]] + grp_trks(g), writes=[t_PM])
                                P.op("act", (lambda c: lambda e: e.activation(out=SIG[:, c, :], in_=B_PM[:, 0:256], func=AF.Sigmoid))(c),
                                     reads=[t_PM], writes=[t_SIG[c]])
                            for ti, (src, t_src) in enumerate(((QTg[:, c, :], trs[0]), (KTg[:, c, :], trs[1]), (vt[:], t_vt))):
                                P.op("pe", (lambda c, ti, src, d: lambda e: e.matmul(
                                    B_GP[0:8, 0:256], lhsT=WGT[:, ti * 8 + c, d * 8:(d + 1) * 8], rhs=src,
                                    start=(c == 0 and ti == 0), stop=(c == 7 and ti == 2)))(c, ti, src, d),
                                    reads=[t_WGT, t_src], writes=[t_GP])
                        if lat:
                            o2_, w2_ = FVCOLS["mlsk"]
                            P.op("dve", lambda e: e.tensor_tensor(
                                out=XM[:], in0=XM[:], in1=FV[:, o2_:o2_ + 8].unsqueeze(2).to_broadcast([128, 8, 256]), op=ALU.mult),
                                reads=[t_XMl, t_FV], writes=[t_XMl])
                        gates_post(d)
                        rec_group(g, d, QTg, KTg, XMBg, UMBg, [trs[0]] * 8, [trs[1]] * 8, [trs[2]] * 8, [trs[3]] * 8, XM, [t_XMl] * 8, hs)
                    P.barrier()
                    P.emit()
        if "yml" in debug:
            dbg_d["yml"] = YML_d
        if stop_after == "ml":
            P.wait_all("sp", P.out_toks)
            P.barrier()
            P.emit()
            ust.close()
            return nc, dbg_d

        x_cm = x_d.rearrange("(r j) d -> j r d", j=64)
        out_cm = out_d.rearrange("(r j) d -> j r d", j=64)
        t_X1 = [Trk("X1_%d" % i) for i in range(32)]
        t_H2 = [Trk("H2_%d" % i) for i in range(16)]
        with ExitStack() as st:
            sb = lambda name, shape, dt: st.enter_context(nc.sbuf_tensor(name, list(shape), dt))
            ps = lambda name, shape, dt: st.enter_context(nc.psum_tensor(name, list(shape), dt))
            WGR = sb("WGR", [128, 8, 8, 128], BF16); WGM = sb("WGM", [128, 8, 8, 128], BF16)
            WBR = sb("WBR", [128, 8, 8, 128], BF16); WBM = sb("WBM", [128, 8, 8, 128], BF16)
            WOUT = sb("WOUT", [128, 8, 1024], BF16)
            t_WGR = [Trk("WGR%d" % i) for i in range(8)]; t_WGM = [Trk("WGM%d" % i) for i in range(8)]
            t_WBR = [Trk("WBR%d" % i) for i in range(8)]; t_WBM = [Trk("WBM%d" % i) for i in range(8)]
            t_WOUT = [Trk("WOUT%d" % i) for i in range(8)]
            for oc in range(8):
                for (W, t_W, src) in ((WGR, t_WGR, win_d[32 + oc]), (WGM, t_WGM, win_d[40 + oc]),
                                      (WBR, t_WBR, wbrg_d[oc]), (WBM, t_WBM, wbml_d[oc])):
                    P.dma("pool", (lambda W, oc, src: lambda e: e.dma_start(
                        out=W[:, oc], in_=src.rearrange("p (k j) -> p k j", j=128)))(W, oc, src), writes=[t_W[oc]])
            for kc in range(8):
                P.dma("pool", (lambda kc: lambda e: e.dma_start(out=WOUT[:, kc, :], in_=wout_d[:, kc, :]))(kc), writes=[t_WOUT[kc]])
            YRt = Rot(sb, "YRt", 1, [128, 8, 512], BF16)
            YMt = Rot(sb, "YMt", 1, [128, 8, 512], BF16)
            SG = Rot(sb, "SG", 1, [128, 1024], F32)
            MIX = Rot(sb, "MIX", 1, [128, 8, 512], BF16)
            XT = Rot(sb, "XTc", 2, [128, 1024], F32)
            X1t = Rot(sb, "X1t", 1, [128, 1024], F32)
            XN = Rot(sb, "XNc", 1, [128, 1024], BF16)
            H2s = Rot(sb, "H2s", 1, [128, 8, 256], BF16)
            STc = sb("STc", [128, 32, 4], F32)
            BA0 = ps("BA0", [128, 512], F32); t_BA0 = Trk("PS_BA0")
            BA1 = ps("BA1", [128, 512], F32); t_BA1 = Trk("PS_BA1")
            BB0 = ps("BB0", [128, 512], F32); t_BB0 = Trk("PS_BB0")
            BB1 = ps("BB1", [128, 512], F32); t_BB1 = Trk("PS_BB1")
            BY = ps("BY", [128, 1024], F32); t_BY = Trk("PS_BY")
            BT = ps("BT", [128, 8, 128], BF16); t_BT = Trk("PS_BT")
            def load_y(T):
                yr, t_yr = YRt.next()
                ym, t_ym = YMt.next()
                P.dma("sp", (lambda yr, T: lambda e: e.dma_start(
                    out=yr[:], in_=YRG_d[:, :, T * 512:(T + 1) * 512].rearrange("c p t -> p c t")))(yr, T), reads=t_YRG, writes=[t_yr])
                P.dma("sp", (lambda ym, T: lambda e: e.dma_start(
                    out=ym[:], in_=YML_d[:, :, T * 512:(T + 1) * 512].rearrange("c p t -> p c t")))(ym, T),
                    reads=t_YML[2 * T:2 * T + 2], writes=[t_ym])
                return yr, t_yr, ym, t_ym

            def load_x(ti):
                xt, t_xt = XT.next()
                for jj in range(2):
                    P.dma("sp", (lambda xt, jj, ti: lambda e: e.dma_start(
                        out=xt[jj * 64:(jj + 1) * 64, :], in_=x_cm[2 * ti + jj]))(xt, jj, ti), writes=[t_xt])
                return xt, t_xt

            ynext = load_y(0)
            xnext = load_x(0)
            for T in range(8):
                yr, t_yr, ym, t_ym = ynext
                mix, t_mix = MIX.next()
                for oc in range(8):
                    sg, t_sg = SG.next()
                    for (W, t_W, bank, t_bank) in ((WGR, t_WGR, BA0, t_BA0), (WGM, t_WGM, BA1, t_BA1)):
                        for kc in range(8):
                            P.op("pe", (lambda W, bank, oc, kc, T: lambda e: e.matmul(
                                bank[:], lhsT=W[:, oc, kc, :],
                                rhs=uT[:, kc, 256 + T * 512:256 + (T + 1) * 512],
                                start=(kc == 0), stop=(kc == 7)))(W, bank, oc, kc, T), reads=[t_W[oc]] + t_uT[2:34], writes=[t_bank])
                    for (W, t_W, src, t_src, bank, t_bank) in ((WBR, t_WBR, yr, t_yr, BB0, t_BB0), (WBM, t_WBM, ym, t_ym, BB1, t_BB1)):
                        for kc in range(8):
                            P.op("pe", (lambda W, bank, oc, kc, src: lambda e: e.matmul(
                                bank[:], lhsT=W[:, oc, kc, :], rhs=src[:, kc, :],
                                start=(kc == 0), stop=(kc == 7)))(W, bank, oc, kc, src), reads=[t_W[oc], t_src], writes=[t_bank])
                    for (i_, bank, t_bank) in ((0, BA0, t_BA0), (1, BA1, t_BA1)):
                        P.op("act", (lambda sg, bank, i_: lambda e: e.activation(
                            out=sg[:, i_ * 512:(i_ + 1) * 512], in_=bank[:], func=AF.Sigmoid))(sg, bank, i_), reads=[t_bank], writes=[t_sg])
                    for (i_, bank, t_bank) in ((0, BB0, t_BB0), (1, BB1, t_BB1)):
                        P.op("dve", (lambda sg, bank, i_: lambda e: e.tensor_tensor(
                            out=sg[:, i_ * 512:(i_ + 1) * 512], in0=bank[:], in1=sg[:, i_ * 512:(i_ + 1) * 512], op=ALU.mult))(sg, bank, i_),
                            reads=[t_bank, t_sg], writes=[t_sg])
                    P.op("dve", (lambda mix, sg, oc: lambda e: e.tensor_tensor(
                        out=mix[:, oc, :], in0=sg[:, 0:512], in1=sg[:, 512:1024], op=ALU.add))(mix, sg, oc),
                        reads=[t_sg], writes=[t_mix])
                if T + 1 < 8:
                    ynext = load_y(T + 1)
                for s_ in range(4):
                    ti = T * 4 + s_
                    if s_ % 2 == 0:
                        h2s, t_h2s = H2s.next()
                    xt, t_xt = xnext
                    if ti + 1 < 32:
                        xnext = load_x(ti + 1)
                    x1, t_x1 = X1t.next()
                    xn, t_xn = XN.next()
                    t_st = Trk("stc%d" % ti)
                    for half in range(2):
                        for kc in range(8):
                            P.op("pe", (lambda mix, half, kc, s_: lambda e: e.matmul(
                                BY[:, half * 512:(half + 1) * 512], lhsT=mix[:, kc, s_ * 128:(s_ + 1) * 128],
                                rhs=WOUT[:, kc, half * 512:(half + 1) * 512], start=(kc == 0), stop=(kc == 7)))(mix, half, kc, s_),
                                reads=[t_mix, t_WOUT[kc]], writes=[t_BY])
                    P.op("dve", (lambda x1: lambda e: e.tensor_tensor(out=x1[:], in0=BY[:], in1=GROW[:, 0, :], op=ALU.mult))(x1),
                         reads=[t_BY, t_GROW], writes=[t_x1])
                    P.op("dve", (lambda x1, xt: lambda e: e.tensor_tensor(out=x1[:], in0=x1[:], in1=xt[:], op=ALU.add))(x1, xt),
                         reads=[t_x1, t_xt], writes=[t_x1])
                    P.dma("sp", (lambda x1, ti: lambda e: e.dma_start(out=X1_d[ti * 128:(ti + 1) * 128, :], in_=x1[:]))(x1, ti),
                          reads=[t_x1], writes=[t_X1[ti]], semtrk=t_x1)
                    P.op("act", (lambda x1, xn, ti: lambda e: e.activation(
                        out=xn[:], in_=x1[:], func=AF.Square, accum_out=STc[:, ti, 0:1]))(x1, xn, ti), reads=[t_x1], writes=[t_xn, t_st])
                    P.op("act", (lambda ti: lambda e: e.activation(
                        out=STc[:, ti, 1:2], in_=STc[:, ti, 0:1], func=AF.Sqrt, scale=1.0 / 1024.0, bias=EPS))(ti), reads=[t_st], writes=[t_st])
                    P.op("dve", (lambda ti: lambda e: e.reciprocal(out=STc[:, ti, 2:3], in_=STc[:, ti, 1:2]))(ti), reads=[t_st], writes=[t_st])
                    P.op("act", (lambda x1, xn, ti: lambda e: e.activation(
                        out=xn[:], in_=x1[:], func=AF.Copy, scale=STc[:, ti, 2:3]))(x1, xn, ti), reads=[t_x1, t_st], writes=[t_xn])
                    for c in range(8):
                        P.op("pe", (lambda xn, c: lambda e: e.transpose(
                            out=BT[:, c, :], in_=xn[:, c * 128:(c + 1) * 128], identity=IDB[:]))(xn, c), reads=[t_xn, t_IDB], writes=[t_BT])
                    for c in range(8):
                        dst = h2s[:, c, (s_ % 2) * 128:(s_ % 2 + 1) * 128]
                        if c % 2 == 0:
                            P.op("dve", (lambda c, dst: lambda e: e.tensor_scalar(
                                out=dst, in0=BT[:, c, :], scalar1=A2[:, c:c + 1], scalar2=MODT[:, 48 + 2 * c:49 + 2 * c],
                                op0=ALU.mult, op1=ALU.add))(c, dst), reads=[t_BT, t_A2, t_MODT], writes=[t_h2s])
                        else:
                            P.op("act", (lambda c, dst: lambda e: e.activation(
                                out=dst, in_=BT[:, c, :], func=AF.Identity, scale=A2[:, c:c + 1], bias=MODT[:, 48 + 2 * c:49 + 2 * c]))(c, dst),
                                reads=[t_BT, t_A2, t_MODT], writes=[t_h2s])
                    if s_ % 2 == 1:
                        Tq = ti // 2
                        P.dma("sp", (lambda h2s, Tq: lambda e: e.dma_start(out=H2_d[Tq], in_=h2s[:]))(h2s, Tq),
                              reads=[t_h2s], writes=[t_H2[Tq]], semtrk=t_h2s)
            P.barrier()
            P.emit()
        ust.close()
        if "x1" in debug:
            dbg_d["x1"] = X1_d

        with ExitStack() as st:
            sb = lambda name, shape, dt: st.enter_context(nc.sbuf_tensor(name, list(shape), dt))
            ps = lambda name, shape, dt: st.enter_context(nc.psum_tensor(name, list(shape), dt))
            WFI = sb("WFI", [128, 44, 8, 128], BF16); t_WFI = [Trk("WFI%d" % i) for i in range(44)]
            WFO = sb("WFO", [128, 22, 1024], BF16); t_WFO = [Trk("WFO%d" % i) for i in range(22)]
            FGR = sb("FGR", [128, 1024], F32); t_FGR = Trk("FGR")
            P.dma("sp", lambda e: e.dma_start(out=FGR[:], in_=fgrow_d), writes=[t_FGR])
            for f_ in range(22):
                for ci in (f_, 22 + f_):
                    P.dma("pool", (lambda ci: lambda e: e.dma_start(
                        out=WFI[:, ci], in_=wffi_d[ci].rearrange("p (k j) -> p k j", j=128)))(ci), writes=[t_WFI[ci]])
                P.dma("pool", (lambda f_: lambda e: e.dma_start(out=WFO[:, f_, :], in_=wffo_d[:, f_, :]))(f_), writes=[t_WFO[f_]])
            H2t = Rot(sb, "H2t", 2, [128, 8, 512], BF16)
            SGt = Rot(sb, "SGt", 2, [128, 512], F32)
            HID = Rot(sb, "HID", 1, [128, 22, 512], BF16)
            X1r = Rot(sb, "X1r", 2, [128, 1024], F32)
            T2 = Rot(sb, "T2", 2, [128, 1024], F32)
            JK = sb("JK", [128, 1024], BF16); t_JK = Trk("JK")
            STf = sb("STf", [128, 32, 4], F32)
            BG0 = Rot(ps, "PS_BG0", 2, [128, 512], F32)
            BG1 = Rot(ps, "PS_BG1", 2, [128, 512], F32)
            BO = Rot(ps, "PS_BO", 2, [128, 1024], F32)
            h2_map = {}
            x1_map = {}
            for T in range(8):
                for Tq in (T, T + 1):
                    if Tq < 8 and Tq not in h2_map:
                        hb, t_hb = H2t.next()
                        for q_ in range(2):
                            P.dma("sp", (lambda hb, Tq, q_: lambda e: e.dma_start(out=hb[:, :, q_ * 256:(q_ + 1) * 256], in_=H2_d[2 * Tq + q_]))(hb, Tq, q_),
                                  reads=[t_H2[2 * Tq + q_]], writes=[t_hb])
                        h2_map[Tq] = (hb, t_hb)
                h2, t_h2 = h2_map.pop(T)
                hid, t_hid = HID.next()
                for f_ in range(22):
                    g0, t_g0 = BG0.next()
                    g1, t_g1 = BG1.next()
                    sg, t_sg = SGt.next()
                    for (ci, bank, t_bank) in ((f_, g0, t_g0), (22 + f_, g1, t_g1)):
                        for kc in range(8):
                            P.op("pe", (lambda bank, ci, kc, h2: lambda e: e.matmul(
                                bank[:], lhsT=WFI[:, ci, kc, :], rhs=h2[:, kc, :], start=(kc == 0), stop=(kc == 7)))(bank, ci, kc, h2),
                                reads=[t_WFI[ci], t_h2], writes=[t_bank])
                    P.op("act", (lambda sg, g0: lambda e: e.activation(out=sg[:], in_=g0[:], func=AF.Silu))(sg, g0),
                         reads=[t_g0], writes=[t_sg])
                    P.op("dve", (lambda hid, f_, sg, g1: lambda e: e.tensor_tensor(
                        out=hid[:, f_, :], in0=g1[:], in1=sg[:], op=ALU.mult))(hid, f_, sg, g1), reads=[t_g1, t_sg], writes=[t_hid])
                for s_ in range(4):
                    ti = T * 4 + s_
                    bo, t_bo = BO.next()
                    for tq in (ti, ti + 1):
                        if tq < 32 and tq not in x1_map:
                            xb, t_xb = X1r.next()
                            P.dma("sp", (lambda xb, tq: lambda e: e.dma_start(out=xb[:], in_=X1_d[tq * 128:(tq + 1) * 128, :]))(xb, tq),
                                  reads=[t_X1[tq]], writes=[t_xb])
                            x1_map[tq] = (xb, t_xb)
                    x1, t_x1 = x1_map.pop(ti)
                    t2, t_t2 = T2.next()
                    t_st = Trk("stf%d" % ti)
                    for half in range(2):
                        for f_ in range(22):
                            P.op("pe", (lambda bo, hid, half, f_, s_: lambda e: e.matmul(
                                bo[:, half * 512:(half + 1) * 512], lhsT=hid[:, f_, s_ * 128:(s_ + 1) * 128],
                                rhs=WFO[:, f_, half * 512:(half + 1) * 512], start=(f_ == 0), stop=(f_ == 21)))(bo, hid, half, f_, s_),
                                reads=[t_hid, t_WFO[f_]], writes=[t_bo])
                    P.op("dve", (lambda t2, bo: lambda e: e.tensor_tensor(out=t2[:], in0=bo[:], in1=GROW[:, 1, :], op=ALU.mult))(t2, bo),
                         reads=[t_bo, t_GROW], writes=[t_t2])
                    P.op("dve", (lambda t2, x1: lambda e: e.tensor_tensor(out=t2[:], in0=t2[:], in1=x1[:], op=ALU.add))(t2, x1),
                         reads=[t_t2, t_x1], writes=[t_t2])
                    P.op("act", (lambda t2, ti: lambda e: e.activation(
                        out=JK[:], in_=t2[:], func=AF.Square, accum_out=STf[:, ti, 0:1]))(t2, ti), reads=[t_t2], writes=[t_JK, t_st])
                    P.op("act", (lambda ti: lambda e: e.activation(
                        out=STf[:, ti, 1:2], in_=STf[:, ti, 0:1], func=AF.Sqrt, scale=1.0 / 1024.0, bias=EPS))(ti), reads=[t_st], writes=[t_st])
                    P.op("dve", (lambda ti: lambda e: e.reciprocal(out=STf[:, ti, 2:3], in_=STf[:, ti, 1:2]))(ti), reads=[t_st], writes=[t_st])
                    P.op("act", (lambda t2, ti: lambda e: e.activation(
                        out=t2[:], in_=t2[:], func=AF.Copy, scale=STf[:, ti, 2:3]))(t2, ti), reads=[t_t2, t_st], writes=[t_t2])
                    P.op("dve", (lambda t2: lambda e: e.tensor_tensor(out=t2[:], in0=t2[:], in1=FGR[:], op=ALU.mult))(t2),
                         reads=[t_t2, t_FGR], writes=[t_t2])
                    for jj in range(2):
                        P.out_toks.append(P.dma("sp", (lambda t2, jj, ti: lambda e: e.dma_start(
                            out=out_cm[2 * ti + jj], in_=t2[jj * 64:(jj + 1) * 64, :]))(t2, jj, ti), reads=[t_t2], semtrk=t_t2))
            P.barrier()
            P.emit()

        P.wait_all("sp", P.out_toks)
        P.emit()
    return nc, dbg_d


def make_in_maps(inputs):
    sh = prep_shared(inputs)
    x = np.asarray(inputs["x"], np.float32)
    c = np.asarray(inputs["c"], np.float32)
    ctx = np.asarray(inputs["ctx"], np.float32)
    c_ctx = np.asarray(inputs["c_ctx"], np.float32)
    maps = []
    for b in range(8):
        m = dict(sh)
        m["x"] = np.ascontiguousarray(x[b])
        m["ctx"] = np.ascontiguousarray(ctx[b])
        m["cv"] = np.ascontiguousarray(np.stack([fm(c[b]), fm(c_ctx)], 2).reshape(128, 16))
        maps.append(m)
    return maps


def kernel(**inputs):
    nc, _ = build()
    maps = make_in_maps(inputs)
    res = run_bass_kernel_spmd(nc, maps, core_ids=list(range(8)))
    return np.stack([r["out"] for r in res.results], 0)
```

```python
import numpy as np
from contextlib import ExitStack
import concourse.bass as bass
import concourse.mybir as mybir
from concourse.bass_utils import run_bass_kernel_spmd

F32 = mybir.dt.float32
BF16 = mybir.dt.bfloat16
AF = mybir.ActivationFunctionType
ALU = mybir.AluOpType
AX = mybir.AxisListType

ENGS = ["pe", "act", "dve", "pool", "sp"]
EPS = 1e-6
NT = 4352
NCTX = 256
NLAT = 4096


class Trk:
    __slots__ = ("name", "w", "r", "dsem", "dcnt", "excl")

    def __init__(self, name=""):
        self.name = name
        self.w = None
        self.r = {}
        self.dsem = None
        self.dcnt = 0
        self.excl = name.startswith("PS_")


class Prog:
    def __init__(self, nc, stack):
        self.nc = nc
        self.stack = stack
        self.ops = {e: [] for e in ENGS}
        self.seq = {e: 0 for e in ENGS}
        self.known = {e: {} for e in ENGS}
        self.esem = {e: stack.enter_context(nc.semaphore("s_" + e)) for e in ENGS}
        self.nsem = len(ENGS)
        self.out_toks = []
        self.dtrks = []
        self.sem_pool = {"sp": [], "pool": []}

    def new_dsem(self, name):
        s = self.stack.enter_context(self.nc.semaphore("d%d_%s" % (self.nsem, name)))
        self.nsem += 1
        return s

    def _need(self, eng, waits, dep):
        sem, val = dep
        if eng == "pe" and sem is self.esem["pe"]:
            return
        k = id(sem)
        if self.known[eng].get(k, 0) >= val:
            return
        self.known[eng][k] = val
        waits[k] = (sem, val)

    def _deps(self, eng, reads, writes):
        waits = {}
        for t in reads:
            if t.w is not None:
                self._need(eng, waits, t.w)
            if t.excl:
                for dep in t.r.values():
                    self._need(eng, waits, dep)
        for t in writes:
            if t.w is not None:
                self._need(eng, waits, t.w)
            for dep in t.r.values():
                self._need(eng, waits, dep)
        return waits

    def _record(self, tok, reads, writes):
        for t in reads:
            t.r[id(tok[0])] = tok
        for t in writes:
            t.w = tok
            t.r = {}

    def op(self, eng, fn, reads=(), writes=()):
        waits = self._deps(eng, reads, writes)
        self.seq[eng] += 1
        tok = (self.esem[eng], self.seq[eng])
        self._record(tok, reads, writes)
        self.ops[eng].append((list(waits.values()), fn, (self.esem[eng], 1)))

    def dma(self, eng, fn, reads=(), writes=(), semtrk=None):
        if semtrk is None:
            semtrk = writes[0] if writes else reads[0]
        if semtrk.dsem is None:
            if self.sem_pool[eng]:
                semtrk.dsem, semtrk.dcnt = self.sem_pool[eng].pop()
            else:
                semtrk.dsem = self.new_dsem(semtrk.name)
            self.dtrks.append((semtrk, eng))
        waits = self._deps(eng, reads, writes)
        if semtrk.dcnt > 0:
            self._need(eng, waits, (semtrk.dsem, semtrk.dcnt))
        semtrk.dcnt += 16
        tok = (semtrk.dsem, semtrk.dcnt)
        self._record(tok, reads, writes)
        self.ops[eng].append((list(waits.values()), fn, (semtrk.dsem, 16)))
        return tok

    def wait_all(self, eng, toks):
        waits = {}
        for t in toks:
            self._need(eng, waits, t)
        self.ops[eng].append((list(waits.values()), None, None))

    def barrier(self):
        waits = {}
        for e in ENGS:
            if e != "sp" and self.seq[e] > 0:
                self._need("sp", waits, (self.esem[e], self.seq[e]))
        for t, _e in self.dtrks:
            if t.dcnt > 0:
                self._need("sp", waits, (t.dsem, t.dcnt))
        self.seq["sp"] += 1
        self.ops["sp"].append((list(waits.values()), lambda e: e.nop(), (self.esem["sp"], 1)))
        for t, e_ in self.dtrks:
            self.sem_pool[e_].append((t.dsem, t.dcnt))
            t.dsem = None
            t.dcnt = 0
        self.dtrks = []
        for e in ENGS:
            if e != "sp":
                w = {}
                self._need(e, w, (self.esem["sp"], self.seq["sp"]))
                self.ops[e].append((list(w.values()), None, None))

    def emit(self):
        nc = self.nc
        ops = self.ops
        self.ops = {e: [] for e in ENGS}
        with nc.Block() as block:
            def run(e, lst):
                for waits, fn, inc in lst:
                    for sem, val in waits:
                        e.wait_ge(sem, val)
                    if fn is not None:
                        fn(e).then_inc(inc[0], inc[1])

            @block.tensor
            def _(e):
                run(e, ops["pe"])

            @block.scalar
            def _(e):
                run(e, ops["act"])

            @block.vector
            def _(e):
                run(e, ops["dve"])

            @block.gpsimd
            def _(e):
                run(e, ops["pool"])

            @block.sync
            def _(e):
                run(e, ops["sp"])


class Rot:
    def __init__(self, alloc, name, n, shape, dt):
        self.bufs = [(alloc("%s%d" % (name, i), shape, dt), Trk("%s%d" % (name, i))) for i in range(n)]
        self.i = 0

    def next(self):
        b = self.bufs[self.i % len(self.bufs)]
        self.i += 1
        return b


FVCOLS = {}
_off = 0
for _n, _w in [("n1g", 8), ("n2g", 8), ("rgcw", 32), ("rgcb", 8), ("rgba", 16), ("rgbx", 16), ("rglam", 16),
               ("mlcw", 32), ("mlcb", 8), ("mlng", 8), ("mlsk", 8)]:
    FVCOLS[_n] = (_off, _w)
    _off += _w
NV = _off


def fm(vec):
    v = np.asarray(vec, np.float32)
    return np.ascontiguousarray(v.reshape(-1, 128).T)


def colchunks(w):
    K, N = w.shape
    a = w.reshape(K // 128, 128, N // 128, 128)
    a = a.transpose(2, 1, 0, 3)
    return np.ascontiguousarray(a.reshape(N // 128, 128, (K // 128) * 128))


def rowchunks(w):
    K, N = w.shape
    return np.ascontiguousarray(w.reshape(K // 128, 128, N).transpose(1, 0, 2))


def blockdiag128(blocks):
    nb, bi, bo = blocks.shape
    per = 128 // bi
    out = np.zeros((nb // per, 128, 128), np.float32)
    for b in range(nb):
        c, q = divmod(b, per)
        out[c, q * bi:(q + 1) * bi, q * bo:(q + 1) * bo] = blocks[b]
    return out


def prep_shared(inp):
    f = lambda k: np.asarray(inp[k], np.float32)
    sh = {}
    w_mod = f("w_mod")[0]
    sh["wmod"] = colchunks(w_mod)
    b_mod = f("b_mod")[0]
    sh["bmod2"] = np.ascontiguousarray(np.repeat(fm(b_mod)[:, :, None], 2, axis=2).reshape(128, 96))
    sh["wmodg"] = np.stack([rowchunks(w_mod[:, 2048:3072]), rowchunks(w_mod[:, 5120:6144])], 0)
    sh["bmodg"] = np.ascontiguousarray(np.broadcast_to(
        np.concatenate([b_mod[2048:3072], b_mod[5120:6144]])[None, :], (128, 2048)))
    fv = np.zeros((128, NV), np.float32)

    def put(name, arr):
        o, w = FVCOLS[name]
        assert arr.shape == (128, w), (name, arr.shape)
        fv[:, o:o + w] = arr
    put("n1g", fm(f("norm1_g")[0]))
    put("n2g", fm(f("norm2_g")[0]))
    cw = f("rg_conv_w")[0]
    put("rgcw", np.stack([fm(cw[j]) for j in range(4)], 2).reshape(128, 32))
    put("rgcb", fm(f("rg_conv_b")[0]))
    put("rgba", np.concatenate([fm(f("rg_ba")[0][d]) for d in range(2)], 1))
    put("rgbx", np.concatenate([fm(f("rg_bx")[0][d]) for d in range(2)], 1))
    put("rglam", np.concatenate([fm(f("rg_lambda")[0][d]) for d in range(2)], 1))
    cw = f("ml_conv_w")[0]
    put("mlcw", np.stack([fm(cw[j]) for j in range(4)], 2).reshape(128, 32))
    put("mlcb", fm(f("ml_conv_b")[0]))
    put("mlng", fm(f("ml_norm_g")[0]))
    put("mlsk", fm(f("ml_skip")[0]))
    sh["fv"] = fv
    sh["win"] = colchunks(f("w_in")[0])
    rg = []
    for d in range(2):
        for w in (f("rg_wa")[0][d], f("rg_wx")[0][d]):
            rg.append(blockdiag128(w).transpose(1, 0, 2))
    sh["rgbd"] = np.ascontiguousarray(np.concatenate(rg, 1))
    ml = [blockdiag128(f(k)[0]).transpose(1, 0, 2) for k in ("ml_wq", "ml_wk", "ml_wv")]
    sh["mlbd"] = np.ascontiguousarray(np.concatenate(ml, 1))
    wi, wf = f("ml_wi")[0], f("ml_wf")[0]
    wg = np.concatenate([wi[0], wf[0], wi[1], wf[1]], 1)
    sh["wgt"] = np.ascontiguousarray(wg.reshape(24, 128, 16).transpose(1, 0, 2))
    bi, bf = f("ml_bi")[0], f("ml_bf")[0]
    gb = np.stack([np.tile(np.concatenate([bi[d], bf[d]]), 2) for d in range(2)], 0)
    sh["gbias"] = np.ascontiguousarray(np.broadcast_to(gb[None], (128, 2, 16)))
    tri = np.zeros((2, 128, 128), np.float32)
    ii = np.arange(128)
    tri[0] = (ii[:, None] <= ii[None, :])
    tri[1] = (ii[:, None] >= ii[None, :])
    sh["tri"] = np.ascontiguousarray(tri.transpose(1, 0, 2))
    ident = np.eye(128, dtype=np.float32)
    sh["ident"] = ident
    sh["wbrg"] = colchunks(f("w_branch_rg")[0])
    sh["wbml"] = colchunks(f("w_branch_ml")[0])
    sh["wout"] = rowchunks(f("w_out")[0])
    sh["wffi"] = colchunks(f("w_ffn_in")[0])
    sh["wffo"] = rowchunks(f("w_ffn_out")[0])
    sh["fgrow"] = np.ascontiguousarray(np.broadcast_to(f("final_norm_g")[None, :], (128, 1024)))
    return sh


def build(debug=(), stop_after=None):
    nc = bass.Bass("TRN2", target_bir_lowering=False)
    din = lambda name, shape, dt=F32: nc.dram_tensor(name, list(shape), dt, kind="ExternalInput").ap()
    x_d = din("x", [NLAT, 1024])
    ctx_d = din("ctx", [NCTX, 1024])
    cv_d = din("cv", [128, 16])
    wmod_d = din("wmod", [48, 128, 1024])
    bmod2_d = din("bmod2", [128, 96])
    wmodg_d = din("wmodg", [2, 128, 8, 1024])
    bmodg_d = din("bmodg", [128, 2048])
    fv_d = din("fv", [128, NV])
    win_d = din("win", [48, 128, 1024])
    ident_d = din("ident", [128, 128])
    rgbd_d = din("rgbd", [128, 32, 128])
    mlbd_d = din("mlbd", [128, 24, 128])
    wgt_d = din("wgt", [128, 24, 16])
    gbias_d = din("gbias", [128, 2, 16])
    tri_d = din("tri", [128, 2, 128])
    wbrg_d = din("wbrg", [8, 128, 1024])
    wbml_d = din("wbml", [8, 128, 1024])
    wout_d = din("wout", [128, 8, 1024])
    wffi_d = din("wffi", [44, 128, 1024])
    wffo_d = din("wffo", [128, 22, 1024])
    fgrow_d = din("fgrow", [128, 1024])
    SPQ_d = nc.dram_tensor("SPQ", [17, 4, 128, 8, 256], BF16).ap()
    SPX_d = nc.dram_tensor("SPX", [16, 128, 8, 256], F32).ap()
    X1_d = nc.dram_tensor("X1", [NLAT, 1024], F32).ap()
    H2_d = nc.dram_tensor("H2", [16, 128, 8, 256], BF16).ap()
    HF_d = nc.dram_tensor("HF", [32, 128, 1024], F32).ap()
    if "yml" in debug:
        YML_d = nc.dram_tensor("dbg_yml", [8, 128, NLAT], BF16, kind="ExternalOutput").ap()
    else:
        YML_d = nc.dram_tensor("YML", [8, 128, NLAT], BF16).ap()
    YRG_d = nc.dram_tensor("YRG", [8, 128, NLAT], BF16).ap()
    out_d = nc.dram_tensor("out", [NLAT, 1024], F32, kind="ExternalOutput").ap()
    dbg_d = {}

    with ExitStack() as gst:
        P = Prog(nc, gst)
        galloc = lambda name, shape, dt: gst.enter_context(nc.sbuf_tensor(name, list(shape), dt))

        def dump(name, ap, trk, shape, dt=F32):
            if name not in debug:
                return
            d = nc.dram_tensor("dbg_" + name, list(shape), dt, kind="ExternalOutput").ap()
            dbg_d[name] = d
            P.out_toks.append(P.dma("sp", lambda e: e.dma_start(out=d, in_=ap), reads=[trk], semtrk=Trk("dbg" + name)))

        t_uT = [Trk("uT%d_%d" % (i // 2, i % 2)) for i in range(68)]
        FV = galloc("FV", [128, NV], F32); t_FV = Trk("FV")
        MODT = galloc("MODT", [128, 96], F32); t_MODT = Trk("MODT")
        A1 = galloc("A1", [128, 16], F32); t_A1 = Trk("A1")
        A2 = galloc("A2", [128, 8], F32); t_A2 = Trk("A2")
        GROW = galloc("GROW", [128, 2, 1024], F32); t_GROW = Trk("GROW")
        KD = galloc("KD", [128, 16], F32); t_KD = Trk("KD")
        IDB = galloc("IDB", [128, 128], BF16); t_IDB = Trk("IDB")
        IDF = galloc("IDF", [128, 128], F32); t_IDF = Trk("IDF")
        t_YRG = [Trk("YRG%d" % c) for c in range(8)]
        ust = ExitStack()
        uT = ust.enter_context(nc.sbuf_tensor("uT", [128, 8, NT], BF16))

        def fvc(name, i=0, n=1):
            o, w = FVCOLS[name]
            return FV[:, o + i:o + i + n]

        with ExitStack() as st:
            sb = lambda name, shape, dt: st.enter_context(nc.sbuf_tensor(name, list(shape), dt))
            ps = lambda name, shape, dt: st.enter_context(nc.psum_tensor(name, list(shape), dt))
            CV = sb("CV", [128, 16], F32); t_CV = Trk("CV")
            S2 = sb("S2", [128, 16], F32); t_S2 = Trk("S2")
            SREP = sb("SREP", [128, 8, 128], F32); t_SREP = Trk("SREP")
            BM2 = sb("BM2", [128, 96], F32); t_BM2 = Trk("BM2")
            BMG = sb("BMG", [128, 2048], F32); t_BMG = Trk("BMG")
            TMPA = sb("TMPA", [128, 16], F32); t_TMPA = Trk("TMPA")
            TMPB = sb("TMPB", [128, 16], F32); t_TMPB = Trk("TMPB")
            WM = Rot(sb, "WM", 3, [128, 1024], F32)
            WGm = sb("WGm", [128, 8, 1024], F32); t_WGm = Trk("WGm")
            MODP = ps("MODP", [128, 512], F32); t_MODP = Trk("PS_MODP")
            GRP = ps("GRP", [128, 1024], F32); t_GRP = Trk("PS_GRP")

            P.dma("sp", lambda e: e.dma_start(out=CV[:], in_=cv_d), writes=[t_CV])
            P.dma("sp", lambda e: e.dma_start(out=FV[:], in_=fv_d), writes=[t_FV])
            P.dma("sp", lambda e: e.dma_start(out=BM2[:], in_=bmod2_d), writes=[t_BM2])
            P.dma("sp", lambda e: e.dma_start(out=BMG[:], in_=bmodg_d), writes=[t_BMG])
            P.dma("sp", lambda e: e.dma_start(out=IDF[:], in_=ident_d), writes=[t_IDF])
            P.dma("pool", lambda e: e.dma_start(out=IDB[:], in_=ident_d), writes=[t_IDB])
            P.op("act", lambda e: e.activation(out=S2[:], in_=CV[:], func=AF.Silu), reads=[t_CV], writes=[t_S2])
            for kc in range(8):
                P.op("dve", (lambda kc: lambda e: e.tensor_copy(
                    out=SREP[:, kc, :], in_=S2[:, 2 * kc:2 * kc + 1].to_broadcast([128, 128])))(kc),
                    reads=[t_S2], writes=[t_SREP])
            for n in range(48):
                wm, t_wm = WM.next()
                P.dma("sp", (lambda wm, n: lambda e: e.dma_start(out=wm[:], in_=wmod_d[n]))(wm, n), writes=[t_wm])
                for kc in range(8):
                    P.op("pe", (lambda wm, n, kc: lambda e: e.matmul(
                        MODP[:, 2 * n:2 * n + 2], lhsT=wm[:, kc * 128:(kc + 1) * 128], rhs=S2[:, 2 * kc:2 * kc + 2],
                        start=(kc == 0), stop=(kc == 7)))(wm, n, kc), reads=[t_wm, t_S2], writes=[t_MODP])
            P.op("dve", lambda e: e.tensor_tensor(out=MODT[:], in0=MODP[:, 0:96], in1=BM2[:], op=ALU.add),
                 reads=[t_MODP, t_BM2], writes=[t_MODT])
            P.op("dve", lambda e: e.tensor_scalar_add(out=TMPA[:], in0=MODT[:, 16:32], scalar1=1.0),
                 reads=[t_MODT], writes=[t_TMPA])
            for j in range(2):
                P.op("dve", (lambda j: lambda e: e.tensor_tensor(
                    out=A1[:, j:16:2], in0=TMPA[:, j:16:2], in1=fvc("n1g", 0, 8), op=ALU.mult))(j),
                    reads=[t_TMPA, t_FV], writes=[t_A1])
            P.op("dve", lambda e: e.tensor_scalar_add(out=TMPB[:, 0:8], in0=MODT[:, 64:80:2], scalar1=1.0),
                 reads=[t_MODT], writes=[t_TMPB])
            P.op("dve", lambda e: e.tensor_tensor(out=A2[:], in0=TMPB[:, 0:8], in1=fvc("n2g", 0, 8), op=ALU.mult),
                 reads=[t_TMPB, t_FV], writes=[t_A2])
            for g in range(2):
                P.dma("sp", (lambda g: lambda e: e.dma_start(out=WGm[:], in_=wmodg_d[g]))(g), writes=[t_WGm])
                for half in range(2):
                    for kc in range(8):
                        P.op("pe", (lambda half, kc: lambda e: e.matmul(
                            GRP[:, half * 512:(half + 1) * 512], lhsT=SREP[:, kc, :],
                            rhs=WGm[:, kc, half * 512:(half + 1) * 512], start=(kc == 0), stop=(kc == 7)))(half, kc),
                            reads=[t_SREP, t_WGm], writes=[t_GRP])
                P.op("dve", (lambda g: lambda e: e.tensor_tensor(
                    out=GROW[:, g, :], in0=GRP[:], in1=BMG[:, g * 1024:(g + 1) * 1024], op=ALU.add))(g),
                    reads=[t_GRP, t_BMG], writes=[t_GROW])
            P.op("act", lambda e: e.activation(out=TMPA[:], in_=fvc("rglam", 0, 16), func=AF.Exp, scale=-1.0),
                 reads=[t_FV], writes=[t_TMPA])
            P.op("act", lambda e: e.activation(out=TMPB[:], in_=TMPA[:], func=AF.Ln, bias=1.0),
                 reads=[t_TMPA], writes=[t_TMPB])
            P.op("dve", lambda e: e.tensor_scalar(out=KD[:], in0=TMPB[:], scalar1=-8.0, scalar2=None, op0=ALU.mult),
                 reads=[t_TMPB], writes=[t_KD])
            dump("modT", MODT[:], t_MODT, [128, 96])
            dump("grow", GROW[:], t_GROW, [128, 2, 1024])
            dump("kd", KD[:], t_KD, [128, 16])
            P.barrier()
            P.emit()

        with ExitStack() as st:
            sb = lambda name, shape, dt: st.enter_context(nc.sbuf_tensor(name, list(shape), dt))
            ps = lambda name, shape, dt: st.enter_context(nc.psum_tensor(name, list(shape), dt))
            XT = Rot(sb, "XT", 3, [128, 1024], F32)
            XN = Rot(sb, "XN", 2, [128, 1024], BF16)
            TP = Rot(ps, "PS_TP", 2, [128, 8, 128], BF16)
            TPb = Rot(ps, "PS_TPb", 2, [128, 8, 128], BF16)
            ST = sb("ST", [128, 34, 4], F32)
            JKA = sb("JKA", [128, 1024], BF16); t_JKA = Trk("JKA")
            stage1 = {}

            def emit_stats(i):
                t_st = Trk("st%d" % i)
                xt, t_xt = XT.next()
                src = ctx_d[i * 128:(i + 1) * 128, :] if i < 2 else x_d[(i - 2) * 128:(i - 1) * 128, :]
                P.dma("sp", (lambda xt, src: lambda e: e.dma_start(out=xt[:], in_=src))(xt, src), writes=[t_xt])
                P.op("act", (lambda xt, i: lambda e: e.activation(
                    out=JKA[:], in_=xt[:], func=AF.Square, accum_out=ST[:, i, 0:1]))(xt, i),
                    reads=[t_xt], writes=[t_JKA, t_st])
                P.op("act", (lambda i: lambda e: e.activation(
                    out=ST[:, i, 1:2], in_=ST[:, i, 0:1], func=AF.Sqrt, scale=1.0 / 1024.0, bias=EPS))(i),
                    reads=[t_st], writes=[t_st])
                P.op("dve", (lambda i: lambda e: e.reciprocal(out=ST[:, i, 2:3], in_=ST[:, i, 1:2]))(i),
                     reads=[t_st], writes=[t_st])
                stage1[i] = (xt, t_xt, t_st)

            emit_stats(0)
            for i in range(34):
                if i + 1 < 34:
                    emit_stats(i + 1)
                xt, t_xt, t_st = stage1.pop(i)
                xn, t_xn = XN.next()
                tpa, t_tpa = TP.next()
                tpb, t_tpb = TPb.next()
                j = 1 if i < 2 else 0
                P.op("act", (lambda xt, xn, i: lambda e: e.activation(
                    out=xn[:], in_=xt[:], func=AF.Copy, scale=ST[:, i, 2:3]))(xt, xn, i),
                    reads=[t_xt, t_st], writes=[t_xn])
                for c in range(8):
                    tp, t_tp = (tpa, t_tpa) if c < 4 else (tpb, t_tpb)
                    P.op("pe", (lambda xn, tp, c: lambda e: e.transpose(
                        out=tp[:, c, :], in_=xn[:, c * 128:(c + 1) * 128], identity=IDB[:]))(xn, tp, c),
                        reads=[t_xn, t_IDB], writes=[t_tp])
                for c in (0, 4, 1, 5, 2, 6, 3, 7):
                    tp, t_tp = (tpa, t_tpa) if c < 4 else (tpb, t_tpb)
                    if i < 2:
                        dst = uT[:, c, i * 128:(i + 1) * 128]
                        src_tp = tp[:, c, :]
                    else:
                        r0 = 2 * (i - 2)
                        dst = uT[:, c, 256:NT].rearrange("p (j r) -> p r j", r=64)[:, r0:r0 + 2, :]
                        src_tp = tp[:, c, :].rearrange("p (r j) -> p r j", j=64)
                    if c < 4:
                        P.op("dve", (lambda tp, c, dst, j: lambda e: e.tensor_scalar(
                            out=dst, in0=tp, scalar1=A1[:, 2 * c + j:2 * c + j + 1],
                            scalar2=MODT[:, 2 * c + j:2 * c + j + 1], op0=ALU.mult, op1=ALU.add))(src_tp, c, dst, j),
                            reads=[t_tp, t_A1, t_MODT], writes=[t_uT[2 * i + (0 if c < 4 else 1)]])
                    else:
                        P.op("act", (lambda tp, c, dst, j: lambda e: e.activation(
                            out=dst, in_=tp, func=AF.Identity, scale=A1[:, 2 * c + j:2 * c + j + 1],
                            bias=MODT[:, 2 * c + j:2 * c + j + 1]))(src_tp, c, dst, j),
                            reads=[t_tp, t_A1, t_MODT], writes=[t_uT[2 * i + (0 if c < 4 else 1)]])
            if "uT" in debug:
                UD = sb("UD", [128, 8, 512], F32); t_UD = Trk("UD")
                P.op("dve", lambda e: e.tensor_copy(out=UD[:], in_=uT[:, :, 128:640]), reads=t_uT[2:10], writes=[t_UD])
                dump("uT", UD[:], t_UD, [128, 8, 512])
            P.barrier()
            P.emit()


        CT0, LT0, PEND = 2, 261, 4357
        with ExitStack() as st:
            sb = lambda name, shape, dt: st.enter_context(nc.sbuf_tensor(name, list(shape), dt))
            ps = lambda name, shape, dt: st.enter_context(nc.psum_tensor(name, list(shape), dt))
            RGBD = sb("RGBD", [128, 32, 128], BF16); t_RGBD = Trk("RGBD")
            P.dma("pool", lambda e: e.dma_start(out=RGBD[:], in_=rgbd_d, max_dma_last_dim=4096), writes=[t_RGBD])
            RX = sb("RX", [128, 4360], F32); t_RX = Trk("RX")
            XC = sb("XC", [128, 4360], F32); t_XC = Trk("XC")
            XCB = sb("XCB", [128, 4360], BF16); t_XCB = Trk("XCB")
            TMP = sb("TMP", [128, 2180], F32); t_TMP = Trk("TMP")
            BB = [sb("B0", [128, 4360], F32), sb("B1", [128, 4360], F32)]; t_BB = [Trk("B0"), Trk("B1")]
            WR = Rot(sb, "WR", 4, [128, 8, 128], BF16)
            PJ = Rot(ps, "PS_PJ", 3, [128, 512], F32)
            GA = Rot(ps, "PS_GA", 4, [128, 512], F32)
            GL = Rot(sb, "GL", 2, [128, 512], F32)
            TS = Rot(sb, "TS", 2, [128, 512], F32)
            YS = Rot(sb, "YS", 1, [128, NLAT], BF16)
            for (a, b) in ((0, 2), (258, 261), (4357, 4360)):
                P.op("dve", (lambda a, b: lambda e: e.memset(RX[:, a:b], 0.0))(a, b), writes=[t_RX])
            for d in range(2):
                P.op("pool", (lambda d: lambda e: e.memset(BB[d][:, 256:264], 0.0))(d), writes=[t_BB[d]])
            blocks = [(0, 256, CT0)] + [(256 + 512 * b, 512, LT0 + 512 * b) for b in range(8)]

            def ut_trks(t0, n):
                return t_uT[2 * (t0 // 128):2 * ((t0 + n - 1) // 128 + 1)]

            def ut_nat(kc, t0, n):
                if t0 < 256:
                    return uT[:, kc, t0:t0 + n]
                r0 = (t0 - 256) // 64
                return uT[:, kc, 256:NT].rearrange("p (j r) -> p r j", r=64)[:, r0:r0 + n // 64, :]

            def rev(ap2d):
                n = ap2d.shape[1]
                return bass.AP(ap2d.tensor, ap2d.offset + (n - 1), [list(ap2d.ap[0]), [-1, n]])

            def load_wr(c):
                wrx, t_wrx = WR.next()
                wrg, t_wrg = WR.next()
                P.dma("pool", (lambda w, c: lambda e: e.dma_start(
                    out=w[:], in_=win_d[c].rearrange("p (k j) -> p k j", j=128)))(wrx, c), writes=[t_wrx])
                P.dma("pool", (lambda w, c: lambda e: e.dma_start(
                    out=w[:], in_=win_d[8 + c].rearrange("p (k j) -> p k j", j=128)))(wrg, c), writes=[t_wrg])
                return wrx, t_wrx, wrg, t_wrg

            wr_next = load_wr(0)
            for c in range(8):
                wrx, t_wrx, wrg, t_wrg = wr_next
                if c + 1 < 8:
                    wr_next = load_wr(c + 1)
                if c > 0:
                    P.op("dve", lambda e: e.memset(RX[:, 258:261], 0.0), writes=[t_RX])
                pj, t_pj = PJ.next()
                for kc in range(8):
                    P.op("pe", (lambda pj, wrx, kc: lambda e: e.matmul(
                        pj[:, 0:256], lhsT=wrx[:, kc, :], rhs=uT[:, kc, 0:256], start=(kc == 0), stop=(kc == 7)))(
                        pj, wrx, kc), reads=[t_wrx] + t_uT[0:4], writes=[t_pj])
                P.op("act", (lambda pj: lambda e: e.activation(
                    out=RX[:, CT0:CT0 + 256], in_=pj[:, 0:256], func=AF.Copy))(pj), reads=[t_pj], writes=[t_RX])
                for b in range(8):
                    pj, t_pj = PJ.next()
                    for kc in range(8):
                        P.op("pe", (lambda pj, wrx, kc, b: lambda e: e.matmul(
                            pj[:], lhsT=wrx[:, kc, :], rhs=uT[:, kc, 256 + b * 512:256 + (b + 1) * 512], start=(kc == 0), stop=(kc == 7)))(
                            pj, wrx, kc, b), reads=[t_wrx] + t_uT[4:68], writes=[t_pj])
                    P.op("act", (lambda pj, b: lambda e: e.activation(
                        out=RX[:, LT0:LT0 + 4096].rearrange("p (r j) -> p j r", j=64)[:, 8 * b:8 * b + 8, :],
                        in_=pj[:].rearrange("p (j r) -> p j r", r=64), func=AF.Copy))(pj, b), reads=[t_pj], writes=[t_RX])
                L = PEND - 2
                P.op("dve", (lambda c: lambda e: e.tensor_scalar(
                    out=XC[:, 2:PEND], in0=RX[:, 0:L], scalar1=fvc("rgcw", c * 4), scalar2=fvc("rgcb", c),
                    op0=ALU.mult, op1=ALU.add))(c), reads=[t_RX, t_FV], writes=[t_XC])
                for j in range(1, 4):
                    P.op("dve", (lambda c, j: lambda e: e.scalar_tensor_tensor(
                        out=XC[:, 2:PEND], in0=RX[:, j:j + L], scalar=fvc("rgcw", c * 4 + j), in1=XC[:, 2:PEND],
                        op0=ALU.mult, op1=ALU.add))(c, j), reads=[t_RX, t_FV, t_XC], writes=[t_XC])
                P.op("act", lambda e: e.activation(out=XCB[:, 2:PEND], in_=XC[:, 2:PEND], func=AF.Copy),
                     reads=[t_XC], writes=[t_XCB])
                if c == 3:
                    dump("xc", XC[:], t_XC, [128, 4360])
                for d in range(2):
                    Bd, t_Bd = BB[d], t_BB[d]
                    for (t0, n, pos) in blocks:
                        ga, t_ga = GA.next()
                        gx, t_gx = GA.next()
                        P.op("pe", (lambda ga, d, c, n, pos: lambda e: e.matmul(
                            ga[:, 0:n], lhsT=RGBD[:, (d * 2) * 8 + c, :], rhs=XCB[:, pos:pos + n], start=True, stop=True))(
                            ga, d, c, n, pos), reads=[t_RGBD, t_XCB], writes=[t_ga])
                        P.op("pe", (lambda gx, d, c, n, pos: lambda e: e.matmul(
                            gx[:, 0:n], lhsT=RGBD[:, (d * 2 + 1) * 8 + c, :], rhs=XCB[:, pos:pos + n], start=True, stop=True))(
                            gx, d, c, n, pos), reads=[t_RGBD, t_XCB], writes=[t_gx])
                        P.op("act", (lambda ga, d, c, n, pos: lambda e: e.activation(
                            out=RX[:, pos:pos + n], in_=ga[:, 0:n], func=AF.Sigmoid, bias=fvc("rgba", d * 8 + c)))(
                            ga, d, c, n, pos), reads=[t_ga, t_FV], writes=[t_RX])
                        P.op("act", (lambda gx, Bd, d, c, n, pos: lambda e: e.activation(
                            out=Bd[:, pos:pos + n], in_=gx[:, 0:n], func=AF.Sigmoid, bias=fvc("rgbx", d * 8 + c)))(
                            gx, Bd, d, c, n, pos), reads=[t_gx, t_FV], writes=[t_Bd])
                    P.op("act", (lambda d, c: lambda e: e.activation(
                        out=RX[:, 2:PEND], in_=RX[:, 2:PEND], func=AF.Exp, scale=KD[:, d * 8 + c:d * 8 + c + 1]))(d, c),
                        reads=[t_RX, t_KD], writes=[t_RX])
                    P.op("dve", (lambda Bd: lambda e: e.tensor_tensor(
                        out=Bd[:, 2:PEND], in0=Bd[:, 2:PEND], in1=XC[:, 2:PEND], op=ALU.mult))(Bd),
                        reads=[t_Bd, t_XC], writes=[t_Bd])
                    for (ra, rb) in ((2, 2180), (2180, PEND)):
                        P.op("act", (lambda ra, rb: lambda e: e.activation(
                            out=TMP[:, 0:rb - ra], in_=RX[:, ra:rb], func=AF.Square))(ra, rb), reads=[t_RX], writes=[t_TMP])
                        P.op("act", (lambda ra, rb: lambda e: e.activation(
                            out=TMP[:, 0:rb - ra], in_=TMP[:, 0:rb - ra], func=AF.Sqrt, scale=-1.0, bias=1.0))(ra, rb),
                            reads=[t_TMP], writes=[t_TMP])
                        P.op("dve", (lambda Bd, ra, rb: lambda e: e.tensor_tensor(
                            out=Bd[:, ra:rb], in0=Bd[:, ra:rb], in1=TMP[:, 0:rb - ra], op=ALU.mult))(Bd, ra, rb),
                            reads=[t_Bd, t_TMP], writes=[t_Bd])
                    f_ = (lambda ap: ap) if d == 0 else rev
                    c0, c1 = CT0, CT0 + 256
                    l0, l1 = LT0, LT0 + 4096
                    P.op("dve", (lambda Bd, f_: lambda e: e.tensor_tensor_scan(
                        out=f_(Bd[:, c0:c1]), data0=f_(RX[:, c0:c1]), data1=f_(Bd[:, c0:c1]), initial=0.0,
                        op0=ALU.mult, op1=ALU.add))(Bd, f_), reads=[t_RX, t_Bd], writes=[t_Bd])
                    ini = (c1 - 1) if d == 0 else c0
                    P.op("dve", (lambda Bd, f_, ini: lambda e: e.tensor_tensor_scan(
                        out=f_(Bd[:, l0:l1]), data0=f_(RX[:, l0:l1]), data1=f_(Bd[:, l0:l1]), initial=Bd[:, ini:ini + 1],
                        op0=ALU.mult, op1=ALU.add))(Bd, f_, ini), reads=[t_RX, t_Bd], writes=[t_Bd])
                ys, t_ys = YS.next()
                for b in range(8):
                    pj, t_pj = PJ.next()
                    gl, t_gl = GL.next()
                    ts, t_ts = TS.next()
                    for kc in range(8):
                        P.op("pe", (lambda pj, wrg, kc, b: lambda e: e.matmul(
                            pj[:], lhsT=wrg[:, kc, :], rhs=uT[:, kc, 256 + b * 512:256 + (b + 1) * 512], start=(kc == 0), stop=(kc == 7)))(
                            pj, wrg, kc, b), reads=[t_wrg] + t_uT[4:68], writes=[t_pj])
                    P.op("act", (lambda pj, gl: lambda e: e.activation(out=gl[:], in_=pj[:], func=AF.Gelu))(pj, gl),
                         reads=[t_pj], writes=[t_gl])
                    P.op("dve", (lambda ts, b: lambda e: e.tensor_tensor(
                        out=ts[:].rearrange("p (j r) -> p j r", r=64),
                        in0=BB[0][:, LT0:LT0 + 4096].rearrange("p (r j) -> p j r", j=64)[:, 8 * b:8 * b + 8, :],
                        in1=BB[1][:, LT0:LT0 + 4096].rearrange("p (r j) -> p j r", j=64)[:, 8 * b:8 * b + 8, :], op=ALU.add))(ts, b),
                        reads=t_BB, writes=[t_ts])
                    P.op("dve", (lambda ys, ts, gl, b: lambda e: e.tensor_tensor(
                        out=ys[:, b * 512:(b + 1) * 512], in0=ts[:], in1=gl[:], op=ALU.mult))(ys, ts, gl, b),
                        reads=[t_ts, t_gl], writes=[t_ys])
                P.dma("sp", (lambda ys, c: lambda e: e.dma_start(out=YRG_d[c], in_=ys[:]))(ys, c), reads=[t_ys], writes=[t_YRG[c]],
                      semtrk=t_ys)
                if c == 3:
                    if "hrg" in debug:
                        dump("hrg", HD[:], t_HD, [128, NLAT])
                    dump("yrg", ys[:], t_ys, [128, NLAT], BF16)
            P.barrier()
            P.emit()
        if stop_after == "rg":
            P.wait_all("sp", P.out_toks)
            P.barrier()
            P.emit()
            ust.close()
            return nc, dbg_d

        t_HF = [Trk("HF%d" % i) for i in range(32)]
        t_YML = [Trk("YML%d" % i) for i in range(16)]
        t_SPQ = [[Trk("SPQ%d_%d" % (g, i)) for i in range(4)] for g in range(17)]
        t_SPX = [Trk("SPX%d" % g) for g in range(16)]
        t_spd = [Trk("spd%d" % i) for i in range(5)]
        with ExitStack() as st:
            sb = lambda name, shape, dt: st.enter_context(nc.sbuf_tensor(name, list(shape), dt))
            ps = lambda name, shape, dt: st.enter_context(nc.psum_tensor(name, list(shape), dt))
            MLBD = sb("MLBD", [128, 24, 128], BF16); t_MLBD = Trk("MLBD")
            WGT = sb("WGT", [128, 24, 16], BF16); t_WGT = Trk("WGT")
            GBI = sb("GBI", [128, 2, 16], F32); t_GBI = Trk("GBI")
            TRI = sb("TRI", [128, 2, 128], F32); t_TRI = Trk("TRI")
            ONES = sb("ONES", [128, 128], F32); t_ONES = Trk("ONES")
            P.dma("pool", lambda e: e.dma_start(out=MLBD[:], in_=mlbd_d, max_dma_last_dim=4096), writes=[t_MLBD])
            P.dma("pool", lambda e: e.dma_start(out=WGT[:], in_=wgt_d), writes=[t_WGT])
            P.dma("sp", lambda e: e.dma_start(out=GBI[:], in_=gbias_d), writes=[t_GBI])
            P.dma("sp", lambda e: e.dma_start(out=TRI[:], in_=tri_d), writes=[t_TRI])
            P.op("dve", lambda e: e.memset(ONES[:], 1.0), writes=[t_ONES])
            B_PM = ps("B_PM", [128, 512], F32); t_PM = Trk("PS_PM")
            B_QK = ps("B_QK", [128, 512], F32); t_PQ = Trk("PS_PQ")
            B_VO = ps("B_VO", [128, 512], F32); t_PV = Trk("PS_PV")
            B_GP = ps("B_GP", [128, 512], F32); t_GP = Trk("PS_GP")
            B_N = ps("B_N", [128, 4, 512], F32); t_N = [Trk("PS_N%d" % i) for i in range(4)]
            BU = [B_PM, B_QK, B_VO, B_GP]; t_BU = [t_PM, t_PQ, t_PV, t_GP]
            VT = Rot(sb, "VT", 2, [128, 256], BF16)
            XM = sb("XM", [128, 8, 256], F32); t_XM = [Trk("XM%d" % c) for c in range(8)]
            XMB = sb("XMB", [128, 8, 256], BF16); t_XMB = [Trk("XMB%d" % c) for c in range(8)]
            UMB = sb("UMB", [128, 8, 256], BF16); t_UMB = [Trk("UMB%d" % c) for c in range(8)]
            QT = sb("QT", [128, 8, 256], BF16); t_QT = [Trk("QT%d" % c) for c in range(8)]
            KT = sb("KT", [128, 8, 256], BF16); t_KT = [Trk("KT%d" % c) for c in range(8)]
            GF = sb("GF", [8, 256], F32); t_GF = Trk("GF")
            GG = sb("GG", [128, 16], F32); t_GG = Trk("GG")
            GE = sb("GE", [128, 8], F32); t_GE = Trk("GE")
            GLn = sb("GLn", [128, 8], F32); t_GLn = Trk("GLn")
            GT2 = sb("GT2", [128, 8], F32); t_GT2 = Trk("GT2")
            EB = sb("EB", [128, 8], F32); t_EB = Trk("EB")
            WS = sb("WS", [128, 8], F32); t_WS = Trk("WS")
            EBL = sb("EBL", [128, 8], F32); t_EBL = Trk("EBL")
            DS = sb("DS", [128, 4, 2, 257], F32); t_DS = [Trk("DS%d" % h) for h in range(4)]
            DB = sb("DB", [128, 4, 2, 258], BF16); t_DB = [Trk("DB%d" % h) for h in range(4)]
            VX = sb("VX", [128, 2, 4, 258], BF16); t_VX = [Trk("VX%d" % i) for i in range(2)]
            KTM = sb("KTM", [128, 2, 1024], BF16); t_KTM = [Trk("KTM%d" % i) for i in range(2)]
            STt = sb("STt", [128, 2, 4, 128], BF16); t_STt = [Trk("STt%d" % i) for i in range(2)]
            E1 = sb("E1", [128, 8, 4], F32); t_E1 = Trk("E1")
            HH = Rot(sb, "HH", 1, [128, 1024], F32)

            def grp_rhs(kc, g):
                if g == 0:
                    return uT[:, kc, 0:256]
                gi = g - 1
                return uT[:, kc, 256 + gi * 256:256 + (gi + 1) * 256]

            def grp_trks(g):
                return t_uT[0:4] if g == 0 else t_uT[4:68]

            def gates_post(d):
                P.op("act", lambda e: e.activation(out=GF[:], in_=B_GP[0:8, 0:256], func=AF.Copy), reads=[t_GP], writes=[t_GF])
                for ch in range(2):
                    P.op("pe", (lambda ch: lambda e: e.transpose(
                        out=B_GP[:, 256 + ch * 8:256 + ch * 8 + 8], in_=GF[0:8, ch * 128:(ch + 1) * 128], identity=IDF[0:8, 0:8]))(ch),
                        reads=[t_GF, t_IDF], writes=[t_GP])
                P.op("dve", (lambda d: lambda e: e.tensor_tensor(
                    out=GG[:], in0=B_GP[:, 256:272], in1=GBI[:, d, :], op=ALU.add))(d), reads=[t_GP, t_GBI], writes=[t_GG])
                GGv = GG[:].rearrange("t (c k) -> t c k", k=8)
                P.op("act", lambda e: e.activation(
                    out=GE[:].rearrange("t (c h) -> t c h", h=4), in_=GGv[:, :, 4:8], func=AF.Exp, scale=-1.0),
                    reads=[t_GG], writes=[t_GE])
                P.op("act", lambda e: e.activation(out=GLn[:], in_=GE[:], func=AF.Ln, bias=1.0), reads=[t_GE], writes=[t_GLn])
                P.op("pe", (lambda d: lambda e: e.matmul(
                    B_GP[:, 288:296], lhsT=TRI[:, d, :], rhs=GLn[:], start=True, stop=True))(d),
                    reads=[t_TRI, t_GLn], writes=[t_GP])
                P.op("pe", lambda e: e.matmul(B_GP[:, 304:312], lhsT=ONES[:], rhs=GLn[:], start=True, stop=True),
                     reads=[t_ONES, t_GLn], writes=[t_GP])
                P.op("act", lambda e: e.activation(out=EB[:], in_=B_GP[:, 288:296], func=AF.Exp, scale=-1.0),
                     reads=[t_GP], writes=[t_EB])
                P.op("dve", lambda e: e.tensor_tensor(
                    out=GT2[:].rearrange("t (c h) -> t c h", h=4), in0=B_GP[:, 288:296].rearrange("t (c h) -> t c h", h=4),
                    in1=GGv[:, :, 0:4], op=ALU.add), reads=[t_GP, t_GG], writes=[t_GT2])
                P.op("act", lambda e: e.activation(out=WS[:], in_=GT2[:], func=AF.Exp), reads=[t_GT2], writes=[t_WS])
                P.op("act", lambda e: e.activation(out=EBL[:], in_=B_GP[:, 304:312], func=AF.Exp, scale=-1.0),
                     reads=[t_GP], writes=[t_EBL])
            def rec_group(g, d, QT, KT, XMB, UMB, t_QT, t_KT, t_XMB, t_UMB, XM, t_XM, hs):
                lat = g > 0
                gi = g - 1
                have_state = hs[0]
                for ch in range(2):
                    cols = slice(ch * 128, (ch + 1) * 128)
                    for c in range(8):
                        h, half = divmod(c, 2)
                        bk = h // 2
                        off = (h % 2) * 256 + half * 128
                        P.op("pe", (lambda c, bk, off, cols: lambda e: e.matmul(
                            B_N[:, bk, off:off + 128], lhsT=UMB[:, c, cols], rhs=MLBD[:, 16 + c, :], start=True, stop=True))(c, bk, off, cols),
                            reads=[t_UMB[c], t_MLBD], writes=[t_N[bk]])
                        P.op("pe", (lambda c, bk, off, cols: lambda e: e.matmul(
                            B_N[:, 2 + bk, off:off + 128], lhsT=XMB[:, c, cols], rhs=MLBD[:, 8 + c, :], start=True, stop=True))(c, bk, off, cols),
                            reads=[t_XMB[c], t_MLBD], writes=[t_N[2 + bk]])
                    for h in range(4):
                        bk = h // 2
                        off = (h % 2) * 256
                        P.op("dve", (lambda ch, h, bk, off: lambda e: e.tensor_scalar(
                            out=VX[:, ch, h, 0:256], in0=B_N[:, bk, off:off + 256], scalar1=WS[:, ch * 4 + h:ch * 4 + h + 1],
                            scalar2=None, op0=ALU.mult))(ch, h, bk, off), reads=[t_N[bk], t_WS], writes=[t_VX[ch]])
                    P.op("act", (lambda ch: lambda e: e.activation(
                        out=VX[:, ch, :, 256:257], in_=WS[:, ch * 4:ch * 4 + 4].unsqueeze(2), func=AF.Copy))(ch), reads=[t_WS], writes=[t_VX[ch]])
                    for bk in range(2):
                        P.op("act", (lambda ch, bk: lambda e: e.activation(
                            out=KTM[:, ch, bk * 512:(bk + 1) * 512], in_=B_N[:, 2 + bk, :], func=AF.Copy, scale=1.0 / 16.0))(ch, bk),
                            reads=[t_N[2 + bk]], writes=[t_KTM[ch]])
                    for h in range(4):
                        for half in range(2):
                            c = 2 * h + half
                            P.op("pe", (lambda c, h, half, cols: lambda e: e.matmul(
                                B_GP[:, h * 128:(h + 1) * 128], lhsT=KT[:, c, cols], rhs=QT[:, c, cols], start=(half == 0), stop=(half == 1)))(c, h, half, cols),
                                reads=[t_KT[c], t_QT[c]], writes=[t_GP])
                    P.op("dve", (lambda ch, d: lambda e: e.tensor_tensor(
                        out=STt[:, ch], in0=B_GP[:].rearrange("p (h t) -> p h t", t=128),
                        in1=TRI[:, d, :].unsqueeze(1).to_broadcast([128, 4, 128]), op=ALU.mult))(ch, d),
                        reads=[t_GP, t_TRI], writes=[t_STt[ch]])
                chs = (0, 1) if d == 0 else (1, 0)
                if d == 1 and lat:
                    yg, t_yg = YG.next()
                for ch in chs:
                    cols = slice(ch * 128, (ch + 1) * 128)
                    if d == 1 and lat:
                        for cq in (gi * 2 + ch, gi * 2 + ch - 1):
                            if cq >= 0 and cq not in hft_map:
                                hb, t_hb = HFt.next()
                                P.dma("sp", (lambda hb, cq: lambda e: e.dma_start(out=hb[:], in_=HF_d[cq]))(hb, cq),
                                      reads=[t_HF[cq]], writes=[t_hb])
                                hft_map[cq] = (hb, t_hb)
                    last_chunk = (d == 0 and g == 16 and ch == 1) or (d == 1 and g == 1 and ch == 0)
                    if not last_chunk:
                        for r in range(2):
                            for hh2 in range(2):
                                h = 2 * r + hh2
                                for half in range(2):
                                    bi_ = hh2 * 2 + half
                                    P.op("pe", (lambda ch, h, half, bi_: lambda e: e.matmul(
                                        BU[bi_][:, 0:257], lhsT=KTM[:, ch, h * 256 + half * 128:h * 256 + (half + 1) * 128],
                                        rhs=VX[:, ch, h, 0:257], start=True, stop=True))(ch, h, half, bi_),
                                        reads=[t_KTM[ch], t_VX[ch]], writes=[t_BU[bi_]])
                            for hh2 in range(2):
                                h = 2 * r + hh2
                                for half in range(2):
                                    bi_ = hh2 * 2 + half
                                    if have_state:
                                        P.op("dve", (lambda h, half, bi_: lambda e: e.tensor_tensor(
                                            out=DS[:, h, half, :], in0=DS[:, h, half, :], in1=BU[bi_][:, 0:257], op=ALU.add))(h, half, bi_),
                                            reads=[t_DS[h], t_BU[bi_]], writes=[t_DS[h]])
                                    else:
                                        P.op("dve", (lambda h, half, bi_: lambda e: e.tensor_copy(
                                            out=DS[:, h, half, :], in_=BU[bi_][:, 0:257]))(h, half, bi_),
                                            reads=[t_BU[bi_]], writes=[t_DS[h]])
                    if lat:
                        hh, t_hh = HH.next()
                        for h in range(4):
                            P.op("pe", (lambda ch, h, hs: lambda e: e.matmul(
                                B_N[:, h, 0:257], lhsT=STt[:, ch, h, :], rhs=VX[:, ch, h, 0:257], start=True, stop=(not hs)))(ch, h, have_state),
                                reads=[t_STt[ch], t_VX[ch]], writes=[t_N[h]])
                            if have_state:
                                for half in range(2):
                                    c = 2 * h + half
                                    P.op("pe", (lambda c, h, half, cols: lambda e: e.matmul(
                                        B_N[:, h, 0:257], lhsT=QT[:, c, cols], rhs=DB[:, h, half, 0:257], start=False, stop=(half == 1)))(c, h, half, cols),
                                        reads=[t_QT[c], t_DB[h]], writes=[t_N[h]])
                        e0 = ch * 4
                        P.op("dve", (lambda ch: lambda e: e.tensor_tensor(
                            out=E1[:, 0:4, 0], in0=B_N[:, :, 256], in1=EB[:, ch * 4:ch * 4 + 4], op=ALU.mult))(ch),
                            reads=t_N + [t_EB], writes=[t_E1])
                        P.op("dve", lambda e: e.tensor_scalar(
                            out=E1[:, 0:4, 1], in0=E1[:, 0:4, 0], scalar1=-1.0, scalar2=1.0, op0=ALU.mult, op1=ALU.max),
                            reads=[t_E1], writes=[t_E1])
                        P.op("dve", lambda e: e.scalar_tensor_tensor(
                            out=E1[:, 0:4, 2], in0=E1[:, 0:4, 0], scalar=1.0, in1=E1[:, 0:4, 1], op0=ALU.max, op1=ALU.max),
                            reads=[t_E1], writes=[t_E1])
                        P.op("dve", lambda e: e.reciprocal(out=E1[:, 0:4, 3], in_=E1[:, 0:4, 2]), reads=[t_E1], writes=[t_E1])
                        P.op("dve", (lambda ch: lambda e: e.tensor_tensor(
                            out=E1[:, 4:8, 0], in0=E1[:, 0:4, 3], in1=EB[:, ch * 4:ch * 4 + 4], op=ALU.mult))(ch),
                            reads=[t_E1, t_EB], writes=[t_E1])
                        for h in range(4):
                            P.op("act", (lambda hh, h: lambda e: e.activation(
                                out=hh[:, h * 256:(h + 1) * 256], in_=B_N[:, h, 0:256], func=AF.Copy, scale=E1[:, 4 + h, 0:1]))(hh, h),
                                reads=[t_N[h], t_E1], writes=[t_hh])
                    if not last_chunk:
                        for h in range(4):
                            idx = ch * 4 + h
                            P.op("act", (lambda h, idx: lambda e: e.activation(
                                out=DB[:, h, :, 0:257], in_=DS[:, h, :, :], func=AF.Copy, scale=EBL[:, idx:idx + 1]))(h, idx),
                                reads=[t_DS[h], t_EBL], writes=[t_DB[h]])
                            P.op("dve", (lambda h, idx: lambda e: e.tensor_scalar(
                                out=DS[:, h, :, :], in0=DS[:, h, :, :], scalar1=EBL[:, idx:idx + 1], scalar2=None, op0=ALU.mult))(h, idx),
                                reads=[t_DS[h], t_EBL], writes=[t_DS[h]])
                    have_state = True; hs[0] = True
                    if not lat:
                        continue
                    cg = gi * 2 + ch
                    if d == 0:
                        P.dma("sp", (lambda hh, cg: lambda e: e.dma_start(out=HF_d[cg], in_=hh[:]))(hh, cg),
                              reads=[t_hh], writes=[t_HF[cg]], semtrk=t_hh)
                        continue
                    hft, t_hft = hft_map.pop(cg)
                    P.op("dve", (lambda hh, hft: lambda e: e.tensor_tensor(out=hh[:], in0=hh[:], in1=hft[:], op=ALU.add))(hh, hft),
                         reads=[t_hh, t_hft], writes=[t_hh])
                    for h in range(4):
                        P.op("dve", (lambda hh, h: lambda e: e.bn_stats(out=BS[:, h, :], in_=hh[:, h * 256:(h + 1) * 256]))(hh, h),
                             reads=[t_hh], writes=[t_BS])
                        P.op("dve", (lambda h: lambda e: e.bn_aggr(out=MV[:, h, :], in_=BS[:, h, :]))(h), reads=[t_BS], writes=[t_MV])
                    P.op("act", lambda e: e.activation(out=SD[:, 0:4], in_=MV[:, :, 1], func=AF.Sqrt, bias=EPS), reads=[t_MV], writes=[t_SD])
                    P.op("dve", lambda e: e.reciprocal(out=SD[:, 4:8], in_=SD[:, 0:4]), reads=[t_SD], writes=[t_SD])
                    for h in range(4):
                        P.op("dve", (lambda hh, h: lambda e: e.tensor_scalar(
                            out=HN[:, h * 256:(h + 1) * 256], in0=hh[:, h * 256:(h + 1) * 256], scalar1=MV[:, h, 0:1],
                            scalar2=SD[:, 4 + h:5 + h], op0=ALU.subtract, op1=ALU.mult))(hh, h),
                            reads=[t_hh, t_MV, t_SD], writes=[t_HN])
                    for c in range(8):
                        P.op("pe", (lambda c: lambda e: e.transpose(
                            out=B_N[:, c // 4, (c % 4) * 128:(c % 4 + 1) * 128], in_=HN[:, c * 128:(c + 1) * 128], identity=IDF[:]))(c),
                            reads=[t_HN, t_IDF], writes=[t_N[c // 4]])
                    o_, w_ = FVCOLS["mlng"]
                    for b2 in range(2):
                        P.op("dve", (lambda b2: lambda e: e.tensor_tensor(
                            out=Y1[:, 4 * b2:4 * b2 + 4, :], in0=B_N[:, b2, :].rearrange("p (c t) -> p c t", t=128),
                            in1=FV[:, o_ + 4 * b2:o_ + 4 * b2 + 4].unsqueeze(2).to_broadcast([128, 4, 128]), op=ALU.mult))(b2),
                            reads=[t_N[b2], t_FV], writes=[t_Y1])
                    P.op("dve", (lambda cols: lambda e: e.tensor_tensor(out=Y1[:], in0=Y1[:], in1=XM[:, :, cols], op=ALU.add))(cols),
                         reads=[t_Y1] + t_XM, writes=[t_Y1])
                    P.op("dve", (lambda yg, cols: lambda e: e.tensor_tensor(out=yg[:, :, cols], in0=Y1[:], in1=SIG[:, :, cols], op=ALU.mult))(yg, cols),
                         reads=[t_Y1] + t_SIG, writes=[t_yg])
                if d == 1 and lat:
                    P.dma("sp", (lambda yg, gi: lambda e: e.dma_start(
                        out=YML_d[:, :, gi * 256:(gi + 1) * 256].rearrange("c p t -> p c t"), in_=yg[:]))(yg, gi),
                        reads=[t_yg], writes=[t_YML[gi]], semtrk=t_yg)
            for d in range(2):
                with ExitStack() as st2:
                    sb2 = lambda name, shape, dt: st2.enter_context(nc.sbuf_tensor(name, list(shape), dt))
                    if d == 0:
                        WMX = sb2("WMX", [128, 8, 8, 128], BF16); t_WMX = [Trk("WMX%d" % c) for c in range(8)]
                        for c in range(8):
                            P.dma("pool", (lambda c: lambda e: e.dma_start(
                                out=WMX[:, c], in_=win_d[16 + c].rearrange("p (k j) -> p k j", j=128)))(c), writes=[t_WMX[c]])
                        UMF = Rot(sb2, "UMF", 2, [128, 260], F32)
                        HALO = sb2("HALO", [128, 8, 2], F32); t_HALO = [Trk("HALO%d" % c) for c in range(8)]
                        XCV = Rot(sb2, "XCV", 2, [128, 256], F32)
                    if d == 1:
                        WMO = sb2("WMO", [128, 8, 8, 128], BF16); t_WMO = [Trk("WMO%d" % c) for c in range(8)]
                        for c in range(8):
                            P.dma("pool", (lambda c: lambda e: e.dma_start(
                                out=WMO[:, c], in_=win_d[24 + c].rearrange("p (k j) -> p k j", j=128)))(c), writes=[t_WMO[c]])
                        SIG = sb2("SIG", [128, 8, 256], F32); t_SIG = [Trk("SIG%d" % c) for c in range(8)]
                        HFt = Rot(sb2, "HFt", 2, [128, 1024], F32)
                        hft_map = {}
                        HN = sb2("HN", [128, 1024], F32); t_HN = Trk("HN")
                        BS = sb2("BS", [128, 4, 6], F32); t_BS = Trk("BS")
                        MV = sb2("MV", [128, 4, 2], F32); t_MV = Trk("MV")
                        SD = sb2("SD", [128, 8], F32); t_SD = Trk("SD")
                        Y1 = sb2("Y1", [128, 8, 128], F32); t_Y1 = Trk("Y1")
                        YG = Rot(sb2, "YG", 2, [128, 8, 256], BF16)
                        GS1 = [sb2("QT1", [128, 8, 256], BF16), sb2("KT1", [128, 8, 256], BF16),
                               sb2("XMB1", [128, 8, 256], BF16), sb2("UMB1", [128, 8, 256], BF16)]
                        GSETS = [((QT, KT, XMB, UMB), [Trk("gs0_%d" % i) for i in range(4)]),
                                 (tuple(GS1), [Trk("gs1_%d" % i) for i in range(4)])]
                        t_XMl = Trk("XMl")
                    hs = [False]
                    order = [0] + (list(range(1, 17)) if d == 0 else list(range(16, 0, -1)))
                    for gpos, g in enumerate(order):
                        lat = g > 0
                        gi = g - 1
                        if d == 0:
                            import os as _os2
                            PB = [B_N[:, 0, :], B_N[:, 1, :]]
                            t_PB = [t_N[0], t_N[1]]
                            if _os2.environ.get('PBPM'):
                                PB = [B_PM[:], B_PM[:]]; t_PB = [t_PM, t_PM]
                            hi = 259 if (lat and gi <= 14) else 258
                            lo = 0 if (lat and gi >= 1) else 2

                            def emit_proj(c):
                                pb, t_pb = PB[c % 2], t_PB[c % 2]
                                for kc in range(8):
                                    P.op("pe", (lambda pb, c, kc, g: lambda e: e.matmul(
                                        pb[:, 2:258], lhsT=WMX[:, c, kc, :], rhs=grp_rhs(kc, g), start=(kc == 0), stop=(kc == 7)))(pb, c, kc, g),
                                        reads=[t_WMX[c]] + grp_trks(g), writes=[t_pb])
                                if hi == 259:
                                    b0 = 256 + (gi + 1) * 256
                                    for kc in range(8):
                                        P.op("pe", (lambda pb, c, kc, b0: lambda e: e.matmul(
                                            pb[:, 258:259], lhsT=WMX[:, c, kc, :], rhs=uT[:, kc, b0:b0 + 1], start=(kc == 0), stop=(kc == 7)))(pb, c, kc, b0),
                                            reads=[t_WMX[c]] + grp_trks(g), writes=[t_pb])

                            bufs_c = {}

                            def emit_evac_act(c):
                                pb, t_pb = PB[c % 2], t_PB[c % 2]
                                umf, t_umf = UMF.next()
                                xcv, t_xcv = XCV.next()
                                bufs_c[c] = (umf, t_umf, xcv, t_xcv)
                                P.op("act", (lambda umf, pb, hi: lambda e: e.activation(
                                    out=umf[:, 2:hi], in_=pb[:, 2:hi], func=AF.Copy))(umf, pb, hi), reads=[t_pb], writes=[t_umf])
                                P.op("act", (lambda c, pb: lambda e: e.activation(out=UMB[:, c, :], in_=pb[:, 2:258], func=AF.Copy))(c, pb),
                                     reads=[t_pb], writes=[t_UMB[c]])

                            def emit_conv(c):
                                umf, t_umf, xcv, t_xcv = bufs_c[c]
                                if lo == 0:
                                    P.op("dve", (lambda umf, c: lambda e: e.tensor_copy(out=umf[:, 0:2], in_=HALO[:, c, :]))(umf, c),
                                         reads=[t_HALO[c]], writes=[t_umf])
                                else:
                                    P.op("dve", (lambda umf: lambda e: e.memset(umf[:, 0:2], 0.0))(umf), writes=[t_umf])
                                if hi == 258:
                                    P.op("dve", (lambda umf: lambda e: e.memset(umf[:, 258:259], 0.0))(umf), writes=[t_umf])
                                if lat and gi <= 14:
                                    P.op("dve", (lambda umf, c: lambda e: e.tensor_copy(out=HALO[:, c, :], in_=umf[:, 256:258]))(umf, c),
                                         reads=[t_umf], writes=[t_HALO[c]])
                                P.op("dve", (lambda umf, xcv, c: lambda e: e.tensor_scalar(
                                    out=xcv[:], in0=umf[:, 0:256], scalar1=fvc("mlcw", c * 4), scalar2=fvc("mlcb", c),
                                    op0=ALU.mult, op1=ALU.add))(umf, xcv, c), reads=[t_umf, t_FV], writes=[t_xcv])
                                for j in range(1, 4):
                                    P.op("dve", (lambda umf, xcv, c, j: lambda e: e.scalar_tensor_tensor(
                                        out=xcv[:], in0=umf[:, j:j + 256], scalar=fvc("mlcw", c * 4 + j), in1=xcv[:],
                                        op0=ALU.mult, op1=ALU.add))(umf, xcv, c, j), reads=[t_umf, t_FV, t_xcv], writes=[t_xcv])
                                P.op("act", (lambda xcv, c: lambda e: e.activation(out=XM[:, c, :], in_=xcv[:], func=AF.Silu))(xcv, c),
                                     reads=[t_xcv], writes=[t_XM[c]])
                                P.op("act", (lambda xcv, c: lambda e: e.activation(out=XMB[:, c, :], in_=xcv[:], func=AF.Silu))(xcv, c),
                                     reads=[t_xcv], writes=[t_XMB[c]])

                            vts = {}

                            def emit_qkv(c):
                                vt, t_vt = VT.next()
                                vts[c] = (vt, t_vt)
                                P.op("pe", (lambda c: lambda e: e.matmul(
                                    B_QK[:, 0:256], lhsT=MLBD[:, c, :], rhs=XMB[:, c, :], start=True, stop=True))(c),
                                    reads=[t_MLBD, t_XMB[c]], writes=[t_PQ])
                                P.op("pe", (lambda c: lambda e: e.matmul(
                                    B_QK[:, 256:512], lhsT=MLBD[:, 8 + c, :], rhs=XMB[:, c, :], start=True, stop=True))(c),
                                    reads=[t_MLBD, t_XMB[c]], writes=[t_PQ])
                                P.op("pe", (lambda c: lambda e: e.matmul(
                                    B_VO[:, 0:256], lhsT=MLBD[:, 16 + c, :], rhs=UMB[:, c, :], start=True, stop=True))(c),
                                    reads=[t_MLBD, t_UMB[c]], writes=[t_PV])
                                P.op("act", (lambda c: lambda e: e.activation(out=QT[:, c, :], in_=B_QK[:, 0:256], func=AF.Copy))(c),
                                     reads=[t_PQ], writes=[t_QT[c]])
                                P.op("act", (lambda c: lambda e: e.activation(
                                    out=KT[:, c, :], in_=B_QK[:, 256:512], func=AF.Copy, scale=1.0 / 16.0))(c),
                                    reads=[t_PQ], writes=[t_KT[c]])
                                P.op("dve", (lambda vt: lambda e: e.tensor_copy(out=vt[:], in_=B_VO[:, 0:256]))(vt),
                                     reads=[t_PV], writes=[t_vt])

                            def emit_gates(c):
                                vt, t_vt = vts[c]
                                for ti, (src, t_src) in enumerate(((QT[:, c, :], t_QT[c]), (KT[:, c, :], t_KT[c]), (vt[:], t_vt))):
                                    P.op("pe", (lambda c, ti, src, d: lambda e: e.matmul(
                                        B_GP[0:8, 0:256], lhsT=WGT[:, ti * 8 + c, d * 8:(d + 1) * 8], rhs=src,
                                        start=(c == 0 and ti == 0), stop=(c == 7 and ti == 2)))(c, ti, src, d),
                                        reads=[t_WGT, t_src], writes=[t_GP])

                            if _os2.environ.get("NOPIPE"):
                                for c in range(8):
                                    emit_proj(c)
                                    emit_evac_act(c)
                                    emit_conv(c)
                                    emit_qkv(c)
                                    emit_gates(c)
                            else:
                                emit_proj(0)
                                emit_proj(1)
                                emit_evac_act(0)
                                for c in range(8):
                                    if c + 2 < 8:
                                        emit_proj(c + 2)
                                    if c + 1 < 8:
                                        emit_evac_act(c + 1)
                                    emit_conv(c)
                                    emit_qkv(c)
                                    if c > 0:
                                        emit_gates(c - 1)
                                emit_gates(7)
                            for wi_, (arr, trs) in enumerate(((QT, t_QT), (KT, t_KT), (XMB, t_XMB), (UMB, t_UMB))):
                                P.dma("sp", (lambda arr, g, wi_: lambda e: e.dma_start(out=SPQ_d[g, wi_], in_=arr[:]))(arr, g, wi_),
                                      reads=trs, writes=[t_SPQ[g][wi_]], semtrk=t_spd[wi_])
                            if lat:
                                P.dma("sp", (lambda gi: lambda e: e.dma_start(out=SPX_d[gi], in_=XM[:]))(gi),
                                      reads=t_XM, writes=[t_SPX[gi]], semtrk=t_spd[4])
                            gates_post(d)
                            rec_group(g, d, QT, KT, XMB, UMB, t_QT, t_KT, t_XMB, t_UMB, XM, t_XM, hs)
                            continue
                        def load_set(gq, si):
                            arrs, trs = GSETS[si]
                            for wi_ in range(4):
                                P.dma("sp", (lambda arrs, gq, wi_: lambda e: e.dma_start(out=arrs[wi_][:], in_=SPQ_d[gq, wi_]))(arrs, gq, wi_),
                                      reads=[t_SPQ[gq][wi_]], writes=[trs[wi_]])
                        if gpos == 0:
                            load_set(g, 0)
                        if gpos + 1 < len(order):
                            load_set(order[gpos + 1], (gpos + 1) % 2)
                        (QTg, KTg, XMBg, UMBg), trs = GSETS[gpos % 2]
                        if lat:
                            P.dma("sp", (lambda gi: lambda e: e.dma_start(out=XM[:], in_=SPX_d[gi]))(gi), reads=[t_SPX[gi]], writes=[t_XMl])
                        for c in range(8):
                            vt, t_vt = VT.next()
                            P.op("pe", (lambda c, UMBg: lambda e: e.matmul(
                                B_VO[:, 0:256], lhsT=MLBD[:, 16 + c, :], rhs=UMBg[:, c, :], start=True, stop=True))(c, UMBg),
                                reads=[t_MLBD, trs[3]], writes=[t_PV])
                            P.op("dve", (lambda vt: lambda e: e.tensor_copy(out=vt[:], in_=B_VO[:, 0:256]))(vt),
                                 reads=[t_PV], writes=[t_vt])
                            if lat:
                                for kc in range(8):
                                    P.op("pe", (lambda c, kc, g: lambda e: e.matmul(
                                        B_PM[:, 0:256], lhsT=WMO[:, c, kc, :], rhs=grp_rhs(kc, g), start=(kc == 0), stop=(kc == 7)))(c, kc, g),
                                        reads=[t_WMO[c]] + grp_trks(g), writes=[t_PM])
                                P.op("act", (lambda c: lambda e: e.activation(out=SIG[:, c, :], in_=B_PM[:, 0:256], func=AF.Sigmoid))(c),
                                     reads=[t_PM], writes=[t_SIG[c]])
                            for ti, (src, t_src) in enumerate(((QTg[:, c, :], trs[0]), (KTg[:, c, :], trs[1]), (vt[:], t_vt))):
                                P.op("pe", (lambda c, ti, src, d: lambda e: e.matmul(
                                    B_GP[0:8, 0:256], lhsT=WGT[:, ti * 8 + c, d * 8:(d + 1) * 8], rhs=src,
                                    start=(c == 0 and ti == 0), stop=(c == 7 and ti == 2)))(c, ti, src, d),
                                    reads=[t_WGT, t_src], writes=[t_GP])
                        if lat:
                            o2_, w2_ = FVCOLS["mlsk"]
                            P.op("dve", lambda e: e.tensor_tensor(
                                out=XM[:], in0=XM[:], in1=FV[:, o2_:o2_ + 8].unsqueeze(2).to_broadcast([128, 8, 256]), op=ALU.mult),
                                reads=[t_XMl, t_FV], writes=[t_XMl])
                        gates_post(d)
                        rec_group(g, d, QTg, KTg, XMBg, UMBg, [trs[0]] * 8, [trs[1]] * 8, [trs[2]] * 8, [trs[3]] * 8, XM, [t_XMl] * 8, hs)
                    P.barrier()
                    P.emit()
        if "yml" in debug:
            dbg_d["yml"] = YML_d
        if stop_after == "ml":
            P.wait_all("sp", P.out_toks)
            P.barrier()
            P.emit()
            ust.close()
            return nc, dbg_d

        x_cm = x_d.rearrange("(r j) d -> j r d", j=64)
        out_cm = out_d.rearrange("(r j) d -> j r d", j=64)
        t_X1 = [Trk("X1_%d" % i) for i in range(32)]
        t_H2 = [Trk("H2_%d" % i) for i in range(16)]
        with ExitStack() as st:
            sb = lambda name, shape, dt: st.enter_context(nc.sbuf_tensor(name, list(shape), dt))
            ps = lambda name, shape, dt: st.enter_context(nc.psum_tensor(name, list(shape), dt))
            WGR = sb("WGR", [128, 8, 8, 128], BF16); WGM = sb("WGM", [128, 8, 8, 128], BF16)
            WBR = sb("WBR", [128, 8, 8, 128], BF16); WBM = sb("WBM", [128, 8, 8, 128], BF16)
            WOUT = sb("WOUT", [128, 8, 1024], BF16)
            t_WGR = [Trk("WGR%d" % i) for i in range(8)]; t_WGM = [Trk("WGM%d" % i) for i in range(8)]
            t_WBR = [Trk("WBR%d" % i) for i in range(8)]; t_WBM = [Trk("WBM%d" % i) for i in range(8)]
            t_WOUT = [Trk("WOUT%d" % i) for i in range(8)]
            for oc in range(8):
                for (W, t_W, src) in ((WGR, t_WGR, win_d[32 + oc]), (WGM, t_WGM, win_d[40 + oc]),
                                      (WBR, t_WBR, wbrg_d[oc]), (WBM, t_WBM, wbml_d[oc])):
                    P.dma("pool", (lambda W, oc, src: lambda e: e.dma_start(
                        out=W[:, oc], in_=src.rearrange("p (k j) -> p k j", j=128)))(W, oc, src), writes=[t_W[oc]])
            for kc in range(8):
                P.dma("pool", (lambda kc: lambda e: e.dma_start(out=WOUT[:, kc, :], in_=wout_d[:, kc, :]))(kc), writes=[t_WOUT[kc]])
            YRt = Rot(sb, "YRt", 1, [128, 8, 512], BF16)
            YMt = Rot(sb, "YMt", 1, [128, 8, 512], BF16)
            SG = Rot(sb, "SG", 1, [128, 1024], F32)
            MIX = Rot(sb, "MIX", 1, [128, 8, 512], BF16)
            XT = Rot(sb, "XTc", 2, [128, 1024], F32)
            X1t = Rot(sb, "X1t", 1, [128, 1024], F32)
            XN = Rot(sb, "XNc", 1, [128, 1024], BF16)
            H2s = Rot(sb, "H2s", 1, [128, 8, 256], BF16)
            STc = sb("STc", [128, 32, 4], F32)
            BA0 = ps("BA0", [128, 512], F32); t_BA0 = Trk("PS_BA0")
            BA1 = ps("BA1", [128, 512], F32); t_BA1 = Trk("PS_BA1")
            BB0 = ps("BB0", [128, 512], F32); t_BB0 = Trk("PS_BB0")
            BB1 = ps("BB1", [128, 512], F32); t_BB1 = Trk("PS_BB1")
            BY = ps("BY", [128, 1024], F32); t_BY = Trk("PS_BY")
            BT = ps("BT", [128, 8, 128], BF16); t_BT = Trk("PS_BT")
            BT2 = ps("BT2", [128, 8, 128], BF16); t_BT2 = Trk("PS_BT2")
            def load_y(T):
                yr, t_yr = YRt.next()
                ym, t_ym = YMt.next()
                P.dma("sp", (lambda yr, T: lambda e: e.dma_start(
                    out=yr[:], in_=YRG_d[:, :, T * 512:(T + 1) * 512].rearrange("c p t -> p c t")))(yr, T), reads=t_YRG, writes=[t_yr])
                P.dma("sp", (lambda ym, T: lambda e: e.dma_start(
                    out=ym[:], in_=YML_d[:, :, T * 512:(T + 1) * 512].rearrange("c p t -> p c t")))(ym, T),
                    reads=t_YML[2 * T:2 * T + 2], writes=[t_ym])
                return yr, t_yr, ym, t_ym

            def load_x(ti):
                xt, t_xt = XT.next()
                for jj in range(2):
                    P.dma("sp", (lambda xt, jj, ti: lambda e: e.dma_start(
                        out=xt[jj * 64:(jj + 1) * 64, :], in_=x_cm[2 * ti + jj]))(xt, jj, ti), writes=[t_xt])
                return xt, t_xt

            h2s_trk2 = {}
            ynext = load_y(0)
            xnext = load_x(0)
            for T in range(8):
                yr, t_yr, ym, t_ym = ynext
                mix, t_mix = MIX.next()
                for oc in range(8):
                    sg, t_sg = SG.next()
                    for (W, t_W, bank, t_bank) in ((WGR, t_WGR, BA0, t_BA0), (WGM, t_WGM, BA1, t_BA1)):
                        for kc in range(8):
                            P.op("pe", (lambda W, bank, oc, kc, T: lambda e: e.matmul(
                                bank[:], lhsT=W[:, oc, kc, :],
                                rhs=uT[:, kc, 256 + T * 512:256 + (T + 1) * 512],
                                start=(kc == 0), stop=(kc == 7)))(W, bank, oc, kc, T), reads=[t_W[oc]] + t_uT[4:68], writes=[t_bank])
                    for (W, t_W, src, t_src, bank, t_bank) in ((WBR, t_WBR, yr, t_yr, BB0, t_BB0), (WBM, t_WBM, ym, t_ym, BB1, t_BB1)):
                        for kc in range(8):
                            P.op("pe", (lambda W, bank, oc, kc, src: lambda e: e.matmul(
                                bank[:], lhsT=W[:, oc, kc, :], rhs=src[:, kc, :],
                                start=(kc == 0), stop=(kc == 7)))(W, bank, oc, kc, src), reads=[t_W[oc], t_src], writes=[t_bank])
                    for (i_, bank, t_bank) in ((0, BA0, t_BA0), (1, BA1, t_BA1)):
                        P.op("act", (lambda sg, bank, i_: lambda e: e.activation(
                            out=sg[:, i_ * 512:(i_ + 1) * 512], in_=bank[:], func=AF.Sigmoid))(sg, bank, i_), reads=[t_bank], writes=[t_sg])
                    for (i_, bank, t_bank) in ((0, BB0, t_BB0), (1, BB1, t_BB1)):
                        P.op("dve", (lambda sg, bank, i_: lambda e: e.tensor_tensor(
                            out=sg[:, i_ * 512:(i_ + 1) * 512], in0=bank[:], in1=sg[:, i_ * 512:(i_ + 1) * 512], op=ALU.mult))(sg, bank, i_),
                            reads=[t_bank, t_sg], writes=[t_sg])
                    P.op("dve", (lambda mix, sg, oc: lambda e: e.tensor_tensor(
                        out=mix[:, oc, :], in0=sg[:, 0:512], in1=sg[:, 512:1024], op=ALU.add))(mix, sg, oc),
                        reads=[t_sg], writes=[t_mix])
                if T + 1 < 8:
                    ynext = load_y(T + 1)
                for s_ in range(4):
                    ti = T * 4 + s_
                    if s_ % 2 == 0:
                        h2s, t_h2s = H2s.next()
                        t_h2s2 = h2s_trk2.setdefault(id(t_h2s), Trk("h2s_b"))
                    xt, t_xt = xnext
                    if ti + 1 < 32:
                        xnext = load_x(ti + 1)
                    x1, t_x1 = X1t.next()
                    xn, t_xn = XN.next()
                    t_st = Trk("stc%d" % ti)
                    for half in range(2):
                        for kc in range(8):
                            P.op("pe", (lambda mix, half, kc, s_: lambda e: e.matmul(
                                BY[:, half * 512:(half + 1) * 512], lhsT=mix[:, kc, s_ * 128:(s_ + 1) * 128],
                                rhs=WOUT[:, kc, half * 512:(half + 1) * 512], start=(kc == 0), stop=(kc == 7)))(mix, half, kc, s_),
                                reads=[t_mix, t_WOUT[kc]], writes=[t_BY])
                    P.op("dve", (lambda x1: lambda e: e.tensor_tensor(out=x1[:], in0=BY[:], in1=GROW[:, 0, :], op=ALU.mult))(x1),
                         reads=[t_BY, t_GROW], writes=[t_x1])
                    P.op("dve", (lambda x1, xt: lambda e: e.tensor_tensor(out=x1[:], in0=x1[:], in1=xt[:], op=ALU.add))(x1, xt),
                         reads=[t_x1, t_xt], writes=[t_x1])
                    P.dma("sp", (lambda x1, ti: lambda e: e.dma_start(out=X1_d[ti * 128:(ti + 1) * 128, :], in_=x1[:]))(x1, ti),
                          reads=[t_x1], writes=[t_X1[ti]], semtrk=t_x1)
                    P.op("act", (lambda x1, xn, ti: lambda e: e.activation(
                        out=xn[:], in_=x1[:], func=AF.Square, accum_out=STc[:, ti, 0:1]))(x1, xn, ti), reads=[t_x1], writes=[t_xn, t_st])
                    P.op("act", (lambda ti: lambda e: e.activation(
                        out=STc[:, ti, 1:2], in_=STc[:, ti, 0:1], func=AF.Sqrt, scale=1.0 / 1024.0, bias=EPS))(ti), reads=[t_st], writes=[t_st])
                    P.op("dve", (lambda ti: lambda e: e.reciprocal(out=STc[:, ti, 2:3], in_=STc[:, ti, 1:2]))(ti), reads=[t_st], writes=[t_st])
                    P.op("act", (lambda x1, xn, ti: lambda e: e.activation(
                        out=xn[:], in_=x1[:], func=AF.Copy, scale=STc[:, ti, 2:3]))(x1, xn, ti), reads=[t_x1, t_st], writes=[t_xn])
                    for c in range(8):
                        btc, t_btc = (BT, t_BT) if c < 4 else (BT2, t_BT2)
                        P.op("pe", (lambda xn, c, btc: lambda e: e.transpose(
                            out=btc[:, c, :], in_=xn[:, c * 128:(c + 1) * 128], identity=IDB[:]))(xn, c, btc), reads=[t_xn, t_IDB], writes=[t_btc])
                    for c in (0, 4, 1, 5, 2, 6, 3, 7):
                        dst = h2s[:, c, (s_ % 2) * 128:(s_ % 2 + 1) * 128]
                        if c < 4:
                            P.op("dve", (lambda c, dst: lambda e: e.tensor_scalar(
                                out=dst, in0=BT[:, c, :], scalar1=A2[:, c:c + 1], scalar2=MODT[:, 48 + 2 * c:49 + 2 * c],
                                op0=ALU.mult, op1=ALU.add))(c, dst), reads=[t_BT, t_A2, t_MODT], writes=[t_h2s])
                        else:
                            P.op("act", (lambda c, dst: lambda e: e.activation(
                                out=dst, in_=BT2[:, c, :], func=AF.Identity, scale=A2[:, c:c + 1], bias=MODT[:, 48 + 2 * c:49 + 2 * c]))(c, dst),
                                reads=[t_BT2, t_A2, t_MODT], writes=[t_h2s2])
                    if s_ % 2 == 1:
                        Tq = ti // 2
                        P.dma("sp", (lambda h2s, Tq: lambda e: e.dma_start(out=H2_d[Tq], in_=h2s[:]))(h2s, Tq),
                              reads=[t_h2s, t_h2s2], writes=[t_H2[Tq]], semtrk=t_h2s)
            P.barrier()
            P.emit()
        ust.close()
        if "x1" in debug:
            dbg_d["x1"] = X1_d

        with ExitStack() as st:
            sb = lambda name, shape, dt: st.enter_context(nc.sbuf_tensor(name, list(shape), dt))
            ps = lambda name, shape, dt: st.enter_context(nc.psum_tensor(name, list(shape), dt))
            WFI = sb("WFI", [128, 44, 8, 128], BF16); t_WFI = [Trk("WFI%d" % i) for i in range(44)]
            WFO = sb("WFO", [128, 22, 1024], BF16); t_WFO = [Trk("WFO%d" % i) for i in range(22)]
            FGR = sb("FGR", [128, 1024], F32); t_FGR = Trk("FGR")
            P.dma("sp", lambda e: e.dma_start(out=FGR[:], in_=fgrow_d), writes=[t_FGR])
            for f_ in range(22):
                for ci in (f_, 22 + f_):
                    P.dma("pool", (lambda ci: lambda e: e.dma_start(
                        out=WFI[:, ci], in_=wffi_d[ci].rearrange("p (k j) -> p k j", j=128)))(ci), writes=[t_WFI[ci]])
                P.dma("pool", (lambda f_: lambda e: e.dma_start(out=WFO[:, f_, :], in_=wffo_d[:, f_, :]))(f_), writes=[t_WFO[f_]])
            H2t = Rot(sb, "H2t", 2, [128, 8, 512], BF16)
            SGt = Rot(sb, "SGt", 2, [128, 512], F32)
            HID = Rot(sb, "HID", 1, [128, 22, 512], BF16)
            X1r = Rot(sb, "X1r", 2, [128, 1024], F32)
            T2 = Rot(sb, "T2", 2, [128, 1024], F32)
            JK = sb("JK", [128, 1024], BF16); t_JK = Trk("JK")
            STf = sb("STf", [128, 32, 4], F32)
            BG0 = Rot(ps, "PS_BG0", 2, [128, 512], F32)
            BG1 = Rot(ps, "PS_BG1", 2, [128, 512], F32)
            BO = Rot(ps, "PS_BO", 2, [128, 1024], F32)
            h2_map = {}
            x1_map = {}
            for T in range(8):
                for Tq in (T, T + 1):
                    if Tq < 8 and Tq not in h2_map:
                        hb, t_hb = H2t.next()
                        for q_ in range(2):
                            P.dma("sp", (lambda hb, Tq, q_: lambda e: e.dma_start(out=hb[:, :, q_ * 256:(q_ + 1) * 256], in_=H2_d[2 * Tq + q_]))(hb, Tq, q_),
                                  reads=[t_H2[2 * Tq + q_]], writes=[t_hb])
                        h2_map[Tq] = (hb, t_hb)
                h2, t_h2 = h2_map.pop(T)
                hid, t_hid = HID.next()
                for f_ in range(22):
                    g0, t_g0 = BG0.next()
                    g1, t_g1 = BG1.next()
                    sg, t_sg = SGt.next()
                    for (ci, bank, t_bank) in ((f_, g0, t_g0), (22 + f_, g1, t_g1)):
                        for kc in range(8):
                            P.op("pe", (lambda bank, ci, kc, h2: lambda e: e.matmul(
                                bank[:], lhsT=WFI[:, ci, kc, :], rhs=h2[:, kc, :], start=(kc == 0), stop=(kc == 7)))(bank, ci, kc, h2),
                                reads=[t_WFI[ci], t_h2], writes=[t_bank])
                    P.op("act", (lambda sg, g0: lambda e: e.activation(out=sg[:], in_=g0[:], func=AF.Silu))(sg, g0),
                         reads=[t_g0], writes=[t_sg])
                    P.op("dve", (lambda hid, f_, sg, g1: lambda e: e.tensor_tensor(
                        out=hid[:, f_, :], in0=g1[:], in1=sg[:], op=ALU.mult))(hid, f_, sg, g1), reads=[t_g1, t_sg], writes=[t_hid])
                for s_ in range(4):
                    ti = T * 4 + s_
                    bo, t_bo = BO.next()
                    for tq in (ti, ti + 1):
                        if tq < 32 and tq not in x1_map:
                            xb, t_xb = X1r.next()
                            P.dma("sp", (lambda xb, tq: lambda e: e.dma_start(out=xb[:], in_=X1_d[tq * 128:(tq + 1) * 128, :]))(xb, tq),
                                  reads=[t_X1[tq]], writes=[t_xb])
                            x1_map[tq] = (xb, t_xb)
                    x1, t_x1 = x1_map.pop(ti)
                    t2, t_t2 = T2.next()
                    t_st = Trk("stf%d" % ti)
                    for half in range(2):
                        for f_ in range(22):
                            P.op("pe", (lambda bo, hid, half, f_, s_: lambda e: e.matmul(
                                bo[:, half * 512:(half + 1) * 512], lhsT=hid[:, f_, s_ * 128:(s_ + 1) * 128],
                                rhs=WFO[:, f_, half * 512:(half + 1) * 512], start=(f_ == 0), stop=(f_ == 21)))(bo, hid, half, f_, s_),
                                reads=[t_hid, t_WFO[f_]], writes=[t_bo])
                    P.op("dve", (lambda t2, bo: lambda e: e.tensor_tensor(out=t2[:], in0=bo[:], in1=GROW[:, 1, :], op=ALU.mult))(t2, bo),
                         reads=[t_bo, t_GROW], writes=[t_t2])
                    P.op("dve", (lambda t2, x1: lambda e: e.tensor_tensor(out=t2[:], in0=t2[:], in1=x1[:], op=ALU.add))(t2, x1),
                         reads=[t_t2, t_x1], writes=[t_t2])
                    P.op("act", (lambda t2, ti: lambda e: e.activation(
                        out=JK[:], in_=t2[:], func=AF.Square, accum_out=STf[:, ti, 0:1]))(t2, ti), reads=[t_t2], writes=[t_JK, t_st])
                    P.op("act", (lambda ti: lambda e: e.activation(
                        out=STf[:, ti, 1:2], in_=STf[:, ti, 0:1], func=AF.Sqrt, scale=1.0 / 1024.0, bias=EPS))(ti), reads=[t_st], writes=[t_st])
                    P.op("dve", (lambda ti: lambda e: e.reciprocal(out=STf[:, ti, 2:3], in_=STf[:, ti, 1:2]))(ti), reads=[t_st], writes=[t_st])
                    P.op("act", (lambda t2, ti: lambda e: e.activation(
                        out=t2[:], in_=t2[:], func=AF.Copy, scale=STf[:, ti, 2:3]))(t2, ti), reads=[t_t2, t_st], writes=[t_t2])
                    P.op("dve", (lambda t2: lambda e: e.tensor_tensor(out=t2[:], in0=t2[:], in1=FGR[:], op=ALU.mult))(t2),
                         reads=[t_t2, t_FGR], writes=[t_t2])
                    for jj in range(2):
                        P.out_toks.append(P.dma("sp", (lambda t2, jj, ti: lambda e: e.dma_start(
                            out=out_cm[2 * ti + jj], in_=t2[jj * 64:(jj + 1) * 64, :]))(t2, jj, ti), reads=[t_t2], semtrk=t_t2))
            P.barrier()
            P.emit()

        P.wait_all("sp", P.out_toks)
        P.emit()
    return nc, dbg_d


def make_in_maps(inputs):
    sh = prep_shared(inputs)
    x = np.asarray(inputs["x"], np.float32)
    c = np.asarray(inputs["c"], np.float32)
    ctx = np.asarray(inputs["ctx"], np.float32)
    c_ctx = np.asarray(inputs["c_ctx"], np.float32)
    maps = []
    for b in range(8):
        m = dict(sh)
        m["x"] = np.ascontiguousarray(x[b])
        m["ctx"] = np.ascontiguousarray(ctx[b])
        m["cv"] = np.ascontiguousarray(np.stack([fm(c[b]), fm(c_ctx)], 2).reshape(128, 16))
        maps.append(m)
    return maps


def kernel(**inputs):
    nc, _ = build()
    maps = make_in_maps(inputs)
    res = run_bass_kernel_spmd(nc, maps, core_ids=list(range(8)))
    return np.stack([r["out"] for r in res.results], 0)
```

```python
import numpy as np
from contextlib import ExitStack
import concourse.bass as bass
import concourse.mybir as mybir
from concourse.bass_utils import run_bass_kernel_spmd

F32 = mybir.dt.float32
BF16 = mybir.dt.bfloat16
AF = mybir.ActivationFunctionType
ALU = mybir.AluOpType
AX = mybir.AxisListType

ENGS = ["pe", "act", "dve", "pool", "sp"]
EPS = 1e-6
NT = 4352
NCTX = 256
NLAT = 4096


class Trk:
    __slots__ = ("name", "w", "r", "dsem", "dcnt", "excl")

    def __init__(self, name=""):
        self.name = name
        self.w = None
        self.r = {}
        self.dsem = None
        self.dcnt = 0
        self.excl = name.startswith("PS_")


class Prog:
    def __init__(self, nc, stack):
        self.nc = nc
        self.stack = stack
        self.ops = {e: [] for e in ENGS}
        self.seq = {e: 0 for e in ENGS}
        self.known = {e: {} for e in ENGS}
        self.esem = {e: stack.enter_context(nc.semaphore("s_" + e)) for e in ENGS}
        self.nsem = len(ENGS)
        self.out_toks = []
        self.dtrks = []
        self.sem_pool = {"sp": [], "pool": []}

    def new_dsem(self, name):
        s = self.stack.enter_context(self.nc.semaphore("d%d_%s" % (self.nsem, name)))
        self.nsem += 1
        return s

    def _need(self, eng, waits, dep):
        sem, val = dep
        if eng == "pe" and sem is self.esem["pe"]:
            return
        k = id(sem)
        if self.known[eng].get(k, 0) >= val:
            return
        self.known[eng][k] = val
        waits[k] = (sem, val)

    def _deps(self, eng, reads, writes):
        waits = {}
        for t in reads:
            if t.w is not None:
                self._need(eng, waits, t.w)
            if t.excl:
                for dep in t.r.values():
                    self._need(eng, waits, dep)
        for t in writes:
            if t.w is not None:
                self._need(eng, waits, t.w)
            for dep in t.r.values():
                self._need(eng, waits, dep)
        return waits

    def _record(self, tok, reads, writes):
        for t in reads:
            t.r[id(tok[0])] = tok
        for t in writes:
            t.w = tok
            t.r = {}

    def op(self, eng, fn, reads=(), writes=()):
        waits = self._deps(eng, reads, writes)
        self.seq[eng] += 1
        tok = (self.esem[eng], self.seq[eng])
        self._record(tok, reads, writes)
        self.ops[eng].append((list(waits.values()), fn, (self.esem[eng], 1)))

    def dma(self, eng, fn, reads=(), writes=(), semtrk=None):
        if semtrk is None:
            semtrk = writes[0] if writes else reads[0]
        if semtrk.dsem is None:
            if self.sem_pool[eng]:
                semtrk.dsem, semtrk.dcnt = self.sem_pool[eng].pop()
            else:
                semtrk.dsem = self.new_dsem(semtrk.name)
            self.dtrks.append((semtrk, eng))
        waits = self._deps(eng, reads, writes)
        if semtrk.dcnt > 0:
            self._need(eng, waits, (semtrk.dsem, semtrk.dcnt))
        semtrk.dcnt += 16
        tok = (semtrk.dsem, semtrk.dcnt)
        self._record(tok, reads, writes)
        self.ops[eng].append((list(waits.values()), fn, (semtrk.dsem, 16)))
        return tok

    def wait_all(self, eng, toks):
        waits = {}
        for t in toks:
            self._need(eng, waits, t)
        self.ops[eng].append((list(waits.values()), None, None))

    def barrier(self):
        waits = {}
        for e in ENGS:
            if e != "sp" and self.seq[e] > 0:
                self._need("sp", waits, (self.esem[e], self.seq[e]))
        for t, _e in self.dtrks:
            if t.dcnt > 0:
                self._need("sp", waits, (t.dsem, t.dcnt))
        self.seq["sp"] += 1
        self.ops["sp"].append((list(waits.values()), lambda e: e.nop(), (self.esem["sp"], 1)))
        for t, e_ in self.dtrks:
            self.sem_pool[e_].append((t.dsem, t.dcnt))
            t.dsem = None
            t.dcnt = 0
        self.dtrks = []
        for e in ENGS:
            if e != "sp":
                w = {}
                self._need(e, w, (self.esem["sp"], self.seq["sp"]))
                self.ops[e].append((list(w.values()), None, None))

    def emit(self):
        nc = self.nc
        ops = self.ops
        self.ops = {e: [] for e in ENGS}
        with nc.Block() as block:
            def run(e, lst):
                for waits, fn, inc in lst:
                    for sem, val in waits:
                        e.wait_ge(sem, val)
                    if fn is not None:
                        fn(e).then_inc(inc[0], inc[1])

            @block.tensor
            def _(e):
                run(e, ops["pe"])

            @block.scalar
            def _(e):
                run(e, ops["act"])

            @block.vector
            def _(e):
                run(e, ops["dve"])

            @block.gpsimd
            def _(e):
                run(e, ops["pool"])

            @block.sync
            def _(e):
                run(e, ops["sp"])


class Rot:
    def __init__(self, alloc, name, n, shape, dt):
        self.bufs = [(alloc("%s%d" % (name, i), shape, dt), Trk("%s%d" % (name, i))) for i in range(n)]
        self.i = 0

    def next(self):
        b = self.bufs[self.i % len(self.bufs)]
        self.i += 1
        return b


FVCOLS = {}
_off = 0
for _n, _w in [("n1g", 8), ("n2g", 8), ("rgcw", 32), ("rgcb", 8), ("rgba", 16), ("rgbx", 16), ("rglam", 16),
               ("mlcw", 32), ("mlcb", 8), ("mlng", 8), ("mlsk", 8)]:
    FVCOLS[_n] = (_off, _w)
    _off += _w
NV = _off


def fm(vec):
    v = np.asarray(vec, np.float32)
    return np.ascontiguousarray(v.reshape(-1, 128).T)


def colchunks(w):
    K, N = w.shape
    a = w.reshape(K // 128, 128, N // 128, 128)
    a = a.transpose(2, 1, 0, 3)
    return np.ascontiguousarray(a.reshape(N // 128, 128, (K // 128) * 128))


def rowchunks(w):
    K, N = w.shape
    return np.ascontiguousarray(w.reshape(K // 128, 128, N).transpose(1, 0, 2))


def blockdiag128(blocks):
    nb, bi, bo = blocks.shape
    per = 128 // bi
    out = np.zeros((nb // per, 128, 128), np.float32)
    for b in range(nb):
        c, q = divmod(b, per)
        out[c, q * bi:(q + 1) * bi, q * bo:(q + 1) * bo] = blocks[b]
    return out


def prep_shared(inp):
    f = lambda k: np.asarray(inp[k], np.float32)
    sh = {}
    w_mod = f("w_mod")[0]
    sh["wmod"] = colchunks(w_mod)
    b_mod = f("b_mod")[0]
    sh["bmod2"] = np.ascontiguousarray(np.repeat(fm(b_mod)[:, :, None], 2, axis=2).reshape(128, 96))
    sh["wmodg"] = np.stack([rowchunks(w_mod[:, 2048:3072]), rowchunks(w_mod[:, 5120:6144])], 0)
    sh["bmodg"] = np.ascontiguousarray(np.broadcast_to(
        np.concatenate([b_mod[2048:3072], b_mod[5120:6144]])[None, :], (128, 2048)))
    fv = np.zeros((128, NV), np.float32)

    def put(name, arr):
        o, w = FVCOLS[name]
        assert arr.shape == (128, w), (name, arr.shape)
        fv[:, o:o + w] = arr
    put("n1g", fm(f("norm1_g")[0]))
    put("n2g", fm(f("norm2_g")[0]))
    cw = f("rg_conv_w")[0]
    put("rgcw", np.stack([fm(cw[j]) for j in range(4)], 2).reshape(128, 32))
    put("rgcb", fm(f("rg_conv_b")[0]))
    put("rgba", np.concatenate([fm(f("rg_ba")[0][d]) for d in range(2)], 1))
    put("rgbx", np.concatenate([fm(f("rg_bx")[0][d]) for d in range(2)], 1))
    put("rglam", np.concatenate([fm(f("rg_lambda")[0][d]) for d in range(2)], 1))
    cw = f("ml_conv_w")[0]
    put("mlcw", np.stack([fm(cw[j]) for j in range(4)], 2).reshape(128, 32))
    put("mlcb", fm(f("ml_conv_b")[0]))
    put("mlng", fm(f("ml_norm_g")[0]))
    put("mlsk", fm(f("ml_skip")[0]))
    sh["fv"] = fv
    sh["win"] = colchunks(f("w_in")[0])
    rg = []
    for d in range(2):
        for w in (f("rg_wa")[0][d], f("rg_wx")[0][d]):
            rg.append(blockdiag128(w).transpose(1, 0, 2))
    sh["rgbd"] = np.ascontiguousarray(np.concatenate(rg, 1))
    ml = [blockdiag128(f(k)[0]).transpose(1, 0, 2) for k in ("ml_wq", "ml_wk", "ml_wv")]
    sh["mlbd"] = np.ascontiguousarray(np.concatenate(ml, 1))
    wi, wf = f("ml_wi")[0], f("ml_wf")[0]
    wg = np.concatenate([wi[0], wf[0], wi[1], wf[1]], 1)
    sh["wgt"] = np.ascontiguousarray(wg.reshape(24, 128, 16).transpose(1, 0, 2))
    bi, bf = f("ml_bi")[0], f("ml_bf")[0]
    gb = np.stack([np.tile(np.concatenate([bi[d], bf[d]]), 2) for d in range(2)], 0)
    sh["gbias"] = np.ascontiguousarray(np.broadcast_to(gb[None], (128, 2, 16)))
    tri = np.zeros((2, 128, 128), np.float32)
    ii = np.arange(128)
    tri[0] = (ii[:, None] <= ii[None, :])
    tri[1] = (ii[:, None] >= ii[None, :])
    sh["tri"] = np.ascontiguousarray(tri.transpose(1, 0, 2))
    ident = np.eye(128, dtype=np.float32)
    sh["ident"] = ident
    sh["wbrg"] = colchunks(f("w_branch_rg")[0])
    sh["wbml"] = colchunks(f("w_branch_ml")[0])
    sh["wout"] = rowchunks(f("w_out")[0])
    sh["wffi"] = colchunks(f("w_ffn_in")[0])
    sh["wffo"] = rowchunks(f("w_ffn_out")[0])
    sh["fgrow"] = np.ascontiguousarray(np.broadcast_to(f("final_norm_g")[None, :], (128, 1024)))
    return sh


def build(debug=(), stop_after=None):
    nc = bass.Bass("TRN2", target_bir_lowering=False)
    din = lambda name, shape, dt=F32: nc.dram_tensor(name, list(shape), dt, kind="ExternalInput").ap()
    x_d = din("x", [NLAT, 1024])
    ctx_d = din("ctx", [NCTX, 1024])
    cv_d = din("cv", [128, 16])
    wmod_d = din("wmod", [48, 128, 1024])
    bmod2_d = din("bmod2", [128, 96])
    wmodg_d = din("wmodg", [2, 128, 8, 1024])
    bmodg_d = din("bmodg", [128, 2048])
    fv_d = din("fv", [128, NV])
    win_d = din("win", [48, 128, 1024])
    ident_d = din("ident", [128, 128])
    rgbd_d = din("rgbd", [128, 32, 128])
    mlbd_d = din("mlbd", [128, 24, 128])
    wgt_d = din("wgt", [128, 24, 16])
    gbias_d = din("gbias", [128, 2, 16])
    tri_d = din("tri", [128, 2, 128])
    wbrg_d = din("wbrg", [8, 128, 1024])
    wbml_d = din("wbml", [8, 128, 1024])
    wout_d = din("wout", [128, 8, 1024])
    wffi_d = din("wffi", [44, 128, 1024])
    wffo_d = din("wffo", [128, 22, 1024])
    fgrow_d = din("fgrow", [128, 1024])
    SPQ_d = nc.dram_tensor("SPQ", [17, 4, 128, 8, 256], BF16).ap()
    SPX_d = nc.dram_tensor("SPX", [16, 128, 8, 256], F32).ap()
    X1_d = nc.dram_tensor("X1", [NLAT, 1024], F32).ap()
    H2_d = nc.dram_tensor("H2", [16, 128, 8, 256], BF16).ap()
    HF_d = nc.dram_tensor("HF", [32, 128, 1024], F32).ap()
    if "yml" in debug:
        YML_d = nc.dram_tensor("dbg_yml", [8, 128, NLAT], BF16, kind="ExternalOutput").ap()
    else:
        YML_d = nc.dram_tensor("YML", [8, 128, NLAT], BF16).ap()
    YRG_d = nc.dram_tensor("YRG", [8, 128, NLAT], BF16).ap()
    out_d = nc.dram_tensor("out", [NLAT, 1024], F32, kind="ExternalOutput").ap()
    dbg_d = {}

    with ExitStack() as gst:
        P = Prog(nc, gst)
        galloc = lambda name, shape, dt: gst.enter_context(nc.sbuf_tensor(name, list(shape), dt))

        def dump(name, ap, trk, shape, dt=F32):
            if name not in debug:
                return
            d = nc.dram_tensor("dbg_" + name, list(shape), dt, kind="ExternalOutput").ap()
            dbg_d[name] = d
            P.out_toks.append(P.dma("sp", lambda e: e.dma_start(out=d, in_=ap), reads=[trk], semtrk=Trk("dbg" + name)))

        t_uT = [Trk("uT%d_%d" % (i // 2, i % 2)) for i in range(68)]
        FV = galloc("FV", [128, NV], F32); t_FV = Trk("FV")
        MODT = galloc("MODT", [128, 96], F32); t_MODT = Trk("MODT")
        A1 = galloc("A1", [128, 16], F32); t_A1 = Trk("A1")
        A2 = galloc("A2", [128, 8], F32); t_A2 = Trk("A2")
        GROW = galloc("GROW", [128, 2, 1024], F32); t_GROW = Trk("GROW")
        KD = galloc("KD", [128, 16], F32); t_KD = Trk("KD")
        IDB = galloc("IDB", [128, 128], BF16); t_IDB = Trk("IDB")
        IDF = galloc("IDF", [128, 128], F32); t_IDF = Trk("IDF")
        t_YRG = [Trk("YRG%d" % c) for c in range(8)]
        ust = ExitStack()
        uT = ust.enter_context(nc.sbuf_tensor("uT", [128, 8, NT], BF16))

        def fvc(name, i=0, n=1):
            o, w = FVCOLS[name]
            return FV[:, o + i:o + i + n]

        with ExitStack() as st:
            sb = lambda name, shape, dt: st.enter_context(nc.sbuf_tensor(name, list(shape), dt))
            ps = lambda name, shape, dt: st.enter_context(nc.psum_tensor(name, list(shape), dt))
            CV = sb("CV", [128, 16], F32); t_CV = Trk("CV")
            S2 = sb("S2", [128, 16], F32); t_S2 = Trk("S2")
            SREP = sb("SREP", [128, 8, 128], F32); t_SREP = Trk("SREP")
            BM2 = sb("BM2", [128, 96], F32); t_BM2 = Trk("BM2")
            BMG = sb("BMG", [128, 2048], F32); t_BMG = Trk("BMG")
            TMPA = sb("TMPA", [128, 16], F32); t_TMPA = Trk("TMPA")
            TMPB = sb("TMPB", [128, 16], F32); t_TMPB = Trk("TMPB")
            WM = Rot(sb, "WM", 3, [128, 1024], F32)
            WGm = sb("WGm", [128, 8, 1024], F32); t_WGm = Trk("WGm")
            MODP = ps("MODP", [128, 512], F32); t_MODP = Trk("PS_MODP")
            GRP = ps("GRP", [128, 1024], F32); t_GRP = Trk("PS_GRP")

            P.dma("sp", lambda e: e.dma_start(out=CV[:], in_=cv_d), writes=[t_CV])
            P.dma("sp", lambda e: e.dma_start(out=FV[:], in_=fv_d), writes=[t_FV])
            P.dma("sp", lambda e: e.dma_start(out=BM2[:], in_=bmod2_d), writes=[t_BM2])
            P.dma("sp", lambda e: e.dma_start(out=BMG[:], in_=bmodg_d), writes=[t_BMG])
            P.dma("sp", lambda e: e.dma_start(out=IDF[:], in_=ident_d), writes=[t_IDF])
            P.dma("pool", lambda e: e.dma_start(out=IDB[:], in_=ident_d), writes=[t_IDB])
            P.op("act", lambda e: e.activation(out=S2[:], in_=CV[:], func=AF.Silu), reads=[t_CV], writes=[t_S2])
            for kc in range(8):
                P.op("dve", (lambda kc: lambda e: e.tensor_copy(
                    out=SREP[:, kc, :], in_=S2[:, 2 * kc:2 * kc + 1].to_broadcast([128, 128])))(kc),
                    reads=[t_S2], writes=[t_SREP])
            for n in range(48):
                wm, t_wm = WM.next()
                P.dma("sp", (lambda wm, n: lambda e: e.dma_start(out=wm[:], in_=wmod_d[n]))(wm, n), writes=[t_wm])
                for kc in range(8):
                    P.op("pe", (lambda wm, n, kc: lambda e: e.matmul(
                        MODP[:, 2 * n:2 * n + 2], lhsT=wm[:, kc * 128:(kc + 1) * 128], rhs=S2[:, 2 * kc:2 * kc + 2],
                        start=(kc == 0), stop=(kc == 7)))(wm, n, kc), reads=[t_wm, t_S2], writes=[t_MODP])
            P.op("dve", lambda e: e.tensor_tensor(out=MODT[:], in0=MODP[:, 0:96], in1=BM2[:], op=ALU.add),
                 reads=[t_MODP, t_BM2], writes=[t_MODT])
            P.op("dve", lambda e: e.tensor_scalar_add(out=TMPA[:], in0=MODT[:, 16:32], scalar1=1.0),
                 reads=[t_MODT], writes=[t_TMPA])
            for j in range(2):
                P.op("dve", (lambda j: lambda e: e.tensor_tensor(
                    out=A1[:, j:16:2], in0=TMPA[:, j:16:2], in1=fvc("n1g", 0, 8), op=ALU.mult))(j),
                    reads=[t_TMPA, t_FV], writes=[t_A1])
            P.op("dve", lambda e: e.tensor_scalar_add(out=TMPB[:, 0:8], in0=MODT[:, 64:80:2], scalar1=1.0),
                 reads=[t_MODT], writes=[t_TMPB])
            P.op("dve", lambda e: e.tensor_tensor(out=A2[:], in0=TMPB[:, 0:8], in1=fvc("n2g", 0, 8), op=ALU.mult),
                 reads=[t_TMPB, t_FV], writes=[t_A2])
            for g in range(2):
                P.dma("sp", (lambda g: lambda e: e.dma_start(out=WGm[:], in_=wmodg_d[g]))(g), writes=[t_WGm])
                for half in range(2):
                    for kc in range(8):
                        P.op("pe", (lambda half, kc: lambda e: e.matmul(
                            GRP[:, half * 512:(half + 1) * 512], lhsT=SREP[:, kc, :],
                            rhs=WGm[:, kc, half * 512:(half + 1) * 512], start=(kc == 0), stop=(kc == 7)))(half, kc),
                            reads=[t_SREP, t_WGm], writes=[t_GRP])
                P.op("dve", (lambda g: lambda e: e.tensor_tensor(
                    out=GROW[:, g, :], in0=GRP[:], in1=BMG[:, g * 1024:(g + 1) * 1024], op=ALU.add))(g),
                    reads=[t_GRP, t_BMG], writes=[t_GROW])
            P.op("act", lambda e: e.activation(out=TMPA[:], in_=fvc("rglam", 0, 16), func=AF.Exp, scale=-1.0),
                 reads=[t_FV], writes=[t_TMPA])
            P.op("act", lambda e: e.activation(out=TMPB[:], in_=TMPA[:], func=AF.Ln, bias=1.0),
                 reads=[t_TMPA], writes=[t_TMPB])
            P.op("dve", lambda e: e.tensor_scalar(out=KD[:], in0=TMPB[:], scalar1=-8.0, scalar2=None, op0=ALU.mult),
                 reads=[t_TMPB], writes=[t_KD])
            dump("modT", MODT[:], t_MODT, [128, 96])
            dump("grow", GROW[:], t_GROW, [128, 2, 1024])
            dump("kd", KD[:], t_KD, [128, 16])
            P.barrier()
            P.emit()

        with ExitStack() as st:
            sb = lambda name, shape, dt: st.enter_context(nc.sbuf_tensor(name, list(shape), dt))
            ps = lambda name, shape, dt: st.enter_context(nc.psum_tensor(name, list(shape), dt))
            XT = Rot(sb, "XT", 3, [128, 1024], F32)
            XN = Rot(sb, "XN", 2, [128, 1024], BF16)
            TP = Rot(ps, "PS_TP", 2, [128, 8, 128], BF16)
            TPb = Rot(ps, "PS_TPb", 2, [128, 8, 128], BF16)
            ST = sb("ST", [128, 34, 4], F32)
            JKA = sb("JKA", [128, 1024], BF16); t_JKA = Trk("JKA")
            stage1 = {}

            def emit_stats(i):
                t_st = Trk("st%d" % i)
                xt, t_xt = XT.next()
                src = ctx_d[i * 128:(i + 1) * 128, :] if i < 2 else x_d[(i - 2) * 128:(i - 1) * 128, :]
                P.dma("sp", (lambda xt, src: lambda e: e.dma_start(out=xt[:], in_=src))(xt, src), writes=[t_xt])
                P.op("act", (lambda xt, i: lambda e: e.activation(
                    out=JKA[:], in_=xt[:], func=AF.Square, accum_out=ST[:, i, 0:1]))(xt, i),
                    reads=[t_xt], writes=[t_JKA, t_st])
                P.op("act", (lambda i: lambda e: e.activation(
                    out=ST[:, i, 1:2], in_=ST[:, i, 0:1], func=AF.Sqrt, scale=1.0 / 1024.0, bias=EPS))(i),
                    reads=[t_st], writes=[t_st])
                P.op("dve", (lambda i: lambda e: e.reciprocal(out=ST[:, i, 2:3], in_=ST[:, i, 1:2]))(i),
                     reads=[t_st], writes=[t_st])
                stage1[i] = (xt, t_xt, t_st)

            emit_stats(0)
            for i in range(34):
                if i + 1 < 34:
                    emit_stats(i + 1)
                xt, t_xt, t_st = stage1.pop(i)
                xn, t_xn = XN.next()
                tpa, t_tpa = TP.next()
                tpb, t_tpb = TPb.next()
                j = 1 if i < 2 else 0
                P.op("act", (lambda xt, xn, i: lambda e: e.activation(
                    out=xn[:], in_=xt[:], func=AF.Copy, scale=ST[:, i, 2:3]))(xt, xn, i),
                    reads=[t_xt, t_st], writes=[t_xn])
                for c in range(8):
                    tp, t_tp = (tpa, t_tpa) if c < 4 else (tpb, t_tpb)
                    P.op("pe", (lambda xn, tp, c: lambda e: e.transpose(
                        out=tp[:, c, :], in_=xn[:, c * 128:(c + 1) * 128], identity=IDB[:]))(xn, tp, c),
                        reads=[t_xn, t_IDB], writes=[t_tp])
                for c in (0, 4, 1, 5, 2, 6, 3, 7):
                    tp, t_tp = (tpa, t_tpa) if c < 4 else (tpb, t_tpb)
                    if i < 2:
                        dst = uT[:, c, i * 128:(i + 1) * 128]
                        src_tp = tp[:, c, :]
                    else:
                        r0 = 2 * (i - 2)
                        dst = uT[:, c, 256:NT].rearrange("p (j r) -> p r j", r=64)[:, r0:r0 + 2, :]
                        src_tp = tp[:, c, :].rearrange("p (r j) -> p r j", j=64)
                    if c < 4:
                        P.op("dve", (lambda tp, c, dst, j: lambda e: e.tensor_scalar(
                            out=dst, in0=tp, scalar1=A1[:, 2 * c + j:2 * c + j + 1],
                            scalar2=MODT[:, 2 * c + j:2 * c + j + 1], op0=ALU.mult, op1=ALU.add))(src_tp, c, dst, j),
                            reads=[t_tp, t_A1, t_MODT], writes=[t_uT[2 * i + (0 if c < 4 else 1)]])
                    else:
                        P.op("act", (lambda tp, c, dst, j: lambda e: e.activation(
                            out=dst, in_=tp, func=AF.Identity, scale=A1[:, 2 * c + j:2 * c + j + 1],
                            bias=MODT[:, 2 * c + j:2 * c + j + 1]))(src_tp, c, dst, j),
                            reads=[t_tp, t_A1, t_MODT], writes=[t_uT[2 * i + (0 if c < 4 else 1)]])
            if "uT" in debug:
                UD = sb("UD", [128, 8, 512], F32); t_UD = Trk("UD")
                P.op("dve", lambda e: e.tensor_copy(out=UD[:], in_=uT[:, :, 128:640]), reads=t_uT[2:10], writes=[t_UD])
                dump("uT", UD[:], t_UD, [128, 8, 512])
            P.barrier()
            P.emit()


        CT0, LT0, PEND = 2, 261, 4357
        with ExitStack() as st:
            sb = lambda name, shape, dt: st.enter_context(nc.sbuf_tensor(name, list(shape), dt))
            ps = lambda name, shape, dt: st.enter_context(nc.psum_tensor(name, list(shape), dt))
            RGBD = sb("RGBD", [128, 32, 128], BF16); t_RGBD = Trk("RGBD")
            P.dma("pool", lambda e: e.dma_start(out=RGBD[:], in_=rgbd_d, max_dma_last_dim=4096), writes=[t_RGBD])
            RX = sb("RX", [128, 4360], F32); t_RX = Trk("RX")
            XC = sb("XC", [128, 4360], F32); t_XC = Trk("XC")
            XCB = sb("XCB", [128, 4360], BF16); t_XCB = Trk("XCB")
            TMP = sb("TMP", [128, 2180], F32); t_TMP = Trk("TMP")
            BB = [sb("B0", [128, 4360], F32), sb("B1", [128, 4360], F32)]; t_BB = [Trk("B0"), Trk("B1")]
            WR = Rot(sb, "WR", 4, [128, 8, 128], BF16)
            PJ = Rot(ps, "PS_PJ", 3, [128, 512], F32)
            GA = Rot(ps, "PS_GA", 4, [128, 512], F32)
            GL = Rot(sb, "GL", 2, [128, 512], F32)
            TS = Rot(sb, "TS", 2, [128, 512], F32)
            YS = Rot(sb, "YS", 1, [128, NLAT], BF16)
            for (a, b) in ((0, 2), (258, 261), (4357, 4360)):
                P.op("dve", (lambda a, b: lambda e: e.memset(RX[:, a:b], 0.0))(a, b), writes=[t_RX])
            for d in range(2):
                P.op("pool", (lambda d: lambda e: e.memset(BB[d][:, 256:264], 0.0))(d), writes=[t_BB[d]])
            blocks = [(0, 256, CT0)] + [(256 + 512 * b, 512, LT0 + 512 * b) for b in range(8)]

            def ut_trks(t0, n):
                return t_uT[2 * (t0 // 128):2 * ((t0 + n - 1) // 128 + 1)]

            def ut_nat(kc, t0, n):
                if t0 < 256:
                    return uT[:, kc, t0:t0 + n]
                r0 = (t0 - 256) // 64
                return uT[:, kc, 256:NT].rearrange("p (j r) -> p r j", r=64)[:, r0:r0 + n // 64, :]

            def rev(ap2d):
                n = ap2d.shape[1]
                return bass.AP(ap2d.tensor, ap2d.offset + (n - 1), [list(ap2d.ap[0]), [-1, n]])

            def load_wr(c):
                wrx, t_wrx = WR.next()
                wrg, t_wrg = WR.next()
                P.dma("pool", (lambda w, c: lambda e: e.dma_start(
                    out=w[:], in_=win_d[c].rearrange("p (k j) -> p k j", j=128)))(wrx, c), writes=[t_wrx])
                P.dma("pool", (lambda w, c: lambda e: e.dma_start(
                    out=w[:], in_=win_d[8 + c].rearrange("p (k j) -> p k j", j=128)))(wrg, c), writes=[t_wrg])
                return wrx, t_wrx, wrg, t_wrg

            wr_next = load_wr(0)
            for c in range(8):
                wrx, t_wrx, wrg, t_wrg = wr_next
                if c + 1 < 8:
                    wr_next = load_wr(c + 1)
                if c > 0:
                    P.op("dve", lambda e: e.memset(RX[:, 258:261], 0.0), writes=[t_RX])
                pj, t_pj = PJ.next()
                for kc in range(8):
                    P.op("pe", (lambda pj, wrx, kc: lambda e: e.matmul(
                        pj[:, 0:256], lhsT=wrx[:, kc, :], rhs=uT[:, kc, 0:256], start=(kc == 0), stop=(kc == 7)))(
                        pj, wrx, kc), reads=[t_wrx] + t_uT[0:4], writes=[t_pj])
                P.op("act", (lambda pj: lambda e: e.activation(
                    out=RX[:, CT0:CT0 + 256], in_=pj[:, 0:256], func=AF.Copy))(pj), reads=[t_pj], writes=[t_RX])
                for b in range(8):
                    pj, t_pj = PJ.next()
                    for kc in range(8):
                        P.op("pe", (lambda pj, wrx, kc, b: lambda e: e.matmul(
                            pj[:], lhsT=wrx[:, kc, :], rhs=uT[:, kc, 256 + b * 512:256 + (b + 1) * 512], start=(kc == 0), stop=(kc == 7)))(
                            pj, wrx, kc, b), reads=[t_wrx] + t_uT[4:68], writes=[t_pj])
                    P.op("act", (lambda pj, b: lambda e: e.activation(
                        out=RX[:, LT0:LT0 + 4096].rearrange("p (r j) -> p j r", j=64)[:, 8 * b:8 * b + 8, :],
                        in_=pj[:].rearrange("p (j r) -> p j r", r=64), func=AF.Copy))(pj, b), reads=[t_pj], writes=[t_RX])
                L = PEND - 2
                P.op("dve", (lambda c: lambda e: e.tensor_scalar(
                    out=XC[:, 2:PEND], in0=RX[:, 0:L], scalar1=fvc("rgcw", c * 4), scalar2=fvc("rgcb", c),
                    op0=ALU.mult, op1=ALU.add))(c), reads=[t_RX, t_FV], writes=[t_XC])
                for j in range(1, 4):
                    P.op("dve", (lambda c, j: lambda e: e.scalar_tensor_tensor(
                        out=XC[:, 2:PEND], in0=RX[:, j:j + L], scalar=fvc("rgcw", c * 4 + j), in1=XC[:, 2:PEND],
                        op0=ALU.mult, op1=ALU.add))(c, j), reads=[t_RX, t_FV, t_XC], writes=[t_XC])
                P.op("act", lambda e: e.activation(out=XCB[:, 2:PEND], in_=XC[:, 2:PEND], func=AF.Copy),
                     reads=[t_XC], writes=[t_XCB])
                if c == 3:
                    dump("xc", XC[:], t_XC, [128, 4360])
                for d in range(2):
                    Bd, t_Bd = BB[d], t_BB[d]
                    for (t0, n, pos) in blocks:
                        ga, t_ga = GA.next()
                        gx, t_gx = GA.next()
                        P.op("pe", (lambda ga, d, c, n, pos: lambda e: e.matmul(
                            ga[:, 0:n], lhsT=RGBD[:, (d * 2) * 8 + c, :], rhs=XCB[:, pos:pos + n], start=True, stop=True))(
                            ga, d, c, n, pos), reads=[t_RGBD, t_XCB], writes=[t_ga])
                        P.op("pe", (lambda gx, d, c, n, pos: lambda e: e.matmul(
                            gx[:, 0:n], lhsT=RGBD[:, (d * 2 + 1) * 8 + c, :], rhs=XCB[:, pos:pos + n], start=True, stop=True))(
                            gx, d, c, n, pos), reads=[t_RGBD, t_XCB], writes=[t_gx])
                        P.op("act", (lambda ga, d, c, n, pos: lambda e: e.activation(
                            out=RX[:, pos:pos + n], in_=ga[:, 0:n], func=AF.Sigmoid, bias=fvc("rgba", d * 8 + c)))(
                            ga, d, c, n, pos), reads=[t_ga, t_FV], writes=[t_RX])
                        P.op("act", (lambda gx, Bd, d, c, n, pos: lambda e: e.activation(
                            out=Bd[:, pos:pos + n], in_=gx[:, 0:n], func=AF.Sigmoid, bias=fvc("rgbx", d * 8 + c)))(
                            gx, Bd, d, c, n, pos), reads=[t_gx, t_FV], writes=[t_Bd])
                    P.op("act", (lambda d, c: lambda e: e.activation(
                        out=RX[:, 2:PEND], in_=RX[:, 2:PEND], func=AF.Exp, scale=KD[:, d * 8 + c:d * 8 + c + 1]))(d, c),
                        reads=[t_RX, t_KD], writes=[t_RX])
                    P.op("dve", (lambda Bd: lambda e: e.tensor_tensor(
                        out=Bd[:, 2:PEND], in0=Bd[:, 2:PEND], in1=XC[:, 2:PEND], op=ALU.mult))(Bd),
                        reads=[t_Bd, t_XC], writes=[t_Bd])
                    for (ra, rb) in ((2, 2180), (2180, PEND)):
                        P.op("act", (lambda ra, rb: lambda e: e.activation(
                            out=TMP[:, 0:rb - ra], in_=RX[:, ra:rb], func=AF.Square))(ra, rb), reads=[t_RX], writes=[t_TMP])
                        P.op("act", (lambda ra, rb: lambda e: e.activation(
                            out=TMP[:, 0:rb - ra], in_=TMP[:, 0:rb - ra], func=AF.Sqrt, scale=-1.0, bias=1.0))(ra, rb),
                            reads=[t_TMP], writes=[t_TMP])
                        P.op("dve", (lambda Bd, ra, rb: lambda e: e.tensor_tensor(
                            out=Bd[:, ra:rb], in0=Bd[:, ra:rb], in1=TMP[:, 0:rb - ra], op=ALU.mult))(Bd, ra, rb),
                            reads=[t_Bd, t_TMP], writes=[t_Bd])
                    f_ = (lambda ap: ap) if d == 0 else rev
                    c0, c1 = CT0, CT0 + 256
                    l0, l1 = LT0, LT0 + 4096
                    P.op("dve", (lambda Bd, f_: lambda e: e.tensor_tensor_scan(
                        out=f_(Bd[:, c0:c1]), data0=f_(RX[:, c0:c1]), data1=f_(Bd[:, c0:c1]), initial=0.0,
                        op0=ALU.mult, op1=ALU.add))(Bd, f_), reads=[t_RX, t_Bd], writes=[t_Bd])
                    ini = (c1 - 1) if d == 0 else c0
                    P.op("dve", (lambda Bd, f_, ini: lambda e: e.tensor_tensor_scan(
                        out=f_(Bd[:, l0:l1]), data0=f_(RX[:, l0:l1]), data1=f_(Bd[:, l0:l1]), initial=Bd[:, ini:ini + 1],
                        op0=ALU.mult, op1=ALU.add))(Bd, f_, ini), reads=[t_RX, t_Bd], writes=[t_Bd])
                ys, t_ys = YS.next()
                for b in range(8):
                    pj, t_pj = PJ.next()
                    gl, t_gl = GL.next()
                    ts, t_ts = TS.next()
                    for kc in range(8):
                        P.op("pe", (lambda pj, wrg, kc, b: lambda e: e.matmul(
                            pj[:], lhsT=wrg[:, kc, :], rhs=uT[:, kc, 256 + b * 512:256 + (b + 1) * 512], start=(kc == 0), stop=(kc == 7)))(
                            pj, wrg, kc, b), reads=[t_wrg] + t_uT[4:68], writes=[t_pj])
                    P.op("act", (lambda pj, gl: lambda e: e.activation(out=gl[:], in_=pj[:], func=AF.Gelu))(pj, gl),
                         reads=[t_pj], writes=[t_gl])
                    P.op("dve", (lambda ts, b: lambda e: e.tensor_tensor(
                        out=ts[:].rearrange("p (j r) -> p j r", r=64),
                        in0=BB[0][:, LT0:LT0 + 4096].rearrange("p (r j) -> p j r", j=64)[:, 8 * b:8 * b + 8, :],
                        in1=BB[1][:, LT0:LT0 + 4096].rearrange("p (r j) -> p j r", j=64)[:, 8 * b:8 * b + 8, :], op=ALU.add))(ts, b),
                        reads=t_BB, writes=[t_ts])
                    P.op("dve", (lambda ys, ts, gl, b: lambda e: e.tensor_tensor(
                        out=ys[:, b * 512:(b + 1) * 512], in0=ts[:], in1=gl[:], op=ALU.mult))(ys, ts, gl, b),
                        reads=[t_ts, t_gl], writes=[t_ys])
                P.dma("sp", (lambda ys, c: lambda e: e.dma_start(out=YRG_d[c], in_=ys[:]))(ys, c), reads=[t_ys], writes=[t_YRG[c]],
                      semtrk=t_ys)
                if c == 3:
                    if "hrg" in debug:
                        dump("hrg", HD[:], t_HD, [128, NLAT])
                    dump("yrg", ys[:], t_ys, [128, NLAT], BF16)
            P.barrier()
            P.emit()
        if stop_after == "rg":
            P.wait_all("sp", P.out_toks)
            P.barrier()
            P.emit()
            ust.close()
            return nc, dbg_d

        t_HF = [Trk("HF%d" % i) for i in range(32)]
        t_YML = [Trk("YML%d" % i) for i in range(16)]
        t_SPQ = [[Trk("SPQ%d_%d" % (g, i)) for i in range(4)] for g in range(17)]
        t_SPX = [Trk("SPX%d" % g) for g in range(16)]
        t_spd = [Trk("spd%d" % i) for i in range(5)]
        with ExitStack() as st:
            sb = lambda name, shape, dt: st.enter_context(nc.sbuf_tensor(name, list(shape), dt))
            ps = lambda name, shape, dt: st.enter_context(nc.psum_tensor(name, list(shape), dt))
            MLBD = sb("MLBD", [128, 24, 128], BF16); t_MLBD = Trk("MLBD")
            WGT = sb("WGT", [128, 24, 16], BF16); t_WGT = Trk("WGT")
            GBI = sb("GBI", [128, 2, 16], F32); t_GBI = Trk("GBI")
            TRI = sb("TRI", [128, 2, 128], F32); t_TRI = Trk("TRI")
            ONES = sb("ONES", [128, 128], F32); t_ONES = Trk("ONES")
            P.dma("pool", lambda e: e.dma_start(out=MLBD[:], in_=mlbd_d, max_dma_last_dim=4096), writes=[t_MLBD])
            P.dma("pool", lambda e: e.dma_start(out=WGT[:], in_=wgt_d), writes=[t_WGT])
            P.dma("sp", lambda e: e.dma_start(out=GBI[:], in_=gbias_d), writes=[t_GBI])
            P.dma("sp", lambda e: e.dma_start(out=TRI[:], in_=tri_d), writes=[t_TRI])
            P.op("dve", lambda e: e.memset(ONES[:], 1.0), writes=[t_ONES])
            B_PM = ps("B_PM", [128, 512], F32); t_PM = Trk("PS_PM")
            B_QK = ps("B_QK", [128, 512], F32); t_PQ = Trk("PS_PQ")
            B_VO = ps("B_VO", [128, 512], F32); t_PV = Trk("PS_PV")
            B_GP = ps("B_GP", [128, 512], F32); t_GP = Trk("PS_GP")
            B_N = ps("B_N", [128, 4, 512], F32); t_N = [Trk("PS_N%d" % i) for i in range(4)]
            BU = [B_PM, B_QK, B_VO, B_GP]; t_BU = [t_PM, t_PQ, t_PV, t_GP]
            VT = Rot(sb, "VT", 2, [128, 256], BF16)
            XM = sb("XM", [128, 8, 256], F32); t_XM = [Trk("XM%d" % c) for c in range(8)]
            XMB = sb("XMB", [128, 8, 256], BF16); t_XMB = [Trk("XMB%d" % c) for c in range(8)]
            UMB = sb("UMB", [128, 8, 256], BF16); t_UMB = [Trk("UMB%d" % c) for c in range(8)]
            QT = sb("QT", [128, 8, 256], BF16); t_QT = [Trk("QT%d" % c) for c in range(8)]
            KT = sb("KT", [128, 8, 256], BF16); t_KT = [Trk("KT%d" % c) for c in range(8)]
            GF = sb("GF", [8, 256], F32); t_GF = Trk("GF")
            GG = sb("GG", [128, 16], F32); t_GG = Trk("GG")
            GE = sb("GE", [128, 8], F32); t_GE = Trk("GE")
            GLn = sb("GLn", [128, 8], F32); t_GLn = Trk("GLn")
            GT2 = sb("GT2", [128, 8], F32); t_GT2 = Trk("GT2")
            EB = sb("EB", [128, 8], F32); t_EB = Trk("EB")
            WS = sb("WS", [128, 8], F32); t_WS = Trk("WS")
            EBL = sb("EBL", [128, 8], F32); t_EBL = Trk("EBL")
            DS = sb("DS", [128, 4, 2, 257], F32); t_DS = [Trk("DS%d" % h) for h in range(4)]
            DB = sb("DB", [128, 4, 2, 258], BF16); t_DB = [Trk("DB%d" % h) for h in range(4)]
            VX = sb("VX", [128, 2, 4, 258], BF16); t_VX = [Trk("VX%d" % i) for i in range(2)]
            KTM = sb("KTM", [128, 2, 1024], BF16); t_KTM = [Trk("KTM%d" % i) for i in range(2)]
            STt = sb("STt", [128, 2, 4, 128], BF16); t_STt = [Trk("STt%d" % i) for i in range(2)]
            E1 = sb("E1", [128, 8, 4], F32); t_E1 = Trk("E1")
            HH = Rot(sb, "HH", 1, [128, 1024], F32)

            def grp_rhs(kc, g):
                if g == 0:
                    return uT[:, kc, 0:256]
                gi = g - 1
                return uT[:, kc, 256 + gi * 256:256 + (gi + 1) * 256]

            def grp_trks(g):
                return t_uT[0:4] if g == 0 else t_uT[4:68]

            def m1b_v(ch, UMB, t_UMB):
                cols = slice(ch * 128, (ch + 1) * 128)
                for c in range(8):
                    h, half = divmod(c, 2)
                    bk = h // 2
                    off = (h % 2) * 256 + half * 128
                    P.op("pe", (lambda c, bk, off, cols, UMB: lambda e: e.matmul(
                        B_N[:, bk, off:off + 128], lhsT=UMB[:, c, cols], rhs=MLBD[:, 16 + c, :], start=True, stop=True))(c, bk, off, cols, UMB),
                        reads=[t_UMB[c], t_MLBD], writes=[t_N[bk]])

            def m1b_pre(d, QT, KT, XMB, UMB, t_QT, t_KT, t_XMB, t_UMB):
                for ch in range(2):
                    cols = slice(ch * 128, (ch + 1) * 128)
                    for c in range(8):
                        h, half = divmod(c, 2)
                        bk = h // 2
                        off = (h % 2) * 256 + half * 128
                        P.op("pe", (lambda c, bk, off, cols, XMB: lambda e: e.matmul(
                            B_N[:, 2 + bk, off:off + 128], lhsT=XMB[:, c, cols], rhs=MLBD[:, 8 + c, :], start=True, stop=True))(c, bk, off, cols, XMB),
                            reads=[t_XMB[c], t_MLBD], writes=[t_N[2 + bk]])
                    for bk in range(2):
                        P.op("act", (lambda ch, bk: lambda e: e.activation(
                            out=KTM[:, ch, bk * 512:(bk + 1) * 512], in_=B_N[:, 2 + bk, :], func=AF.Copy, scale=1.0 / 16.0))(ch, bk),
                            reads=[t_N[2 + bk]], writes=[t_KTM[ch]])
                    for h in range(4):
                        for half in range(2):
                            c = 2 * h + half
                            P.op("pe", (lambda c, h, half, cols, KT, QT: lambda e: e.matmul(
                                B_GP[:, h * 128:(h + 1) * 128], lhsT=KT[:, c, cols], rhs=QT[:, c, cols], start=(half == 0), stop=(half == 1)))(c, h, half, cols, KT, QT),
                                reads=[t_KT[c], t_QT[c]], writes=[t_GP])
                    P.op("dve", (lambda ch, d: lambda e: e.tensor_tensor(
                        out=STt[:, ch], in0=B_GP[:].rearrange("p (h t) -> p h t", t=128),
                        in1=TRI[:, d, :].unsqueeze(1).to_broadcast([128, 4, 128]), op=ALU.mult))(ch, d),
                        reads=[t_GP, t_TRI], writes=[t_STt[ch]])
                m1b_v(0, UMB, t_UMB)

            def gates_post(d, mid=None):
                P.op("act", lambda e: e.activation(out=GF[:], in_=B_GP[0:8, 0:256], func=AF.Copy), reads=[t_GP], writes=[t_GF])
                for ch in range(2):
                    P.op("pe", (lambda ch: lambda e: e.transpose(
                        out=B_GP[:, 256 + ch * 8:256 + ch * 8 + 8], in_=GF[0:8, ch * 128:(ch + 1) * 128], identity=IDF[0:8, 0:8]))(ch),
                        reads=[t_GF, t_IDF], writes=[t_GP])
                P.op("dve", (lambda d: lambda e: e.tensor_tensor(
                    out=GG[:], in0=B_GP[:, 256:272], in1=GBI[:, d, :], op=ALU.add))(d), reads=[t_GP, t_GBI], writes=[t_GG])
                GGv = GG[:].rearrange("t (c k) -> t c k", k=8)
                P.op("act", lambda e: e.activation(
                    out=GE[:].rearrange("t (c h) -> t c h", h=4), in_=GGv[:, :, 4:8], func=AF.Exp, scale=-1.0),
                    reads=[t_GG], writes=[t_GE])
                P.op("act", lambda e: e.activation(out=GLn[:], in_=GE[:], func=AF.Ln, bias=1.0), reads=[t_GE], writes=[t_GLn])
                if mid is not None:
                    mid()
                P.op("pe", (lambda d: lambda e: e.matmul(
                    B_GP[:, 288:296], lhsT=TRI[:, d, :], rhs=GLn[:], start=True, stop=True))(d),
                    reads=[t_TRI, t_GLn], writes=[t_GP])
                P.op("pe", lambda e: e.matmul(B_GP[:, 304:312], lhsT=ONES[:], rhs=GLn[:], start=True, stop=True),
                     reads=[t_ONES, t_GLn], writes=[t_GP])
                P.op("act", lambda e: e.activation(out=EB[:], in_=B_GP[:, 288:296], func=AF.Exp, scale=-1.0),
                     reads=[t_GP], writes=[t_EB])
                P.op("dve", lambda e: e.tensor_tensor(
                    out=GT2[:].rearrange("t (c h) -> t c h", h=4), in0=B_GP[:, 288:296].rearrange("t (c h) -> t c h", h=4),
                    in1=GGv[:, :, 0:4], op=ALU.add), reads=[t_GP, t_GG], writes=[t_GT2])
                P.op("act", lambda e: e.activation(out=WS[:], in_=GT2[:], func=AF.Exp), reads=[t_GT2], writes=[t_WS])
                P.op("act", lambda e: e.activation(out=EBL[:], in_=B_GP[:, 304:312], func=AF.Exp, scale=-1.0),
                     reads=[t_GP], writes=[t_EBL])
            def rec_group(g, d, QT, KT, XMB, UMB, t_QT, t_KT, t_XMB, t_UMB, XM, t_XM, hs):
                lat = g > 0
                gi = g - 1
                have_state = hs[0]
                for ch in range(2):
                    cols = slice(ch * 128, (ch + 1) * 128)
                    if ch == 1:
                        m1b_v(ch, UMB, t_UMB)
                    for h in range(4):
                        bk = h // 2
                        off = (h % 2) * 256
                        P.op("dve", (lambda ch, h, bk, off: lambda e: e.tensor_scalar(
                            out=VX[:, ch, h, 0:256], in0=B_N[:, bk, off:off + 256], scalar1=WS[:, ch * 4 + h:ch * 4 + h + 1],
                            scalar2=None, op0=ALU.mult))(ch, h, bk, off), reads=[t_N[bk], t_WS], writes=[t_VX[ch]])
                    P.op("act", (lambda ch: lambda e: e.activation(
                        out=VX[:, ch, :, 256:257], in_=WS[:, ch * 4:ch * 4 + 4].unsqueeze(2), func=AF.Copy))(ch), reads=[t_WS], writes=[t_VX[ch]])
                chs = (0, 1) if d == 0 else (1, 0)
                if d == 1 and lat:
                    yg, t_yg = YG.next()
                for ch in chs:
                    cols = slice(ch * 128, (ch + 1) * 128)
                    if d == 1 and lat:
                        for cq in (gi * 2 + ch, gi * 2 + ch - 1):
                            if cq >= 0 and cq not in hft_map:
                                hb, t_hb = HFt.next()
                                P.dma("sp", (lambda hb, cq: lambda e: e.dma_start(out=hb[:], in_=HF_d[cq]))(hb, cq),
                                      reads=[t_HF[cq]], writes=[t_hb])
                                hft_map[cq] = (hb, t_hb)
                    last_chunk = (d == 0 and g == 16 and ch == 1) or (d == 1 and g == 1 and ch == 0)
                    if not last_chunk:
                        for r in range(2):
                            for hh2 in range(2):
                                h = 2 * r + hh2
                                for half in range(2):
                                    bi_ = hh2 * 2 + half
                                    P.op("pe", (lambda ch, h, half, bi_: lambda e: e.matmul(
                                        BU[bi_][:, 0:257], lhsT=KTM[:, ch, h * 256 + half * 128:h * 256 + (half + 1) * 128],
                                        rhs=VX[:, ch, h, 0:257], start=True, stop=True))(ch, h, half, bi_),
                                        reads=[t_KTM[ch], t_VX[ch]], writes=[t_BU[bi_]])
                            for hh2 in range(2):
                                h = 2 * r + hh2
                                for half in range(2):
                                    bi_ = hh2 * 2 + half
                                    if have_state:
                                        P.op("dve", (lambda h, half, bi_: lambda e: e.tensor_tensor(
                                            out=DS[:, h, half, :], in0=DS[:, h, half, :], in1=BU[bi_][:, 0:257], op=ALU.add))(h, half, bi_),
                                            reads=[t_DS[h], t_BU[bi_]], writes=[t_DS[h]])
                                    else:
                                        P.op("dve", (lambda h, half, bi_: lambda e: e.tensor_copy(
                                            out=DS[:, h, half, :], in_=BU[bi_][:, 0:257]))(h, half, bi_),
                                            reads=[t_BU[bi_]], writes=[t_DS[h]])
                    if lat:
                        hh, t_hh = HH.next()
                        for h in range(4):
                            P.op("pe", (lambda ch, h, hs: lambda e: e.matmul(
                                B_N[:, h, 0:257], lhsT=STt[:, ch, h, :], rhs=VX[:, ch, h, 0:257], start=True, stop=(not hs)))(ch, h, have_state),
                                reads=[t_STt[ch], t_VX[ch]], writes=[t_N[h]])
                            if have_state:
                                for half in range(2):
                                    c = 2 * h + half
                                    P.op("pe", (lambda c, h, half, cols: lambda e: e.matmul(
                                        B_N[:, h, 0:257], lhsT=QT[:, c, cols], rhs=DB[:, h, half, 0:257], start=False, stop=(half == 1)))(c, h, half, cols),
                                        reads=[t_QT[c], t_DB[h]], writes=[t_N[h]])
                        e0 = ch * 4
                        P.op("dve", (lambda ch: lambda e: e.tensor_tensor(
                            out=E1[:, 0:4, 0], in0=B_N[:, :, 256], in1=EB[:, ch * 4:ch * 4 + 4], op=ALU.mult))(ch),
                            reads=t_N + [t_EB], writes=[t_E1])
                        P.op("dve", lambda e: e.tensor_scalar(
                            out=E1[:, 0:4, 1], in0=E1[:, 0:4, 0], scalar1=-1.0, scalar2=1.0, op0=ALU.mult, op1=ALU.max),
                            reads=[t_E1], writes=[t_E1])
                        P.op("dve", lambda e: e.scalar_tensor_tensor(
                            out=E1[:, 0:4, 2], in0=E1[:, 0:4, 0], scalar=1.0, in1=E1[:, 0:4, 1], op0=ALU.max, op1=ALU.max),
                            reads=[t_E1], writes=[t_E1])
                        P.op("dve", lambda e: e.reciprocal(out=E1[:, 0:4, 3], in_=E1[:, 0:4, 2]), reads=[t_E1], writes=[t_E1])
                        P.op("dve", (lambda ch: lambda e: e.tensor_tensor(
                            out=E1[:, 4:8, 0], in0=E1[:, 0:4, 3], in1=EB[:, ch * 4:ch * 4 + 4], op=ALU.mult))(ch),
                            reads=[t_E1, t_EB], writes=[t_E1])
                        for h in range(4):
                            P.op("act", (lambda hh, h: lambda e: e.activation(
                                out=hh[:, h * 256:(h + 1) * 256], in_=B_N[:, h, 0:256], func=AF.Copy, scale=E1[:, 4 + h, 0:1]))(hh, h),
                                reads=[t_N[h], t_E1], writes=[t_hh])
                    if not last_chunk:
                        for h in range(4):
                            idx = ch * 4 + h
                            P.op("act", (lambda h, idx: lambda e: e.activation(
                                out=DB[:, h, :, 0:257], in_=DS[:, h, :, :], func=AF.Copy, scale=EBL[:, idx:idx + 1]))(h, idx),
                                reads=[t_DS[h], t_EBL], writes=[t_DB[h]])
                            P.op("dve", (lambda h, idx: lambda e: e.tensor_scalar(
                                out=DS[:, h, :, :], in0=DS[:, h, :, :], scalar1=EBL[:, idx:idx + 1], scalar2=None, op0=ALU.mult))(h, idx),
                                reads=[t_DS[h], t_EBL], writes=[t_DS[h]])
                    have_state = True; hs[0] = True
                    if not lat:
                        continue
                    cg = gi * 2 + ch
                    if d == 0:
                        P.dma("sp", (lambda hh, cg: lambda e: e.dma_start(out=HF_d[cg], in_=hh[:]))(hh, cg),
                              reads=[t_hh], writes=[t_HF[cg]], semtrk=t_hh)
                        continue
                    hft, t_hft = hft_map.pop(cg)
                    P.op("dve", (lambda hh, hft: lambda e: e.tensor_tensor(out=hh[:], in0=hh[:], in1=hft[:], op=ALU.add))(hh, hft),
                         reads=[t_hh, t_hft], writes=[t_hh])
                    for h in range(4):
                        P.op("dve", (lambda hh, h: lambda e: e.bn_stats(out=BS[:, h, :], in_=hh[:, h * 256:(h + 1) * 256]))(hh, h),
                             reads=[t_hh], writes=[t_BS])
                        P.op("dve", (lambda h: lambda e: e.bn_aggr(out=MV[:, h, :], in_=BS[:, h, :]))(h), reads=[t_BS], writes=[t_MV])
                    P.op("act", lambda e: e.activation(out=SD[:, 0:4], in_=MV[:, :, 1], func=AF.Sqrt, bias=EPS), reads=[t_MV], writes=[t_SD])
                    P.op("dve", lambda e: e.reciprocal(out=SD[:, 4:8], in_=SD[:, 0:4]), reads=[t_SD], writes=[t_SD])
                    for h in range(4):
                        P.op("dve", (lambda hh, h: lambda e: e.tensor_scalar(
                            out=HN[:, h * 256:(h + 1) * 256], in0=hh[:, h * 256:(h + 1) * 256], scalar1=MV[:, h, 0:1],
                            scalar2=SD[:, 4 + h:5 + h], op0=ALU.subtract, op1=ALU.mult))(hh, h),
                            reads=[t_hh, t_MV, t_SD], writes=[t_HN])
                    for c in range(8):
                        P.op("pe", (lambda c: lambda e: e.transpose(
                            out=B_N[:, c // 4, (c % 4) * 128:(c % 4 + 1) * 128], in_=HN[:, c * 128:(c + 1) * 128], identity=IDF[:]))(c),
                            reads=[t_HN, t_IDF], writes=[t_N[c // 4]])
                    o_, w_ = FVCOLS["mlng"]
                    for b2 in range(2):
                        P.op("dve", (lambda b2: lambda e: e.tensor_tensor(
                            out=Y1[:, 4 * b2:4 * b2 + 4, :], in0=B_N[:, b2, :].rearrange("p (c t) -> p c t", t=128),
                            in1=FV[:, o_ + 4 * b2:o_ + 4 * b2 + 4].unsqueeze(2).to_broadcast([128, 4, 128]), op=ALU.mult))(b2),
                            reads=[t_N[b2], t_FV], writes=[t_Y1])
                    P.op("dve", (lambda cols: lambda e: e.tensor_tensor(out=Y1[:], in0=Y1[:], in1=XM[:, :, cols], op=ALU.add))(cols),
                         reads=[t_Y1] + t_XM, writes=[t_Y1])
                    P.op("dve", (lambda yg, cols: lambda e: e.tensor_tensor(out=yg[:, :, cols], in0=Y1[:], in1=SIG[:, :, cols], op=ALU.mult))(yg, cols),
                         reads=[t_Y1] + t_SIG, writes=[t_yg])
                if d == 1 and lat:
                    P.dma("sp", (lambda yg, gi: lambda e: e.dma_start(
                        out=YML_d[:, :, gi * 256:(gi + 1) * 256].rearrange("c p t -> p c t"), in_=yg[:]))(yg, gi),
                        reads=[t_yg], writes=[t_YML[gi]], semtrk=t_yg)
            for d in range(2):
                with ExitStack() as st2:
                    sb2 = lambda name, shape, dt: st2.enter_context(nc.sbuf_tensor(name, list(shape), dt))
                    if d == 0:
                        WMX = sb2("WMX", [128, 8, 8, 128], BF16); t_WMX = [Trk("WMX%d" % c) for c in range(8)]
                        for c in range(8):
                            P.dma("pool", (lambda c: lambda e: e.dma_start(
                                out=WMX[:, c], in_=win_d[16 + c].rearrange("p (k j) -> p k j", j=128)))(c), writes=[t_WMX[c]])
                        UMF = Rot(sb2, "UMF", 2, [128, 260], F32)
                        HALO = sb2("HALO", [128, 8, 2], F32); t_HALO = [Trk("HALO%d" % c) for c in range(8)]
                        XCV = Rot(sb2, "XCV", 2, [128, 256], F32)
                    if d == 1:
                        WMO = sb2("WMO", [128, 8, 8, 128], BF16); t_WMO = [Trk("WMO%d" % c) for c in range(8)]
                        for c in range(8):
                            P.dma("pool", (lambda c: lambda e: e.dma_start(
                                out=WMO[:, c], in_=win_d[24 + c].rearrange("p (k j) -> p k j", j=128)))(c), writes=[t_WMO[c]])
                        SIG = sb2("SIG", [128, 8, 256], F32); t_SIG = [Trk("SIG%d" % c) for c in range(8)]
                        HFt = Rot(sb2, "HFt", 2, [128, 1024], F32)
                        hft_map = {}
                        HN = sb2("HN", [128, 1024], F32); t_HN = Trk("HN")
                        BS = sb2("BS", [128, 4, 6], F32); t_BS = Trk("BS")
                        MV = sb2("MV", [128, 4, 2], F32); t_MV = Trk("MV")
                        SD = sb2("SD", [128, 8], F32); t_SD = Trk("SD")
                        Y1 = sb2("Y1", [128, 8, 128], F32); t_Y1 = Trk("Y1")
                        YG = Rot(sb2, "YG", 2, [128, 8, 256], BF16)
                        GS1 = [sb2("QT1", [128, 8, 256], BF16), sb2("KT1", [128, 8, 256], BF16),
                               sb2("XMB1", [128, 8, 256], BF16), sb2("UMB1", [128, 8, 256], BF16)]
                        GSETS = [((QT, KT, XMB, UMB), [Trk("gs0_%d" % i) for i in range(4)]),
                                 (tuple(GS1), [Trk("gs1_%d" % i) for i in range(4)])]
                        t_XMl = Trk("XMl")
                    hs = [False]
                    order = [0] + (list(range(1, 17)) if d == 0 else list(range(16, 0, -1)))
                    for gpos, g in enumerate(order):
                        lat = g > 0
                        gi = g - 1
                        if d == 0:
                            import os as _os2
                            PB = [B_N[:, 0, :], B_N[:, 1, :]]
                            t_PB = [t_N[0], t_N[1]]
                            if _os2.environ.get('PBPM'):
                                PB = [B_PM[:], B_PM[:]]; t_PB = [t_PM, t_PM]
                            hi = 259 if (lat and gi <= 14) else 258
                            lo = 0 if (lat and gi >= 1) else 2

                            def emit_proj(c):
                                pb, t_pb = PB[c % 2], t_PB[c % 2]
                                for kc in range(8):
                                    P.op("pe", (lambda pb, c, kc, g: lambda e: e.matmul(
                                        pb[:, 2:258], lhsT=WMX[:, c, kc, :], rhs=grp_rhs(kc, g), start=(kc == 0), stop=(kc == 7)))(pb, c, kc, g),
                                        reads=[t_WMX[c]] + grp_trks(g), writes=[t_pb])
                                if hi == 259:
                                    b0 = 256 + (gi + 1) * 256
                                    for kc in range(8):
                                        P.op("pe", (lambda pb, c, kc, b0: lambda e: e.matmul(
                                            pb[:, 258:259], lhsT=WMX[:, c, kc, :], rhs=uT[:, kc, b0:b0 + 1], start=(kc == 0), stop=(kc == 7)))(pb, c, kc, b0),
                                            reads=[t_WMX[c]] + grp_trks(g), writes=[t_pb])

                            bufs_c = {}

                            def emit_evac_act(c):
                                pb, t_pb = PB[c % 2], t_PB[c % 2]
                                umf, t_umf = UMF.next()
                                xcv, t_xcv = XCV.next()
                                bufs_c[c] = (umf, t_umf, xcv, t_xcv)
                                P.op("act", (lambda umf, pb, hi: lambda e: e.activation(
                                    out=umf[:, 2:hi], in_=pb[:, 2:hi], func=AF.Copy))(umf, pb, hi), reads=[t_pb], writes=[t_umf])
                                P.op("act", (lambda c, pb: lambda e: e.activation(out=UMB[:, c, :], in_=pb[:, 2:258], func=AF.Copy))(c, pb),
                                     reads=[t_pb], writes=[t_UMB[c]])

                            def emit_conv(c):
                                umf, t_umf, xcv, t_xcv = bufs_c[c]
                                if lo == 0:
                                    P.op("dve", (lambda umf, c: lambda e: e.tensor_copy(out=umf[:, 0:2], in_=HALO[:, c, :]))(umf, c),
                                         reads=[t_HALO[c]], writes=[t_umf])
                                else:
                                    P.op("dve", (lambda umf: lambda e: e.memset(umf[:, 0:2], 0.0))(umf), writes=[t_umf])
                                if hi == 258:
                                    P.op("dve", (lambda umf: lambda e: e.memset(umf[:, 258:259], 0.0))(umf), writes=[t_umf])
                                if lat and gi <= 14:
                                    P.op("dve", (lambda umf, c: lambda e: e.tensor_copy(out=HALO[:, c, :], in_=umf[:, 256:258]))(umf, c),
                                         reads=[t_umf], writes=[t_HALO[c]])
                                P.op("dve", (lambda umf, xcv, c: lambda e: e.tensor_scalar(
                                    out=xcv[:], in0=umf[:, 0:256], scalar1=fvc("mlcw", c * 4), scalar2=fvc("mlcb", c),
                                    op0=ALU.mult, op1=ALU.add))(umf, xcv, c), reads=[t_umf, t_FV], writes=[t_xcv])
                                for j in range(1, 4):
                                    P.op("dve", (lambda umf, xcv, c, j: lambda e: e.scalar_tensor_tensor(
                                        out=xcv[:], in0=umf[:, j:j + 256], scalar=fvc("mlcw", c * 4 + j), in1=xcv[:],
                                        op0=ALU.mult, op1=ALU.add))(umf, xcv, c, j), reads=[t_umf, t_FV, t_xcv], writes=[t_xcv])
                                P.op("act", (lambda xcv, c: lambda e: e.activation(out=XM[:, c, :], in_=xcv[:], func=AF.Silu))(xcv, c),
                                     reads=[t_xcv], writes=[t_XM[c]])
                                P.op("act", (lambda xcv, c: lambda e: e.activation(out=XMB[:, c, :], in_=xcv[:], func=AF.Silu))(xcv, c),
                                     reads=[t_xcv], writes=[t_XMB[c]])

                            vts = {}

                            def emit_qkv(c):
                                vt, t_vt = VT.next()
                                vts[c] = (vt, t_vt)
                                P.op("pe", (lambda c: lambda e: e.matmul(
                                    B_QK[:, 0:256], lhsT=MLBD[:, c, :], rhs=XMB[:, c, :], start=True, stop=True))(c),
                                    reads=[t_MLBD, t_XMB[c]], writes=[t_PQ])
                                P.op("pe", (lambda c: lambda e: e.matmul(
                                    B_QK[:, 256:512], lhsT=MLBD[:, 8 + c, :], rhs=XMB[:, c, :], start=True, stop=True))(c),
                                    reads=[t_MLBD, t_XMB[c]], writes=[t_PQ])
                                P.op("pe", (lambda c: lambda e: e.matmul(
                                    B_VO[:, 0:256], lhsT=MLBD[:, 16 + c, :], rhs=UMB[:, c, :], start=True, stop=True))(c),
                                    reads=[t_MLBD, t_UMB[c]], writes=[t_PV])
                                P.op("act", (lambda c: lambda e: e.activation(out=QT[:, c, :], in_=B_QK[:, 0:256], func=AF.Copy))(c),
                                     reads=[t_PQ], writes=[t_QT[c]])
                                P.op("act", (lambda c: lambda e: e.activation(
                                    out=KT[:, c, :], in_=B_QK[:, 256:512], func=AF.Copy, scale=1.0 / 16.0))(c),
                                    reads=[t_PQ], writes=[t_KT[c]])
                                P.op("dve", (lambda vt: lambda e: e.tensor_copy(out=vt[:], in_=B_VO[:, 0:256]))(vt),
                                     reads=[t_PV], writes=[t_vt])

                            def emit_gates(c):
                                vt, t_vt = vts[c]
                                for ti, (src, t_src) in enumerate(((QT[:, c, :], t_QT[c]), (KT[:, c, :], t_KT[c]), (vt[:], t_vt))):
                                    P.op("pe", (lambda c, ti, src, d: lambda e: e.matmul(
                                        B_GP[0:8, 0:256], lhsT=WGT[:, ti * 8 + c, d * 8:(d + 1) * 8], rhs=src,
                                        start=(c == 0 and ti == 0), stop=(c == 7 and ti == 2)))(c, ti, src, d),
                                        reads=[t_WGT, t_src], writes=[t_GP])

                            if _os2.environ.get("NOPIPE"):
                                for c in range(8):
                                    emit_proj(c)
                                    emit_evac_act(c)
                                    emit_conv(c)
                                    emit_qkv(c)
                                    emit_gates(c)
                            else:
                                emit_proj(0)
                                emit_proj(1)
                                emit_evac_act(0)
                                for c in range(8):
                                    if c + 2 < 8:
                                        emit_proj(c + 2)
                                    if c + 1 < 8:
                                        emit_evac_act(c + 1)
                                    emit_conv(c)
                                    emit_qkv(c)
                                    if c > 0:
                                        emit_gates(c - 1)
                                emit_gates(7)
                            for wi_, (arr, trs) in enumerate(((QT, t_QT), (KT, t_KT), (XMB, t_XMB), (UMB, t_UMB))):
                                P.dma("sp", (lambda arr, g, wi_: lambda e: e.dma_start(out=SPQ_d[g, wi_], in_=arr[:]))(arr, g, wi_),
                                      reads=trs, writes=[t_SPQ[g][wi_]], semtrk=t_spd[wi_])
                            if lat:
                                P.dma("sp", (lambda gi: lambda e: e.dma_start(out=SPX_d[gi], in_=XM[:]))(gi),
                                      reads=t_XM, writes=[t_SPX[gi]], semtrk=t_spd[4])
                            gates_post(d, mid=lambda: m1b_pre(d, QT, KT, XMB, UMB, t_QT, t_KT, t_XMB, t_UMB))
                            rec_group(g, d, QT, KT, XMB, UMB, t_QT, t_KT, t_XMB, t_UMB, XM, t_XM, hs)
                            continue
                        def load_set(gq, si):
                            arrs, trs = GSETS[si]
                            for wi_ in range(4):
                                P.dma("sp", (lambda arrs, gq, wi_: lambda e: e.dma_start(out=arrs[wi_][:], in_=SPQ_d[gq, wi_]))(arrs, gq, wi_),
                                      reads=[t_SPQ[gq][wi_]], writes=[trs[wi_]])
                        if gpos == 0:
                            load_set(g, 0)
                        if gpos + 1 < len(order):
                            load_set(order[gpos + 1], (gpos + 1) % 2)
                        (QTg, KTg, XMBg, UMBg), trs = GSETS[gpos % 2]
                        if lat:
                            P.dma("sp", (lambda gi: lambda e: e.dma_start(out=XM[:], in_=SPX_d[gi]))(gi), reads=[t_SPX[gi]], writes=[t_XMl])
                        for c in range(8):
                            vt, t_vt = VT.next()
                            P.op("pe", (lambda c, UMBg: lambda e: e.matmul(
                                B_VO[:, 0:256], lhsT=MLBD[:, 16 + c, :], rhs=UMBg[:, c, :], start=True, stop=True))(c, UMBg),
                                reads=[t_MLBD, trs[3]], writes=[t_PV])
                            P.op("dve", (lambda vt: lambda e: e.tensor_copy(out=vt[:], in_=B_VO[:, 0:256]))(vt),
                                 reads=[t_PV], writes=[t_vt])
                            if lat:
                                for kc in range(8):
                                    P.op("pe", (lambda c, kc, g: lambda e: e.matmul(
                                        B_PM[:, 0:256], lhsT=WMO[:, c, kc, :], rhs=grp_rhs(kc, g), start=(kc == 0), stop=(kc == 7)))(c, kc, g),
                                        reads=[t_WMO[c]] + grp_trks(g), writes=[t_PM])
                                P.op("act", (lambda c: lambda e: e.activation(out=SIG[:, c, :], in_=B_PM[:, 0:256], func=AF.Sigmoid))(c),
                                     reads=[t_PM], writes=[t_SIG[c]])
                            for ti, (src, t_src) in enumerate(((QTg[:, c, :], trs[0]), (KTg[:, c, :], trs[1]), (vt[:], t_vt))):
                                P.op("pe", (lambda c, ti, src, d: lambda e: e.matmul(
                                    B_GP[0:8, 0:256], lhsT=WGT[:, ti * 8 + c, d * 8:(d + 1) * 8], rhs=src,
                                    start=(c == 0 and ti == 0), stop=(c == 7 and ti == 2)))(c, ti, src, d),
                                    reads=[t_WGT, t_src], writes=[t_GP])
                        if lat:
                            o2_, w2_ = FVCOLS["mlsk"]
                            P.op("dve", lambda e: e.tensor_tensor(
                                out=XM[:], in0=XM[:], in1=FV[:, o2_:o2_ + 8].unsqueeze(2).to_broadcast([128, 8, 256]), op=ALU.mult),
                                reads=[t_XMl, t_FV], writes=[t_XMl])
                        gates_post(d, mid=lambda: m1b_pre(d, QTg, KTg, XMBg, UMBg, [trs[0]] * 8, [trs[1]] * 8, [trs[2]] * 8, [trs[3]] * 8))
                        rec_group(g, d, QTg, KTg, XMBg, UMBg, [trs[0]] * 8, [trs[1]] * 8, [trs[2]] * 8, [trs[3]] * 8, XM, [t_XMl] * 8, hs)
                    P.barrier()
                    P.emit()
        if "yml" in debug:
            dbg_d["yml"] = YML_d
        if stop_after == "ml":
            P.wait_all("sp", P.out_toks)
            P.barrier()
            P.emit()
            ust.close()
            return nc, dbg_d

        x_cm = x_d.rearrange("(r j) d -> j r d", j=64)
        out_cm = out_d.rearrange("(r j) d -> j r d", j=64)
        t_X1 = [Trk("X1_%d" % i) for i in range(32)]
        t_H2 = [Trk("H2_%d" % i) for i in range(16)]
        with ExitStack() as st:
            sb = lambda name, shape, dt: st.enter_context(nc.sbuf_tensor(name, list(shape), dt))
            ps = lambda name, shape, dt: st.enter_context(nc.psum_tensor(name, list(shape), dt))
            WGR = sb("WGR", [128, 8, 8, 128], BF16); WGM = sb("WGM", [128, 8, 8, 128], BF16)
            WBR = sb("WBR", [128, 8, 8, 128], BF16); WBM = sb("WBM", [128, 8, 8, 128], BF16)
            WOUT = sb("WOUT", [128, 8, 1024], BF16)
            t_WGR = [Trk("WGR%d" % i) for i in range(8)]; t_WGM = [Trk("WGM%d" % i) for i in range(8)]
            t_WBR = [Trk("WBR%d" % i) for i in range(8)]; t_WBM = [Trk("WBM%d" % i) for i in range(8)]
            t_WOUT = [Trk("WOUT%d" % i) for i in range(8)]
            for oc in range(8):
                for (W, t_W, src) in ((WGR, t_WGR, win_d[32 + oc]), (WGM, t_WGM, win_d[40 + oc]),
                                      (WBR, t_WBR, wbrg_d[oc]), (WBM, t_WBM, wbml_d[oc])):
                    P.dma("pool", (lambda W, oc, src: lambda e: e.dma_start(
                        out=W[:, oc], in_=src.rearrange("p (k j) -> p k j", j=128)))(W, oc, src), writes=[t_W[oc]])
            for kc in range(8):
                P.dma("pool", (lambda kc: lambda e: e.dma_start(out=WOUT[:, kc, :], in_=wout_d[:, kc, :]))(kc), writes=[t_WOUT[kc]])
            YRt = Rot(sb, "YRt", 1, [128, 8, 512], BF16)
            YMt = Rot(sb, "YMt", 1, [128, 8, 512], BF16)
            SG = Rot(sb, "SG", 1, [128, 1024], F32)
            MIX = Rot(sb, "MIX", 1, [128, 8, 512], BF16)
            XT = Rot(sb, "XTc", 2, [128, 1024], F32)
            X1t = Rot(sb, "X1t", 1, [128, 1024], F32)
            XN = Rot(sb, "XNc", 1, [128, 1024], BF16)
            H2s = Rot(sb, "H2s", 1, [128, 8, 256], BF16)
            STc = sb("STc", [128, 32, 4], F32)
            BA0 = ps("BA0", [128, 512], F32); t_BA0 = Trk("PS_BA0")
            BA1 = ps("BA1", [128, 512], F32); t_BA1 = Trk("PS_BA1")
            BB0 = ps("BB0", [128, 512], F32); t_BB0 = Trk("PS_BB0")
            BB1 = ps("BB1", [128, 512], F32); t_BB1 = Trk("PS_BB1")
            BY = ps("BY", [128, 1024], F32); t_BY = Trk("PS_BY")
            BT = ps("BT", [128, 8, 128], BF16); t_BT = Trk("PS_BT")
            BT2 = ps("BT2", [128, 8, 128], BF16); t_BT2 = Trk("PS_BT2")
            def load_y(T):
                yr, t_yr = YRt.next()
                ym, t_ym = YMt.next()
                P.dma("sp", (lambda yr, T: lambda e: e.dma_start(
                    out=yr[:], in_=YRG_d[:, :, T * 512:(T + 1) * 512].rearrange("c p t -> p c t")))(yr, T), reads=t_YRG, writes=[t_yr])
                P.dma("sp", (lambda ym, T: lambda e: e.dma_start(
                    out=ym[:], in_=YML_d[:, :, T * 512:(T + 1) * 512].rearrange("c p t -> p c t")))(ym, T),
                    reads=t_YML[2 * T:2 * T + 2], writes=[t_ym])
                return yr, t_yr, ym, t_ym

            def load_x(ti):
                xt, t_xt = XT.next()
                for jj in range(2):
                    P.dma("sp", (lambda xt, jj, ti: lambda e: e.dma_start(
                        out=xt[jj * 64:(jj + 1) * 64, :], in_=x_cm[2 * ti + jj]))(xt, jj, ti), writes=[t_xt])
                return xt, t_xt

            h2s_trk2 = {}
            ynext = load_y(0)
            xnext = load_x(0)
            for T in range(8):
                yr, t_yr, ym, t_ym = ynext
                mix, t_mix = MIX.next()
                for oc in range(8):
                    sg, t_sg = SG.next()
                    for (W, t_W, bank, t_bank) in ((WGR, t_WGR, BA0, t_BA0), (WGM, t_WGM, BA1, t_BA1)):
                        for kc in range(8):
                            P.op("pe", (lambda W, bank, oc, kc, T: lambda e: e.matmul(
                                bank[:], lhsT=W[:, oc, kc, :],
                                rhs=uT[:, kc, 256 + T * 512:256 + (T + 1) * 512],
                                start=(kc == 0), stop=(kc == 7)))(W, bank, oc, kc, T), reads=[t_W[oc]] + t_uT[4:68], writes=[t_bank])
                    for (W, t_W, src, t_src, bank, t_bank) in ((WBR, t_WBR, yr, t_yr, BB0, t_BB0), (WBM, t_WBM, ym, t_ym, BB1, t_BB1)):
                        for kc in range(8):
                            P.op("pe", (lambda W, bank, oc, kc, src: lambda e: e.matmul(
                                bank[:], lhsT=W[:, oc, kc, :], rhs=src[:, kc, :],
                                start=(kc == 0), stop=(kc == 7)))(W, bank, oc, kc, src), reads=[t_W[oc], t_src], writes=[t_bank])
                    for (i_, bank, t_bank) in ((0, BA0, t_BA0), (1, BA1, t_BA1)):
                        P.op("act", (lambda sg, bank, i_: lambda e: e.activation(
                            out=sg[:, i_ * 512:(i_ + 1) * 512], in_=bank[:], func=AF.Sigmoid))(sg, bank, i_), reads=[t_bank], writes=[t_sg])
                    for (i_, bank, t_bank) in ((0, BB0, t_BB0), (1, BB1, t_BB1)):
                        P.op("dve", (lambda sg, bank, i_: lambda e: e.tensor_tensor(
                            out=sg[:, i_ * 512:(i_ + 1) * 512], in0=bank[:], in1=sg[:, i_ * 512:(i_ + 1) * 512], op=ALU.mult))(sg, bank, i_),
                            reads=[t_bank, t_sg], writes=[t_sg])
                    P.op("dve", (lambda mix, sg, oc: lambda e: e.tensor_tensor(
                        out=mix[:, oc, :], in0=sg[:, 0:512], in1=sg[:, 512:1024], op=ALU.add))(mix, sg, oc),
                        reads=[t_sg], writes=[t_mix])
                if T + 1 < 8:
                    ynext = load_y(T + 1)
                for s_ in range(4):
                    ti = T * 4 + s_
                    if s_ % 2 == 0:
                        h2s, t_h2s = H2s.next()
                        t_h2s2 = h2s_trk2.setdefault(id(t_h2s), Trk("h2s_b"))
                    xt, t_xt = xnext
                    if ti + 1 < 32:
                        xnext = load_x(ti + 1)
                    x1, t_x1 = X1t.next()
                    xn, t_xn = XN.next()
                    t_st = Trk("stc%d" % ti)
                    for half in range(2):
                        for kc in range(8):
                            P.op("pe", (lambda mix, half, kc, s_: lambda e: e.matmul(
                                BY[:, half * 512:(half + 1) * 512], lhsT=mix[:, kc, s_ * 128:(s_ + 1) * 128],
                                rhs=WOUT[:, kc, half * 512:(half + 1) * 512], start=(kc == 0), stop=(kc == 7)))(mix, half, kc, s_),
                                reads=[t_mix, t_WOUT[kc]], writes=[t_BY])
                    P.op("dve", (lambda x1: lambda e: e.tensor_tensor(out=x1[:], in0=BY[:], in1=GROW[:, 0, :], op=ALU.mult))(x1),
                         reads=[t_BY, t_GROW], writes=[t_x1])
                    P.op("dve", (lambda x1, xt: lambda e: e.tensor_tensor(out=x1[:], in0=x1[:], in1=xt[:], op=ALU.add))(x1, xt),
                         reads=[t_x1, t_xt], writes=[t_x1])
                    P.dma("sp", (lambda x1, ti: lambda e: e.dma_start(out=X1_d[ti * 128:(ti + 1) * 128, :], in_=x1[:]))(x1, ti),
                          reads=[t_x1], writes=[t_X1[ti]], semtrk=t_x1)
                    P.op("act", (lambda x1, xn, ti: lambda e: e.activation(
                        out=xn[:], in_=x1[:], func=AF.Square, accum_out=STc[:, ti, 0:1]))(x1, xn, ti), reads=[t_x1], writes=[t_xn, t_st])
                    P.op("act", (lambda ti: lambda e: e.activation(
                        out=STc[:, ti, 1:2], in_=STc[:, ti, 0:1], func=AF.Sqrt, scale=1.0 / 1024.0, bias=EPS))(ti), reads=[t_st], writes=[t_st])
                    P.op("dve", (lambda ti: lambda e: e.reciprocal(out=STc[:, ti, 2:3], in_=STc[:, ti, 1:2]))(ti), reads=[t_st], writes=[t_st])
                    P.op("act", (lambda x1, xn, ti: lambda e: e.activation(
                        out=xn[:], in_=x1[:], func=AF.Copy, scale=STc[:, ti, 2:3]))(x1, xn, ti), reads=[t_x1, t_st], writes=[t_xn])
                    for c in range(8):
                        btc, t_btc = (BT, t_BT) if c < 4 else (BT2, t_BT2)
                        P.op("pe", (lambda xn, c, btc: lambda e: e.transpose(
                            out=btc[:, c, :], in_=xn[:, c * 128:(c + 1) * 128], identity=IDB[:]))(xn, c, btc), reads=[t_xn, t_IDB], writes=[t_btc])
                    for c in (0, 4, 1, 5, 2, 6, 3, 7):
                        dst = h2s[:, c, (s_ % 2) * 128:(s_ % 2 + 1) * 128]
                        if c < 4:
                            P.op("dve", (lambda c, dst: lambda e: e.tensor_scalar(
                                out=dst, in0=BT[:, c, :], scalar1=A2[:, c:c + 1], scalar2=MODT[:, 48 + 2 * c:49 + 2 * c],
                                op0=ALU.mult, op1=ALU.add))(c, dst), reads=[t_BT, t_A2, t_MODT], writes=[t_h2s])
                        else:
                            P.op("act", (lambda c, dst: lambda e: e.activation(
                                out=dst, in_=BT2[:, c, :], func=AF.Identity, scale=A2[:, c:c + 1], bias=MODT[:, 48 + 2 * c:49 + 2 * c]))(c, dst),
                                reads=[t_BT2, t_A2, t_MODT], writes=[t_h2s2])
                    if s_ % 2 == 1:
                        Tq = ti // 2
                        P.dma("sp", (lambda h2s, Tq: lambda e: e.dma_start(out=H2_d[Tq], in_=h2s[:]))(h2s, Tq),
                              reads=[t_h2s, t_h2s2], writes=[t_H2[Tq]], semtrk=t_h2s)
            P.barrier()
            P.emit()
        ust.close()
        if "x1" in debug:
            dbg_d["x1"] = X1_d

        with ExitStack() as st:
            sb = lambda name, shape, dt: st.enter_context(nc.sbuf_tensor(name, list(shape), dt))
            ps = lambda name, shape, dt: st.enter_context(nc.psum_tensor(name, list(shape), dt))
            WFI = sb("WFI", [128, 44, 8, 128], BF16); t_WFI = [Trk("WFI%d" % i) for i in range(44)]
            WFO = sb("WFO", [128, 22, 1024], BF16); t_WFO = [Trk("WFO%d" % i) for i in range(22)]
            FGR = sb("FGR", [128, 1024], F32); t_FGR = Trk("FGR")
            P.dma("sp", lambda e: e.dma_start(out=FGR[:], in_=fgrow_d), writes=[t_FGR])
            for f_ in range(22):
                for ci in (f_, 22 + f_):
                    P.dma("pool", (lambda ci: lambda e: e.dma_start(
                        out=WFI[:, ci], in_=wffi_d[ci].rearrange("p (k j) -> p k j", j=128)))(ci), writes=[t_WFI[ci]])
                P.dma("pool", (lambda f_: lambda e: e.dma_start(out=WFO[:, f_, :], in_=wffo_d[:, f_, :]))(f_), writes=[t_WFO[f_]])
            H2t = Rot(sb, "H2t", 2, [128, 8, 512], BF16)
            SGt = Rot(sb, "SGt", 2, [128, 512], F32)
            HID = Rot(sb, "HID", 1, [128, 22, 512], BF16)
            X1r = Rot(sb, "X1r", 2, [128, 1024], F32)
            T2 = Rot(sb, "T2", 2, [128, 1024], F32)
            JK = sb("JK", [128, 1024], BF16); t_JK = Trk("JK")
            STf = sb("STf", [128, 32, 4], F32)
            BG0 = Rot(ps, "PS_BG0", 2, [128, 512], F32)
            BG1 = Rot(ps, "PS_BG1", 2, [128, 512], F32)
            BO = Rot(ps, "PS_BO", 2, [128, 1024], F32)
            h2_map = {}
            x1_map = {}
            for T in range(8):
                for Tq in (T, T + 1):
                    if Tq < 8 and Tq not in h2_map:
                        hb, t_hb = H2t.next()
                        for q_ in range(2):
                            P.dma("sp", (lambda hb, Tq, q_: lambda e: e.dma_start(out=hb[:, :, q_ * 256:(q_ + 1) * 256], in_=H2_d[2 * Tq + q_]))(hb, Tq, q_),
                                  reads=[t_H2[2 * Tq + q_]], writes=[t_hb])
                        h2_map[Tq] = (hb, t_hb)
                h2, t_h2 = h2_map.pop(T)
                hid, t_hid = HID.next()
                for f_ in range(22):
                    g0, t_g0 = BG0.next()
                    g1, t_g1 = BG1.next()
                    sg, t_sg = SGt.next()
                    for (ci, bank, t_bank) in ((f_, g0, t_g0), (22 + f_, g1, t_g1)):
                        for kc in range(8):
                            P.op("pe", (lambda bank, ci, kc, h2: lambda e: e.matmul(
                                bank[:], lhsT=WFI[:, ci, kc, :], rhs=h2[:, kc, :], start=(kc == 0), stop=(kc == 7)))(bank, ci, kc, h2),
                                reads=[t_WFI[ci], t_h2], writes=[t_bank])
                    P.op("act", (lambda sg, g0: lambda e: e.activation(out=sg[:], in_=g0[:], func=AF.Silu))(sg, g0),
                         reads=[t_g0], writes=[t_sg])
                    P.op("dve", (lambda hid, f_, sg, g1: lambda e: e.tensor_tensor(
                        out=hid[:, f_, :], in0=g1[:], in1=sg[:], op=ALU.mult))(hid, f_, sg, g1), reads=[t_g1, t_sg], writes=[t_hid])
                for s_ in range(4):
                    ti = T * 4 + s_
                    bo, t_bo = BO.next()
                    for tq in (ti, ti + 1):
                        if tq < 32 and tq not in x1_map:
                            xb, t_xb = X1r.next()
                            P.dma("sp", (lambda xb, tq: lambda e: e.dma_start(out=xb[:], in_=X1_d[tq * 128:(tq + 1) * 128, :]))(xb, tq),
                                  reads=[t_X1[tq]], writes=[t_xb])
                            x1_map[tq] = (xb, t_xb)
                    x1, t_x1 = x1_map.pop(ti)
                    t2, t_t2 = T2.next()
                    t_st = Trk("stf%d" % ti)
                    for half in range(2):
                        for f_ in range(22):
                            P.op("pe", (lambda bo, hid, half, f_, s_: lambda e: e.matmul(
                                bo[:, half * 512:(half + 1) * 512], lhsT=hid[:, f_, s_ * 128:(s_ + 1) * 128],
                                rhs=WFO[:, f_, half * 512:(half + 1) * 512], start=(f_ == 0), stop=(f_ == 21)))(bo, hid, half, f_, s_),
                                reads=[t_hid, t_WFO[f_]], writes=[t_bo])
                    P.op("dve", (lambda t2, bo: lambda e: e.tensor_tensor(out=t2[:], in0=bo[:], in1=GROW[:, 1, :], op=ALU.mult))(t2, bo),
                         reads=[t_bo, t_GROW], writes=[t_t2])
                    P.op("dve", (lambda t2, x1: lambda e: e.tensor_tensor(out=t2[:], in0=t2[:], in1=x1[:], op=ALU.add))(t2, x1),
                         reads=[t_t2, t_x1], writes=[t_t2])
                    P.op("act", (lambda t2, ti: lambda e: e.activation(
                        out=JK[:], in_=t2[:], func=AF.Square, accum_out=STf[:, ti, 0:1]))(t2, ti), reads=[t_t2], writes=[t_JK, t_st])
                    P.op("act", (lambda ti: lambda e: e.activation(
                        out=STf[:, ti, 1:2], in_=STf[:, ti, 0:1], func=AF.Sqrt, scale=1.0 / 1024.0, bias=EPS))(ti), reads=[t_st], writes=[t_st])
                    P.op("dve", (lambda ti: lambda e: e.reciprocal(out=STf[:, ti, 2:3], in_=STf[:, ti, 1:2]))(ti), reads=[t_st], writes=[t_st])
                    P.op("act", (lambda t2, ti: lambda e: e.activation(
                        out=t2[:], in_=t2[:], func=AF.Copy, scale=STf[:, ti, 2:3]))(t2, ti), reads=[t_t2, t_st], writes=[t_t2])
                    P.op("dve", (lambda t2: lambda e: e.tensor_tensor(out=t2[:], in0=t2[:], in1=FGR[:], op=ALU.mult))(t2),
                         reads=[t_t2, t_FGR], writes=[t_t2])
                    for jj in range(2):
                        P.out_toks.append(P.dma("sp", (lambda t2, jj, ti: lambda e: e.dma_start(
                            out=out_cm[2 * ti + jj], in_=t2[jj * 64:(jj + 1) * 64, :]))(t2, jj, ti), reads=[t_t2], semtrk=t_t2))
            P.barrier()
            P.emit()

        P.wait_all("sp", P.out_toks)
        P.emit()
    return nc, dbg_d


def make_in_maps(inputs):
    sh = prep_shared(inputs)
    x = np.asarray(inputs["x"], np.float32)
    c = np.asarray(inputs["c"], np.float32)
    ctx = np.asarray(inputs["ctx"], np.float32)
    c_ctx = np.asarray(inputs["c_ctx"], np.float32)
    maps = []
    for b in range(8):
        m = dict(sh)
        m["x"] = np.ascontiguousarray(x[b])
        m["ctx"] = np.ascontiguousarray(ctx[b])
        m["cv"] = np.ascontiguousarray(np.stack([fm(c[b]), fm(c_ctx)], 2).reshape(128, 16))
        maps.append(m)
    return maps


def kernel(**inputs):
    nc, _ = build()
    maps = make_in_maps(inputs)
    res = run_bass_kernel_spmd(nc, maps, core_ids=list(range(8)))
    return np.stack([r["out"] for r in res.results], 0)
```

```python
import numpy as np
from contextlib import ExitStack
import concourse.bass as bass
import concourse.mybir as mybir
from concourse.bass_utils import run_bass_kernel_spmd

F32 = mybir.dt.float32
BF16 = mybir.dt.bfloat16
AF = mybir.ActivationFunctionType
ALU = mybir.AluOpType
AX = mybir.AxisListType

ENGS = ["pe", "act", "dve", "pool", "sp"]
EPS = 1e-6
NT = 4352
NCTX = 256
NLAT = 4096


class Trk:
    __slots__ = ("name", "w", "r", "dsem", "dcnt", "excl")

    def __init__(self, name=""):
        self.name = name
        self.w = None
        self.r = {}
        self.dsem = None
        self.dcnt = 0
        self.excl = name.startswith("PS_")


class Prog:
    def __init__(self, nc, stack):
        self.nc = nc
        self.stack = stack
        self.ops = {e: [] for e in ENGS}
        self.seq = {e: 0 for e in ENGS}
        self.known = {e: {} for e in ENGS}
        self.esem = {e: stack.enter_context(nc.semaphore("s_" + e)) for e in ENGS}
        self.nsem = len(ENGS)
        self.out_toks = []
        self.dtrks = []
        self.sem_pool = {"sp": [], "pool": []}

    def new_dsem(self, name):
        s = self.stack.enter_context(self.nc.semaphore("d%d_%s" % (self.nsem, name)))
        self.nsem += 1
        return s

    def _need(self, eng, waits, dep):
        sem, val = dep
        if eng == "pe" and sem is self.esem["pe"]:
            return
        k = id(sem)
        if self.known[eng].get(k, 0) >= val:
            return
        self.known[eng][k] = val
        waits[k] = (sem, val)

    def _deps(self, eng, reads, writes):
        waits = {}
        for t in reads:
            if t.w is not None:
                self._need(eng, waits, t.w)
            if t.excl:
                for dep in t.r.values():
                    self._need(eng, waits, dep)
        for t in writes:
            if t.w is not None:
                self._need(eng, waits, t.w)
            for dep in t.r.values():
                self._need(eng, waits, dep)
        return waits

    def _record(self, tok, reads, writes):
        for t in reads:
            t.r[id(tok[0])] = tok
        for t in writes:
            t.w = tok
            t.r = {}

    def op(self, eng, fn, reads=(), writes=()):
        waits = self._deps(eng, reads, writes)
        self.seq[eng] += 1
        tok = (self.esem[eng], self.seq[eng])
        self._record(tok, reads, writes)
        self.ops[eng].append((list(waits.values()), fn, (self.esem[eng], 1)))

    def dma(self, eng, fn, reads=(), writes=(), semtrk=None):
        if semtrk is None:
            semtrk = writes[0] if writes else reads[0]
        if semtrk.dsem is None:
            if self.sem_pool[eng]:
                semtrk.dsem, semtrk.dcnt = self.sem_pool[eng].pop()
            else:
                semtrk.dsem = self.new_dsem(semtrk.name)
            self.dtrks.append((semtrk, eng))
        waits = self._deps(eng, reads, writes)
        if semtrk.dcnt > 0:
            self._need(eng, waits, (semtrk.dsem, semtrk.dcnt))
        semtrk.dcnt += 16
        tok = (semtrk.dsem, semtrk.dcnt)
        self._record(tok, reads, writes)
        self.ops[eng].append((list(waits.values()), fn, (semtrk.dsem, 16)))
        return tok

    def wait_all(self, eng, toks):
        waits = {}
        for t in toks:
            self._need(eng, waits, t)
        self.ops[eng].append((list(waits.values()), None, None))

    def barrier(self):
        waits = {}
        for e in ENGS:
            if e != "sp" and self.seq[e] > 0:
                self._need("sp", waits, (self.esem[e], self.seq[e]))
        for t, _e in self.dtrks:
            if t.dcnt > 0:
                self._need("sp", waits, (t.dsem, t.dcnt))
        self.seq["sp"] += 1
        self.ops["sp"].append((list(waits.values()), lambda e: e.nop(), (self.esem["sp"], 1)))
        for t, e_ in self.dtrks:
            self.sem_pool[e_].append((t.dsem, t.dcnt))
            t.dsem = None
            t.dcnt = 0
        self.dtrks = []
        for e in ENGS:
            if e != "sp":
                w = {}
                self._need(e, w, (self.esem["sp"], self.seq["sp"]))
                self.ops[e].append((list(w.values()), None, None))

    def emit(self):
        nc = self.nc
        ops = self.ops
        self.ops = {e: [] for e in ENGS}
        with nc.Block() as block:
            def run(e, lst):
                for waits, fn, inc in lst:
                    for sem, val in waits:
                        e.wait_ge(sem, val)
                    if fn is not None:
                        fn(e).then_inc(inc[0], inc[1])

            @block.tensor
            def _(e):
                run(e, ops["pe"])

            @block.scalar
            def _(e):
                run(e, ops["act"])

            @block.vector
            def _(e):
                run(e, ops["dve"])

            @block.gpsimd
            def _(e):
                run(e, ops["pool"])

            @block.sync
            def _(e):
                run(e, ops["sp"])


class Rot:
    def __init__(self, alloc, name, n, shape, dt):
        self.bufs = [(alloc("%s%d" % (name, i), shape, dt), Trk("%s%d" % (name, i))) for i in range(n)]
        self.i = 0

    def next(self):
        b = self.bufs[self.i % len(self.bufs)]
        self.i += 1
        return b


FVCOLS = {}
_off = 0
for _n, _w in [("n1g", 8), ("n2g", 8), ("rgcw", 32), ("rgcb", 8), ("rgba", 16), ("rgbx", 16), ("rglam", 16),
               ("mlcw", 32), ("mlcb", 8), ("mlng", 8), ("mlsk", 8)]:
    FVCOLS[_n] = (_off, _w)
    _off += _w
NV = _off


def fm(vec):
    v = np.asarray(vec, np.float32)
    return np.ascontiguousarray(v.reshape(-1, 128).T)


def colchunks(w):
    K, N = w.shape
    a = w.reshape(K // 128, 128, N // 128, 128)
    a = a.transpose(2, 1, 0, 3)
    return np.ascontiguousarray(a.reshape(N // 128, 128, (K // 128) * 128))


def rowchunks(w):
    K, N = w.shape
    return np.ascontiguousarray(w.reshape(K // 128, 128, N).transpose(1, 0, 2))


def blockdiag128(blocks):
    nb, bi, bo = blocks.shape
    per = 128 // bi
    out = np.zeros((nb // per, 128, 128), np.float32)
    for b in range(nb):
        c, q = divmod(b, per)
        out[c, q * bi:(q + 1) * bi, q * bo:(q + 1) * bo] = blocks[b]
    return out


def prep_shared(inp):
    f = lambda k: np.asarray(inp[k], np.float32)
    sh = {}
    w_mod = f("w_mod")[0]
    sh["wmod"] = colchunks(w_mod)
    b_mod = f("b_mod")[0]
    sh["bmod2"] = np.ascontiguousarray(np.repeat(fm(b_mod)[:, :, None], 2, axis=2).reshape(128, 96))
    sh["wmodg"] = np.stack([rowchunks(w_mod[:, 2048:3072]), rowchunks(w_mod[:, 5120:6144])], 0)
    sh["bmodg"] = np.ascontiguousarray(np.broadcast_to(
        np.concatenate([b_mod[2048:3072], b_mod[5120:6144]])[None, :], (128, 2048)))
    fv = np.zeros((128, NV), np.float32)

    def put(name, arr):
        o, w = FVCOLS[name]
        assert arr.shape == (128, w), (name, arr.shape)
        fv[:, o:o + w] = arr
    put("n1g", fm(f("norm1_g")[0]))
    put("n2g", fm(f("norm2_g")[0]))
    cw = f("rg_conv_w")[0]
    put("rgcw", np.stack([fm(cw[j]) for j in range(4)], 2).reshape(128, 32))
    put("rgcb", fm(f("rg_conv_b")[0]))
    put("rgba", np.concatenate([fm(f("rg_ba")[0][d]) for d in range(2)], 1))
    put("rgbx", np.concatenate([fm(f("rg_bx")[0][d]) for d in range(2)], 1))
    put("rglam", np.concatenate([fm(f("rg_lambda")[0][d]) for d in range(2)], 1))
    cw = f("ml_conv_w")[0]
    put("mlcw", np.stack([fm(cw[j]) for j in range(4)], 2).reshape(128, 32))
    put("mlcb", fm(f("ml_conv_b")[0]))
    put("mlng", fm(f("ml_norm_g")[0]))
    put("mlsk", fm(f("ml_skip")[0]))
    sh["fv"] = fv
    sh["win"] = colchunks(f("w_in")[0])
    rg = []
    for d in range(2):
        for w in (f("rg_wa")[0][d], f("rg_wx")[0][d]):
            rg.append(blockdiag128(w).transpose(1, 0, 2))
    sh["rgbd"] = np.ascontiguousarray(np.concatenate(rg, 1))
    ml = [blockdiag128(f(k)[0]).transpose(1, 0, 2) for k in ("ml_wq", "ml_wk", "ml_wv")]
    sh["mlbd"] = np.ascontiguousarray(np.concatenate(ml, 1))
    wi, wf = f("ml_wi")[0], f("ml_wf")[0]
    wg = np.concatenate([wi[0], wf[0], wi[1], wf[1]], 1)
    sh["wgt"] = np.ascontiguousarray(wg.reshape(24, 128, 16).transpose(1, 0, 2))
    bi, bf = f("ml_bi")[0], f("ml_bf")[0]
    gb = np.stack([np.tile(np.concatenate([bi[d], bf[d]]), 2) for d in range(2)], 0)
    sh["gbias"] = np.ascontiguousarray(np.broadcast_to(gb[None], (128, 2, 16)))
    tri = np.zeros((2, 128, 128), np.float32)
    ii = np.arange(128)
    tri[0] = (ii[:, None] <= ii[None, :])
    tri[1] = (ii[:, None] >= ii[None, :])
    sh["tri"] = np.ascontiguousarray(tri.transpose(1, 0, 2))
    ident = np.eye(128, dtype=np.float32)
    sh["ident"] = ident
    sh["wbrg"] = colchunks(f("w_branch_rg")[0])
    sh["wbml"] = colchunks(f("w_branch_ml")[0])
    sh["wout"] = rowchunks(f("w_out")[0])
    sh["wffi"] = colchunks(f("w_ffn_in")[0])
    sh["wffo"] = rowchunks(f("w_ffn_out")[0])
    sh["fgrow"] = np.ascontiguousarray(np.broadcast_to(f("final_norm_g")[None, :], (128, 1024)))
    return sh


def build(debug=(), stop_after=None):
    nc = bass.Bass("TRN2", target_bir_lowering=False)
    din = lambda name, shape, dt=F32: nc.dram_tensor(name, list(shape), dt, kind="ExternalInput").ap()
    x_d = din("x", [NLAT, 1024])
    ctx_d = din("ctx", [NCTX, 1024])
    cv_d = din("cv", [128, 16])
    wmod_d = din("wmod", [48, 128, 1024])
    bmod2_d = din("bmod2", [128, 96])
    wmodg_d = din("wmodg", [2, 128, 8, 1024])
    bmodg_d = din("bmodg", [128, 2048])
    fv_d = din("fv", [128, NV])
    win_d = din("win", [48, 128, 1024])
    ident_d = din("ident", [128, 128])
    rgbd_d = din("rgbd", [128, 32, 128])
    mlbd_d = din("mlbd", [128, 24, 128])
    wgt_d = din("wgt", [128, 24, 16])
    gbias_d = din("gbias", [128, 2, 16])
    tri_d = din("tri", [128, 2, 128])
    wbrg_d = din("wbrg", [8, 128, 1024])
    wbml_d = din("wbml", [8, 128, 1024])
    wout_d = din("wout", [128, 8, 1024])
    wffi_d = din("wffi", [44, 128, 1024])
    wffo_d = din("wffo", [128, 22, 1024])
    fgrow_d = din("fgrow", [128, 1024])
    SPQ_d = nc.dram_tensor("SPQ", [17, 4, 128, 8, 256], BF16).ap()
    SPX_d = nc.dram_tensor("SPX", [16, 128, 8, 256], F32).ap()
    X1_d = nc.dram_tensor("X1", [NLAT, 1024], F32).ap()
    H2_d = nc.dram_tensor("H2", [16, 128, 8, 256], BF16).ap()
    HF_d = nc.dram_tensor("HF", [32, 128, 1024], F32).ap()
    if "yml" in debug:
        YML_d = nc.dram_tensor("dbg_yml", [8, 128, NLAT], BF16, kind="ExternalOutput").ap()
    else:
        YML_d = nc.dram_tensor("YML", [8, 128, NLAT], BF16).ap()
    YRG_d = nc.dram_tensor("YRG", [8, 128, NLAT], BF16).ap()
    out_d = nc.dram_tensor("out", [NLAT, 1024], F32, kind="ExternalOutput").ap()
    dbg_d = {}

    with ExitStack() as gst:
        P = Prog(nc, gst)
        galloc = lambda name, shape, dt: gst.enter_context(nc.sbuf_tensor(name, list(shape), dt))

        def dump(name, ap, trk, shape, dt=F32):
            if name not in debug:
                return
            d = nc.dram_tensor("dbg_" + name, list(shape), dt, kind="ExternalOutput").ap()
            dbg_d[name] = d
            P.out_toks.append(P.dma("sp", lambda e: e.dma_start(out=d, in_=ap), reads=[trk], semtrk=Trk("dbg" + name)))

        t_uT = [Trk("uT%d_%d" % (i // 2, i % 2)) for i in range(68)]
        FV = galloc("FV", [128, NV], F32); t_FV = Trk("FV")
        MODT = galloc("MODT", [128, 96], F32); t_MODT = Trk("MODT")
        A1 = galloc("A1", [128, 16], F32); t_A1 = Trk("A1")
        A2 = galloc("A2", [128, 8], F32); t_A2 = Trk("A2")
        GROW = galloc("GROW", [128, 2, 1024], F32); t_GROW = Trk("GROW")
        KD = galloc("KD", [128, 16], F32); t_KD = Trk("KD")
        IDB = galloc("IDB", [128, 128], BF16); t_IDB = Trk("IDB")
        IDF = galloc("IDF", [128, 128], F32); t_IDF = Trk("IDF")
        t_YRG = [Trk("YRG%d" % c) for c in range(8)]
        ust = ExitStack()
        uT = ust.enter_context(nc.sbuf_tensor("uT", [128, 8, NT], BF16))

        def fvc(name, i=0, n=1):
            o, w = FVCOLS[name]
            return FV[:, o + i:o + i + n]

        with ExitStack() as st:
            sb = lambda name, shape, dt: st.enter_context(nc.sbuf_tensor(name, list(shape), dt))
            ps = lambda name, shape, dt: st.enter_context(nc.psum_tensor(name, list(shape), dt))
            CV = sb("CV", [128, 16], F32); t_CV = Trk("CV")
            S2 = sb("S2", [128, 16], F32); t_S2 = Trk("S2")
            SREP = sb("SREP", [128, 8, 128], F32); t_SREP = Trk("SREP")
            BM2 = sb("BM2", [128, 96], F32); t_BM2 = Trk("BM2")
            BMG = sb("BMG", [128, 2048], F32); t_BMG = Trk("BMG")
            TMPA = sb("TMPA", [128, 16], F32); t_TMPA = Trk("TMPA")
            TMPB = sb("TMPB", [128, 16], F32); t_TMPB = Trk("TMPB")
            WM = Rot(sb, "WM", 3, [128, 1024], F32)
            WGm = sb("WGm", [128, 8, 1024], F32); t_WGm = Trk("WGm")
            MODP = ps("MODP", [128, 512], F32); t_MODP = Trk("PS_MODP")
            GRP = ps("GRP", [128, 1024], F32); t_GRP = Trk("PS_GRP")

            P.dma("sp", lambda e: e.dma_start(out=CV[:], in_=cv_d), writes=[t_CV])
            P.dma("sp", lambda e: e.dma_start(out=FV[:], in_=fv_d), writes=[t_FV])
            P.dma("sp", lambda e: e.dma_start(out=BM2[:], in_=bmod2_d), writes=[t_BM2])
            P.dma("sp", lambda e: e.dma_start(out=BMG[:], in_=bmodg_d), writes=[t_BMG])
            P.dma("sp", lambda e: e.dma_start(out=IDF[:], in_=ident_d), writes=[t_IDF])
            P.dma("pool", lambda e: e.dma_start(out=IDB[:], in_=ident_d), writes=[t_IDB])
            P.op("act", lambda e: e.activation(out=S2[:], in_=CV[:], func=AF.Silu), reads=[t_CV], writes=[t_S2])
            for kc in range(8):
                P.op("dve", (lambda kc: lambda e: e.tensor_copy(
                    out=SREP[:, kc, :], in_=S2[:, 2 * kc:2 * kc + 1].to_broadcast([128, 128])))(kc),
                    reads=[t_S2], writes=[t_SREP])
            for n in range(48):
                wm, t_wm = WM.next()
                P.dma("sp", (lambda wm, n: lambda e: e.dma_start(out=wm[:], in_=wmod_d[n]))(wm, n), writes=[t_wm])
                for kc in range(8):
                    P.op("pe", (lambda wm, n, kc: lambda e: e.matmul(
                        MODP[:, 2 * n:2 * n + 2], lhsT=wm[:, kc * 128:(kc + 1) * 128], rhs=S2[:, 2 * kc:2 * kc + 2],
                        start=(kc == 0), stop=(kc == 7)))(wm, n, kc), reads=[t_wm, t_S2], writes=[t_MODP])
            P.op("dve", lambda e: e.tensor_tensor(out=MODT[:], in0=MODP[:, 0:96], in1=BM2[:], op=ALU.add),
                 reads=[t_MODP, t_BM2], writes=[t_MODT])
            P.op("dve", lambda e: e.tensor_scalar_add(out=TMPA[:], in0=MODT[:, 16:32], scalar1=1.0),
                 reads=[t_MODT], writes=[t_TMPA])
            for j in range(2):
                P.op("dve", (lambda j: lambda e: e.tensor_tensor(
                    out=A1[:, j:16:2], in0=TMPA[:, j:16:2], in1=fvc("n1g", 0, 8), op=ALU.mult))(j),
                    reads=[t_TMPA, t_FV], writes=[t_A1])
            P.op("dve", lambda e: e.tensor_scalar_add(out=TMPB[:, 0:8], in0=MODT[:, 64:80:2], scalar1=1.0),
                 reads=[t_MODT], writes=[t_TMPB])
            P.op("dve", lambda e: e.tensor_tensor(out=A2[:], in0=TMPB[:, 0:8], in1=fvc("n2g", 0, 8), op=ALU.mult),
                 reads=[t_TMPB, t_FV], writes=[t_A2])
            for g in range(2):
                P.dma("sp", (lambda g: lambda e: e.dma_start(out=WGm[:], in_=wmodg_d[g]))(g), writes=[t_WGm])
                for half in range(2):
                    for kc in range(8):
                        P.op("pe", (lambda half, kc: lambda e: e.matmul(
                            GRP[:, half * 512:(half + 1) * 512], lhsT=SREP[:, kc, :],
                            rhs=WGm[:, kc, half * 512:(half + 1) * 512], start=(kc == 0), stop=(kc == 7)))(half, kc),
                            reads=[t_SREP, t_WGm], writes=[t_GRP])
                P.op("dve", (lambda g: lambda e: e.tensor_tensor(
                    out=GROW[:, g, :], in0=GRP[:], in1=BMG[:, g * 1024:(g + 1) * 1024], op=ALU.add))(g),
                    reads=[t_GRP, t_BMG], writes=[t_GROW])
            P.op("act", lambda e: e.activation(out=TMPA[:], in_=fvc("rglam", 0, 16), func=AF.Exp, scale=-1.0),
                 reads=[t_FV], writes=[t_TMPA])
            P.op("act", lambda e: e.activation(out=TMPB[:], in_=TMPA[:], func=AF.Ln, bias=1.0),
                 reads=[t_TMPA], writes=[t_TMPB])
            P.op("dve", lambda e: e.tensor_scalar(out=KD[:], in0=TMPB[:], scalar1=-8.0, scalar2=None, op0=ALU.mult),
                 reads=[t_TMPB], writes=[t_KD])
            dump("modT", MODT[:], t_MODT, [128, 96])
            dump("grow", GROW[:], t_GROW, [128, 2, 1024])
            dump("kd", KD[:], t_KD, [128, 16])
            P.barrier()
            P.emit()

        with ExitStack() as st:
            sb = lambda name, shape, dt: st.enter_context(nc.sbuf_tensor(name, list(shape), dt))
            ps = lambda name, shape, dt: st.enter_context(nc.psum_tensor(name, list(shape), dt))
            XT = Rot(sb, "XT", 3, [128, 1024], F32)
            XN = Rot(sb, "XN", 2, [128, 1024], BF16)
            TP = Rot(ps, "PS_TP", 2, [128, 8, 128], BF16)
            TPb = Rot(ps, "PS_TPb", 2, [128, 8, 128], BF16)
            ST = sb("ST", [128, 34, 4], F32)
            JKA = sb("JKA", [128, 1024], BF16); t_JKA = Trk("JKA")
            stage1 = {}

            def emit_stats(i):
                t_st = Trk("st%d" % i)
                xt, t_xt = XT.next()
                src = ctx_d[i * 128:(i + 1) * 128, :] if i < 2 else x_d[(i - 2) * 128:(i - 1) * 128, :]
                P.dma("sp", (lambda xt, src: lambda e: e.dma_start(out=xt[:], in_=src))(xt, src), writes=[t_xt])
                P.op("act", (lambda xt, i: lambda e: e.activation(
                    out=JKA[:], in_=xt[:], func=AF.Square, accum_out=ST[:, i, 0:1]))(xt, i),
                    reads=[t_xt], writes=[t_JKA, t_st])
                P.op("act", (lambda i: lambda e: e.activation(
                    out=ST[:, i, 1:2], in_=ST[:, i, 0:1], func=AF.Sqrt, scale=1.0 / 1024.0, bias=EPS))(i),
                    reads=[t_st], writes=[t_st])
                P.op("dve", (lambda i: lambda e: e.reciprocal(out=ST[:, i, 2:3], in_=ST[:, i, 1:2]))(i),
                     reads=[t_st], writes=[t_st])
                stage1[i] = (xt, t_xt, t_st)

            emit_stats(0)
            for i in range(34):
                if i + 1 < 34:
                    emit_stats(i + 1)
                xt, t_xt, t_st = stage1.pop(i)
                xn, t_xn = XN.next()
                tpa, t_tpa = TP.next()
                tpb, t_tpb = TPb.next()
                j = 1 if i < 2 else 0
                P.op("act", (lambda xt, xn, i: lambda e: e.activation(
                    out=xn[:], in_=xt[:], func=AF.Copy, scale=ST[:, i, 2:3]))(xt, xn, i),
                    reads=[t_xt, t_st], writes=[t_xn])
                for c in range(8):
                    tp, t_tp = (tpa, t_tpa) if c < 4 else (tpb, t_tpb)
                    P.op("pe", (lambda xn, tp, c: lambda e: e.transpose(
                        out=tp[:, c, :], in_=xn[:, c * 128:(c + 1) * 128], identity=IDB[:]))(xn, tp, c),
                        reads=[t_xn, t_IDB], writes=[t_tp])
                for c in (0, 4, 1, 5, 2, 6, 3, 7):
                    tp, t_tp = (tpa, t_tpa) if c < 4 else (tpb, t_tpb)
                    if i < 2:
                        dst = uT[:, c, i * 128:(i + 1) * 128]
                        src_tp = tp[:, c, :]
                    else:
                        r0 = 2 * (i - 2)
                        dst = uT[:, c, 256:NT].rearrange("p (j r) -> p r j", r=64)[:, r0:r0 + 2, :]
                        src_tp = tp[:, c, :].rearrange("p (r j) -> p r j", j=64)
                    if c < 4:
                        P.op("dve", (lambda tp, c, dst, j: lambda e: e.tensor_scalar(
                            out=dst, in0=tp, scalar1=A1[:, 2 * c + j:2 * c + j + 1],
                            scalar2=MODT[:, 2 * c + j:2 * c + j + 1], op0=ALU.mult, op1=ALU.add))(src_tp, c, dst, j),
                            reads=[t_tp, t_A1, t_MODT], writes=[t_uT[2 * i + (0 if c < 4 else 1)]])
                    else:
                        P.op("act", (lambda tp, c, dst, j: lambda e: e.activation(
                            out=dst, in_=tp, func=AF.Identity, scale=A1[:, 2 * c + j:2 * c + j + 1],
                            bias=MODT[:, 2 * c + j:2 * c + j + 1]))(src_tp, c, dst, j),
                            reads=[t_tp, t_A1, t_MODT], writes=[t_uT[2 * i + (0 if c < 4 else 1)]])
            if "uT" in debug:
                UD = sb("UD", [128, 8, 512], F32); t_UD = Trk("UD")
                P.op("dve", lambda e: e.tensor_copy(out=UD[:], in_=uT[:, :, 128:640]), reads=t_uT[2:10], writes=[t_UD])
                dump("uT", UD[:], t_UD, [128, 8, 512])
            P.barrier()
            P.emit()


        CT0, LT0, PEND = 2, 261, 4357
        with ExitStack() as st:
            sb = lambda name, shape, dt: st.enter_context(nc.sbuf_tensor(name, list(shape), dt))
            ps = lambda name, shape, dt: st.enter_context(nc.psum_tensor(name, list(shape), dt))
            RGBD = sb("RGBD", [128, 32, 128], BF16); t_RGBD = Trk("RGBD")
            P.dma("pool", lambda e: e.dma_start(out=RGBD[:], in_=rgbd_d, max_dma_last_dim=4096), writes=[t_RGBD])
            RX = sb("RX", [128, 4360], F32); t_RX = Trk("RX")
            XC = sb("XC", [128, 4360], F32); t_XC = Trk("XC")
            XCB = sb("XCB", [128, 4360], BF16); t_XCB = Trk("XCB")
            TMP = sb("TMP", [128, 2180], F32); t_TMP = Trk("TMP")
            BB = [sb("B0", [128, 4360], F32), sb("B1", [128, 4360], F32)]; t_BB = [Trk("B0"), Trk("B1")]
            WR = Rot(sb, "WR", 4, [128, 8, 128], BF16)
            PJ = Rot(ps, "PS_PJ", 3, [128, 512], F32)
            GA = Rot(ps, "PS_GA", 4, [128, 512], F32)
            GL = Rot(sb, "GL", 2, [128, 512], F32)
            TS = Rot(sb, "TS", 2, [128, 512], F32)
            YS = Rot(sb, "YS", 1, [128, NLAT], BF16)
            for (a, b) in ((0, 2), (258, 261), (4357, 4360)):
                P.op("dve", (lambda a, b: lambda e: e.memset(RX[:, a:b], 0.0))(a, b), writes=[t_RX])
            for d in range(2):
                P.op("pool", (lambda d: lambda e: e.memset(BB[d][:, 256:264], 0.0))(d), writes=[t_BB[d]])
            blocks = [(0, 256, CT0)] + [(256 + 512 * b, 512, LT0 + 512 * b) for b in range(8)]

            def ut_trks(t0, n):
                return t_uT[2 * (t0 // 128):2 * ((t0 + n - 1) // 128 + 1)]

            def ut_nat(kc, t0, n):
                if t0 < 256:
                    return uT[:, kc, t0:t0 + n]
                r0 = (t0 - 256) // 64
                return uT[:, kc, 256:NT].rearrange("p (j r) -> p r j", r=64)[:, r0:r0 + n // 64, :]

            def rev(ap2d):
                n = ap2d.shape[1]
                return bass.AP(ap2d.tensor, ap2d.offset + (n - 1), [list(ap2d.ap[0]), [-1, n]])

            def load_wr(c):
                wrx, t_wrx = WR.next()
                wrg, t_wrg = WR.next()
                P.dma("pool", (lambda w, c: lambda e: e.dma_start(
                    out=w[:], in_=win_d[c].rearrange("p (k j) -> p k j", j=128)))(wrx, c), writes=[t_wrx])
                P.dma("pool", (lambda w, c: lambda e: e.dma_start(
                    out=w[:], in_=win_d[8 + c].rearrange("p (k j) -> p k j", j=128)))(wrg, c), writes=[t_wrg])
                return wrx, t_wrx, wrg, t_wrg

            wr_next = load_wr(0)
            for c in range(8):
                wrx, t_wrx, wrg, t_wrg = wr_next
                if c + 1 < 8:
                    wr_next = load_wr(c + 1)
                if c > 0:
                    P.op("dve", lambda e: e.memset(RX[:, 258:261], 0.0), writes=[t_RX])
                pj, t_pj = PJ.next()
                for kc in range(8):
                    P.op("pe", (lambda pj, wrx, kc: lambda e: e.matmul(
                        pj[:, 0:256], lhsT=wrx[:, kc, :], rhs=uT[:, kc, 0:256], start=(kc == 0), stop=(kc == 7)))(
                        pj, wrx, kc), reads=[t_wrx] + t_uT[0:4], writes=[t_pj])
                P.op("act", (lambda pj: lambda e: e.activation(
                    out=RX[:, CT0:CT0 + 256], in_=pj[:, 0:256], func=AF.Copy))(pj), reads=[t_pj], writes=[t_RX])
                for b in range(8):
                    pj, t_pj = PJ.next()
                    for kc in range(8):
                        P.op("pe", (lambda pj, wrx, kc, b: lambda e: e.matmul(
                            pj[:], lhsT=wrx[:, kc, :], rhs=uT[:, kc, 256 + b * 512:256 + (b + 1) * 512], start=(kc == 0), stop=(kc == 7)))(
                            pj, wrx, kc, b), reads=[t_wrx] + t_uT[4:68], writes=[t_pj])
                    P.op("act", (lambda pj, b: lambda e: e.activation(
                        out=RX[:, LT0:LT0 + 4096].rearrange("p (r j) -> p j r", j=64)[:, 8 * b:8 * b + 8, :],
                        in_=pj[:].rearrange("p (j r) -> p j r", r=64), func=AF.Copy))(pj, b), reads=[t_pj], writes=[t_RX])
                L = PEND - 2
                P.op("dve", (lambda c: lambda e: e.tensor_scalar(
                    out=XC[:, 2:PEND], in0=RX[:, 0:L], scalar1=fvc("rgcw", c * 4), scalar2=fvc("rgcb", c),
                    op0=ALU.mult, op1=ALU.add))(c), reads=[t_RX, t_FV], writes=[t_XC])
                for j in range(1, 4):
                    P.op("dve", (lambda c, j: lambda e: e.scalar_tensor_tensor(
                        out=XC[:, 2:PEND], in0=RX[:, j:j + L], scalar=fvc("rgcw", c * 4 + j), in1=XC[:, 2:PEND],
                        op0=ALU.mult, op1=ALU.add))(c, j), reads=[t_RX, t_FV, t_XC], writes=[t_XC])
                P.op("act", lambda e: e.activation(out=XCB[:, 2:PEND], in_=XC[:, 2:PEND], func=AF.Copy),
                     reads=[t_XC], writes=[t_XCB])
                if c == 3:
                    dump("xc", XC[:], t_XC, [128, 4360])
                for d in range(2):
                    Bd, t_Bd = BB[d], t_BB[d]
                    for (t0, n, pos) in blocks:
                        ga, t_ga = GA.next()
                        gx, t_gx = GA.next()
                        P.op("pe", (lambda ga, d, c, n, pos: lambda e: e.matmul(
                            ga[:, 0:n], lhsT=RGBD[:, (d * 2) * 8 + c, :], rhs=XCB[:, pos:pos + n], start=True, stop=True))(
                            ga, d, c, n, pos), reads=[t_RGBD, t_XCB], writes=[t_ga])
                        P.op("pe", (lambda gx, d, c, n, pos: lambda e: e.matmul(
                            gx[:, 0:n], lhsT=RGBD[:, (d * 2 + 1) * 8 + c, :], rhs=XCB[:, pos:pos + n], start=True, stop=True))(
                            gx, d, c, n, pos), reads=[t_RGBD, t_XCB], writes=[t_gx])
                        P.op("act", (lambda ga, d, c, n, pos: lambda e: e.activation(
                            out=RX[:, pos:pos + n], in_=ga[:, 0:n], func=AF.Sigmoid, bias=fvc("rgba", d * 8 + c)))(
                            ga, d, c, n, pos), reads=[t_ga, t_FV], writes=[t_RX])
                        P.op("act", (lambda gx, Bd, d, c, n, pos: lambda e: e.activation(
                            out=Bd[:, pos:pos + n], in_=gx[:, 0:n], func=AF.Sigmoid, bias=fvc("rgbx", d * 8 + c)))(
                            gx, Bd, d, c, n, pos), reads=[t_gx, t_FV], writes=[t_Bd])
                    P.op("act", (lambda d, c: lambda e: e.activation(
                        out=RX[:, 2:PEND], in_=RX[:, 2:PEND], func=AF.Exp, scale=KD[:, d * 8 + c:d * 8 + c + 1]))(d, c),
                        reads=[t_RX, t_KD], writes=[t_RX])
                    P.op("dve", (lambda Bd: lambda e: e.tensor_tensor(
                        out=Bd[:, 2:PEND], in0=Bd[:, 2:PEND], in1=XC[:, 2:PEND], op=ALU.mult))(Bd),
                        reads=[t_Bd, t_XC], writes=[t_Bd])
                    for (ra, rb) in ((2, 2180), (2180, PEND)):
                        P.op("act", (lambda ra, rb: lambda e: e.activation(
                            out=TMP[:, 0:rb - ra], in_=RX[:, ra:rb], func=AF.Square))(ra, rb), reads=[t_RX], writes=[t_TMP])
                        P.op("act", (lambda ra, rb: lambda e: e.activation(
                            out=TMP[:, 0:rb - ra], in_=TMP[:, 0:rb - ra], func=AF.Sqrt, scale=-1.0, bias=1.0))(ra, rb),
                            reads=[t_TMP], writes=[t_TMP])
                        P.op("dve", (lambda Bd, ra, rb: lambda e: e.tensor_tensor(
                            out=Bd[:, ra:rb], in0=Bd[:, ra:rb], in1=TMP[:, 0:rb - ra], op=ALU.mult))(Bd, ra, rb),
                            reads=[t_Bd, t_TMP], writes=[t_Bd])
                    f_ = (lambda ap: ap) if d == 0 else rev
                    c0, c1 = CT0, CT0 + 256
                    l0, l1 = LT0, LT0 + 4096
                    P.op("dve", (lambda Bd, f_: lambda e: e.tensor_tensor_scan(
                        out=f_(Bd[:, c0:c1]), data0=f_(RX[:, c0:c1]), data1=f_(Bd[:, c0:c1]), initial=0.0,
                        op0=ALU.mult, op1=ALU.add))(Bd, f_), reads=[t_RX, t_Bd], writes=[t_Bd])
                    ini = (c1 - 1) if d == 0 else c0
                    P.op("dve", (lambda Bd, f_, ini: lambda e: e.tensor_tensor_scan(
                        out=f_(Bd[:, l0:l1]), data0=f_(RX[:, l0:l1]), data1=f_(Bd[:, l0:l1]), initial=Bd[:, ini:ini + 1],
                        op0=ALU.mult, op1=ALU.add))(Bd, f_, ini), reads=[t_RX, t_Bd], writes=[t_Bd])
                ys, t_ys = YS.next()
                for b in range(8):
                    pj, t_pj = PJ.next()
                    gl, t_gl = GL.next()
                    ts, t_ts = TS.next()
                    for kc in range(8):
                        P.op("pe", (lambda pj, wrg, kc, b: lambda e: e.matmul(
                            pj[:], lhsT=wrg[:, kc, :], rhs=uT[:, kc, 256 + b * 512:256 + (b + 1) * 512], start=(kc == 0), stop=(kc == 7)))(
                            pj, wrg, kc, b), reads=[t_wrg] + t_uT[4:68], writes=[t_pj])
                    P.op("act", (lambda pj, gl: lambda e: e.activation(out=gl[:], in_=pj[:], func=AF.Gelu))(pj, gl),
                         reads=[t_pj], writes=[t_gl])
                    P.op("dve", (lambda ts, b: lambda e: e.tensor_tensor(
                        out=ts[:].rearrange("p (j r) -> p j r", r=64),
                        in0=BB[0][:, LT0:LT0 + 4096].rearrange("p (r j) -> p j r", j=64)[:, 8 * b:8 * b + 8, :],
                        in1=BB[1][:, LT0:LT0 + 4096].rearrange("p (r j) -> p j r", j=64)[:, 8 * b:8 * b + 8, :], op=ALU.add))(ts, b),
                        reads=t_BB, writes=[t_ts])
                    P.op("dve", (lambda ys, ts, gl, b: lambda e: e.tensor_tensor(
                        out=ys[:, b * 512:(b + 1) * 512], in0=ts[:], in1=gl[:], op=ALU.mult))(ys, ts, gl, b),
                        reads=[t_ts, t_gl], writes=[t_ys])
                P.dma("sp", (lambda ys, c: lambda e: e.dma_start(out=YRG_d[c], in_=ys[:]))(ys, c), reads=[t_ys], writes=[t_YRG[c]],
                      semtrk=t_ys)
                if c == 3:
                    if "hrg" in debug:
                        dump("hrg", HD[:], t_HD, [128, NLAT])
                    dump("yrg", ys[:], t_ys, [128, NLAT], BF16)
            P.barrier()
            P.emit()
        if stop_after == "rg":
            P.wait_all("sp", P.out_toks)
            P.barrier()
            P.emit()
            ust.close()
            return nc, dbg_d

        t_HF = [Trk("HF%d" % i) for i in range(32)]
        t_YML = [Trk("YML%d" % i) for i in range(16)]
        t_SPQ = [[Trk("SPQ%d_%d" % (g, i)) for i in range(4)] for g in range(17)]
        t_SPX = [Trk("SPX%d" % g) for g in range(16)]
        t_spd = [Trk("spd%d" % i) for i in range(5)]
        with ExitStack() as st:
            sb = lambda name, shape, dt: st.enter_context(nc.sbuf_tensor(name, list(shape), dt))
            ps = lambda name, shape, dt: st.enter_context(nc.psum_tensor(name, list(shape), dt))
            MLBD = sb("MLBD", [128, 24, 128], BF16); t_MLBD = Trk("MLBD")
            WGT = sb("WGT", [128, 24, 16], BF16); t_WGT = Trk("WGT")
            GBI = sb("GBI", [128, 2, 16], F32); t_GBI = Trk("GBI")
            TRI = sb("TRI", [128, 2, 128], F32); t_TRI = Trk("TRI")
            ONES = sb("ONES", [128, 128], F32); t_ONES = Trk("ONES")
            P.dma("pool", lambda e: e.dma_start(out=MLBD[:], in_=mlbd_d, max_dma_last_dim=4096), writes=[t_MLBD])
            P.dma("pool", lambda e: e.dma_start(out=WGT[:], in_=wgt_d), writes=[t_WGT])
            P.dma("sp", lambda e: e.dma_start(out=GBI[:], in_=gbias_d), writes=[t_GBI])
            P.dma("sp", lambda e: e.dma_start(out=TRI[:], in_=tri_d), writes=[t_TRI])
            P.op("dve", lambda e: e.memset(ONES[:], 1.0), writes=[t_ONES])
            B_PM = ps("B_PM", [128, 512], F32); t_PM = Trk("PS_PM")
            B_QK = ps("B_QK", [128, 512], F32); t_PQ = Trk("PS_PQ")
            B_VO = ps("B_VO", [128, 512], F32); t_PV = Trk("PS_PV")
            B_GP = ps("B_GP", [128, 512], F32); t_GP = Trk("PS_GP")
            B_N = ps("B_N", [128, 4, 512], F32); t_N = [Trk("PS_N%d" % i) for i in range(4)]
            BU = [B_PM, B_QK, B_VO, B_GP]; t_BU = [t_PM, t_PQ, t_PV, t_GP]
            VT = Rot(sb, "VT", 2, [128, 256], BF16)
            XM = sb("XM", [128, 8, 256], F32); t_XM = [Trk("XM%d" % c) for c in range(8)]
            XMB = sb("XMB", [128, 8, 256], BF16); t_XMB = [Trk("XMB%d" % c) for c in range(8)]
            UMB = sb("UMB", [128, 8, 256], BF16); t_UMB = [Trk("UMB%d" % c) for c in range(8)]
            QT = sb("QT", [128, 8, 256], BF16); t_QT = [Trk("QT%d" % c) for c in range(8)]
            KT = sb("KT", [128, 8, 256], BF16); t_KT = [Trk("KT%d" % c) for c in range(8)]
            GF = sb("GF", [8, 256], F32); t_GF = Trk("GF")
            GG = sb("GG", [128, 16], F32); t_GG = Trk("GG")
            GE = sb("GE", [128, 8], F32); t_GE = Trk("GE")
            GLn = sb("GLn", [128, 8], F32); t_GLn = Trk("GLn")
            GT2 = sb("GT2", [128, 8], F32); t_GT2 = Trk("GT2")
            EB = sb("EB", [128, 8], F32); t_EB = Trk("EB")
            WS = sb("WS", [128, 8], F32); t_WS = Trk("WS")
            EBL = sb("EBL", [128, 8], F32); t_EBL = Trk("EBL")
            EBLP = sb("EBLP", [128, 4], F32); t_EBLP = Trk("EBLP")
            DS = sb("DS", [128, 4, 2, 257], F32); t_DS = [Trk("DS%d" % h) for h in range(4)]
            DB = sb("DB", [128, 4, 2, 258], BF16); t_DB = [Trk("DB%d" % h) for h in range(4)]
            VX = sb("VX", [128, 2, 4, 258], BF16); t_VX = [Trk("VX%d" % i) for i in range(2)]
            KTM = sb("KTM", [128, 2, 1024], BF16); t_KTM = [Trk("KTM%d" % i) for i in range(2)]
            STt = sb("STt", [128, 2, 4, 128], BF16); t_STt = [Trk("STt%d" % i) for i in range(2)]
            E1 = sb("E1", [128, 8, 4], F32); t_E1 = Trk("E1")
            HH = Rot(sb, "HH", 1, [128, 1024], F32)

            def grp_rhs(kc, g):
                if g == 0:
                    return uT[:, kc, 0:256]
                gi = g - 1
                return uT[:, kc, 256 + gi * 256:256 + (gi + 1) * 256]

            def grp_trks(g):
                return t_uT[0:4] if g == 0 else t_uT[4:68]

            def gates_post(d):
                P.op("act", lambda e: e.activation(out=GF[:], in_=B_GP[0:8, 0:256], func=AF.Copy), reads=[t_GP], writes=[t_GF])
                for ch in range(2):
                    P.op("pe", (lambda ch: lambda e: e.transpose(
                        out=B_GP[:, 256 + ch * 8:256 + ch * 8 + 8], in_=GF[0:8, ch * 128:(ch + 1) * 128], identity=IDF[0:8, 0:8]))(ch),
                        reads=[t_GF, t_IDF], writes=[t_GP])
                P.op("dve", (lambda d: lambda e: e.tensor_tensor(
                    out=GG[:], in0=B_GP[:, 256:272], in1=GBI[:, d, :], op=ALU.add))(d), reads=[t_GP, t_GBI], writes=[t_GG])
                GGv = GG[:].rearrange("t (c k) -> t c k", k=8)
                P.op("act", lambda e: e.activation(
                    out=GE[:].rearrange("t (c h) -> t c h", h=4), in_=GGv[:, :, 4:8], func=AF.Exp, scale=-1.0),
                    reads=[t_GG], writes=[t_GE])
                P.op("act", lambda e: e.activation(out=GLn[:], in_=GE[:], func=AF.Ln, bias=1.0), reads=[t_GE], writes=[t_GLn])
                P.op("pe", (lambda d: lambda e: e.matmul(
                    B_GP[:, 288:296], lhsT=TRI[:, d, :], rhs=GLn[:], start=True, stop=True))(d),
                    reads=[t_TRI, t_GLn], writes=[t_GP])
                P.op("pe", lambda e: e.matmul(B_GP[:, 304:312], lhsT=ONES[:], rhs=GLn[:], start=True, stop=True),
                     reads=[t_ONES, t_GLn], writes=[t_GP])
                P.op("act", lambda e: e.activation(out=EB[:], in_=B_GP[:, 288:296], func=AF.Exp, scale=-1.0),
                     reads=[t_GP], writes=[t_EB])
                P.op("dve", lambda e: e.tensor_tensor(
                    out=GT2[:].rearrange("t (c h) -> t c h", h=4), in0=B_GP[:, 288:296].rearrange("t (c h) -> t c h", h=4),
                    in1=GGv[:, :, 0:4], op=ALU.add), reads=[t_GP, t_GG], writes=[t_GT2])
                P.op("act", lambda e: e.activation(out=WS[:], in_=GT2[:], func=AF.Exp), reads=[t_GT2], writes=[t_WS])
                P.op("act", lambda e: e.activation(out=EBL[:], in_=B_GP[:, 304:312], func=AF.Exp, scale=-1.0),
                     reads=[t_GP], writes=[t_EBL])
            def rec_group(g, d, QT, KT, XMB, UMB, t_QT, t_KT, t_XMB, t_UMB, XM, t_XM, hs):
                lat = g > 0
                gi = g - 1
                have_state = hs[0]
                for ch in range(2):
                    cols = slice(ch * 128, (ch + 1) * 128)
                    for c in range(8):
                        h, half = divmod(c, 2)
                        bk = h // 2
                        off = (h % 2) * 256 + half * 128
                        P.op("pe", (lambda c, bk, off, cols: lambda e: e.matmul(
                            B_N[:, bk, off:off + 128], lhsT=UMB[:, c, cols], rhs=MLBD[:, 16 + c, :], start=True, stop=True))(c, bk, off, cols),
                            reads=[t_UMB[c], t_MLBD], writes=[t_N[bk]])
                        P.op("pe", (lambda c, bk, off, cols: lambda e: e.matmul(
                            B_N[:, 2 + bk, off:off + 128], lhsT=XMB[:, c, cols], rhs=MLBD[:, 8 + c, :], start=True, stop=True))(c, bk, off, cols),
                            reads=[t_XMB[c], t_MLBD], writes=[t_N[2 + bk]])
                    for h in range(4):
                        bk = h // 2
                        off = (h % 2) * 256
                        P.op("dve", (lambda ch, h, bk, off: lambda e: e.tensor_scalar(
                            out=VX[:, ch, h, 0:256], in0=B_N[:, bk, off:off + 256], scalar1=WS[:, ch * 4 + h:ch * 4 + h + 1],
                            scalar2=None, op0=ALU.mult))(ch, h, bk, off), reads=[t_N[bk], t_WS], writes=[t_VX[ch]])
                    P.op("act", (lambda ch: lambda e: e.activation(
                        out=VX[:, ch, :, 256:257], in_=WS[:, ch * 4:ch * 4 + 4].unsqueeze(2), func=AF.Copy))(ch), reads=[t_WS], writes=[t_VX[ch]])
                    for bk in range(2):
                        P.op("act", (lambda ch, bk: lambda e: e.activation(
                            out=KTM[:, ch, bk * 512:(bk + 1) * 512], in_=B_N[:, 2 + bk, :], func=AF.Copy, scale=1.0 / 16.0))(ch, bk),
                            reads=[t_N[2 + bk]], writes=[t_KTM[ch]])
                    for h in range(4):
                        for half in range(2):
                            c = 2 * h + half
                            P.op("pe", (lambda c, h, half, cols: lambda e: e.matmul(
                                B_GP[:, h * 128:(h + 1) * 128], lhsT=KT[:, c, cols], rhs=QT[:, c, cols], start=(half == 0), stop=(half == 1)))(c, h, half, cols),
                                reads=[t_KT[c], t_QT[c]], writes=[t_GP])
                    P.op("dve", (lambda ch, d: lambda e: e.tensor_tensor(
                        out=STt[:, ch], in0=B_GP[:].rearrange("p (h t) -> p h t", t=128),
                        in1=TRI[:, d, :].unsqueeze(1).to_broadcast([128, 4, 128]), op=ALU.mult))(ch, d),
                        reads=[t_GP, t_TRI], writes=[t_STt[ch]])
                chs = (0, 1) if d == 0 else (1, 0)
                if d == 1 and lat:
                    yg, t_yg = YG.next()
                for ch in chs:
                    cols = slice(ch * 128, (ch + 1) * 128)
                    if d == 1 and lat:
                        for cq in (gi * 2 + ch, gi * 2 + ch - 1):
                            if cq >= 0 and cq not in hft_map:
                                hb, t_hb = HFt.next()
                                P.dma("sp", (lambda hb, cq: lambda e: e.dma_start(out=hb[:], in_=HF_d[cq]))(hb, cq),
                                      reads=[t_HF[cq]], writes=[t_hb])
                                hft_map[cq] = (hb, t_hb)
                    last_chunk = (d == 0 and g == 16 and ch == 1) or (d == 1 and g == 1 and ch == 0)
                    if not last_chunk:
                        for r in range(2):
                            for hh2 in range(2):
                                h = 2 * r + hh2
                                for half in range(2):
                                    bi_ = hh2 * 2 + half
                                    P.op("pe", (lambda ch, h, half, bi_: lambda e: e.matmul(
                                        BU[bi_][:, 0:257], lhsT=KTM[:, ch, h * 256 + half * 128:h * 256 + (half + 1) * 128],
                                        rhs=VX[:, ch, h, 0:257], start=True, stop=True))(ch, h, half, bi_),
                                        reads=[t_KTM[ch], t_VX[ch]], writes=[t_BU[bi_]])
                            for hh2 in range(2):
                                h = 2 * r + hh2
                                for half in range(2):
                                    bi_ = hh2 * 2 + half
                                    if have_state:
                                        P.op("dve", (lambda h, half, bi_: lambda e: e.scalar_tensor_tensor(
                                            out=DS[:, h, half, :], in0=DS[:, h, half, :], scalar=EBLP[:, h:h + 1], in1=BU[bi_][:, 0:257],
                                            op0=ALU.mult, op1=ALU.add))(h, half, bi_),
                                            reads=[t_DS[h], t_BU[bi_], t_EBLP], writes=[t_DS[h]])
                                    else:
                                        P.op("dve", (lambda h, half, bi_: lambda e: e.tensor_copy(
                                            out=DS[:, h, half, :], in_=BU[bi_][:, 0:257]))(h, half, bi_),
                                            reads=[t_BU[bi_]], writes=[t_DS[h]])
                    if lat:
                        hh, t_hh = HH.next()
                        for h in range(4):
                            P.op("pe", (lambda ch, h, hs: lambda e: e.matmul(
                                B_N[:, h, 0:257], lhsT=STt[:, ch, h, :], rhs=VX[:, ch, h, 0:257], start=True, stop=(not hs)))(ch, h, have_state),
                                reads=[t_STt[ch], t_VX[ch]], writes=[t_N[h]])
                            if have_state:
                                for half in range(2):
                                    c = 2 * h + half
                                    P.op("pe", (lambda c, h, half, cols: lambda e: e.matmul(
                                        B_N[:, h, 0:257], lhsT=QT[:, c, cols], rhs=DB[:, h, half, 0:257], start=False, stop=(half == 1)))(c, h, half, cols),
                                        reads=[t_QT[c], t_DB[h]], writes=[t_N[h]])
                        e0 = ch * 4
                        P.op("dve", (lambda ch: lambda e: e.tensor_tensor(
                            out=E1[:, 0:4, 0], in0=B_N[:, :, 256], in1=EB[:, ch * 4:ch * 4 + 4], op=ALU.mult))(ch),
                            reads=t_N + [t_EB], writes=[t_E1])
                        P.op("dve", lambda e: e.tensor_scalar(
                            out=E1[:, 0:4, 1], in0=E1[:, 0:4, 0], scalar1=-1.0, scalar2=1.0, op0=ALU.mult, op1=ALU.max),
                            reads=[t_E1], writes=[t_E1])
                        P.op("dve", lambda e: e.scalar_tensor_tensor(
                            out=E1[:, 0:4, 2], in0=E1[:, 0:4, 0], scalar=1.0, in1=E1[:, 0:4, 1], op0=ALU.max, op1=ALU.max),
                            reads=[t_E1], writes=[t_E1])
                        P.op("dve", lambda e: e.reciprocal(out=E1[:, 0:4, 3], in_=E1[:, 0:4, 2]), reads=[t_E1], writes=[t_E1])
                        P.op("dve", (lambda ch: lambda e: e.tensor_tensor(
                            out=E1[:, 4:8, 0], in0=E1[:, 0:4, 3], in1=EB[:, ch * 4:ch * 4 + 4], op=ALU.mult))(ch),
                            reads=[t_E1, t_EB], writes=[t_E1])
                        for h in range(4):
                            P.op("act", (lambda hh, h: lambda e: e.activation(
                                out=hh[:, h * 256:(h + 1) * 256], in_=B_N[:, h, 0:256], func=AF.Copy, scale=E1[:, 4 + h, 0:1]))(hh, h),
                                reads=[t_N[h], t_E1], writes=[t_hh])
                    if not last_chunk:
                        for h in range(4):
                            idx = ch * 4 + h
                            P.op("act", (lambda h, idx: lambda e: e.activation(
                                out=DB[:, h, :, 0:257], in_=DS[:, h, :, :], func=AF.Copy, scale=EBL[:, idx:idx + 1]))(h, idx),
                                reads=[t_DS[h], t_EBL], writes=[t_DB[h]])
                        P.op("dve", (lambda ch: lambda e: e.tensor_copy(out=EBLP[:], in_=EBL[:, ch * 4:ch * 4 + 4]))(ch),
                             reads=[t_EBL], writes=[t_EBLP])
                    have_state = True; hs[0] = True
                    if not lat:
                        continue
                    cg = gi * 2 + ch
                    if d == 0:
                        P.dma("sp", (lambda hh, cg: lambda e: e.dma_start(out=HF_d[cg], in_=hh[:]))(hh, cg),
                              reads=[t_hh], writes=[t_HF[cg]], semtrk=t_hh)
                        continue
                    hft, t_hft = hft_map.pop(cg)
                    P.op("dve", (lambda hh, hft: lambda e: e.tensor_tensor(out=hh[:], in0=hh[:], in1=hft[:], op=ALU.add))(hh, hft),
                         reads=[t_hh, t_hft], writes=[t_hh])
                    for h in range(4):
                        P.op("dve", (lambda hh, h: lambda e: e.bn_stats(out=BS[:, h, :], in_=hh[:, h * 256:(h + 1) * 256]))(hh, h),
                             reads=[t_hh], writes=[t_BS])
                        P.op("dve", (lambda h: lambda e: e.bn_aggr(out=MV[:, h, :], in_=BS[:, h, :]))(h), reads=[t_BS], writes=[t_MV])
                    P.op("act", lambda e: e.activation(out=SD[:, 0:4], in_=MV[:, :, 1], func=AF.Sqrt, bias=EPS), reads=[t_MV], writes=[t_SD])
                    P.op("dve", lambda e: e.reciprocal(out=SD[:, 4:8], in_=SD[:, 0:4]), reads=[t_SD], writes=[t_SD])
                    for h in range(4):
                        P.op("dve", (lambda hh, h: lambda e: e.tensor_scalar(
                            out=HN[:, h * 256:(h + 1) * 256], in0=hh[:, h * 256:(h + 1) * 256], scalar1=MV[:, h, 0:1],
                            scalar2=SD[:, 4 + h:5 + h], op0=ALU.subtract, op1=ALU.mult))(hh, h),
                            reads=[t_hh, t_MV, t_SD], writes=[t_HN])
                    for c in range(8):
                        P.op("pe", (lambda c: lambda e: e.transpose(
                            out=B_N[:, c // 4, (c % 4) * 128:(c % 4 + 1) * 128], in_=HN[:, c * 128:(c + 1) * 128], identity=IDF[:]))(c),
                            reads=[t_HN, t_IDF], writes=[t_N[c // 4]])
                    o_, w_ = FVCOLS["mlng"]
                    for b2 in range(2):
                        P.op("dve", (lambda b2: lambda e: e.tensor_tensor(
                            out=Y1[:, 4 * b2:4 * b2 + 4, :], in0=B_N[:, b2, :].rearrange("p (c t) -> p c t", t=128),
                            in1=FV[:, o_ + 4 * b2:o_ + 4 * b2 + 4].unsqueeze(2).to_broadcast([128, 4, 128]), op=ALU.mult))(b2),
                            reads=[t_N[b2], t_FV], writes=[t_Y1])
                    P.op("dve", (lambda cols: lambda e: e.tensor_tensor(out=Y1[:], in0=Y1[:], in1=XM[:, :, cols], op=ALU.add))(cols),
                         reads=[t_Y1] + t_XM, writes=[t_Y1])
                    P.op("dve", (lambda yg, cols: lambda e: e.tensor_tensor(out=yg[:, :, cols], in0=Y1[:], in1=SIG[:, :, cols], op=ALU.mult))(yg, cols),
                         reads=[t_Y1] + t_SIG, writes=[t_yg])
                if d == 1 and lat:
                    P.dma("sp", (lambda yg, gi: lambda e: e.dma_start(
                        out=YML_d[:, :, gi * 256:(gi + 1) * 256].rearrange("c p t -> p c t"), in_=yg[:]))(yg, gi),
                        reads=[t_yg], writes=[t_YML[gi]], semtrk=t_yg)
            for d in range(2):
                with ExitStack() as st2:
                    sb2 = lambda name, shape, dt: st2.enter_context(nc.sbuf_tensor(name, list(shape), dt))
                    if d == 0:
                        WMX = sb2("WMX", [128, 8, 8, 128], BF16); t_WMX = [Trk("WMX%d" % c) for c in range(8)]
                        for c in range(8):
                            P.dma("pool", (lambda c: lambda e: e.dma_start(
                                out=WMX[:, c], in_=win_d[16 + c].rearrange("p (k j) -> p k j", j=128)))(c), writes=[t_WMX[c]])
                        UMF = Rot(sb2, "UMF", 2, [128, 260], F32)
                        HALO = sb2("HALO", [128, 8, 2], F32); t_HALO = [Trk("HALO%d" % c) for c in range(8)]
                        XCV = Rot(sb2, "XCV", 2, [128, 256], F32)
                    if d == 1:
                        WMO = sb2("WMO", [128, 8, 8, 128], BF16); t_WMO = [Trk("WMO%d" % c) for c in range(8)]
                        for c in range(8):
                            P.dma("pool", (lambda c: lambda e: e.dma_start(
                                out=WMO[:, c], in_=win_d[24 + c].rearrange("p (k j) -> p k j", j=128)))(c), writes=[t_WMO[c]])
                        SIG = sb2("SIG", [128, 8, 256], F32); t_SIG = [Trk("SIG%d" % c) for c in range(8)]
                        HFt = Rot(sb2, "HFt", 2, [128, 1024], F32)
                        hft_map = {}
                        HN = sb2("HN", [128, 1024], F32); t_HN = Trk("HN")
                        BS = sb2("BS", [128, 4, 6], F32); t_BS = Trk("BS")
                        MV = sb2("MV", [128, 4, 2], F32); t_MV = Trk("MV")
                        SD = sb2("SD", [128, 8], F32); t_SD = Trk("SD")
                        Y1 = sb2("Y1", [128, 8, 128], F32); t_Y1 = Trk("Y1")
                        YG = Rot(sb2, "YG", 2, [128, 8, 256], BF16)
                        GS1 = [sb2("QT1", [128, 8, 256], BF16), sb2("KT1", [128, 8, 256], BF16),
                               sb2("XMB1", [128, 8, 256], BF16), sb2("UMB1", [128, 8, 256], BF16)]
                        GSETS = [((QT, KT, XMB, UMB), [Trk("gs0_%d" % i) for i in range(4)]),
                                 (tuple(GS1), [Trk("gs1_%d" % i) for i in range(4)])]
                        t_XMl = Trk("XMl")
                    hs = [False]
                    order = [0] + (list(range(1, 17)) if d == 0 else list(range(16, 0, -1)))
                    for gpos, g in enumerate(order):
                        lat = g > 0
                        gi = g - 1
                        if d == 0:
                            import os as _os2
                            PB = [B_N[:, 0, :], B_N[:, 1, :]]
                            t_PB = [t_N[0], t_N[1]]
                            if _os2.environ.get('PBPM'):
                                PB = [B_PM[:], B_PM[:]]; t_PB = [t_PM, t_PM]
                            hi = 259 if (lat and gi <= 14) else 258
                            lo = 0 if (lat and gi >= 1) else 2

                            def emit_proj(c):
                                pb, t_pb = PB[c % 2], t_PB[c % 2]
                                for kc in range(8):
                                    P.op("pe", (lambda pb, c, kc, g: lambda e: e.matmul(
                                        pb[:, 2:258], lhsT=WMX[:, c, kc, :], rhs=grp_rhs(kc, g), start=(kc == 0), stop=(kc == 7)))(pb, c, kc, g),
                                        reads=[t_WMX[c]] + grp_trks(g), writes=[t_pb])
                                if hi == 259:
                                    b0 = 256 + (gi + 1) * 256
                                    for kc in range(8):
                                        P.op("pe", (lambda pb, c, kc, b0: lambda e: e.matmul(
                                            pb[:, 258:259], lhsT=WMX[:, c, kc, :], rhs=uT[:, kc, b0:b0 + 1], start=(kc == 0), stop=(kc == 7)))(pb, c, kc, b0),
                                            reads=[t_WMX[c]] + grp_trks(g), writes=[t_pb])

                            bufs_c = {}

                            def emit_evac_act(c):
                                pb, t_pb = PB[c % 2], t_PB[c % 2]
                                umf, t_umf = UMF.next()
                                xcv, t_xcv = XCV.next()
                                bufs_c[c] = (umf, t_umf, xcv, t_xcv)
                                P.op("act", (lambda umf, pb, hi: lambda e: e.activation(
                                    out=umf[:, 2:hi], in_=pb[:, 2:hi], func=AF.Copy))(umf, pb, hi), reads=[t_pb], writes=[t_umf])
                                P.op("act", (lambda c, pb: lambda e: e.activation(out=UMB[:, c, :], in_=pb[:, 2:258], func=AF.Copy))(c, pb),
                                     reads=[t_pb], writes=[t_UMB[c]])

                            def emit_conv(c):
                                umf, t_umf, xcv, t_xcv = bufs_c[c]
                                if lo == 0:
                                    P.op("dve", (lambda umf, c: lambda e: e.tensor_copy(out=umf[:, 0:2], in_=HALO[:, c, :]))(umf, c),
                                         reads=[t_HALO[c]], writes=[t_umf])
                                else:
                                    P.op("dve", (lambda umf: lambda e: e.memset(umf[:, 0:2], 0.0))(umf), writes=[t_umf])
                                if hi == 258:
                                    P.op("dve", (lambda umf: lambda e: e.memset(umf[:, 258:259], 0.0))(umf), writes=[t_umf])
                                if lat and gi <= 14:
                                    P.op("dve", (lambda umf, c: lambda e: e.tensor_copy(out=HALO[:, c, :], in_=umf[:, 256:258]))(umf, c),
                                         reads=[t_umf], writes=[t_HALO[c]])
                                P.op("dve", (lambda umf, xcv, c: lambda e: e.tensor_scalar(
                                    out=xcv[:], in0=umf[:, 0:256], scalar1=fvc("mlcw", c * 4), scalar2=fvc("mlcb", c),
                                    op0=ALU.mult, op1=ALU.add))(umf, xcv, c), reads=[t_umf, t_FV], writes=[t_xcv])
                                for j in range(1, 4):
                                    P.op("dve", (lambda umf, xcv, c, j: lambda e: e.scalar_tensor_tensor(
                                        out=xcv[:], in0=umf[:, j:j + 256], scalar=fvc("mlcw", c * 4 + j), in1=xcv[:],
                                        op0=ALU.mult, op1=ALU.add))(umf, xcv, c, j), reads=[t_umf, t_FV, t_xcv], writes=[t_xcv])
                                P.op("act", (lambda xcv, c: lambda e: e.activation(out=XM[:, c, :], in_=xcv[:], func=AF.Silu))(xcv, c),
                                     reads=[t_xcv], writes=[t_XM[c]])
                                P.op("act", (lambda xcv, c: lambda e: e.activation(out=XMB[:, c, :], in_=xcv[:], func=AF.Silu))(xcv, c),
                                     reads=[t_xcv], writes=[t_XMB[c]])

                            vts = {}

                            def emit_qkv(c):
                                vt, t_vt = VT.next()
                                vts[c] = (vt, t_vt)
                                P.op("pe", (lambda c: lambda e: e.matmul(
                                    B_QK[:, 0:256], lhsT=MLBD[:, c, :], rhs=XMB[:, c, :], start=True, stop=True))(c),
                                    reads=[t_MLBD, t_XMB[c]], writes=[t_PQ])
                                P.op("pe", (lambda c: lambda e: e.matmul(
                                    B_QK[:, 256:512], lhsT=MLBD[:, 8 + c, :], rhs=XMB[:, c, :], start=True, stop=True))(c),
                                    reads=[t_MLBD, t_XMB[c]], writes=[t_PQ])
                                P.op("pe", (lambda c: lambda e: e.matmul(
                                    B_VO[:, 0:256], lhsT=MLBD[:, 16 + c, :], rhs=UMB[:, c, :], start=True, stop=True))(c),
                                    reads=[t_MLBD, t_UMB[c]], writes=[t_PV])
                                P.op("act", (lambda c: lambda e: e.activation(out=QT[:, c, :], in_=B_QK[:, 0:256], func=AF.Copy))(c),
                                     reads=[t_PQ], writes=[t_QT[c]])
                                P.op("act", (lambda c: lambda e: e.activation(
                                    out=KT[:, c, :], in_=B_QK[:, 256:512], func=AF.Copy, scale=1.0 / 16.0))(c),
                                    reads=[t_PQ], writes=[t_KT[c]])
                                P.op("dve", (lambda vt: lambda e: e.tensor_copy(out=vt[:], in_=B_VO[:, 0:256]))(vt),
                                     reads=[t_PV], writes=[t_vt])

                            def emit_gates(c):
                                vt, t_vt = vts[c]
                                for ti, (src, t_src) in enumerate(((QT[:, c, :], t_QT[c]), (KT[:, c, :], t_KT[c]), (vt[:], t_vt))):
                                    P.op("pe", (lambda c, ti, src, d: lambda e: e.matmul(
                                        B_GP[0:8, 0:256], lhsT=WGT[:, ti * 8 + c, d * 8:(d + 1) * 8], rhs=src,
                                        start=(c == 0 and ti == 0), stop=(c == 7 and ti == 2)))(c, ti, src, d),
                                        reads=[t_WGT, t_src], writes=[t_GP])

                            if _os2.environ.get("NOPIPE"):
                                for c in range(8):
                                    emit_proj(c)
                                    emit_evac_act(c)
                                    emit_conv(c)
                                    emit_qkv(c)
                                    emit_gates(c)
                            else:
                                emit_proj(0)
                                emit_proj(1)
                                emit_evac_act(0)
                                for c in range(8):
                                    if c + 2 < 8:
                                        emit_proj(c + 2)
                                    if c + 1 < 8:
                                        emit_evac_act(c + 1)
                                    emit_conv(c)
                                    emit_qkv(c)
                                    if c > 0:
                                        emit_gates(c - 1)
                                emit_gates(7)
                            for wi_, (arr, trs) in enumerate(((QT, t_QT), (KT, t_KT), (XMB, t_XMB), (UMB, t_UMB))):
                                P.dma("sp", (lambda arr, g, wi_: lambda e: e.dma_start(out=SPQ_d[g, wi_], in_=arr[:]))(arr, g, wi_),
                                      reads=trs, writes=[t_SPQ[g][wi_]], semtrk=t_spd[wi_])
                            if lat:
                                P.dma("sp", (lambda gi: lambda e: e.dma_start(out=SPX_d[gi], in_=XM[:]))(gi),
                                      reads=t_XM, writes=[t_SPX[gi]], semtrk=t_spd[4])
                            gates_post(d)
                            rec_group(g, d, QT, KT, XMB, UMB, t_QT, t_KT, t_XMB, t_UMB, XM, t_XM, hs)
                            continue
                        def load_set(gq, si):
                            arrs, trs = GSETS[si]
                            for wi_ in range(4):
                                P.dma("sp", (lambda arrs, gq, wi_: lambda e: e.dma_start(out=arrs[wi_][:], in_=SPQ_d[gq, wi_]))(arrs, gq, wi_),
                                      reads=[t_SPQ[gq][wi_]], writes=[trs[wi_]])
                        if gpos == 0:
                            load_set(g, 0)
                        if gpos + 1 < len(order):
                            load_set(order[gpos + 1], (gpos + 1) % 2)
                        (QTg, KTg, XMBg, UMBg), trs = GSETS[gpos % 2]
                        if lat:
                            P.dma("sp", (lambda gi: lambda e: e.dma_start(out=XM[:], in_=SPX_d[gi]))(gi), reads=[t_SPX[gi]], writes=[t_XMl])
                        for c in range(8):
                            vt, t_vt = VT.next()
                            P.op("pe", (lambda c, UMBg: lambda e: e.matmul(
                                B_VO[:, 0:256], lhsT=MLBD[:, 16 + c, :], rhs=UMBg[:, c, :], start=True, stop=True))(c, UMBg),
                                reads=[t_MLBD, trs[3]], writes=[t_PV])
                            P.op("dve", (lambda vt: lambda e: e.tensor_copy(out=vt[:], in_=B_VO[:, 0:256]))(vt),
                                 reads=[t_PV], writes=[t_vt])
                            if lat:
                                for kc in range(8):
                                    P.op("pe", (lambda c, kc, g: lambda e: e.matmul(
                                        B_PM[:, 0:256], lhsT=WMO[:, c, kc, :], rhs=grp_rhs(kc, g), start=(kc == 0), stop=(kc == 7)))(c, kc, g),
                                        reads=[t_WMO[c]] + grp_trks(g), writes=[t_PM])
                                P.op("act", (lambda c: lambda e: e.activation(out=SIG[:, c, :], in_=B_PM[:, 0:256], func=AF.Sigmoid))(c),
                                     reads=[t_PM], writes=[t_SIG[c]])
                            for ti, (src, t_src) in enumerate(((QTg[:, c, :], trs[0]), (KTg[:, c, :], trs[1]), (vt[:], t_vt))):
                                P.op("pe", (lambda c, ti, src, d: lambda e: e.matmul(
                                    B_GP[0:8, 0:256], lhsT=WGT[:, ti * 8 + c, d * 8:(d + 1) * 8], rhs=src,
                                    start=(c == 0 and ti == 0), stop=(c == 7 and ti == 2)))(c, ti, src, d),
                                    reads=[t_WGT, t_src], writes=[t_GP])
                        if lat:
                            o2_, w2_ = FVCOLS["mlsk"]
                            P.op("dve", lambda e: e.tensor_tensor(
                                out=XM[:], in0=XM[:], in1=FV[:, o2_:o2_ + 8].unsqueeze(2).to_broadcast([128, 8, 256]), op=ALU.mult),
                                reads=[t_XMl, t_FV], writes=[t_XMl])
                        gates_post(d)
                        rec_group(g, d, QTg, KTg, XMBg, UMBg, [trs[0]] * 8, [trs[1]] * 8, [trs[2]] * 8, [trs[3]] * 8, XM, [t_XMl] * 8, hs)
                    P.barrier()
                    P.emit()
        if "yml" in debug:
            dbg_d["yml"] = YML_d
        if stop_after == "ml":
            P.wait_all("sp", P.out_toks)
            P.barrier()
            P.emit()
            ust.close()
            return nc, dbg_d

        x_cm = x_d.rearrange("(r j) d -> j r d", j=64)
        out_cm = out_d.rearrange("(r j) d -> j r d", j=64)
        t_X1 = [Trk("X1_%d" % i) for i in range(32)]
        t_H2 = [Trk("H2_%d" % i) for i in range(16)]
        with ExitStack() as st:
            sb = lambda name, shape, dt: st.enter_context(nc.sbuf_tensor(name, list(shape), dt))
            ps = lambda name, shape, dt: st.enter_context(nc.psum_tensor(name, list(shape), dt))
            WGR = sb("WGR", [128, 8, 8, 128], BF16); WGM = sb("WGM", [128, 8, 8, 128], BF16)
            WBR = sb("WBR", [128, 8, 8, 128], BF16); WBM = sb("WBM", [128, 8, 8, 128], BF16)
            WOUT = sb("WOUT", [128, 8, 1024], BF16)
            t_WGR = [Trk("WGR%d" % i) for i in range(8)]; t_WGM = [Trk("WGM%d" % i) for i in range(8)]
            t_WBR = [Trk("WBR%d" % i) for i in range(8)]; t_WBM = [Trk("WBM%d" % i) for i in range(8)]
            t_WOUT = [Trk("WOUT%d" % i) for i in range(8)]
            for oc in range(8):
                for (W, t_W, src) in ((WGR, t_WGR, win_d[32 + oc]), (WGM, t_WGM, win_d[40 + oc]),
                                      (WBR, t_WBR, wbrg_d[oc]), (WBM, t_WBM, wbml_d[oc])):
                    P.dma("pool", (lambda W, oc, src: lambda e: e.dma_start(
                        out=W[:, oc], in_=src.rearrange("p (k j) -> p k j", j=128)))(W, oc, src), writes=[t_W[oc]])
            for kc in range(8):
                P.dma("pool", (lambda kc: lambda e: e.dma_start(out=WOUT[:, kc, :], in_=wout_d[:, kc, :]))(kc), writes=[t_WOUT[kc]])
            YRt = Rot(sb, "YRt", 1, [128, 8, 512], BF16)
            YMt = Rot(sb, "YMt", 1, [128, 8, 512], BF16)
            SG = Rot(sb, "SG", 1, [128, 1024], F32)
            MIX = Rot(sb, "MIX", 1, [128, 8, 512], BF16)
            XT = Rot(sb, "XTc", 2, [128, 1024], F32)
            X1t = Rot(sb, "X1t", 1, [128, 1024], F32)
            XN = Rot(sb, "XNc", 1, [128, 1024], BF16)
            H2s = Rot(sb, "H2s", 1, [128, 8, 256], BF16)
            STc = sb("STc", [128, 32, 4], F32)
            BA0 = ps("BA0", [128, 512], F32); t_BA0 = Trk("PS_BA0")
            BA1 = ps("BA1", [128, 512], F32); t_BA1 = Trk("PS_BA1")
            BB0 = ps("BB0", [128, 512], F32); t_BB0 = Trk("PS_BB0")
            BB1 = ps("BB1", [128, 512], F32); t_BB1 = Trk("PS_BB1")
            BY = ps("BY", [128, 1024], F32); t_BY = Trk("PS_BY")
            BT = ps("BT", [128, 8, 128], BF16); t_BT = Trk("PS_BT")
            BT2 = ps("BT2", [128, 8, 128], BF16); t_BT2 = Trk("PS_BT2")
            def load_y(T):
                yr, t_yr = YRt.next()
                ym, t_ym = YMt.next()
                P.dma("sp", (lambda yr, T: lambda e: e.dma_start(
                    out=yr[:], in_=YRG_d[:, :, T * 512:(T + 1) * 512].rearrange("c p t -> p c t")))(yr, T), reads=t_YRG, writes=[t_yr])
                P.dma("sp", (lambda ym, T: lambda e: e.dma_start(
                    out=ym[:], in_=YML_d[:, :, T * 512:(T + 1) * 512].rearrange("c p t -> p c t")))(ym, T),
                    reads=t_YML[2 * T:2 * T + 2], writes=[t_ym])
                return yr, t_yr, ym, t_ym

            def load_x(ti):
                xt, t_xt = XT.next()
                for jj in range(2):
                    P.dma("sp", (lambda xt, jj, ti: lambda e: e.dma_start(
                        out=xt[jj * 64:(jj + 1) * 64, :], in_=x_cm[2 * ti + jj]))(xt, jj, ti), writes=[t_xt])
                return xt, t_xt

            h2s_trk2 = {}
            ynext = load_y(0)
            xnext = load_x(0)
            for T in range(8):
                yr, t_yr, ym, t_ym = ynext
                mix, t_mix = MIX.next()
                for oc in range(8):
                    sg, t_sg = SG.next()
                    for (W, t_W, bank, t_bank) in ((WGR, t_WGR, BA0, t_BA0), (WGM, t_WGM, BA1, t_BA1)):
                        for kc in range(8):
                            P.op("pe", (lambda W, bank, oc, kc, T: lambda e: e.matmul(
                                bank[:], lhsT=W[:, oc, kc, :],
                                rhs=uT[:, kc, 256 + T * 512:256 + (T + 1) * 512],
                                start=(kc == 0), stop=(kc == 7)))(W, bank, oc, kc, T), reads=[t_W[oc]] + t_uT[4:68], writes=[t_bank])
                    for (W, t_W, src, t_src, bank, t_bank) in ((WBR, t_WBR, yr, t_yr, BB0, t_BB0), (WBM, t_WBM, ym, t_ym, BB1, t_BB1)):
                        for kc in range(8):
                            P.op("pe", (lambda W, bank, oc, kc, src: lambda e: e.matmul(
                                bank[:], lhsT=W[:, oc, kc, :], rhs=src[:, kc, :],
                                start=(kc == 0), stop=(kc == 7)))(W, bank, oc, kc, src), reads=[t_W[oc], t_src], writes=[t_bank])
                    for (i_, bank, t_bank) in ((0, BA0, t_BA0), (1, BA1, t_BA1)):
                        P.op("act", (lambda sg, bank, i_: lambda e: e.activation(
                            out=sg[:, i_ * 512:(i_ + 1) * 512], in_=bank[:], func=AF.Sigmoid))(sg, bank, i_), reads=[t_bank], writes=[t_sg])
                    for (i_, bank, t_bank) in ((0, BB0, t_BB0), (1, BB1, t_BB1)):
                        P.op("dve", (lambda sg, bank, i_: lambda e: e.tensor_tensor(
                            out=sg[:, i_ * 512:(i_ + 1) * 512], in0=bank[:], in1=sg[:, i_ * 512:(i_ + 1) * 512], op=ALU.mult))(sg, bank, i_),
                            reads=[t_bank, t_sg], writes=[t_sg])
                    P.op("dve", (lambda mix, sg, oc: lambda e: e.tensor_tensor(
                        out=mix[:, oc, :], in0=sg[:, 0:512], in1=sg[:, 512:1024], op=ALU.add))(mix, sg, oc),
                        reads=[t_sg], writes=[t_mix])
                if T + 1 < 8:
                    ynext = load_y(T + 1)
                for s_ in range(4):
                    ti = T * 4 + s_
                    if s_ % 2 == 0:
                        h2s, t_h2s = H2s.next()
                        t_h2s2 = h2s_trk2.setdefault(id(t_h2s), Trk("h2s_b"))
                    xt, t_xt = xnext
                    if ti + 1 < 32:
                        xnext = load_x(ti + 1)
                    x1, t_x1 = X1t.next()
                    xn, t_xn = XN.next()
                    t_st = Trk("stc%d" % ti)
                    for half in range(2):
                        for kc in range(8):
                            P.op("pe", (lambda mix, half, kc, s_: lambda e: e.matmul(
                                BY[:, half * 512:(half + 1) * 512], lhsT=mix[:, kc, s_ * 128:(s_ + 1) * 128],
                                rhs=WOUT[:, kc, half * 512:(half + 1) * 512], start=(kc == 0), stop=(kc == 7)))(mix, half, kc, s_),
                                reads=[t_mix, t_WOUT[kc]], writes=[t_BY])
                    P.op("dve", (lambda x1: lambda e: e.tensor_tensor(out=x1[:], in0=BY[:], in1=GROW[:, 0, :], op=ALU.mult))(x1),
                         reads=[t_BY, t_GROW], writes=[t_x1])
                    P.op("dve", (lambda x1, xt: lambda e: e.tensor_tensor(out=x1[:], in0=x1[:], in1=xt[:], op=ALU.add))(x1, xt),
                         reads=[t_x1, t_xt], writes=[t_x1])
                    P.dma("sp", (lambda x1, ti: lambda e: e.dma_start(out=X1_d[ti * 128:(ti + 1) * 128, :], in_=x1[:]))(x1, ti),
                          reads=[t_x1], writes=[t_X1[ti]], semtrk=t_x1)
                    P.op("act", (lambda x1, xn, ti: lambda e: e.activation(
                        out=xn[:], in_=x1[:], func=AF.Square, accum_out=STc[:, ti, 0:1]))(x1, xn, ti), reads=[t_x1], writes=[t_xn, t_st])
                    P.op("act", (lambda ti: lambda e: e.activation(
                        out=STc[:, ti, 1:2], in_=STc[:, ti, 0:1], func=AF.Sqrt, scale=1.0 / 1024.0, bias=EPS))(ti), reads=[t_st], writes=[t_st])
                    P.op("dve", (lambda ti: lambda e: e.reciprocal(out=STc[:, ti, 2:3], in_=STc[:, ti, 1:2]))(ti), reads=[t_st], writes=[t_st])
                    P.op("act", (lambda x1, xn, ti: lambda e: e.activation(
                        out=xn[:], in_=x1[:], func=AF.Copy, scale=STc[:, ti, 2:3]))(x1, xn, ti), reads=[t_x1, t_st], writes=[t_xn])
                    for c in range(8):
                        btc, t_btc = (BT, t_BT) if c < 4 else (BT2, t_BT2)
                        P.op("pe", (lambda xn, c, btc: lambda e: e.transpose(
                            out=btc[:, c, :], in_=xn[:, c * 128:(c + 1) * 128], identity=IDB[:]))(xn, c, btc), reads=[t_xn, t_IDB], writes=[t_btc])
                    for c in (0, 4, 1, 5, 2, 6, 3, 7):
                        dst = h2s[:, c, (s_ % 2) * 128:(s_ % 2 + 1) * 128]
                        if c < 4:
                            P.op("dve", (lambda c, dst: lambda e: e.tensor_scalar(
                                out=dst, in0=BT[:, c, :], scalar1=A2[:, c:c + 1], scalar2=MODT[:, 48 + 2 * c:49 + 2 * c],
                                op0=ALU.mult, op1=ALU.add))(c, dst), reads=[t_BT, t_A2, t_MODT], writes=[t_h2s])
                        else:
                            P.op("act", (lambda c, dst: lambda e: e.activation(
                                out=dst, in_=BT2[:, c, :], func=AF.Identity, scale=A2[:, c:c + 1], bias=MODT[:, 48 + 2 * c:49 + 2 * c]))(c, dst),
                                reads=[t_BT2, t_A2, t_MODT], writes=[t_h2s2])
                    if s_ % 2 == 1:
                        Tq = ti // 2
                        P.dma("sp", (lambda h2s, Tq: lambda e: e.dma_start(out=H2_d[Tq], in_=h2s[:]))(h2s, Tq),
                              reads=[t_h2s, t_h2s2], writes=[t_H2[Tq]], semtrk=t_h2s)
            P.barrier()
            P.emit()
        ust.close()
        if "x1" in debug:
            dbg_d["x1"] = X1_d

        with ExitStack() as st:
            sb = lambda name, shape, dt: st.enter_context(nc.sbuf_tensor(name, list(shape), dt))
            ps = lambda name, shape, dt: st.enter_context(nc.psum_tensor(name, list(shape), dt))
            WFI = sb("WFI", [128, 44, 8, 128], BF16); t_WFI = [Trk("WFI%d" % i) for i in range(44)]
            WFO = sb("WFO", [128, 22, 1024], BF16); t_WFO = [Trk("WFO%d" % i) for i in range(22)]
            FGR = sb("FGR", [128, 1024], F32); t_FGR = Trk("FGR")
            P.dma("sp", lambda e: e.dma_start(out=FGR[:], in_=fgrow_d), writes=[t_FGR])
            for f_ in range(22):
                for ci in (f_, 22 + f_):
                    P.dma("pool", (lambda ci: lambda e: e.dma_start(
                        out=WFI[:, ci], in_=wffi_d[ci].rearrange("p (k j) -> p k j", j=128)))(ci), writes=[t_WFI[ci]])
                P.dma("pool", (lambda f_: lambda e: e.dma_start(out=WFO[:, f_, :], in_=wffo_d[:, f_, :]))(f_), writes=[t_WFO[f_]])
            H2t = Rot(sb, "H2t", 2, [128, 8, 512], BF16)
            SGt = Rot(sb, "SGt", 2, [128, 512], F32)
            HID = Rot(sb, "HID", 1, [128, 22, 512], BF16)
            X1r = Rot(sb, "X1r", 2, [128, 1024], F32)
            T2 = Rot(sb, "T2", 2, [128, 1024], F32)
            JK = sb("JK", [128, 1024], BF16); t_JK = Trk("JK")
            STf = sb("STf", [128, 32, 4], F32)
            BG0 = Rot(ps, "PS_BG0", 2, [128, 512], F32)
            BG1 = Rot(ps, "PS_BG1", 2, [128, 512], F32)
            BO = Rot(ps, "PS_BO", 2, [128, 1024], F32)
            h2_map = {}
            x1_map = {}
            for T in range(8):
                for Tq in (T, T + 1):
                    if Tq < 8 and Tq not in h2_map:
                        hb, t_hb = H2t.next()
                        for q_ in range(2):
                            P.dma("sp", (lambda hb, Tq, q_: lambda e: e.dma_start(out=hb[:, :, q_ * 256:(q_ + 1) * 256], in_=H2_d[2 * Tq + q_]))(hb, Tq, q_),
                                  reads=[t_H2[2 * Tq + q_]], writes=[t_hb])
                        h2_map[Tq] = (hb, t_hb)
                h2, t_h2 = h2_map.pop(T)
                hid, t_hid = HID.next()
                for f_ in range(22):
                    g0, t_g0 = BG0.next()
                    g1, t_g1 = BG1.next()
                    sg, t_sg = SGt.next()
                    for (ci, bank, t_bank) in ((f_, g0, t_g0), (22 + f_, g1, t_g1)):
                        for kc in range(8):
                            P.op("pe", (lambda bank, ci, kc, h2: lambda e: e.matmul(
                                bank[:], lhsT=WFI[:, ci, kc, :], rhs=h2[:, kc, :], start=(kc == 0), stop=(kc == 7)))(bank, ci, kc, h2),
                                reads=[t_WFI[ci], t_h2], writes=[t_bank])
                    P.op("act", (lambda sg, g0: lambda e: e.activation(out=sg[:], in_=g0[:], func=AF.Silu))(sg, g0),
                         reads=[t_g0], writes=[t_sg])
                    P.op("dve", (lambda hid, f_, sg, g1: lambda e: e.tensor_tensor(
                        out=hid[:, f_, :], in0=g1[:], in1=sg[:], op=ALU.mult))(hid, f_, sg, g1), reads=[t_g1, t_sg], writes=[t_hid])
                for s_ in range(4):
                    ti = T * 4 + s_
                    bo, t_bo = BO.next()
                    for tq in (ti, ti + 1):
                        if tq < 32 and tq not in x1_map:
                            xb, t_xb = X1r.next()
                            P.dma("sp", (lambda xb, tq: lambda e: e.dma_start(out=xb[:], in_=X1_d[tq * 128:(tq + 1) * 128, :]))(xb, tq),
                                  reads=[t_X1[tq]], writes=[t_xb])
                            x1_map[tq] = (xb, t_xb)
                    x1, t_x1 = x1_map.pop(ti)
                    t2, t_t2 = T2.next()
                    t_st = Trk("stf%d" % ti)
                    for half in range(2):
                        for f_ in range(22):
                            P.op("pe", (lambda bo, hid, half, f_, s_: lambda e: e.matmul(
                                bo[:, half * 512:(half + 1) * 512], lhsT=hid[:, f_, s_ * 128:(s_ + 1) * 128],
                                rhs=WFO[:, f_, half * 512:(half + 1) * 512], start=(f_ == 0), stop=(f_ == 21)))(bo, hid, half, f_, s_),
                                reads=[t_hid, t_WFO[f_]], writes=[t_bo])
                    P.op("dve", (lambda t2, bo: lambda e: e.tensor_tensor(out=t2[:], in0=bo[:], in1=GROW[:, 1, :], op=ALU.mult))(t2, bo),
                         reads=[t_bo, t_GROW], writes=[t_t2])
                    P.op("dve", (lambda t2, x1: lambda e: e.tensor_tensor(out=t2[:], in0=t2[:], in1=x1[:], op=ALU.add))(t2, x1),
                         reads=[t_t2, t_x1], writes=[t_t2])
                    P.op("act", (lambda t2, ti: lambda e: e.activation(
                        out=JK[:], in_=t2[:], func=AF.Square, accum_out=STf[:, ti, 0:1]))(t2, ti), reads=[t_t2], writes=[t_JK, t_st])
                    P.op("act", (lambda ti: lambda e: e.activation(
                        out=STf[:, ti, 1:2], in_=STf[:, ti, 0:1], func=AF.Sqrt, scale=1.0 / 1024.0, bias=EPS))(ti), reads=[t_st], writes=[t_st])
                    P.op("dve", (lambda ti: lambda e: e.reciprocal(out=STf[:, ti, 2:3], in_=STf[:, ti, 1:2]))(ti), reads=[t_st], writes=[t_st])
                    P.op("act", (lambda t2, ti: lambda e: e.activation(
                        out=t2[:], in_=t2[:], func=AF.Copy, scale=STf[:, ti, 2:3]))(t2, ti), reads=[t_t2, t_st], writes=[t_t2])
                    P.op("dve", (lambda t2: lambda e: e.tensor_tensor(out=t2[:], in0=t2[:], in1=FGR[:], op=ALU.mult))(t2),
                         reads=[t_t2, t_FGR], writes=[t_t2])
                    for jj in range(2):
                        P.out_toks.append(P.dma("sp", (lambda t2, jj, ti: lambda e: e.dma_start(
                            out=out_cm[2 * ti + jj], in_=t2[jj * 64:(jj + 1) * 64, :]))(t2, jj, ti), reads=[t_t2], semtrk=t_t2))
            P.barrier()
            P.emit()

        P.wait_all("sp", P.out_toks)
        P.emit()
    return nc, dbg_d


def make_in_maps(inputs):
    sh = prep_shared(inputs)
    x = np.asarray(inputs["x"], np.float32)
    c = np.asarray(inputs["c"], np.float32)
    ctx = np.asarray(inputs["ctx"], np.float32)
    c_ctx = np.asarray(inputs["c_ctx"], np.float32)
    maps = []
    for b in range(8):
        m = dict(sh)
        m["x"] = np.ascontiguousarray(x[b])
        m["ctx"] = np.ascontiguousarray(ctx[b])
        m["cv"] = np.ascontiguousarray(np.stack([fm(c[b]), fm(c_ctx)], 2).reshape(128, 16))
        maps.append(m)
    return maps


def kernel(**inputs):
    nc, _ = build()
    maps = make_in_maps(inputs)
    res = run_bass_kernel_spmd(nc, maps, core_ids=list(range(8)))
    return np.stack([r["out"] for r in res.results], 0)
```

```python
import numpy as np
from contextlib import ExitStack
import concourse.bass as bass
import concourse.mybir as mybir
from concourse.bass_utils import run_bass_kernel_spmd

F32 = mybir.dt.float32
BF16 = mybir.dt.bfloat16
AF = mybir.ActivationFunctionType
ALU = mybir.AluOpType
AX = mybir.AxisListType

ENGS = ["pe", "act", "dve", "pool", "sp"]
EPS = 1e-6
NT = 4352
NCTX = 256
NLAT = 4096


class Trk:
    __slots__ = ("name", "w", "r", "dsem", "dcnt", "excl")

    def __init__(self, name=""):
        self.name = name
        self.w = None
        self.r = {}
        self.dsem = None
        self.dcnt = 0
        self.excl = name.startswith("PS_")


class Prog:
    def __init__(self, nc, stack):
        self.nc = nc
        self.stack = stack
        self.ops = {e: [] for e in ENGS}
        self.seq = {e: 0 for e in ENGS}
        self.known = {e: {} for e in ENGS}
        self.esem = {e: stack.enter_context(nc.semaphore("s_" + e)) for e in ENGS}
        self.nsem = len(ENGS)
        self.out_toks = []
        self.dtrks = []
        self.sem_pool = {"sp": [], "pool": []}

    def new_dsem(self, name):
        s = self.stack.enter_context(self.nc.semaphore("d%d_%s" % (self.nsem, name)))
        self.nsem += 1
        return s

    def _need(self, eng, waits, dep):
        sem, val = dep
        if eng == "pe" and sem is self.esem["pe"]:
            return
        k = id(sem)
        if self.known[eng].get(k, 0) >= val:
            return
        self.known[eng][k] = val
        waits[k] = (sem, val)

    def _deps(self, eng, reads, writes):
        waits = {}
        for t in reads:
            if t.w is not None:
                self._need(eng, waits, t.w)
            if t.excl:
                for dep in t.r.values():
                    self._need(eng, waits, dep)
        for t in writes:
            if t.w is not None:
                self._need(eng, waits, t.w)
            for dep in t.r.values():
                self._need(eng, waits, dep)
        return waits

    def _record(self, tok, reads, writes):
        for t in reads:
            t.r[id(tok[0])] = tok
        for t in writes:
            t.w = tok
            t.r = {}

    def op(self, eng, fn, reads=(), writes=()):
        waits = self._deps(eng, reads, writes)
        self.seq[eng] += 1
        tok = (self.esem[eng], self.seq[eng])
        self._record(tok, reads, writes)
        self.ops[eng].append((list(waits.values()), fn, (self.esem[eng], 1)))

    def dma(self, eng, fn, reads=(), writes=(), semtrk=None):
        if semtrk is None:
            semtrk = writes[0] if writes else reads[0]
        if semtrk.dsem is None:
            if self.sem_pool[eng]:
                semtrk.dsem, semtrk.dcnt = self.sem_pool[eng].pop()
            else:
                semtrk.dsem = self.new_dsem(semtrk.name)
            self.dtrks.append((semtrk, eng))
        waits = self._deps(eng, reads, writes)
        if semtrk.dcnt > 0:
            self._need(eng, waits, (semtrk.dsem, semtrk.dcnt))
        semtrk.dcnt += 16
        tok = (semtrk.dsem, semtrk.dcnt)
        self._record(tok, reads, writes)
        self.ops[eng].append((list(waits.values()), fn, (semtrk.dsem, 16)))
        return tok

    def wait_all(self, eng, toks):
        waits = {}
        for t in toks:
            self._need(eng, waits, t)
        self.ops[eng].append((list(waits.values()), None, None))

    def barrier(self):
        waits = {}
        for e in ENGS:
            if e != "sp" and self.seq[e] > 0:
                self._need("sp", waits, (self.esem[e], self.seq[e]))
        for t, _e in self.dtrks:
            if t.dcnt > 0:
                self._need("sp", waits, (t.dsem, t.dcnt))
        self.seq["sp"] += 1
        self.ops["sp"].append((list(waits.values()), lambda e: e.nop(), (self.esem["sp"], 1)))
        for t, e_ in self.dtrks:
            self.sem_pool[e_].append((t.dsem, t.dcnt))
            t.dsem = None
            t.dcnt = 0
        self.dtrks = []
        for e in ENGS:
            if e != "sp":
                w = {}
                self._need(e, w, (self.esem["sp"], self.seq["sp"]))
                self.ops[e].append((list(w.values()), None, None))

    def emit(self):
        nc = self.nc
        ops = self.ops
        self.ops = {e: [] for e in ENGS}
        with nc.Block() as block:
            def run(e, lst):
                for waits, fn, inc in lst:
                    for sem, val in waits:
                        e.wait_ge(sem, val)
                    if fn is not None:
                        fn(e).then_inc(inc[0], inc[1])

            @block.tensor
            def _(e):
                run(e, ops["pe"])

            @block.scalar
            def _(e):
                run(e, ops["act"])

            @block.vector
            def _(e):
                run(e, ops["dve"])

            @block.gpsimd
            def _(e):
                run(e, ops["pool"])

            @block.sync
            def _(e):
                run(e, ops["sp"])


class Rot:
    def __init__(self, alloc, name, n, shape, dt):
        self.bufs = [(alloc("%s%d" % (name, i), shape, dt), Trk("%s%d" % (name, i))) for i in range(n)]
        self.i = 0

    def next(self):
        b = self.bufs[self.i % len(self.bufs)]
        self.i += 1
        return b


FVCOLS = {}
_off = 0
for _n, _w in [("n1g", 8), ("n2g", 8), ("rgcw", 32), ("rgcb", 8), ("rgba", 16), ("rgbx", 16), ("rglam", 16),
               ("mlcw", 32), ("mlcb", 8), ("mlng", 8), ("mlsk", 8)]:
    FVCOLS[_n] = (_off, _w)
    _off += _w
NV = _off


def fm(vec):
    v = np.asarray(vec, np.float32)
    return np.ascontiguousarray(v.reshape(-1, 128).T)


def colchunks(w):
    K, N = w.shape
    a = w.reshape(K // 128, 128, N // 128, 128)
    a = a.transpose(2, 1, 0, 3)
    return np.ascontiguousarray(a.reshape(N // 128, 128, (K // 128) * 128))


def rowchunks(w):
    K, N = w.shape
    return np.ascontiguousarray(w.reshape(K // 128, 128, N).transpose(1, 0, 2))


def blockdiag128(blocks):
    nb, bi, bo = blocks.shape
    per = 128 // bi
    out = np.zeros((nb // per, 128, 128), np.float32)
    for b in range(nb):
        c, q = divmod(b, per)
        out[c, q * bi:(q + 1) * bi, q * bo:(q + 1) * bo] = blocks[b]
    return out


def prep_shared(inp):
    f = lambda k: np.asarray(inp[k], np.float32)
    sh = {}
    w_mod = f("w_mod")[0]
    sh["wmod"] = colchunks(w_mod)
    b_mod = f("b_mod")[0]
    sh["bmod2"] = np.ascontiguousarray(np.repeat(fm(b_mod)[:, :, None], 2, axis=2).reshape(128, 96))
    sh["wmodg"] = np.stack([rowchunks(w_mod[:, 2048:3072]), rowchunks(w_mod[:, 5120:6144])], 0)
    sh["bmodg"] = np.ascontiguousarray(np.broadcast_to(
        np.concatenate([b_mod[2048:3072], b_mod[5120:6144]])[None, :], (128, 2048)))
    fv = np.zeros((128, NV), np.float32)

    def put(name, arr):
        o, w = FVCOLS[name]
        assert arr.shape == (128, w), (name, arr.shape)
        fv[:, o:o + w] = arr
    put("n1g", fm(f("norm1_g")[0]))
    put("n2g", fm(f("norm2_g")[0]))
    cw = f("rg_conv_w")[0]
    put("rgcw", np.stack([fm(cw[j]) for j in range(4)], 2).reshape(128, 32))
    put("rgcb", fm(f("rg_conv_b")[0]))
    put("rgba", np.concatenate([fm(f("rg_ba")[0][d]) for d in range(2)], 1))
    put("rgbx", np.concatenate([fm(f("rg_bx")[0][d]) for d in range(2)], 1))
    put("rglam", np.concatenate([fm(f("rg_lambda")[0][d]) for d in range(2)], 1))
    cw = f("ml_conv_w")[0]
    put("mlcw", np.stack([fm(cw[j]) for j in range(4)], 2).reshape(128, 32))
    put("mlcb", fm(f("ml_conv_b")[0]))
    put("mlng", fm(f("ml_norm_g")[0]))
    put("mlsk", fm(f("ml_skip")[0]))
    sh["fv"] = fv
    sh["win"] = colchunks(f("w_in")[0])
    rg = []
    for d in range(2):
        for w in (f("rg_wa")[0][d], f("rg_wx")[0][d]):
            rg.append(blockdiag128(w).transpose(1, 0, 2))
    sh["rgbd"] = np.ascontiguousarray(np.concatenate(rg, 1))
    ml = [blockdiag128(f(k)[0]).transpose(1, 0, 2) for k in ("ml_wq", "ml_wk", "ml_wv")]
    sh["mlbd"] = np.ascontiguousarray(np.concatenate(ml, 1))
    wi, wf = f("ml_wi")[0], f("ml_wf")[0]
    wg = np.concatenate([wi[0], wf[0], wi[1], wf[1]], 1)
    sh["wgt"] = np.ascontiguousarray(wg.reshape(24, 128, 16).transpose(1, 0, 2))
    bi, bf = f("ml_bi")[0], f("ml_bf")[0]
    gb = np.stack([np.tile(np.concatenate([bi[d], bf[d]]), 2) for d in range(2)], 0)
    sh["gbias"] = np.ascontiguousarray(np.broadcast_to(gb[None], (128, 2, 16)))
    tri = np.zeros((2, 128, 128), np.float32)
    ii = np.arange(128)
    tri[0] = (ii[:, None] <= ii[None, :])
    tri[1] = (ii[:, None] >= ii[None, :])
    sh["tri"] = np.ascontiguousarray(tri.transpose(1, 0, 2))
    ident = np.eye(128, dtype=np.float32)
    sh["ident"] = ident
    sh["wbrg"] = colchunks(f("w_branch_rg")[0])
    sh["wbml"] = colchunks(f("w_branch_ml")[0])
    sh["wout"] = rowchunks(f("w_out")[0])
    sh["wffi"] = colchunks(f("w_ffn_in")[0])
    sh["wffo"] = rowchunks(f("w_ffn_out")[0])
    sh["fgrow"] = np.ascontiguousarray(np.broadcast_to(f("final_norm_g")[None, :], (128, 1024)))
    return sh


def build(debug=(), stop_after=None):
    nc = bass.Bass("TRN2", target_bir_lowering=False)
    din = lambda name, shape, dt=F32: nc.dram_tensor(name, list(shape), dt, kind="ExternalInput").ap()
    x_d = din("x", [NLAT, 1024])
    ctx_d = din("ctx", [NCTX, 1024])
    cv_d = din("cv", [128, 16])
    wmod_d = din("wmod", [48, 128, 1024])
    bmod2_d = din("bmod2", [128, 96])
    wmodg_d = din("wmodg", [2, 128, 8, 1024])
    bmodg_d = din("bmodg", [128, 2048])
    fv_d = din("fv", [128, NV])
    win_d = din("win", [48, 128, 1024])
    ident_d = din("ident", [128, 128])
    rgbd_d = din("rgbd", [128, 32, 128])
    mlbd_d = din("mlbd", [128, 24, 128])
    wgt_d = din("wgt", [128, 24, 16])
    gbias_d = din("gbias", [128, 2, 16])
    tri_d = din("tri", [128, 2, 128])
    wbrg_d = din("wbrg", [8, 128, 1024])
    wbml_d = din("wbml", [8, 128, 1024])
    wout_d = din("wout", [128, 8, 1024])
    wffi_d = din("wffi", [44, 128, 1024])
    wffo_d = din("wffo", [128, 22, 1024])
    fgrow_d = din("fgrow", [128, 1024])
    SPQ_d = nc.dram_tensor("SPQ", [17, 4, 128, 8, 256], BF16).ap()
    SPX_d = nc.dram_tensor("SPX", [16, 128, 8, 256], F32).ap()
    X1_d = nc.dram_tensor("X1", [NLAT, 1024], F32).ap()
    H2_d = nc.dram_tensor("H2", [16, 128, 8, 256], BF16).ap()
    HF_d = nc.dram_tensor("HF", [32, 128, 1024], F32).ap()
    if "yml" in debug:
        YML_d = nc.dram_tensor("dbg_yml", [8, 128, NLAT], BF16, kind="ExternalOutput").ap()
    else:
        YML_d = nc.dram_tensor("YML", [8, 128, NLAT], BF16).ap()
    YRG_d = nc.dram_tensor("YRG", [8, 128, NLAT], BF16).ap()
    out_d = nc.dram_tensor("out", [NLAT, 1024], F32, kind="ExternalOutput").ap()
    dbg_d = {}

    with ExitStack() as gst:
        P = Prog(nc, gst)
        galloc = lambda name, shape, dt: gst.enter_context(nc.sbuf_tensor(name, list(shape), dt))

        def dump(name, ap, trk, shape, dt=F32):
            if name not in debug:
                return
            d = nc.dram_tensor("dbg_" + name, list(shape), dt, kind="ExternalOutput").ap()
            dbg_d[name] = d
            P.out_toks.append(P.dma("sp", lambda e: e.dma_start(out=d, in_=ap), reads=[trk], semtrk=Trk("dbg" + name)))

        t_uT = [Trk("uT%d_%d" % (i // 2, i % 2)) for i in range(68)]
        FV = galloc("FV", [128, NV], F32); t_FV = Trk("FV")
        MODT = galloc("MODT", [128, 96], F32); t_MODT = Trk("MODT")
        A1 = galloc("A1", [128, 16], F32); t_A1 = Trk("A1")
        A2 = galloc("A2", [128, 8], F32); t_A2 = Trk("A2")
        GROW = galloc("GROW", [128, 2, 1024], F32); t_GROW = Trk("GROW")
        KD = galloc("KD", [128, 16], F32); t_KD = Trk("KD")
        IDB = galloc("IDB", [128, 128], BF16); t_IDB = Trk("IDB")
        IDF = galloc("IDF", [128, 128], F32); t_IDF = Trk("IDF")
        t_YRG = [Trk("YRG%d" % c) for c in range(8)]
        ust = ExitStack()
        uT = ust.enter_context(nc.sbuf_tensor("uT", [128, 8, NT], BF16))

        def fvc(name, i=0, n=1):
            o, w = FVCOLS[name]
            return FV[:, o + i:o + i + n]

        with ExitStack() as st:
            sb = lambda name, shape, dt: st.enter_context(nc.sbuf_tensor(name, list(shape), dt))
            ps = lambda name, shape, dt: st.enter_context(nc.psum_tensor(name, list(shape), dt))
            CV = sb("CV", [128, 16], F32); t_CV = Trk("CV")
            S2 = sb("S2", [128, 16], F32); t_S2 = Trk("S2")
            SREP = sb("SREP", [128, 8, 128], F32); t_SREP = Trk("SREP")
            BM2 = sb("BM2", [128, 96], F32); t_BM2 = Trk("BM2")
            BMG = sb("BMG", [128, 2048], F32); t_BMG = Trk("BMG")
            TMPA = sb("TMPA", [128, 16], F32); t_TMPA = Trk("TMPA")
            TMPB = sb("TMPB", [128, 16], F32); t_TMPB = Trk("TMPB")
            WM = Rot(sb, "WM", 3, [128, 1024], F32)
            WGm = sb("WGm", [128, 8, 1024], F32); t_WGm = Trk("WGm")
            MODP = ps("MODP", [128, 512], F32); t_MODP = Trk("PS_MODP")
            GRP = ps("GRP", [128, 1024], F32); t_GRP = Trk("PS_GRP")

            P.dma("sp", lambda e: e.dma_start(out=CV[:], in_=cv_d), writes=[t_CV])
            P.dma("sp", lambda e: e.dma_start(out=FV[:], in_=fv_d), writes=[t_FV])
            P.dma("sp", lambda e: e.dma_start(out=BM2[:], in_=bmod2_d), writes=[t_BM2])
            P.dma("sp", lambda e: e.dma_start(out=BMG[:], in_=bmodg_d), writes=[t_BMG])
            P.dma("sp", lambda e: e.dma_start(out=IDF[:], in_=ident_d), writes=[t_IDF])
            P.dma("pool", lambda e: e.dma_start(out=IDB[:], in_=ident_d), writes=[t_IDB])
            P.op("act", lambda e: e.activation(out=S2[:], in_=CV[:], func=AF.Silu), reads=[t_CV], writes=[t_S2])
            for kc in range(8):
                P.op("dve", (lambda kc: lambda e: e.tensor_copy(
                    out=SREP[:, kc, :], in_=S2[:, 2 * kc:2 * kc + 1].to_broadcast([128, 128])))(kc),
                    reads=[t_S2], writes=[t_SREP])
            for n in range(48):
                wm, t_wm = WM.next()
                P.dma("sp", (lambda wm, n: lambda e: e.dma_start(out=wm[:], in_=wmod_d[n]))(wm, n), writes=[t_wm])
                for kc in range(8):
                    P.op("pe", (lambda wm, n, kc: lambda e: e.matmul(
                        MODP[:, 2 * n:2 * n + 2], lhsT=wm[:, kc * 128:(kc + 1) * 128], rhs=S2[:, 2 * kc:2 * kc + 2],
                        start=(kc == 0), stop=(kc == 7)))(wm, n, kc), reads=[t_wm, t_S2], writes=[t_MODP])
            P.op("dve", lambda e: e.tensor_tensor(out=MODT[:], in0=MODP[:, 0:96], in1=BM2[:], op=ALU.add),
                 reads=[t_MODP, t_BM2], writes=[t_MODT])
            P.op("dve", lambda e: e.tensor_scalar_add(out=TMPA[:], in0=MODT[:, 16:32], scalar1=1.0),
                 reads=[t_MODT], writes=[t_TMPA])
            for j in range(2):
                P.op("dve", (lambda j: lambda e: e.tensor_tensor(
                    out=A1[:, j:16:2], in0=TMPA[:, j:16:2], in1=fvc("n1g", 0, 8), op=ALU.mult))(j),
                    reads=[t_TMPA, t_FV], writes=[t_A1])
            P.op("dve", lambda e: e.tensor_scalar_add(out=TMPB[:, 0:8], in0=MODT[:, 64:80:2], scalar1=1.0),
                 reads=[t_MODT], writes=[t_TMPB])
            P.op("dve", lambda e: e.tensor_tensor(out=A2[:], in0=TMPB[:, 0:8], in1=fvc("n2g", 0, 8), op=ALU.mult),
                 reads=[t_TMPB, t_FV], writes=[t_A2])
            for g in range(2):
                P.dma("sp", (lambda g: lambda e: e.dma_start(out=WGm[:], in_=wmodg_d[g]))(g), writes=[t_WGm])
                for half in range(2):
                    for kc in range(8):
                        P.op("pe", (lambda half, kc: lambda e: e.matmul(
                            GRP[:, half * 512:(half + 1) * 512], lhsT=SREP[:, kc, :],
                            rhs=WGm[:, kc, half * 512:(half + 1) * 512], start=(kc == 0), stop=(kc == 7)))(half, kc),
                            reads=[t_SREP, t_WGm], writes=[t_GRP])
                P.op("dve", (lambda g: lambda e: e.tensor_tensor(
                    out=GROW[:, g, :], in0=GRP[:], in1=BMG[:, g * 1024:(g + 1) * 1024], op=ALU.add))(g),
                    reads=[t_GRP, t_BMG], writes=[t_GROW])
            P.op("act", lambda e: e.activation(out=TMPA[:], in_=fvc("rglam", 0, 16), func=AF.Exp, scale=-1.0),
                 reads=[t_FV], writes=[t_TMPA])
            P.op("act", lambda e: e.activation(out=TMPB[:], in_=TMPA[:], func=AF.Ln, bias=1.0),
                 reads=[t_TMPA], writes=[t_TMPB])
            P.op("dve", lambda e: e.tensor_scalar(out=KD[:], in0=TMPB[:], scalar1=-8.0, scalar2=None, op0=ALU.mult),
                 reads=[t_TMPB], writes=[t_KD])
            dump("modT", MODT[:], t_MODT, [128, 96])
            dump("grow", GROW[:], t_GROW, [128, 2, 1024])
            dump("kd", KD[:], t_KD, [128, 16])
            P.barrier()
            P.emit()

        with ExitStack() as st:
            sb = lambda name, shape, dt: st.enter_context(nc.sbuf_tensor(name, list(shape), dt))
            ps = lambda name, shape, dt: st.enter_context(nc.psum_tensor(name, list(shape), dt))
            XT = Rot(sb, "XT", 3, [128, 1024], F32)
            XN = Rot(sb, "XN", 2, [128, 1024], BF16)
            TP = Rot(ps, "PS_TP", 2, [128, 8, 128], BF16)
            TPb = Rot(ps, "PS_TPb", 2, [128, 8, 128], BF16)
            ST = sb("ST", [128, 34, 4], F32)
            JKA = sb("JKA", [128, 1024], BF16); t_JKA = Trk("JKA")
            stage1 = {}

            def emit_stats(i):
                t_st = Trk("st%d" % i)
                xt, t_xt = XT.next()
                src = ctx_d[i * 128:(i + 1) * 128, :] if i < 2 else x_d[(i - 2) * 128:(i - 1) * 128, :]
                P.dma("sp", (lambda xt, src: lambda e: e.dma_start(out=xt[:], in_=src))(xt, src), writes=[t_xt])
                P.op("act", (lambda xt, i: lambda e: e.activation(
                    out=JKA[:], in_=xt[:], func=AF.Square, accum_out=ST[:, i, 0:1]))(xt, i),
                    reads=[t_xt], writes=[t_JKA, t_st])
                P.op("act", (lambda i: lambda e: e.activation(
                    out=ST[:, i, 1:2], in_=ST[:, i, 0:1], func=AF.Sqrt, scale=1.0 / 1024.0, bias=EPS))(i),
                    reads=[t_st], writes=[t_st])
                P.op("dve", (lambda i: lambda e: e.reciprocal(out=ST[:, i, 2:3], in_=ST[:, i, 1:2]))(i),
                     reads=[t_st], writes=[t_st])
                stage1[i] = (xt, t_xt, t_st)

            emit_stats(0)
            for i in range(34):
                if i + 1 < 34:
                    emit_stats(i + 1)
                xt, t_xt, t_st = stage1.pop(i)
                xn, t_xn = XN.next()
                tpa, t_tpa = TP.next()
                tpb, t_tpb = TPb.next()
                j = 1 if i < 2 else 0
                P.op("act", (lambda xt, xn, i: lambda e: e.activation(
                    out=xn[:], in_=xt[:], func=AF.Copy, scale=ST[:, i, 2:3]))(xt, xn, i),
                    reads=[t_xt, t_st], writes=[t_xn])
                for c in range(8):
                    tp, t_tp = (tpa, t_tpa) if c < 4 else (tpb, t_tpb)
                    P.op("pe", (lambda xn, tp, c: lambda e: e.transpose(
                        out=tp[:, c, :], in_=xn[:, c * 128:(c + 1) * 128], identity=IDB[:]))(xn, tp, c),
                        reads=[t_xn, t_IDB], writes=[t_tp])
                for c in (0, 4, 1, 5, 2, 6, 3, 7):
                    tp, t_tp = (tpa, t_tpa) if c < 4 else (tpb, t_tpb)
                    if i < 2:
                        dst = uT[:, c, i * 128:(i + 1) * 128]
                        src_tp = tp[:, c, :]
                    else:
                        r0 = 2 * (i - 2)
                        dst = uT[:, c, 256:NT].rearrange("p (j r) -> p r j", r=64)[:, r0:r0 + 2, :]
                        src_tp = tp[:, c, :].rearrange("p (r j) -> p r j", j=64)
                    if c < 4:
                        P.op("dve", (lambda tp, c, dst, j: lambda e: e.tensor_scalar(
                            out=dst, in0=tp, scalar1=A1[:, 2 * c + j:2 * c + j + 1],
                            scalar2=MODT[:, 2 * c + j:2 * c + j + 1], op0=ALU.mult, op1=ALU.add))(src_tp, c, dst, j),
                            reads=[t_tp, t_A1, t_MODT], writes=[t_uT[2 * i + (0 if c < 4 else 1)]])
                    else:
                        P.op("act", (lambda tp, c, dst, j: lambda e: e.activation(
                            out=dst, in_=tp, func=AF.Identity, scale=A1[:, 2 * c + j:2 * c + j + 1],
                            bias=MODT[:, 2 * c + j:2 * c + j + 1]))(src_tp, c, dst, j),
                            reads=[t_tp, t_A1, t_MODT], writes=[t_uT[2 * i + (0 if c < 4 else 1)]])
            if "uT" in debug:
                UD = sb("UD", [128, 8, 512], F32); t_UD = Trk("UD")
                P.op("dve", lambda e: e.tensor_copy(out=UD[:], in_=uT[:, :, 128:640]), reads=t_uT[2:10], writes=[t_UD])
                dump("uT", UD[:], t_UD, [128, 8, 512])
            P.barrier()
            P.emit()


        CT0, LT0, PEND = 2, 261, 4357
        with ExitStack() as st:
            sb = lambda name, shape, dt: st.enter_context(nc.sbuf_tensor(name, list(shape), dt))
            ps = lambda name, shape, dt: st.enter_context(nc.psum_tensor(name, list(shape), dt))
            RGBD = sb("RGBD", [128, 32, 128], BF16); t_RGBD = Trk("RGBD")
            P.dma("pool", lambda e: e.dma_start(out=RGBD[:], in_=rgbd_d, max_dma_last_dim=4096), writes=[t_RGBD])
            RX = sb("RX", [128, 4360], F32); t_RX = Trk("RX")
            XC = sb("XC", [128, 4360], F32); t_XC = Trk("XC")
            XCB = sb("XCB", [128, 4360], BF16); t_XCB = Trk("XCB")
            TMP = sb("TMP", [128, 2180], F32); t_TMP = Trk("TMP")
            BB = [sb("B0", [128, 4360], F32), sb("B1", [128, 4360], F32)]; t_BB = [Trk("B0"), Trk("B1")]
            WR = Rot(sb, "WR", 4, [128, 8, 128], BF16)
            PJ = Rot(ps, "PS_PJ", 3, [128, 512], F32)
            GA = Rot(ps, "PS_GA", 4, [128, 512], F32)
            GL = Rot(sb, "GL", 2, [128, 512], F32)
            TS = Rot(sb, "TS", 2, [128, 512], F32)
            YS = Rot(sb, "YS", 1, [128, NLAT], BF16)
            for (a, b) in ((0, 2), (258, 261), (4357, 4360)):
                P.op("dve", (lambda a, b: lambda e: e.memset(RX[:, a:b], 0.0))(a, b), writes=[t_RX])
            for d in range(2):
                P.op("pool", (lambda d: lambda e: e.memset(BB[d][:, 256:264], 0.0))(d), writes=[t_BB[d]])
            blocks = [(0, 256, CT0)] + [(256 + 512 * b, 512, LT0 + 512 * b) for b in range(8)]

            def ut_trks(t0, n):
                return t_uT[2 * (t0 // 128):2 * ((t0 + n - 1) // 128 + 1)]

            def ut_nat(kc, t0, n):
                if t0 < 256:
                    return uT[:, kc, t0:t0 + n]
                r0 = (t0 - 256) // 64
                return uT[:, kc, 256:NT].rearrange("p (j r) -> p r j", r=64)[:, r0:r0 + n // 64, :]

            def rev(ap2d):
                n = ap2d.shape[1]
                return bass.AP(ap2d.tensor, ap2d.offset + (n - 1), [list(ap2d.ap[0]), [-1, n]])

            def load_wr(c):
                wrx, t_wrx = WR.next()
                wrg, t_wrg = WR.next()
                P.dma("pool", (lambda w, c: lambda e: e.dma_start(
                    out=w[:], in_=win_d[c].rearrange("p (k j) -> p k j", j=128)))(wrx, c), writes=[t_wrx])
                P.dma("pool", (lambda w, c: lambda e: e.dma_start(
                    out=w[:], in_=win_d[8 + c].rearrange("p (k j) -> p k j", j=128)))(wrg, c), writes=[t_wrg])
                return wrx, t_wrx, wrg, t_wrg

            wr_next = load_wr(0)
            for c in range(8):
                wrx, t_wrx, wrg, t_wrg = wr_next
                if c + 1 < 8:
                    wr_next = load_wr(c + 1)
                if c > 0:
                    P.op("dve", lambda e: e.memset(RX[:, 258:261], 0.0), writes=[t_RX])
                pj, t_pj = PJ.next()
                for kc in range(8):
                    P.op("pe", (lambda pj, wrx, kc: lambda e: e.matmul(
                        pj[:, 0:256], lhsT=wrx[:, kc, :], rhs=uT[:, kc, 0:256], start=(kc == 0), stop=(kc == 7)))(
                        pj, wrx, kc), reads=[t_wrx] + t_uT[0:4], writes=[t_pj])
                P.op("act", (lambda pj: lambda e: e.activation(
                    out=RX[:, CT0:CT0 + 256], in_=pj[:, 0:256], func=AF.Copy))(pj), reads=[t_pj], writes=[t_RX])
                for b in range(8):
                    pj, t_pj = PJ.next()
                    for kc in range(8):
                        P.op("pe", (lambda pj, wrx, kc, b: lambda e: e.matmul(
                            pj[:], lhsT=wrx[:, kc, :], rhs=uT[:, kc, 256 + b * 512:256 + (b + 1) * 512], start=(kc == 0), stop=(kc == 7)))(
                            pj, wrx, kc, b), reads=[t_wrx] + t_uT[4:68], writes=[t_pj])
                    P.op("act", (lambda pj, b: lambda e: e.activation(
                        out=RX[:, LT0:LT0 + 4096].rearrange("p (r j) -> p j r", j=64)[:, 8 * b:8 * b + 8, :],
                        in_=pj[:].rearrange("p (j r) -> p j r", r=64), func=AF.Copy))(pj, b), reads=[t_pj], writes=[t_RX])
                L = PEND - 2
                P.op("dve", (lambda c: lambda e: e.tensor_scalar(
                    out=XC[:, 2:PEND], in0=RX[:, 0:L], scalar1=fvc("rgcw", c * 4), scalar2=fvc("rgcb", c),
                    op0=ALU.mult, op1=ALU.add))(c), reads=[t_RX, t_FV], writes=[t_XC])
                for j in range(1, 4):
                    P.op("dve", (lambda c, j: lambda e: e.scalar_tensor_tensor(
                        out=XC[:, 2:PEND], in0=RX[:, j:j + L], scalar=fvc("rgcw", c * 4 + j), in1=XC[:, 2:PEND],
                        op0=ALU.mult, op1=ALU.add))(c, j), reads=[t_RX, t_FV, t_XC], writes=[t_XC])
                P.op("act", lambda e: e.activation(out=XCB[:, 2:PEND], in_=XC[:, 2:PEND], func=AF.Copy),
                     reads=[t_XC], writes=[t_XCB])
                if c == 3:
                    dump("xc", XC[:], t_XC, [128, 4360])
                for d in range(2):
                    Bd, t_Bd = BB[d], t_BB[d]
                    for (t0, n, pos) in blocks:
                        ga, t_ga = GA.next()
                        gx, t_gx = GA.next()
                        P.op("pe", (lambda ga, d, c, n, pos: lambda e: e.matmul(
                            ga[:, 0:n], lhsT=RGBD[:, (d * 2) * 8 + c, :], rhs=XCB[:, pos:pos + n], start=True, stop=True))(
                            ga, d, c, n, pos), reads=[t_RGBD, t_XCB], writes=[t_ga])
                        P.op("pe", (lambda gx, d, c, n, pos: lambda e: e.matmul(
                            gx[:, 0:n], lhsT=RGBD[:, (d * 2 + 1) * 8 + c, :], rhs=XCB[:, pos:pos + n], start=True, stop=True))(
                            gx, d, c, n, pos), reads=[t_RGBD, t_XCB], writes=[t_gx])
                        P.op("act", (lambda ga, d, c, n, pos: lambda e: e.activation(
                            out=RX[:, pos:pos + n], in_=ga[:, 0:n], func=AF.Sigmoid, bias=fvc("rgba", d * 8 + c)))(
                            ga, d, c, n, pos), reads=[t_ga, t_FV], writes=[t_RX])
                        P.op("act", (lambda gx, Bd, d, c, n, pos: lambda e: e.activation(
                            out=Bd[:, pos:pos + n], in_=gx[:, 0:n], func=AF.Sigmoid, bias=fvc("rgbx", d * 8 + c)))(
                            gx, Bd, d, c, n, pos), reads=[t_gx, t_FV], writes=[t_Bd])
                    P.op("act", (lambda d, c: lambda e: e.activation(
                        out=RX[:, 2:PEND], in_=RX[:, 2:PEND], func=AF.Exp, scale=KD[:, d * 8 + c:d * 8 + c + 1]))(d, c),
                        reads=[t_RX, t_KD], writes=[t_RX])
                    P.op("dve", (lambda Bd: lambda e: e.tensor_tensor(
                        out=Bd[:, 2:PEND], in0=Bd[:, 2:PEND], in1=XC[:, 2:PEND], op=ALU.mult))(Bd),
                        reads=[t_Bd, t_XC], writes=[t_Bd])
                    for (ra, rb) in ((2, 2180), (2180, PEND)):
                        P.op("act", (lambda ra, rb: lambda e: e.activation(
                            out=TMP[:, 0:rb - ra], in_=RX[:, ra:rb], func=AF.Square))(ra, rb), reads=[t_RX], writes=[t_TMP])
                        P.op("act", (lambda ra, rb: lambda e: e.activation(
                            out=TMP[:, 0:rb - ra], in_=TMP[:, 0:rb - ra], func=AF.Sqrt, scale=-1.0, bias=1.0))(ra, rb),
                            reads=[t_TMP], writes=[t_TMP])
                        P.op("dve", (lambda Bd, ra, rb: lambda e: e.tensor_tensor(
                            out=Bd[:, ra:rb], in0=Bd[:, ra:rb], in1=TMP[:, 0:rb - ra], op=ALU.mult))(Bd, ra, rb),
                            reads=[t_Bd, t_TMP], writes=[t_Bd])
                    f_ = (lambda ap: ap) if d == 0 else rev
                    c0, c1 = CT0, CT0 + 256
                    l0, l1 = LT0, LT0 + 4096
                    P.op("dve", (lambda Bd, f_: lambda e: e.tensor_tensor_scan(
                        out=f_(Bd[:, c0:c1]), data0=f_(RX[:, c0:c1]), data1=f_(Bd[:, c0:c1]), initial=0.0,
                        op0=ALU.mult, op1=ALU.add))(Bd, f_), reads=[t_RX, t_Bd], writes=[t_Bd])
                    ini = (c1 - 1) if d == 0 else c0
                    P.op("dve", (lambda Bd, f_, ini: lambda e: e.tensor_tensor_scan(
                        out=f_(Bd[:, l0:l1]), data0=f_(RX[:, l0:l1]), data1=f_(Bd[:, l0:l1]), initial=Bd[:, ini:ini + 1],
                        op0=ALU.mult, op1=ALU.add))(Bd, f_, ini), reads=[t_RX, t_Bd], writes=[t_Bd])
                ys, t_ys = YS.next()
                for b in range(8):
                    pj, t_pj = PJ.next()
                    gl, t_gl = GL.next()
                    ts, t_ts = TS.next()
                    for kc in range(8):
                        P.op("pe", (lambda pj, wrg, kc, b: lambda e: e.matmul(
                            pj[:], lhsT=wrg[:, kc, :], rhs=uT[:, kc, 256 + b * 512:256 + (b + 1) * 512], start=(kc == 0), stop=(kc == 7)))(
                            pj, wrg, kc, b), reads=[t_wrg] + t_uT[4:68], writes=[t_pj])
                    P.op("act", (lambda pj, gl: lambda e: e.activation(out=gl[:], in_=pj[:], func=AF.Gelu))(pj, gl),
                         reads=[t_pj], writes=[t_gl])
                    P.op("dve", (lambda ts, b: lambda e: e.tensor_tensor(
                        out=ts[:].rearrange("p (j r) -> p j r", r=64),
                        in0=BB[0][:, LT0:LT0 + 4096].rearrange("p (r j) -> p j r", j=64)[:, 8 * b:8 * b + 8, :],
                        in1=BB[1][:, LT0:LT0 + 4096].rearrange("p (r j) -> p j r", j=64)[:, 8 * b:8 * b + 8, :], op=ALU.add))(ts, b),
                        reads=t_BB, writes=[t_ts])
                    P.op("dve", (lambda ys, ts, gl, b: lambda e: e.tensor_tensor(
                        out=ys[:, b * 512:(b + 1) * 512], in0=ts[:], in1=gl[:], op=ALU.mult))(ys, ts, gl, b),
                        reads=[t_ts, t_gl], writes=[t_ys])
                P.dma("sp", (lambda ys, c: lambda e: e.dma_start(out=YRG_d[c], in_=ys[:]))(ys, c), reads=[t_ys], writes=[t_YRG[c]],
                      semtrk=t_ys)
                if c == 3:
                    if "hrg" in debug:
                        dump("hrg", HD[:], t_HD, [128, NLAT])
                    dump("yrg", ys[:], t_ys, [128, NLAT], BF16)
            P.barrier()
            P.emit()
        if stop_after == "rg":
            P.wait_all("sp", P.out_toks)
            P.barrier()
            P.emit()
            ust.close()
            return nc, dbg_d

        t_HF = [Trk("HF%d" % i) for i in range(32)]
        t_YML = [Trk("YML%d" % i) for i in range(16)]
        t_SPQ = [[Trk("SPQ%d_%d" % (g, i)) for i in range(4)] for g in range(17)]
        t_SPX = [Trk("SPX%d" % g) for g in range(16)]
        t_spd = [Trk("spd%d" % i) for i in range(5)]
        with ExitStack() as st:
            sb = lambda name, shape, dt: st.enter_context(nc.sbuf_tensor(name, list(shape), dt))
            ps = lambda name, shape, dt: st.enter_context(nc.psum_tensor(name, list(shape), dt))
            MLBD = sb("MLBD", [128, 24, 128], BF16); t_MLBD = Trk("MLBD")
            WGT = sb("WGT", [128, 24, 16], BF16); t_WGT = Trk("WGT")
            GBI = sb("GBI", [128, 2, 16], F32); t_GBI = Trk("GBI")
            TRI = sb("TRI", [128, 2, 128], F32); t_TRI = Trk("TRI")
            ONES = sb("ONES", [128, 128], F32); t_ONES = Trk("ONES")
            P.dma("pool", lambda e: e.dma_start(out=MLBD[:], in_=mlbd_d, max_dma_last_dim=4096), writes=[t_MLBD])
            P.dma("pool", lambda e: e.dma_start(out=WGT[:], in_=wgt_d), writes=[t_WGT])
            P.dma("sp", lambda e: e.dma_start(out=GBI[:], in_=gbias_d), writes=[t_GBI])
            P.dma("sp", lambda e: e.dma_start(out=TRI[:], in_=tri_d), writes=[t_TRI])
            P.op("dve", lambda e: e.memset(ONES[:], 1.0), writes=[t_ONES])
            B_PM = ps("B_PM", [128, 512], F32); t_PM = Trk("PS_PM")
            B_QK = ps("B_QK", [128, 512], F32); t_PQ = Trk("PS_PQ")
            B_VO = ps("B_VO", [128, 512], F32); t_PV = Trk("PS_PV")
            B_GP = ps("B_GP", [128, 512], F32); t_GP = Trk("PS_GP")
            B_N = ps("B_N", [128, 4, 512], F32); t_N = [Trk("PS_N%d" % i) for i in range(4)]
            BU = [B_PM, B_QK, B_VO, B_GP]; t_BU = [t_PM, t_PQ, t_PV, t_GP]
            VT = Rot(sb, "VT", 2, [128, 256], BF16)
            XM = sb("XM", [128, 8, 256], F32); t_XM = [Trk("XM%d" % c) for c in range(8)]
            XMB = sb("XMB", [128, 8, 256], BF16); t_XMB = [Trk("XMB%d" % c) for c in range(8)]
            UMB = sb("UMB", [128, 8, 256], BF16); t_UMB = [Trk("UMB%d" % c) for c in range(8)]
            QT = sb("QT", [128, 8, 256], BF16); t_QT = [Trk("QT%d" % c) for c in range(8)]
            KT = sb("KT", [128, 8, 256], BF16); t_KT = [Trk("KT%d" % c) for c in range(8)]
            GF = sb("GF", [8, 256], F32); t_GF = Trk("GF")
            GG = sb("GG", [128, 16], F32); t_GG = Trk("GG")
            GE = sb("GE", [128, 8], F32); t_GE = Trk("GE")
            GLn = sb("GLn", [128, 8], F32); t_GLn = Trk("GLn")
            GT2 = sb("GT2", [128, 8], F32); t_GT2 = Trk("GT2")
            EB = sb("EB", [128, 8], F32); t_EB = Trk("EB")
            WS = sb("WS", [128, 8], F32); t_WS = Trk("WS")
            EBL = sb("EBL", [128, 8], F32); t_EBL = Trk("EBL")
            EBLP = sb("EBLP", [128, 4], F32); t_EBLP = Trk("EBLP")
            DS = sb("DS", [128, 4, 2, 257], F32); t_DS = [Trk("DS%d" % h) for h in range(4)]
            DB = sb("DB", [128, 4, 2, 258], BF16); t_DB = [Trk("DB%d" % h) for h in range(4)]
            VX = sb("VX", [128, 2, 4, 258], BF16); t_VX = [Trk("VX%d" % i) for i in range(2)]
            KTM = sb("KTM", [128, 2, 1024], BF16); t_KTM = [Trk("KTM%d" % i) for i in range(2)]
            STt = sb("STt", [128, 2, 4, 128], BF16); t_STt = [Trk("STt%d" % i) for i in range(2)]
            E1 = sb("E1", [128, 8, 4], F32); t_E1 = Trk("E1")
            HH = Rot(sb, "HH", 1, [128, 1024], F32)

            def grp_rhs(kc, g):
                if g == 0:
                    return uT[:, kc, 0:256]
                gi = g - 1
                return uT[:, kc, 256 + gi * 256:256 + (gi + 1) * 256]

            def grp_trks(g):
                return t_uT[0:4] if g == 0 else t_uT[4:68]

            def gates_post(d):
                P.op("act", lambda e: e.activation(out=GF[:], in_=B_GP[0:8, 0:256], func=AF.Copy), reads=[t_GP], writes=[t_GF])
                for ch in range(2):
                    P.op("pe", (lambda ch: lambda e: e.transpose(
                        out=B_GP[:, 256 + ch * 8:256 + ch * 8 + 8], in_=GF[0:8, ch * 128:(ch + 1) * 128], identity=IDF[0:8, 0:8]))(ch),
                        reads=[t_GF, t_IDF], writes=[t_GP])
                P.op("dve", (lambda d: lambda e: e.tensor_tensor(
                    out=GG[:], in0=B_GP[:, 256:272], in1=GBI[:, d, :], op=ALU.add))(d), reads=[t_GP, t_GBI], writes=[t_GG])
                GGv = GG[:].rearrange("t (c k) -> t c k", k=8)
                P.op("act", lambda e: e.activation(
                    out=GE[:].rearrange("t (c h) -> t c h", h=4), in_=GGv[:, :, 4:8], func=AF.Exp, scale=-1.0),
                    reads=[t_GG], writes=[t_GE])
                P.op("act", lambda e: e.activation(out=GLn[:], in_=GE[:], func=AF.Ln, bias=1.0), reads=[t_GE], writes=[t_GLn])
                P.op("pe", (lambda d: lambda e: e.matmul(
                    B_GP[:, 288:296], lhsT=TRI[:, d, :], rhs=GLn[:], start=True, stop=True))(d),
                    reads=[t_TRI, t_GLn], writes=[t_GP])
                P.op("pe", lambda e: e.matmul(B_GP[:, 304:312], lhsT=ONES[:], rhs=GLn[:], start=True, stop=True),
                     reads=[t_ONES, t_GLn], writes=[t_GP])
                P.op("act", lambda e: e.activation(out=EB[:], in_=B_GP[:, 288:296], func=AF.Exp, scale=-1.0),
                     reads=[t_GP], writes=[t_EB])
                P.op("dve", lambda e: e.tensor_tensor(
                    out=GT2[:].rearrange("t (c h) -> t c h", h=4), in0=B_GP[:, 288:296].rearrange("t (c h) -> t c h", h=4),
                    in1=GGv[:, :, 0:4], op=ALU.add), reads=[t_GP, t_GG], writes=[t_GT2])
                P.op("act", lambda e: e.activation(out=WS[:], in_=GT2[:], func=AF.Exp), reads=[t_GT2], writes=[t_WS])
                P.op("act", lambda e: e.activation(out=EBL[:], in_=B_GP[:, 304:312], func=AF.Exp, scale=-1.0),
                     reads=[t_GP], writes=[t_EBL])
            def rec_group(g, d, QT, KT, XMB, UMB, t_QT, t_KT, t_XMB, t_UMB, XM, t_XM, hs):
                lat = g > 0
                gi = g - 1
                have_state = hs[0]
                for ch in range(2):
                    cols = slice(ch * 128, (ch + 1) * 128)
                    for c in range(8):
                        h, half = divmod(c, 2)
                        bk = h // 2
                        off = (h % 2) * 256 + half * 128
                        P.op("pe", (lambda c, bk, off, cols: lambda e: e.matmul(
                            B_N[:, bk, off:off + 128], lhsT=UMB[:, c, cols], rhs=MLBD[:, 16 + c, :], start=True, stop=True))(c, bk, off, cols),
                            reads=[t_UMB[c], t_MLBD], writes=[t_N[bk]])
                        P.op("pe", (lambda c, bk, off, cols: lambda e: e.matmul(
                            B_N[:, 2 + bk, off:off + 128], lhsT=XMB[:, c, cols], rhs=MLBD[:, 8 + c, :], start=True, stop=True))(c, bk, off, cols),
                            reads=[t_XMB[c], t_MLBD], writes=[t_N[2 + bk]])
                    for h in range(4):
                        bk = h // 2
                        off = (h % 2) * 256
                        P.op("dve", (lambda ch, h, bk, off: lambda e: e.tensor_scalar(
                            out=VX[:, ch, h, 0:256], in0=B_N[:, bk, off:off + 256], scalar1=WS[:, ch * 4 + h:ch * 4 + h + 1],
                            scalar2=None, op0=ALU.mult))(ch, h, bk, off), reads=[t_N[bk], t_WS], writes=[t_VX[ch]])
                    P.op("act", (lambda ch: lambda e: e.activation(
                        out=VX[:, ch, :, 256:257], in_=WS[:, ch * 4:ch * 4 + 4].unsqueeze(2), func=AF.Copy))(ch), reads=[t_WS], writes=[t_VX[ch]])
                    for bk in range(2):
                        P.op("act", (lambda ch, bk: lambda e: e.activation(
                            out=KTM[:, ch, bk * 512:(bk + 1) * 512], in_=B_N[:, 2 + bk, :], func=AF.Copy, scale=1.0 / 16.0))(ch, bk),
                            reads=[t_N[2 + bk]], writes=[t_KTM[ch]])
                    for h in range(4):
                        for half in range(2):
                            c = 2 * h + half
                            P.op("pe", (lambda c, h, half, cols: lambda e: e.matmul(
                                B_GP[:, h * 128:(h + 1) * 128], lhsT=KT[:, c, cols], rhs=QT[:, c, cols], start=(half == 0), stop=(half == 1)))(c, h, half, cols),
                                reads=[t_KT[c], t_QT[c]], writes=[t_GP])
                    P.op("dve", (lambda ch, d: lambda e: e.tensor_tensor(
                        out=STt[:, ch], in0=B_GP[:].rearrange("p (h t) -> p h t", t=128),
                        in1=TRI[:, d, :].unsqueeze(1).to_broadcast([128, 4, 128]), op=ALU.mult))(ch, d),
                        reads=[t_GP, t_TRI], writes=[t_STt[ch]])
                chs = (0, 1) if d == 0 else (1, 0)
                if d == 1 and lat:
                    yg, t_yg = YG.next()
                for ch in chs:
                    cols = slice(ch * 128, (ch + 1) * 128)
                    if d == 1 and lat:
                        for cq in (gi * 2 + ch, gi * 2 + ch - 1):
                            if cq >= 0 and cq not in hft_map:
                                hb, t_hb = HFt.next()
                                P.dma("sp", (lambda hb, cq: lambda e: e.dma_start(out=hb[:], in_=HF_d[cq]))(hb, cq),
                                      reads=[t_HF[cq]], writes=[t_hb])
                                hft_map[cq] = (hb, t_hb)
                    last_chunk = (d == 0 and g == 16 and ch == 1) or (d == 1 and g == 1 and ch == 0)
                    if not last_chunk:
                        for r in range(2):
                            for hh2 in range(2):
                                h = 2 * r + hh2
                                for half in range(2):
                                    bi_ = hh2 * 2 + half
                                    P.op("pe", (lambda ch, h, half, bi_: lambda e: e.matmul(
                                        BU[bi_][:, 0:257], lhsT=KTM[:, ch, h * 256 + half * 128:h * 256 + (half + 1) * 128],
                                        rhs=VX[:, ch, h, 0:257], start=True, stop=True))(ch, h, half, bi_),
                                        reads=[t_KTM[ch], t_VX[ch]], writes=[t_BU[bi_]])
                            for hh2 in range(2):
                                h = 2 * r + hh2
                                for half in range(2):
                                    bi_ = hh2 * 2 + half
                                    if have_state:
                                        P.op("dve", (lambda h, half, bi_: lambda e: e.scalar_tensor_tensor(
                                            out=DS[:, h, half, :], in0=DS[:, h, half, :], scalar=EBLP[:, h:h + 1], in1=BU[bi_][:, 0:257],
                                            op0=ALU.mult, op1=ALU.add))(h, half, bi_),
                                            reads=[t_DS[h], t_BU[bi_], t_EBLP], writes=[t_DS[h]])
                                    else:
                                        P.op("dve", (lambda h, half, bi_: lambda e: e.tensor_copy(
                                            out=DS[:, h, half, :], in_=BU[bi_][:, 0:257]))(h, half, bi_),
                                            reads=[t_BU[bi_]], writes=[t_DS[h]])
                    if lat:
                        hh, t_hh = HH.next()
                        for h in range(4):
                            P.op("pe", (lambda ch, h, hs: lambda e: e.matmul(
                                B_N[:, h, 0:257], lhsT=STt[:, ch, h, :], rhs=VX[:, ch, h, 0:257], start=True, stop=(not hs)))(ch, h, have_state),
                                reads=[t_STt[ch], t_VX[ch]], writes=[t_N[h]])
                            if have_state:
                                for half in range(2):
                                    c = 2 * h + half
                                    P.op("pe", (lambda c, h, half, cols: lambda e: e.matmul(
                                        B_N[:, h, 0:257], lhsT=QT[:, c, cols], rhs=DB[:, h, half, 0:257], start=False, stop=(half == 1)))(c, h, half, cols),
                                        reads=[t_QT[c], t_DB[h]], writes=[t_N[h]])
                        e0 = ch * 4
                        P.op("dve", (lambda ch: lambda e: e.tensor_tensor(
                            out=E1[:, 0:4, 0], in0=B_N[:, :, 256], in1=EB[:, ch * 4:ch * 4 + 4], op=ALU.mult))(ch),
                            reads=t_N + [t_EB], writes=[t_E1])
                        P.op("dve", lambda e: e.tensor_scalar(
                            out=E1[:, 0:4, 1], in0=E1[:, 0:4, 0], scalar1=-1.0, scalar2=1.0, op0=ALU.mult, op1=ALU.max),
                            reads=[t_E1], writes=[t_E1])
                        P.op("dve", lambda e: e.scalar_tensor_tensor(
                            out=E1[:, 0:4, 2], in0=E1[:, 0:4, 0], scalar=1.0, in1=E1[:, 0:4, 1], op0=ALU.max, op1=ALU.max),
                            reads=[t_E1], writes=[t_E1])
                        P.op("dve", lambda e: e.reciprocal(out=E1[:, 0:4, 3], in_=E1[:, 0:4, 2]), reads=[t_E1], writes=[t_E1])
                        P.op("dve", (lambda ch: lambda e: e.tensor_tensor(
                            out=E1[:, 4:8, 0], in0=E1[:, 0:4, 3], in1=EB[:, ch * 4:ch * 4 + 4], op=ALU.mult))(ch),
                            reads=[t_E1, t_EB], writes=[t_E1])
                        for h in range(4):
                            P.op("act", (lambda hh, h: lambda e: e.activation(
                                out=hh[:, h * 256:(h + 1) * 256], in_=B_N[:, h, 0:256], func=AF.Copy, scale=E1[:, 4 + h, 0:1]))(hh, h),
                                reads=[t_N[h], t_E1], writes=[t_hh])
                    if not last_chunk:
                        for h in range(4):
                            idx = ch * 4 + h
                            P.op("act", (lambda h, idx: lambda e: e.activation(
                                out=DB[:, h, :, 0:257], in_=DS[:, h, :, :], func=AF.Copy, scale=EBL[:, idx:idx + 1]))(h, idx),
                                reads=[t_DS[h], t_EBL], writes=[t_DB[h]])
                        P.op("dve", (lambda ch: lambda e: e.tensor_copy(out=EBLP[:], in_=EBL[:, ch * 4:ch * 4 + 4]))(ch),
                             reads=[t_EBL], writes=[t_EBLP])
                    have_state = True; hs[0] = True
                    if not lat:
                        continue
                    cg = gi * 2 + ch
                    if d == 0:
                        P.dma("sp", (lambda hh, cg: lambda e: e.dma_start(out=HF_d[cg], in_=hh[:]))(hh, cg),
                              reads=[t_hh], writes=[t_HF[cg]], semtrk=t_hh)
                        continue
                    hft, t_hft = hft_map.pop(cg)
                    P.op("dve", (lambda hh, hft: lambda e: e.tensor_tensor(out=hh[:], in0=hh[:], in1=hft[:], op=ALU.add))(hh, hft),
                         reads=[t_hh, t_hft], writes=[t_hh])
                    for h in range(4):
                        P.op("dve", (lambda hh, h: lambda e: e.bn_stats(out=BS[:, h, :], in_=hh[:, h * 256:(h + 1) * 256]))(hh, h),
                             reads=[t_hh], writes=[t_BS])
                        P.op("dve", (lambda h: lambda e: e.bn_aggr(out=MV[:, h, :], in_=BS[:, h, :]))(h), reads=[t_BS], writes=[t_MV])
                    P.op("act", lambda e: e.activation(out=SD[:, 0:4], in_=MV[:, :, 1], func=AF.Sqrt, bias=EPS), reads=[t_MV], writes=[t_SD])
                    P.op("dve", lambda e: e.reciprocal(out=SD[:, 4:8], in_=SD[:, 0:4]), reads=[t_SD], writes=[t_SD])
                    for h in range(4):
                        P.op("dve", (lambda hh, h: lambda e: e.tensor_scalar(
                            out=HN[:, h * 256:(h + 1) * 256], in0=hh[:, h * 256:(h + 1) * 256], scalar1=MV[:, h, 0:1],
                            scalar2=SD[:, 4 + h:5 + h], op0=ALU.subtract, op1=ALU.mult))(hh, h),
                            reads=[t_hh, t_MV, t_SD], writes=[t_HN])
                    for c in range(8):
                        P.op("pe", (lambda c: lambda e: e.transpose(
                            out=B_N[:, c // 4, (c % 4) * 128:(c % 4 + 1) * 128], in_=HN[:, c * 128:(c + 1) * 128], identity=IDF[:]))(c),
                            reads=[t_HN, t_IDF], writes=[t_N[c // 4]])
                    o_, w_ = FVCOLS["mlng"]
                    for b2 in range(2):
                        P.op("dve", (lambda b2: lambda e: e.tensor_tensor(
                            out=Y1[:, 4 * b2:4 * b2 + 4, :], in0=B_N[:, b2, :].rearrange("p (c t) -> p c t", t=128),
                            in1=FV[:, o_ + 4 * b2:o_ + 4 * b2 + 4].unsqueeze(2).to_broadcast([128, 4, 128]), op=ALU.mult))(b2),
                            reads=[t_N[b2], t_FV], writes=[t_Y1])
                    P.op("dve", (lambda cols: lambda e: e.tensor_tensor(out=Y1[:], in0=Y1[:], in1=XM[:, :, cols], op=ALU.add))(cols),
                         reads=[t_Y1] + t_XM, writes=[t_Y1])
                    P.op("dve", (lambda yg, cols: lambda e: e.tensor_tensor(out=yg[:, :, cols], in0=Y1[:], in1=SIG[:, :, cols], op=ALU.mult))(yg, cols),
                         reads=[t_Y1] + t_SIG, writes=[t_yg])
                if d == 1 and lat:
                    P.dma("sp", (lambda yg, gi: lambda e: e.dma_start(
                        out=YML_d[:, :, gi * 256:(gi + 1) * 256].rearrange("c p t -> p c t"), in_=yg[:]))(yg, gi),
                        reads=[t_yg], writes=[t_YML[gi]], semtrk=t_yg)
            for d in range(2):
                with ExitStack() as st2:
                    sb2 = lambda name, shape, dt: st2.enter_context(nc.sbuf_tensor(name, list(shape), dt))
                    if d == 0:
                        WMX = sb2("WMX", [128, 8, 8, 128], BF16); t_WMX = [Trk("WMX%d" % c) for c in range(8)]
                        for c in range(8):
                            P.dma("pool", (lambda c: lambda e: e.dma_start(
                                out=WMX[:, c], in_=win_d[16 + c].rearrange("p (k j) -> p k j", j=128)))(c), writes=[t_WMX[c]])
                        UMF = Rot(sb2, "UMF", 2, [128, 260], F32)
                        HALO = sb2("HALO", [128, 8, 2], F32); t_HALO = [Trk("HALO%d" % c) for c in range(8)]
                        XCV = Rot(sb2, "XCV", 2, [128, 256], F32)
                    if d == 1:
                        WMO = sb2("WMO", [128, 8, 8, 128], BF16); t_WMO = [Trk("WMO%d" % c) for c in range(8)]
                        for c in range(8):
                            P.dma("pool", (lambda c: lambda e: e.dma_start(
                                out=WMO[:, c], in_=win_d[24 + c].rearrange("p (k j) -> p k j", j=128)))(c), writes=[t_WMO[c]])
                        SIG = sb2("SIG", [128, 8, 256], F32); t_SIG = [Trk("SIG%d" % c) for c in range(8)]
                        HFt = Rot(sb2, "HFt", 2, [128, 1024], F32)
                        hft_map = {}
                        HN = sb2("HN", [128, 1024], F32); t_HN = Trk("HN")
                        BS = sb2("BS", [128, 4, 6], F32); t_BS = Trk("BS")
                        MV = sb2("MV", [128, 4, 2], F32); t_MV = Trk("MV")
                        SD = sb2("SD", [128, 8], F32); t_SD = Trk("SD")
                        Y1 = sb2("Y1", [128, 8, 128], F32); t_Y1 = Trk("Y1")
                        YG = Rot(sb2, "YG", 2, [128, 8, 256], BF16)
                        GS1 = [sb2("QT1", [128, 8, 256], BF16), sb2("KT1", [128, 8, 256], BF16),
                               sb2("XMB1", [128, 8, 256], BF16), sb2("UMB1", [128, 8, 256], BF16)]
                        GSETS = [((QT, KT, XMB, UMB), [Trk("gs0_%d" % i) for i in range(4)]),
                                 (tuple(GS1), [Trk("gs1_%d" % i) for i in range(4)])]
                        t_XMl = Trk("XMl")
                    hs = [False]
                    order = [0] + (list(range(1, 17)) if d == 0 else list(range(16, 0, -1)))
                    for gpos, g in enumerate(order):
                        lat = g > 0
                        gi = g - 1
                        if d == 0:
                            import os as _os2
                            PB = [B_N[:, 0, :], B_N[:, 1, :]]
                            t_PB = [t_N[0], t_N[1]]
                            if _os2.environ.get('PBPM'):
                                PB = [B_PM[:], B_PM[:]]; t_PB = [t_PM, t_PM]
                            hi = 259 if (lat and gi <= 14) else 258
                            lo = 0 if (lat and gi >= 1) else 2

                            def emit_proj(c):
                                pb, t_pb = PB[c % 2], t_PB[c % 2]
                                for kc in range(8):
                                    P.op("pe", (lambda pb, c, kc, g: lambda e: e.matmul(
                                        pb[:, 2:258], lhsT=WMX[:, c, kc, :], rhs=grp_rhs(kc, g), start=(kc == 0), stop=(kc == 7)))(pb, c, kc, g),
                                        reads=[t_WMX[c]] + grp_trks(g), writes=[t_pb])
                                if hi == 259:
                                    b0 = 256 + (gi + 1) * 256
                                    for kc in range(8):
                                        P.op("pe", (lambda pb, c, kc, b0: lambda e: e.matmul(
                                            pb[:, 258:259], lhsT=WMX[:, c, kc, :], rhs=uT[:, kc, b0:b0 + 1], start=(kc == 0), stop=(kc == 7)))(pb, c, kc, b0),
                                            reads=[t_WMX[c]] + grp_trks(g), writes=[t_pb])

                            bufs_c = {}

                            def emit_evac_act(c):
                                pb, t_pb = PB[c % 2], t_PB[c % 2]
                                umf, t_umf = UMF.next()
                                xcv, t_xcv = XCV.next()
                                bufs_c[c] = (umf, t_umf, xcv, t_xcv)
                                P.op("act", (lambda umf, pb, hi: lambda e: e.activation(
                                    out=umf[:, 2:hi], in_=pb[:, 2:hi], func=AF.Copy))(umf, pb, hi), reads=[t_pb], writes=[t_umf])
                                P.op("act", (lambda c, pb: lambda e: e.activation(out=UMB[:, c, :], in_=pb[:, 2:258], func=AF.Copy))(c, pb),
                                     reads=[t_pb], writes=[t_UMB[c]])

                            def emit_conv(c):
                                umf, t_umf, xcv, t_xcv = bufs_c[c]
                                if lo == 0:
                                    P.op("dve", (lambda umf, c: lambda e: e.tensor_copy(out=umf[:, 0:2], in_=HALO[:, c, :]))(umf, c),
                                         reads=[t_HALO[c]], writes=[t_umf])
                                else:
                                    P.op("dve", (lambda umf: lambda e: e.memset(umf[:, 0:2], 0.0))(umf), writes=[t_umf])
                                if hi == 258:
                                    P.op("dve", (lambda umf: lambda e: e.memset(umf[:, 258:259], 0.0))(umf), writes=[t_umf])
                                if lat and gi <= 14:
                                    P.op("dve", (lambda umf, c: lambda e: e.tensor_copy(out=HALO[:, c, :], in_=umf[:, 256:258]))(umf, c),
                                         reads=[t_umf], writes=[t_HALO[c]])
                                P.op("dve", (lambda umf, xcv, c: lambda e: e.tensor_scalar(
                                    out=xcv[:], in0=umf[:, 0:256], scalar1=fvc("mlcw", c * 4), scalar2=fvc("mlcb", c),
                                    op0=ALU.mult, op1=ALU.add))(umf, xcv, c), reads=[t_umf, t_FV], writes=[t_xcv])
                                for j in range(1, 4):
                                    P.op("dve", (lambda umf, xcv, c, j: lambda e: e.scalar_tensor_tensor(
                                        out=xcv[:], in0=umf[:, j:j + 256], scalar=fvc("mlcw", c * 4 + j), in1=xcv[:],
                                        op0=ALU.mult, op1=ALU.add))(umf, xcv, c, j), reads=[t_umf, t_FV, t_xcv], writes=[t_xcv])
                                P.op("act", (lambda xcv, c: lambda e: e.activation(out=XM[:, c, :], in_=xcv[:], func=AF.Silu))(xcv, c),
                                     reads=[t_xcv], writes=[t_XM[c]])
                                P.op("act", (lambda xcv, c: lambda e: e.activation(out=XMB[:, c, :], in_=xcv[:], func=AF.Silu))(xcv, c),
                                     reads=[t_xcv], writes=[t_XMB[c]])

                            vts = {}

                            def emit_qkv(c):
                                vt, t_vt = VT.next()
                                vts[c] = (vt, t_vt)
                                P.op("pe", (lambda c: lambda e: e.matmul(
                                    B_QK[:, 0:256], lhsT=MLBD[:, c, :], rhs=XMB[:, c, :], start=True, stop=True))(c),
                                    reads=[t_MLBD, t_XMB[c]], writes=[t_PQ])
                                P.op("pe", (lambda c: lambda e: e.matmul(
                                    B_QK[:, 256:512], lhsT=MLBD[:, 8 + c, :], rhs=XMB[:, c, :], start=True, stop=True))(c),
                                    reads=[t_MLBD, t_XMB[c]], writes=[t_PQ])
                                P.op("act", (lambda c: lambda e: e.activation(out=QT[:, c, :], in_=B_QK[:, 0:256], func=AF.Copy))(c),
                                     reads=[t_PQ], writes=[t_QT[c]])
                                P.op("act", (lambda c: lambda e: e.activation(
                                    out=KT[:, c, :], in_=B_QK[:, 256:512], func=AF.Copy, scale=1.0 / 16.0))(c),
                                    reads=[t_PQ], writes=[t_KT[c]])

                            def emit_gates(c):
                                vt, t_vt = vts[c]
                                P.op("pe", (lambda c: lambda e: e.matmul(
                                    B_VO[:, 0:256], lhsT=MLBD[:, 16 + c, :], rhs=UMB[:, c, :], start=True, stop=True))(c),
                                    reads=[t_MLBD, t_UMB[c]], writes=[t_PV])
                                P.op("dve", (lambda vt: lambda e: e.tensor_copy(out=vt[:], in_=B_VO[:, 0:256]))(vt),
                                     reads=[t_PV], writes=[t_vt])
                                for ti, (src, t_src) in enumerate(((QT[:, c, :], t_QT[c]), (KT[:, c, :], t_KT[c]), (vt[:], t_vt))):
                                    P.op("pe", (lambda c, ti, src, d: lambda e: e.matmul(
                                        B_GP[0:8, 0:256], lhsT=WGT[:, ti * 8 + c, d * 8:(d + 1) * 8], rhs=src,
                                        start=(c == 0 and ti == 0), stop=(c == 7 and ti == 2)))(c, ti, src, d),
                                        reads=[t_WGT, t_src], writes=[t_GP])

                            if _os2.environ.get("NOPIPE"):
                                for c in range(8):
                                    emit_proj(c)
                                    emit_evac_act(c)
                                    emit_conv(c)
                                    emit_qkv(c)
                                    emit_gates(c)
                            else:
                                emit_proj(0)
                                emit_proj(1)
                                emit_evac_act(0)
                                for c in range(8):
                                    if c + 2 < 8:
                                        emit_proj(c + 2)
                                    if c + 1 < 8:
                                        emit_evac_act(c + 1)
                                    emit_conv(c)
                                    emit_qkv(c)
                                    if c > 0:
                                        emit_gates(c - 1)
                                emit_gates(7)
                            for wi_, (arr, trs) in enumerate(((QT, t_QT), (KT, t_KT), (XMB, t_XMB), (UMB, t_UMB))):
                                P.dma("sp", (lambda arr, g, wi_: lambda e: e.dma_start(out=SPQ_d[g, wi_], in_=arr[:]))(arr, g, wi_),
                                      reads=trs, writes=[t_SPQ[g][wi_]], semtrk=t_spd[wi_])
                            if lat:
                                P.dma("sp", (lambda gi: lambda e: e.dma_start(out=SPX_d[gi], in_=XM[:]))(gi),
                                      reads=t_XM, writes=[t_SPX[gi]], semtrk=t_spd[4])
                            gates_post(d)
                            rec_group(g, d, QT, KT, XMB, UMB, t_QT, t_KT, t_XMB, t_UMB, XM, t_XM, hs)
                            continue
                        def load_set(gq, si):
                            arrs, trs = GSETS[si]
                            for wi_ in range(4):
                                P.dma("sp", (lambda arrs, gq, wi_: lambda e: e.dma_start(out=arrs[wi_][:], in_=SPQ_d[gq, wi_]))(arrs, gq, wi_),
                                      reads=[t_SPQ[gq][wi_]], writes=[trs[wi_]])
                        if gpos == 0:
                            load_set(g, 0)
                        if gpos + 1 < len(order):
                            load_set(order[gpos + 1], (gpos + 1) % 2)
                        (QTg, KTg, XMBg, UMBg), trs = GSETS[gpos % 2]
                        if lat:
                            P.dma("sp", (lambda gi: lambda e: e.dma_start(out=XM[:], in_=SPX_d[gi]))(gi), reads=[t_SPX[gi]], writes=[t_XMl])
                        for c in range(8):
                            vt, t_vt = VT.next()
                            P.op("pe", (lambda c, UMBg: lambda e: e.matmul(
                                B_VO[:, 0:256], lhsT=MLBD[:, 16 + c, :], rhs=UMBg[:, c, :], start=True, stop=True))(c, UMBg),
                                reads=[t_MLBD, trs[3]], writes=[t_PV])
                            P.op("dve", (lambda vt: lambda e: e.tensor_copy(out=vt[:], in_=B_VO[:, 0:256]))(vt),
                                 reads=[t_PV], writes=[t_vt])
                            if lat:
                                for kc in range(8):
                                    P.op("pe", (lambda c, kc, g: lambda e: e.matmul(
                                        B_PM[:, 0:256], lhsT=WMO[:, c, kc, :], rhs=grp_rhs(kc, g), start=(kc == 0), stop=(kc == 7)))(c, kc, g),
                                        reads=[t_WMO[c]] + grp_trks(g), writes=[t_PM])
                                P.op("act", (lambda c: lambda e: e.activation(out=SIG[:, c, :], in_=B_PM[:, 0:256], func=AF.Sigmoid))(c),
                                     reads=[t_PM], writes=[t_SIG[c]])
                            for ti, (src, t_src) in enumerate(((QTg[:, c, :], trs[0]), (KTg[:, c, :], trs[1]), (vt[:], t_vt))):
                                P.op("pe", (lambda c, ti, src, d: lambda e: e.matmul(
                                    B_GP[0:8, 0:256], lhsT=WGT[:, ti * 8 + c, d * 8:(d + 1) * 8], rhs=src,
                                    start=(c == 0 and ti == 0), stop=(c == 7 and ti == 2)))(c, ti, src, d),
                                    reads=[t_WGT, t_src], writes=[t_GP])
                        if lat:
                            o2_, w2_ = FVCOLS["mlsk"]
                            P.op("dve", lambda e: e.tensor_tensor(
                                out=XM[:], in0=XM[:], in1=FV[:, o2_:o2_ + 8].unsqueeze(2).to_broadcast([128, 8, 256]), op=ALU.mult),
                                reads=[t_XMl, t_FV], writes=[t_XMl])
                        gates_post(d)
                        rec_group(g, d, QTg, KTg, XMBg, UMBg, [trs[0]] * 8, [trs[1]] * 8, [trs[2]] * 8, [trs[3]] * 8, XM, [t_XMl] * 8, hs)
                    P.barrier()
                    P.emit()
        if "yml" in debug:
            dbg_d["yml"] = YML_d
        if stop_after == "ml":
            P.wait_all("sp", P.out_toks)
            P.barrier()
            P.emit()
            ust.close()
            return nc, dbg_d

        x_cm = x_d.rearrange("(r j) d -> j r d", j=64)
        out_cm = out_d.rearrange("(r j) d -> j r d", j=64)
        t_X1 = [Trk("X1_%d" % i) for i in range(32)]
        t_H2 = [Trk("H2_%d" % i) for i in range(16)]
        with ExitStack() as st:
            sb = lambda name, shape, dt: st.enter_context(nc.sbuf_tensor(name, list(shape), dt))
            ps = lambda name, shape, dt: st.enter_context(nc.psum_tensor(name, list(shape), dt))
            WGR = sb("WGR", [128, 8, 8, 128], BF16); WGM = sb("WGM", [128, 8, 8, 128], BF16)
            WBR = sb("WBR", [128, 8, 8, 128], BF16); WBM = sb("WBM", [128, 8, 8, 128], BF16)
            WOUT = sb("WOUT", [128, 8, 1024], BF16)
            t_WGR = [Trk("WGR%d" % i) for i in range(8)]; t_WGM = [Trk("WGM%d" % i) for i in range(8)]
            t_WBR = [Trk("WBR%d" % i) for i in range(8)]; t_WBM = [Trk("WBM%d" % i) for i in range(8)]
            t_WOUT = [Trk("WOUT%d" % i) for i in range(8)]
            for oc in range(8):
                for (W, t_W, src) in ((WGR, t_WGR, win_d[32 + oc]), (WGM, t_WGM, win_d[40 + oc]),
                                      (WBR, t_WBR, wbrg_d[oc]), (WBM, t_WBM, wbml_d[oc])):
                    P.dma("pool", (lambda W, oc, src: lambda e: e.dma_start(
                        out=W[:, oc], in_=src.rearrange("p (k j) -> p k j", j=128)))(W, oc, src), writes=[t_W[oc]])
            for kc in range(8):
                P.dma("pool", (lambda kc: lambda e: e.dma_start(out=WOUT[:, kc, :], in_=wout_d[:, kc, :]))(kc), writes=[t_WOUT[kc]])
            YRt = Rot(sb, "YRt", 1, [128, 8, 512], BF16)
            YMt = Rot(sb, "YMt", 1, [128, 8, 512], BF16)
            SG = Rot(sb, "SG", 1, [128, 1024], F32)
            MIX = Rot(sb, "MIX", 1, [128, 8, 512], BF16)
            XT = Rot(sb, "XTc", 2, [128, 1024], F32)
            X1t = Rot(sb, "X1t", 1, [128, 1024], F32)
            XN = Rot(sb, "XNc", 1, [128, 1024], BF16)
            H2s = Rot(sb, "H2s", 1, [128, 8, 256], BF16)
            STc = sb("STc", [128, 32, 4], F32)
            BA0 = ps("BA0", [128, 512], F32); t_BA0 = Trk("PS_BA0")
            BA1 = ps("BA1", [128, 512], F32); t_BA1 = Trk("PS_BA1")
            BB0 = ps("BB0", [128, 512], F32); t_BB0 = Trk("PS_BB0")
            BB1 = ps("BB1", [128, 512], F32); t_BB1 = Trk("PS_BB1")
            BY = ps("BY", [128, 1024], F32); t_BY = Trk("PS_BY")
            BT = ps("BT", [128, 8, 128], BF16); t_BT = Trk("PS_BT")
            BT2 = ps("BT2", [128, 8, 128], BF16); t_BT2 = Trk("PS_BT2")
            def load_y(T):
                yr, t_yr = YRt.next()
                ym, t_ym = YMt.next()
                P.dma("sp", (lambda yr, T: lambda e: e.dma_start(
                    out=yr[:], in_=YRG_d[:, :, T * 512:(T + 1) * 512].rearrange("c p t -> p c t")))(yr, T), reads=t_YRG, writes=[t_yr])
                P.dma("sp", (lambda ym, T: lambda e: e.dma_start(
                    out=ym[:], in_=YML_d[:, :, T * 512:(T + 1) * 512].rearrange("c p t -> p c t")))(ym, T),
                    reads=t_YML[2 * T:2 * T + 2], writes=[t_ym])
                return yr, t_yr, ym, t_ym

            def load_x(ti):
                xt, t_xt = XT.next()
                for jj in range(2):
                    P.dma("sp", (lambda xt, jj, ti: lambda e: e.dma_start(
                        out=xt[jj * 64:(jj + 1) * 64, :], in_=x_cm[2 * ti + jj]))(xt, jj, ti), writes=[t_xt])
                return xt, t_xt

            h2s_trk2 = {}
            ynext = load_y(0)
            xnext = load_x(0)
            for T in range(8):
                yr, t_yr, ym, t_ym = ynext
                mix, t_mix = MIX.next()
                for oc in range(8):
                    sg, t_sg = SG.next()
                    for (W, t_W, bank, t_bank) in ((WGR, t_WGR, BA0, t_BA0), (WGM, t_WGM, BA1, t_BA1)):
                        for kc in range(8):
                            P.op("pe", (lambda W, bank, oc, kc, T: lambda e: e.matmul(
                                bank[:], lhsT=W[:, oc, kc, :],
                                rhs=uT[:, kc, 256 + T * 512:256 + (T + 1) * 512],
                                start=(kc == 0), stop=(kc == 7)))(W, bank, oc, kc, T), reads=[t_W[oc]] + t_uT[4:68], writes=[t_bank])
                    for (W, t_W, src, t_src, bank, t_bank) in ((WBR, t_WBR, yr, t_yr, BB0, t_BB0), (WBM, t_WBM, ym, t_ym, BB1, t_BB1)):
                        for kc in range(8):
                            P.op("pe", (lambda W, bank, oc, kc, src: lambda e: e.matmul(
                                bank[:], lhsT=W[:, oc, kc, :], rhs=src[:, kc, :],
                                start=(kc == 0), stop=(kc == 7)))(W, bank, oc, kc, src), reads=[t_W[oc], t_src], writes=[t_bank])
                    for (i_, bank, t_bank) in ((0, BA0, t_BA0), (1, BA1, t_BA1)):
                        P.op("act", (lambda sg, bank, i_: lambda e: e.activation(
                            out=sg[:, i_ * 512:(i_ + 1) * 512], in_=bank[:], func=AF.Sigmoid))(sg, bank, i_), reads=[t_bank], writes=[t_sg])
                    for (i_, bank, t_bank) in ((0, BB0, t_BB0), (1, BB1, t_BB1)):
                        P.op("dve", (lambda sg, bank, i_: lambda e: e.tensor_tensor(
                            out=sg[:, i_ * 512:(i_ + 1) * 512], in0=bank[:], in1=sg[:, i_ * 512:(i_ + 1) * 512], op=ALU.mult))(sg, bank, i_),
                            reads=[t_bank, t_sg], writes=[t_sg])
                    P.op("dve", (lambda mix, sg, oc: lambda e: e.tensor_tensor(
                        out=mix[:, oc, :], in0=sg[:, 0:512], in1=sg[:, 512:1024], op=ALU.add))(mix, sg, oc),
                        reads=[t_sg], writes=[t_mix])
                if T + 1 < 8:
                    ynext = load_y(T + 1)
                for s_ in range(4):
                    ti = T * 4 + s_
                    if s_ % 2 == 0:
                        h2s, t_h2s = H2s.next()
                        t_h2s2 = h2s_trk2.setdefault(id(t_h2s), Trk("h2s_b"))
                    xt, t_xt = xnext
                    if ti + 1 < 32:
                        xnext = load_x(ti + 1)
                    x1, t_x1 = X1t.next()
                    xn, t_xn = XN.next()
                    t_st = Trk("stc%d" % ti)
                    for half in range(2):
                        for kc in range(8):
                            P.op("pe", (lambda mix, half, kc, s_: lambda e: e.matmul(
                                BY[:, half * 512:(half + 1) * 512], lhsT=mix[:, kc, s_ * 128:(s_ + 1) * 128],
                                rhs=WOUT[:, kc, half * 512:(half + 1) * 512], start=(kc == 0), stop=(kc == 7)))(mix, half, kc, s_),
                                reads=[t_mix, t_WOUT[kc]], writes=[t_BY])
                    P.op("dve", (lambda x1: lambda e: e.tensor_tensor(out=x1[:], in0=BY[:], in1=GROW[:, 0, :], op=ALU.mult))(x1),
                         reads=[t_BY, t_GROW], writes=[t_x1])
                    P.op("dve", (lambda x1, xt: lambda e: e.tensor_tensor(out=x1[:], in0=x1[:], in1=xt[:], op=ALU.add))(x1, xt),
                         reads=[t_x1, t_xt], writes=[t_x1])
                    P.dma("sp", (lambda x1, ti: lambda e: e.dma_start(out=X1_d[ti * 128:(ti + 1) * 128, :], in_=x1[:]))(x1, ti),
                          reads=[t_x1], writes=[t_X1[ti]], semtrk=t_x1)
                    P.op("act", (lambda x1, xn, ti: lambda e: e.activation(
                        out=xn[:], in_=x1[:], func=AF.Square, accum_out=STc[:, ti, 0:1]))(x1, xn, ti), reads=[t_x1], writes=[t_xn, t_st])
                    P.op("act", (lambda ti: lambda e: e.activation(
                        out=STc[:, ti, 1:2], in_=STc[:, ti, 0:1], func=AF.Sqrt, scale=1.0 / 1024.0, bias=EPS))(ti), reads=[t_st], writes=[t_st])
                    P.op("dve", (lambda ti: lambda e: e.reciprocal(out=STc[:, ti, 2:3], in_=STc[:, ti, 1:2]))(ti), reads=[t_st], writes=[t_st])
                    P.op("act", (lambda x1, xn, ti: lambda e: e.activation(
                        out=xn[:], in_=x1[:], func=AF.Copy, scale=STc[:, ti, 2:3]))(x1, xn, ti), reads=[t_x1, t_st], writes=[t_xn])
                    for c in range(8):
                        btc, t_btc = (BT, t_BT) if c < 4 else (BT2, t_BT2)
                        P.op("pe", (lambda xn, c, btc: lambda e: e.transpose(
                            out=btc[:, c, :], in_=xn[:, c * 128:(c + 1) * 128], identity=IDB[:]))(xn, c, btc), reads=[t_xn, t_IDB], writes=[t_btc])
                    for c in (0, 4, 1, 5, 2, 6, 3, 7):
                        dst = h2s[:, c, (s_ % 2) * 128:(s_ % 2 + 1) * 128]
                        if c < 4:
                            P.op("dve", (lambda c, dst: lambda e: e.tensor_scalar(
                                out=dst, in0=BT[:, c, :], scalar1=A2[:, c:c + 1], scalar2=MODT[:, 48 + 2 * c:49 + 2 * c],
                                op0=ALU.mult, op1=ALU.add))(c, dst), reads=[t_BT, t_A2, t_MODT], writes=[t_h2s])
                        else:
                            P.op("act", (lambda c, dst: lambda e: e.activation(
                                out=dst, in_=BT2[:, c, :], func=AF.Identity, scale=A2[:, c:c + 1], bias=MODT[:, 48 + 2 * c:49 + 2 * c]))(c, dst),
                                reads=[t_BT2, t_A2, t_MODT], writes=[t_h2s2])
                    if s_ % 2 == 1:
                        Tq = ti // 2
                        P.dma("sp", (lambda h2s, Tq: lambda e: e.dma_start(out=H2_d[Tq], in_=h2s[:]))(h2s, Tq),
                              reads=[t_h2s, t_h2s2], writes=[t_H2[Tq]], semtrk=t_h2s)
            P.barrier()
            P.emit()
        ust.close()
        if "x1" in debug:
            dbg_d["x1"] = X1_d

        with ExitStack() as st:
            sb = lambda name, shape, dt: st.enter_context(nc.sbuf_tensor(name, list(shape), dt))
            ps = lambda name, shape, dt: st.enter_context(nc.psum_tensor(name, list(shape), dt))
            WFI = sb("WFI", [128, 44, 8, 128], BF16); t_WFI = [Trk("WFI%d" % i) for i in range(44)]
            WFO = sb("WFO", [128, 22, 1024], BF16); t_WFO = [Trk("WFO%d" % i) for i in range(22)]
            FGR = sb("FGR", [128, 1024], F32); t_FGR = Trk("FGR")
            P.dma("sp", lambda e: e.dma_start(out=FGR[:], in_=fgrow_d), writes=[t_FGR])
            for f_ in range(22):
                for ci in (f_, 22 + f_):
                    P.dma("pool", (lambda ci: lambda e: e.dma_start(
                        out=WFI[:, ci], in_=wffi_d[ci].rearrange("p (k j) -> p k j", j=128)))(ci), writes=[t_WFI[ci]])
                P.dma("pool", (lambda f_: lambda e: e.dma_start(out=WFO[:, f_, :], in_=wffo_d[:, f_, :]))(f_), writes=[t_WFO[f_]])
            H2t = Rot(sb, "H2t", 2, [128, 8, 512], BF16)
            SGt = Rot(sb, "SGt", 2, [128, 512], F32)
            HID = Rot(sb, "HID", 1, [128, 22, 512], BF16)
            X1r = Rot(sb, "X1r", 2, [128, 1024], F32)
            T2 = Rot(sb, "T2", 2, [128, 1024], F32)
            JK = sb("JK", [128, 1024], BF16); t_JK = Trk("JK")
            STf = sb("STf", [128, 32, 4], F32)
            BG0 = Rot(ps, "PS_BG0", 2, [128, 512], F32)
            BG1 = Rot(ps, "PS_BG1", 2, [128, 512], F32)
            BO = Rot(ps, "PS_BO", 2, [128, 1024], F32)
            h2_map = {}
            x1_map = {}
            for T in range(8):
                for Tq in (T, T + 1):
                    if Tq < 8 and Tq not in h2_map:
                        hb, t_hb = H2t.next()
                        for q_ in range(2):
                            P.dma("sp", (lambda hb, Tq, q_: lambda e: e.dma_start(out=hb[:, :, q_ * 256:(q_ + 1) * 256], in_=H2_d[2 * Tq + q_]))(hb, Tq, q_),
                                  reads=[t_H2[2 * Tq + q_]], writes=[t_hb])
                        h2_map[Tq] = (hb, t_hb)
                h2, t_h2 = h2_map.pop(T)
                hid, t_hid = HID.next()
                for f_ in range(22):
                    g0, t_g0 = BG0.next()
                    g1, t_g1 = BG1.next()
                    sg, t_sg = SGt.next()
                    for (ci, bank, t_bank) in ((f_, g0, t_g0), (22 + f_, g1, t_g1)):
                        for kc in range(8):
                            P.op("pe", (lambda bank, ci, kc, h2: lambda e: e.matmul(
                                bank[:], lhsT=WFI[:, ci, kc, :], rhs=h2[:, kc, :], start=(kc == 0), stop=(kc == 7)))(bank, ci, kc, h2),
                                reads=[t_WFI[ci], t_h2], writes=[t_bank])
                    P.op("act", (lambda sg, g0: lambda e: e.activation(out=sg[:], in_=g0[:], func=AF.Silu))(sg, g0),
                         reads=[t_g0], writes=[t_sg])
                    P.op("dve", (lambda hid, f_, sg, g1: lambda e: e.tensor_tensor(
                        out=hid[:, f_, :], in0=g1[:], in1=sg[:], op=ALU.mult))(hid, f_, sg, g1), reads=[t_g1, t_sg], writes=[t_hid])
                for s_ in range(4):
                    ti = T * 4 + s_
                    bo, t_bo = BO.next()
                    for tq in (ti, ti + 1):
                        if tq < 32 and tq not in x1_map:
                            xb, t_xb = X1r.next()
                            P.dma("sp", (lambda xb, tq: lambda e: e.dma_start(out=xb[:], in_=X1_d[tq * 128:(tq + 1) * 128, :]))(xb, tq),
                                  reads=[t_X1[tq]], writes=[t_xb])
                            x1_map[tq] = (xb, t_xb)
                    x1, t_x1 = x1_map.pop(ti)
                    t2, t_t2 = T2.next()
                    t_st = Trk("stf%d" % ti)
                    for half in range(2):
                        for f_ in range(22):
                            P.op("pe", (lambda bo, hid, half, f_, s_: lambda e: e.matmul(
                                bo[:, half * 512:(half + 1) * 512], lhsT=hid[:, f_, s_ * 128:(s_ + 1) * 128],
                                rhs=WFO[:, f_, half * 512:(half + 1) * 512], start=(f_ == 0), stop=(f_ == 21)))(bo, hid, half, f_, s_),
                                reads=[t_hid, t_WFO[f_]], writes=[t_bo])
                    P.op("dve", (lambda t2, bo: lambda e: e.tensor_tensor(out=t2[:], in0=bo[:], in1=GROW[:, 1, :], op=ALU.mult))(t2, bo),
                         reads=[t_bo, t_GROW], writes=[t_t2])
                    P.op("dve", (lambda t2, x1: lambda e: e.tensor_tensor(out=t2[:], in0=t2[:], in1=x1[:], op=ALU.add))(t2, x1),
                         reads=[t_t2, t_x1], writes=[t_t2])
                    P.op("act", (lambda t2, ti: lambda e: e.activation(
                        out=JK[:], in_=t2[:], func=AF.Square, accum_out=STf[:, ti, 0:1]))(t2, ti), reads=[t_t2], writes=[t_JK, t_st])
                    P.op("act", (lambda ti: lambda e: e.activation(
                        out=STf[:, ti, 1:2], in_=STf[:, ti, 0:1], func=AF.Sqrt, scale=1.0 / 1024.0, bias=EPS))(ti), reads=[t_st], writes=[t_st])
                    P.op("dve", (lambda ti: lambda e: e.reciprocal(out=STf[:, ti, 2:3], in_=STf[:, ti, 1:2]))(ti), reads=[t_st], writes=[t_st])
                    P.op("act", (lambda t2, ti: lambda e: e.activation(
                        out=t2[:], in_=t2[:], func=AF.Copy, scale=STf[:, ti, 2:3]))(t2, ti), reads=[t_t2, t_st], writes=[t_t2])
                    P.op("dve", (lambda t2: lambda e: e.tensor_tensor(out=t2[:], in0=t2[:], in1=FGR[:], op=ALU.mult))(t2),
                         reads=[t_t2, t_FGR], writes=[t_t2])
                    for jj in range(2):
                        P.out_toks.append(P.dma("sp", (lambda t2, jj, ti: lambda e: e.dma_start(
                            out=out_cm[2 * ti + jj], in_=t2[jj * 64:(jj + 1) * 64, :]))(t2, jj, ti), reads=[t_t2], semtrk=t_t2))
            P.barrier()
            P.emit()

        P.wait_all("sp", P.out_toks)
        P.emit()
    return nc, dbg_d


def make_in_maps(inputs):
    sh = prep_shared(inputs)
    x = np.asarray(inputs["x"], np.float32)
    c = np.asarray(inputs["c"], np.float32)
    ctx = np.asarray(inputs["ctx"], np.float32)
    c_ctx = np.asarray(inputs["c_ctx"], np.float32)
    maps = []
    for b in range(8):
        m = dict(sh)
        m["x"] = np.ascontiguousarray(x[b])
        m["ctx"] = np.ascontiguousarray(ctx[b])
        m["cv"] = np.ascontiguousarray(np.stack([fm(c[b]), fm(c_ctx)], 2).reshape(128, 16))
        maps.append(m)
    return maps


def kernel(**inputs):
    nc, _ = build()
    maps = make_in_maps(inputs)
    res = run_bass_kernel_spmd(nc, maps, core_ids=list(range(8)))
    return np.stack([r["out"] for r in res.results], 0)
```

```python
import numpy as np
from contextlib import ExitStack
import concourse.bass as bass
import concourse.mybir as mybir
from concourse.bass_utils import run_bass_kernel_spmd

F32 = mybir.dt.float32
BF16 = mybir.dt.bfloat16
AF = mybir.ActivationFunctionType
ALU = mybir.AluOpType
AX = mybir.AxisListType

ENGS = ["pe", "act", "dve", "pool", "sp"]
EPS = 1e-6
NT = 4352
NCTX = 256
NLAT = 4096


class Trk:
    __slots__ = ("name", "w", "r", "dsem", "dcnt", "excl")

    def __init__(self, name=""):
        self.name = name
        self.w = None
        self.r = {}
        self.dsem = None
        self.dcnt = 0
        self.excl = name.startswith("PS_")


class Prog:
    def __init__(self, nc, stack):
        self.nc = nc
        self.stack = stack
        self.ops = {e: [] for e in ENGS}
        self.seq = {e: 0 for e in ENGS}
        self.known = {e: {} for e in ENGS}
        self.esem = {e: stack.enter_context(nc.semaphore("s_" + e)) for e in ENGS}
        self.nsem = len(ENGS)
        self.out_toks = []
        self.dtrks = []
        self.sem_pool = {"sp": [], "pool": []}

    def new_dsem(self, name):
        s = self.stack.enter_context(self.nc.semaphore("d%d_%s" % (self.nsem, name)))
        self.nsem += 1
        return s

    def _need(self, eng, waits, dep):
        sem, val = dep
        if eng == "pe" and sem is self.esem["pe"]:
            return
        k = id(sem)
        if self.known[eng].get(k, 0) >= val:
            return
        self.known[eng][k] = val
        waits[k] = (sem, val)

    def _deps(self, eng, reads, writes):
        waits = {}
        for t in reads:
            if t.w is not None:
                self._need(eng, waits, t.w)
            if t.excl:
                for dep in t.r.values():
                    self._need(eng, waits, dep)
        for t in writes:
            if t.w is not None:
                self._need(eng, waits, t.w)
            for dep in t.r.values():
                self._need(eng, waits, dep)
        return waits

    def _record(self, tok, reads, writes):
        for t in reads:
            t.r[id(tok[0])] = tok
        for t in writes:
            t.w = tok
            t.r = {}

    def op(self, eng, fn, reads=(), writes=()):
        waits = self._deps(eng, reads, writes)
        self.seq[eng] += 1
        tok = (self.esem[eng], self.seq[eng])
        self._record(tok, reads, writes)
        self.ops[eng].append((list(waits.values()), fn, (self.esem[eng], 1)))

    def dma(self, eng, fn, reads=(), writes=(), semtrk=None):
        if semtrk is None:
            semtrk = writes[0] if writes else reads[0]
        if semtrk.dsem is None:
            if self.sem_pool[eng]:
                semtrk.dsem, semtrk.dcnt = self.sem_pool[eng].pop()
            else:
                semtrk.dsem = self.new_dsem(semtrk.name)
            self.dtrks.append((semtrk, eng))
        waits = self._deps(eng, reads, writes)
        if semtrk.dcnt > 0:
            self._need(eng, waits, (semtrk.dsem, semtrk.dcnt))
        semtrk.dcnt += 16
        tok = (semtrk.dsem, semtrk.dcnt)
        self._record(tok, reads, writes)
        self.ops[eng].append((list(waits.values()), fn, (semtrk.dsem, 16)))
        return tok

    def wait_all(self, eng, toks):
        waits = {}
        for t in toks:
            self._need(eng, waits, t)
        self.ops[eng].append((list(waits.values()), None, None))

    def barrier(self):
        waits = {}
        for e in ENGS:
            if e != "sp" and self.seq[e] > 0:
                self._need("sp", waits, (self.esem[e], self.seq[e]))
        for t, _e in self.dtrks:
            if t.dcnt > 0:
                self._need("sp", waits, (t.dsem, t.dcnt))
        self.seq["sp"] += 1
        self.ops["sp"].append((list(waits.values()), lambda e: e.nop(), (self.esem["sp"], 1)))
        for t, e_ in self.dtrks:
            self.sem_pool[e_].append((t.dsem, t.dcnt))
            t.dsem = None
            t.dcnt = 0
        self.dtrks = []
        for e in ENGS:
            if e != "sp":
                w = {}
                self._need(e, w, (self.esem["sp"], self.seq["sp"]))
                self.ops[e].append((list(w.values()), None, None))

    def emit(self):
        nc = self.nc
        ops = self.ops
        self.ops = {e: [] for e in ENGS}
        with nc.Block() as block:
            def run(e, lst):
                for waits, fn, inc in lst:
                    for sem, val in waits:
                        e.wait_ge(sem, val)
                    if fn is not None:
                        fn(e).then_inc(inc[0], inc[1])

            @block.tensor
            def _(e):
                run(e, ops["pe"])

            @block.scalar
            def _(e):
                run(e, ops["act"])

            @block.vector
            def _(e):
                run(e, ops["dve"])

            @block.gpsimd
            def _(e):
                run(e, ops["pool"])

            @block.sync
            def _(e):
                run(e, ops["sp"])


class Rot:
    def __init__(self, alloc, name, n, shape, dt):
        self.bufs = [(alloc("%s%d" % (name, i), shape, dt), Trk("%s%d" % (name, i))) for i in range(n)]
        self.i = 0

    def next(self):
        b = self.bufs[self.i % len(self.bufs)]
        self.i += 1
        return b


FVCOLS = {}
_off = 0
for _n, _w in [("n1g", 8), ("n2g", 8), ("rgcw", 32), ("rgcb", 8), ("rgba", 16), ("rgbx", 16), ("rglam", 16),
               ("mlcw", 32), ("mlcb", 8), ("mlng", 8), ("mlsk", 8)]:
    FVCOLS[_n] = (_off, _w)
    _off += _w
NV = _off


def fm(vec):
    v = np.asarray(vec, np.float32)
    return np.ascontiguousarray(v.reshape(-1, 128).T)


def colchunks(w):
    K, N = w.shape
    a = w.reshape(K // 128, 128, N // 128, 128)
    a = a.transpose(2, 1, 0, 3)
    return np.ascontiguousarray(a.reshape(N // 128, 128, (K // 128) * 128))


def rowchunks(w):
    K, N = w.shape
    return np.ascontiguousarray(w.reshape(K // 128, 128, N).transpose(1, 0, 2))


def blockdiag128(blocks):
    nb, bi, bo = blocks.shape
    per = 128 // bi
    out = np.zeros((nb // per, 128, 128), np.float32)
    for b in range(nb):
        c, q = divmod(b, per)
        out[c, q * bi:(q + 1) * bi, q * bo:(q + 1) * bo] = blocks[b]
    return out


def prep_shared(inp):
    f = lambda k: np.asarray(inp[k], np.float32)
    sh = {}
    w_mod = f("w_mod")[0]
    sh["wmod"] = colchunks(w_mod)
    b_mod = f("b_mod")[0]
    sh["bmod2"] = np.ascontiguousarray(np.repeat(fm(b_mod)[:, :, None], 2, axis=2).reshape(128, 96))
    sh["wmodg"] = np.stack([rowchunks(w_mod[:, 2048:3072]), rowchunks(w_mod[:, 5120:6144])], 0)
    sh["bmodg"] = np.ascontiguousarray(np.broadcast_to(
        np.concatenate([b_mod[2048:3072], b_mod[5120:6144]])[None, :], (128, 2048)))
    fv = np.zeros((128, NV), np.float32)

    def put(name, arr):
        o, w = FVCOLS[name]
        assert arr.shape == (128, w), (name, arr.shape)
        fv[:, o:o + w] = arr
    put("n1g", fm(f("norm1_g")[0]))
    put("n2g", fm(f("norm2_g")[0]))
    cw = f("rg_conv_w")[0]
    put("rgcw", np.stack([fm(cw[j]) for j in range(4)], 2).reshape(128, 32))
    put("rgcb", fm(f("rg_conv_b")[0]))
    put("rgba", np.concatenate([fm(f("rg_ba")[0][d]) for d in range(2)], 1))
    put("rgbx", np.concatenate([fm(f("rg_bx")[0][d]) for d in range(2)], 1))
    put("rglam", np.concatenate([fm(f("rg_lambda")[0][d]) for d in range(2)], 1))
    cw = f("ml_conv_w")[0]
    put("mlcw", np.stack([fm(cw[j]) for j in range(4)], 2).reshape(128, 32))
    put("mlcb", fm(f("ml_conv_b")[0]))
    put("mlng", fm(f("ml_norm_g")[0]))
    put("mlsk", fm(f("ml_skip")[0]))
    sh["fv"] = fv
    sh["win"] = colchunks(f("w_in")[0])
    rg = []
    for d in range(2):
        for w in (f("rg_wa")[0][d], f("rg_wx")[0][d]):
            rg.append(blockdiag128(w).transpose(1, 0, 2))
    sh["rgbd"] = np.ascontiguousarray(np.concatenate(rg, 1))
    ml = [blockdiag128(f(k)[0]).transpose(1, 0, 2) for k in ("ml_wq", "ml_wk", "ml_wv")]
    sh["mlbd"] = np.ascontiguousarray(np.concatenate(ml, 1))
    wi, wf = f("ml_wi")[0], f("ml_wf")[0]
    wg = np.concatenate([wi[0], wf[0], wi[1], wf[1]], 1)
    sh["wgt"] = np.ascontiguousarray(wg.reshape(24, 128, 16).transpose(1, 0, 2))
    bi, bf = f("ml_bi")[0], f("ml_bf")[0]
    gb = np.stack([np.tile(np.concatenate([bi[d], bf[d]]), 2) for d in range(2)], 0)
    sh["gbias"] = np.ascontiguousarray(np.broadcast_to(gb[None], (128, 2, 16)))
    tri = np.zeros((2, 128, 128), np.float32)
    ii = np.arange(128)
    tri[0] = (ii[:, None] <= ii[None, :])
    tri[1] = (ii[:, None] >= ii[None, :])
    sh["tri"] = np.ascontiguousarray(tri.transpose(1, 0, 2))
    ident = np.eye(128, dtype=np.float32)
    sh["ident"] = ident
    sh["wbrg"] = colchunks(f("w_branch_rg")[0])
    sh["wbml"] = colchunks(f("w_branch_ml")[0])
    sh["wout"] = rowchunks(f("w_out")[0])
    sh["wffi"] = colchunks(f("w_ffn_in")[0])
    sh["wffo"] = rowchunks(f("w_ffn_out")[0])
    sh["fgrow"] = np.ascontiguousarray(np.broadcast_to(f("final_norm_g")[None, :], (128, 1024)))
    return sh


def build(debug=(), stop_after=None):
    nc = bass.Bass("TRN2", target_bir_lowering=False)
    din = lambda name, shape, dt=F32: nc.dram_tensor(name, list(shape), dt, kind="ExternalInput").ap()
    x_d = din("x", [NLAT, 1024])
    ctx_d = din("ctx", [NCTX, 1024])
    cv_d = din("cv", [128, 16])
    wmod_d = din("wmod", [48, 128, 1024])
    bmod2_d = din("bmod2", [128, 96])
    wmodg_d = din("wmodg", [2, 128, 8, 1024])
    bmodg_d = din("bmodg", [128, 2048])
    fv_d = din("fv", [128, NV])
    win_d = din("win", [48, 128, 1024])
    ident_d = din("ident", [128, 128])
    rgbd_d = din("rgbd", [128, 32, 128])
    mlbd_d = din("mlbd", [128, 24, 128])
    wgt_d = din("wgt", [128, 24, 16])
    gbias_d = din("gbias", [128, 2, 16])
    tri_d = din("tri", [128, 2, 128])
    wbrg_d = din("wbrg", [8, 128, 1024])
    wbml_d = din("wbml", [8, 128, 1024])
    wout_d = din("wout", [128, 8, 1024])
    wffi_d = din("wffi", [44, 128, 1024])
    wffo_d = din("wffo", [128, 22, 1024])
    fgrow_d = din("fgrow", [128, 1024])
    SPQ_d = nc.dram_tensor("SPQ", [17, 4, 128, 8, 256], BF16).ap()
    SPX_d = nc.dram_tensor("SPX", [16, 128, 8, 256], F32).ap()
    X1_d = nc.dram_tensor("X1", [NLAT, 1024], F32).ap()
    H2_d = nc.dram_tensor("H2", [16, 128, 8, 256], BF16).ap()
    HF_d = nc.dram_tensor("HF", [32, 128, 1024], F32).ap()
    if "yml" in debug:
        YML_d = nc.dram_tensor("dbg_yml", [8, 128, NLAT], BF16, kind="ExternalOutput").ap()
    else:
        YML_d = nc.dram_tensor("YML", [8, 128, NLAT], BF16).ap()
    YRG_d = nc.dram_tensor("YRG", [8, 128, NLAT], BF16).ap()
    out_d = nc.dram_tensor("out", [NLAT, 1024], F32, kind="ExternalOutput").ap()
    dbg_d = {}

    with ExitStack() as gst:
        P = Prog(nc, gst)
        galloc = lambda name, shape, dt: gst.enter_context(nc.sbuf_tensor(name, list(shape), dt))

        def dump(name, ap, trk, shape, dt=F32):
            if name not in debug:
                return
            d = nc.dram_tensor("dbg_" + name, list(shape), dt, kind="ExternalOutput").ap()
            dbg_d[name] = d
            P.out_toks.append(P.dma("sp", lambda e: e.dma_start(out=d, in_=ap), reads=[trk], semtrk=Trk("dbg" + name)))

        t_uT = [Trk("uT%d_%d" % (i // 2, i % 2)) for i in range(68)]
        FV = galloc("FV", [128, NV], F32); t_FV = Trk("FV")
        MODT = galloc("MODT", [128, 96], F32); t_MODT = Trk("MODT")
        A1 = galloc("A1", [128, 16], F32); t_A1 = Trk("A1")
        A2 = galloc("A2", [128, 8], F32); t_A2 = Trk("A2")
        GROW = galloc("GROW", [128, 2, 1024], F32); t_GROW = Trk("GROW")
        KD = galloc("KD", [128, 16], F32); t_KD = Trk("KD")
        IDB = galloc("IDB", [128, 128], BF16); t_IDB = Trk("IDB")
        IDF = galloc("IDF", [128, 128], F32); t_IDF = Trk("IDF")
        t_YRG = [Trk("YRG%d" % c) for c in range(8)]
        ust = ExitStack()
        uT = ust.enter_context(nc.sbuf_tensor("uT", [128, 8, NT], BF16))

        def fvc(name, i=0, n=1):
            o, w = FVCOLS[name]
            return FV[:, o + i:o + i + n]

        with ExitStack() as st:
            sb = lambda name, shape, dt: st.enter_context(nc.sbuf_tensor(name, list(shape), dt))
            ps = lambda name, shape, dt: st.enter_context(nc.psum_tensor(name, list(shape), dt))
            CV = sb("CV", [128, 16], F32); t_CV = Trk("CV")
            S2 = sb("S2", [128, 16], F32); t_S2 = Trk("S2")
            SREP = sb("SREP", [128, 8, 128], F32); t_SREP = Trk("SREP")
            BM2 = sb("BM2", [128, 96], F32); t_BM2 = Trk("BM2")
            BMG = sb("BMG", [128, 2048], F32); t_BMG = Trk("BMG")
            TMPA = sb("TMPA", [128, 16], F32); t_TMPA = Trk("TMPA")
            TMPB = sb("TMPB", [128, 16], F32); t_TMPB = Trk("TMPB")
            WM = Rot(sb, "WM", 3, [128, 1024], F32)
            WGm = sb("WGm", [128, 8, 1024], F32); t_WGm = Trk("WGm")
            MODP = ps("MODP", [128, 512], F32); t_MODP = Trk("PS_MODP")
            GRP = ps("GRP", [128, 1024], F32); t_GRP = Trk("PS_GRP")

            P.dma("sp", lambda e: e.dma_start(out=CV[:], in_=cv_d), writes=[t_CV])
            P.dma("sp", lambda e: e.dma_start(out=FV[:], in_=fv_d), writes=[t_FV])
            P.dma("sp", lambda e: e.dma_start(out=BM2[:], in_=bmod2_d), writes=[t_BM2])
            P.dma("sp", lambda e: e.dma_start(out=BMG[:], in_=bmodg_d), writes=[t_BMG])
            P.dma("sp", lambda e: e.dma_start(out=IDF[:], in_=ident_d), writes=[t_IDF])
            P.dma("pool", lambda e: e.dma_start(out=IDB[:], in_=ident_d), writes=[t_IDB])
            P.op("act", lambda e: e.activation(out=S2[:], in_=CV[:], func=AF.Silu), reads=[t_CV], writes=[t_S2])
            for kc in range(8):
                P.op("dve", (lambda kc: lambda e: e.tensor_copy(
                    out=SREP[:, kc, :], in_=S2[:, 2 * kc:2 * kc + 1].to_broadcast([128, 128])))(kc),
                    reads=[t_S2], writes=[t_SREP])
            for n in range(48):
                wm, t_wm = WM.next()
                P.dma("sp", (lambda wm, n: lambda e: e.dma_start(out=wm[:], in_=wmod_d[n]))(wm, n), writes=[t_wm])
                for kc in range(8):
                    P.op("pe", (lambda wm, n, kc: lambda e: e.matmul(
                        MODP[:, 2 * n:2 * n + 2], lhsT=wm[:, kc * 128:(kc + 1) * 128], rhs=S2[:, 2 * kc:2 * kc + 2],
                        start=(kc == 0), stop=(kc == 7)))(wm, n, kc), reads=[t_wm, t_S2], writes=[t_MODP])
            P.op("dve", lambda e: e.tensor_tensor(out=MODT[:], in0=MODP[:, 0:96], in1=BM2[:], op=ALU.add),
                 reads=[t_MODP, t_BM2], writes=[t_MODT])
            P.op("dve", lambda e: e.tensor_scalar_add(out=TMPA[:], in0=MODT[:, 16:32], scalar1=1.0),
                 reads=[t_MODT], writes=[t_TMPA])
            for j in range(2):
                P.op("dve", (lambda j: lambda e: e.tensor_tensor(
                    out=A1[:, j:16:2], in0=TMPA[:, j:16:2], in1=fvc("n1g", 0, 8), op=ALU.mult))(j),
                    reads=[t_TMPA, t_FV], writes=[t_A1])
            P.op("dve", lambda e: e.tensor_scalar_add(out=TMPB[:, 0:8], in0=MODT[:, 64:80:2], scalar1=1.0),
                 reads=[t_MODT], writes=[t_TMPB])
            P.op("dve", lambda e: e.tensor_tensor(out=A2[:], in0=TMPB[:, 0:8], in1=fvc("n2g", 0, 8), op=ALU.mult),
                 reads=[t_TMPB, t_FV], writes=[t_A2])
            for g in range(2):
                P.dma("sp", (lambda g: lambda e: e.dma_start(out=WGm[:], in_=wmodg_d[g]))(g), writes=[t_WGm])
                for half in range(2):
                    for kc in range(8):
                        P.op("pe", (lambda half, kc: lambda e: e.matmul(
                            GRP[:, half * 512:(half + 1) * 512], lhsT=SREP[:, kc, :],
                            rhs=WGm[:, kc, half * 512:(half + 1) * 512], start=(kc == 0), stop=(kc == 7)))(half, kc),
                            reads=[t_SREP, t_WGm], writes=[t_GRP])
                P.op("dve", (lambda g: lambda e: e.tensor_tensor(
                    out=GROW[:, g, :], in0=GRP[:], in1=BMG[:, g * 1024:(g + 1) * 1024], op=ALU.add))(g),
                    reads=[t_GRP, t_BMG], writes=[t_GROW])
            P.op("act", lambda e: e.activation(out=TMPA[:], in_=fvc("rglam", 0, 16), func=AF.Exp, scale=-1.0),
                 reads=[t_FV], writes=[t_TMPA])
            P.op("act", lambda e: e.activation(out=TMPB[:], in_=TMPA[:], func=AF.Ln, bias=1.0),
                 reads=[t_TMPA], writes=[t_TMPB])
            P.op("dve", lambda e: e.tensor_scalar(out=KD[:], in0=TMPB[:], scalar1=-8.0, scalar2=None, op0=ALU.mult),
                 reads=[t_TMPB], writes=[t_KD])
            dump("modT", MODT[:], t_MODT, [128, 96])
            dump("grow", GROW[:], t_GROW, [128, 2, 1024])
            dump("kd", KD[:], t_KD, [128, 16])
            P.barrier()
            P.emit()

        with ExitStack() as st:
            sb = lambda name, shape, dt: st.enter_context(nc.sbuf_tensor(name, list(shape), dt))
            ps = lambda name, shape, dt: st.enter_context(nc.psum_tensor(name, list(shape), dt))
            XT = Rot(sb, "XT", 3, [128, 1024], F32)
            XN = Rot(sb, "XN", 2, [128, 1024], BF16)
            TP = Rot(ps, "PS_TP", 2, [128, 8, 128], BF16)
            TPb = Rot(ps, "PS_TPb", 2, [128, 8, 128], BF16)
            ST = sb("ST", [128, 34, 4], F32)
            JKA = sb("JKA", [128, 1024], BF16); t_JKA = Trk("JKA")
            stage1 = {}

            def emit_stats(i):
                t_st = Trk("st%d" % i)
                xt, t_xt = XT.next()
                src = ctx_d[i * 128:(i + 1) * 128, :] if i < 2 else x_d[(i - 2) * 128:(i - 1) * 128, :]
                P.dma("sp", (lambda xt, src: lambda e: e.dma_start(out=xt[:], in_=src))(xt, src), writes=[t_xt])
                P.op("act", (lambda xt, i: lambda e: e.activation(
                    out=JKA[:], in_=xt[:], func=AF.Square, accum_out=ST[:, i, 0:1]))(xt, i),
                    reads=[t_xt], writes=[t_JKA, t_st])
                P.op("act", (lambda i: lambda e: e.activation(
                    out=ST[:, i, 1:2], in_=ST[:, i, 0:1], func=AF.Sqrt, scale=1.0 / 1024.0, bias=EPS))(i),
                    reads=[t_st], writes=[t_st])
                P.op("dve", (lambda i: lambda e: e.reciprocal(out=ST[:, i, 2:3], in_=ST[:, i, 1:2]))(i),
                     reads=[t_st], writes=[t_st])
                stage1[i] = (xt, t_xt, t_st)

            emit_stats(0)
            for i in range(34):
                if i + 1 < 34:
                    emit_stats(i + 1)
                xt, t_xt, t_st = stage1.pop(i)
                xn, t_xn = XN.next()
                tpa, t_tpa = TP.next()
                tpb, t_tpb = TPb.next()
                j = 1 if i < 2 else 0
                P.op("act", (lambda xt, xn, i: lambda e: e.activation(
                    out=xn[:], in_=xt[:], func=AF.Copy, scale=ST[:, i, 2:3]))(xt, xn, i),
                    reads=[t_xt, t_st], writes=[t_xn])
                for c in range(8):
                    tp, t_tp = (tpa, t_tpa) if c < 4 else (tpb, t_tpb)
                    P.op("pe", (lambda xn, tp, c: lambda e: e.transpose(
                        out=tp[:, c, :], in_=xn[:, c * 128:(c + 1) * 128], identity=IDB[:]))(xn, tp, c),
                        reads=[t_xn, t_IDB], writes=[t_tp])
                for c in (0, 4, 1, 5, 2, 6, 3, 7):
                    tp, t_tp = (tpa, t_tpa) if c < 4 else (tpb, t_tpb)
                    if i < 2:
                        dst = uT[:, c, i * 128:(i + 1) * 128]
                        src_tp = tp[:, c, :]
                    else:
                        r0 = 2 * (i - 2)
                        dst = uT[:, c, 256:NT].rearrange("p (j r) -> p r j", r=64)[:, r0:r0 + 2, :]
                        src_tp = tp[:, c, :].rearrange("p (r j) -> p r j", j=64)
                    if c < 4:
                        P.op("dve", (lambda tp, c, dst, j: lambda e: e.tensor_scalar(
                            out=dst, in0=tp, scalar1=A1[:, 2 * c + j:2 * c + j + 1],
                            scalar2=MODT[:, 2 * c + j:2 * c + j + 1], op0=ALU.mult, op1=ALU.add))(src_tp, c, dst, j),
                            reads=[t_tp, t_A1, t_MODT], writes=[t_uT[2 * i + (0 if c < 4 else 1)]])
                    else:
                        P.op("act", (lambda tp, c, dst, j: lambda e: e.activation(
                            out=dst, in_=tp, func=AF.Identity, scale=A1[:, 2 * c + j:2 * c + j + 1],
                            bias=MODT[:, 2 * c + j:2 * c + j + 1]))(src_tp, c, dst, j),
                            reads=[t_tp, t_A1, t_MODT], writes=[t_uT[2 * i + (0 if c < 4 else 1)]])
            if "uT" in debug:
                UD = sb("UD", [128, 8, 512], F32); t_UD = Trk("UD")
                P.op("dve", lambda e: e.tensor_copy(out=UD[:], in_=uT[:, :, 128:640]), reads=t_uT[2:10], writes=[t_UD])
                dump("uT", UD[:], t_UD, [128, 8, 512])
            P.barrier()
            P.emit()


        CT0, LT0, PEND = 2, 261, 4357
        with ExitStack() as st:
            sb = lambda name, shape, dt: st.enter_context(nc.sbuf_tensor(name, list(shape), dt))
            ps = lambda name, shape, dt: st.enter_context(nc.psum_tensor(name, list(shape), dt))
            RGBD = sb("RGBD", [128, 32, 128], BF16); t_RGBD = Trk("RGBD")
            P.dma("pool", lambda e: e.dma_start(out=RGBD[:], in_=rgbd_d, max_dma_last_dim=4096), writes=[t_RGBD])
            RX = sb("RX", [128, 4360], F32); t_RX = Trk("RX")
            XC = sb("XC", [128, 4360], F32); t_XC = Trk("XC")
            XCB = sb("XCB", [128, 4360], BF16); t_XCB = Trk("XCB")
            TMP = sb("TMP", [128, 2180], F32); t_TMP = Trk("TMP")
            BB = [sb("B0", [128, 4360], F32), sb("B1", [128, 4360], F32)]; t_BB = [Trk("B0"), Trk("B1")]
            WR = Rot(sb, "WR", 4, [128, 8, 128], BF16)
            PJ = Rot(ps, "PS_PJ", 3, [128, 512], F32)
            GA = Rot(ps, "PS_GA", 4, [128, 512], F32)
            GL = Rot(sb, "GL", 2, [128, 512], F32)
            TS = Rot(sb, "TS", 2, [128, 512], F32)
            YS = Rot(sb, "YS", 1, [128, NLAT], BF16)
            for (a, b) in ((0, 2), (258, 261), (4357, 4360)):
                P.op("dve", (lambda a, b: lambda e: e.memset(RX[:, a:b], 0.0))(a, b), writes=[t_RX])
            for d in range(2):
                P.op("pool", (lambda d: lambda e: e.memset(BB[d][:, 256:264], 0.0))(d), writes=[t_BB[d]])
            blocks = [(0, 256, CT0)] + [(256 + 512 * b, 512, LT0 + 512 * b) for b in range(8)]

            def ut_trks(t0, n):
                return t_uT[2 * (t0 // 128):2 * ((t0 + n - 1) // 128 + 1)]

            def ut_nat(kc, t0, n):
                if t0 < 256:
                    return uT[:, kc, t0:t0 + n]
                r0 = (t0 - 256) // 64
                return uT[:, kc, 256:NT].rearrange("p (j r) -> p r j", r=64)[:, r0:r0 + n // 64, :]

            def rev(ap2d):
                n = ap2d.shape[1]
                return bass.AP(ap2d.tensor, ap2d.offset + (n - 1), [list(ap2d.ap[0]), [-1, n]])

            def load_wr(c):
                wrx, t_wrx = WR.next()
                wrg, t_wrg = WR.next()
                P.dma("pool", (lambda w, c: lambda e: e.dma_start(
                    out=w[:], in_=win_d[c].rearrange("p (k j) -> p k j", j=128)))(wrx, c), writes=[t_wrx])
                P.dma("pool", (lambda w, c: lambda e: e.dma_start(
                    out=w[:], in_=win_d[8 + c].rearrange("p (k j) -> p k j", j=128)))(wrg, c), writes=[t_wrg])
                return wrx, t_wrx, wrg, t_wrg

            wr_next = load_wr(0)
            for c in range(8):
                wrx, t_wrx, wrg, t_wrg = wr_next
                if c + 1 < 8:
                    wr_next = load_wr(c + 1)
                if c > 0:
                    P.op("dve", lambda e: e.memset(RX[:, 258:261], 0.0), writes=[t_RX])
                pj, t_pj = PJ.next()
                for kc in range(8):
                    P.op("pe", (lambda pj, wrx, kc: lambda e: e.matmul(
                        pj[:, 0:256], lhsT=wrx[:, kc, :], rhs=uT[:, kc, 0:256], start=(kc == 0), stop=(kc == 7)))(
                        pj, wrx, kc), reads=[t_wrx] + t_uT[0:4], writes=[t_pj])
                P.op("act", (lambda pj: lambda e: e.activation(
                    out=RX[:, CT0:CT0 + 256], in_=pj[:, 0:256], func=AF.Copy))(pj), reads=[t_pj], writes=[t_RX])
                for b in range(8):
                    pj, t_pj = PJ.next()
                    for kc in range(8):
                        P.op("pe", (lambda pj, wrx, kc, b: lambda e: e.matmul(
                            pj[:], lhsT=wrx[:, kc, :], rhs=uT[:, kc, 256 + b * 512:256 + (b + 1) * 512], start=(kc == 0), stop=(kc == 7)))(
                            pj, wrx, kc, b), reads=[t_wrx] + t_uT[4:68], writes=[t_pj])
                    P.op("act", (lambda pj, b: lambda e: e.activation(
                        out=RX[:, LT0:LT0 + 4096].rearrange("p (r j) -> p j r", j=64)[:, 8 * b:8 * b + 8, :],
                        in_=pj[:].rearrange("p (j r) -> p j r", r=64), func=AF.Copy))(pj, b), reads=[t_pj], writes=[t_RX])
                L = PEND - 2
                P.op("dve", (lambda c: lambda e: e.tensor_scalar(
                    out=XC[:, 2:PEND], in0=RX[:, 0:L], scalar1=fvc("rgcw", c * 4), scalar2=fvc("rgcb", c),
                    op0=ALU.mult, op1=ALU.add))(c), reads=[t_RX, t_FV], writes=[t_XC])
                for j in range(1, 4):
                    P.op("dve", (lambda c, j: lambda e: e.scalar_tensor_tensor(
                        out=XC[:, 2:PEND], in0=RX[:, j:j + L], scalar=fvc("rgcw", c * 4 + j), in1=XC[:, 2:PEND],
                        op0=ALU.mult, op1=ALU.add))(c, j), reads=[t_RX, t_FV, t_XC], writes=[t_XC])
                P.op("act", lambda e: e.activation(out=XCB[:, 2:PEND], in_=XC[:, 2:PEND], func=AF.Copy),
                     reads=[t_XC], writes=[t_XCB])
                if c == 3:
                    dump("xc", XC[:], t_XC, [128, 4360])
                for d in range(2):
                    Bd, t_Bd = BB[d], t_BB[d]
                    for (t0, n, pos) in blocks:
                        ga, t_ga = GA.next()
                        gx, t_gx = GA.next()
                        P.op("pe", (lambda ga, d, c, n, pos: lambda e: e.matmul(
                            ga[:, 0:n], lhsT=RGBD[:, (d * 2) * 8 + c, :], rhs=XCB[:, pos:pos + n], start=True, stop=True))(
                            ga, d, c, n, pos), reads=[t_RGBD, t_XCB], writes=[t_ga])
                        P.op("pe", (lambda gx, d, c, n, pos: lambda e: e.matmul(
                            gx[:, 0:n], lhsT=RGBD[:, (d * 2 + 1) * 8 + c, :], rhs=XCB[:, pos:pos + n], start=True, stop=True))(
                            gx, d, c, n, pos), reads=[t_RGBD, t_XCB], writes=[t_gx])
                        P.op("act", (lambda ga, d, c, n, pos: lambda e: e.activation(
                            out=RX[:, pos:pos + n], in_=ga[:, 0:n], func=AF.Sigmoid, bias=fvc("rgba", d * 8 + c)))(
                            ga, d, c, n, pos), reads=[t_ga, t_FV], writes=[t_RX])
                        P.op("act", (lambda gx, Bd, d, c, n, pos: lambda e: e.activation(
                            out=Bd[:, pos:pos + n], in_=gx[:, 0:n], func=AF.Sigmoid, bias=fvc("rgbx", d * 8 + c)))(
                            gx, Bd, d, c, n, pos), reads=[t_gx, t_FV], writes=[t_Bd])
                    P.op("act", (lambda d, c: lambda e: e.activation(
                        out=RX[:, 2:PEND], in_=RX[:, 2:PEND], func=AF.Exp, scale=KD[:, d * 8 + c:d * 8 + c + 1]))(d, c),
                        reads=[t_RX, t_KD], writes=[t_RX])
                    P.op("dve", (lambda Bd: lambda e: e.tensor_tensor(
                        out=Bd[:, 2:PEND], in0=Bd[:, 2:PEND], in1=XC[:, 2:PEND], op=ALU.mult))(Bd),
                        reads=[t_Bd, t_XC], writes=[t_Bd])
                    for (ra, rb) in ((2, 2180), (2180, PEND)):
                        P.op("act", (lambda ra, rb: lambda e: e.activation(
                            out=TMP[:, 0:rb - ra], in_=RX[:, ra:rb], func=AF.Square))(ra, rb), reads=[t_RX], writes=[t_TMP])
                        P.op("act", (lambda ra, rb: lambda e: e.activation(
                            out=TMP[:, 0:rb - ra], in_=TMP[:, 0:rb - ra], func=AF.Sqrt, scale=-1.0, bias=1.0))(ra, rb),
                            reads=[t_TMP], writes=[t_TMP])
                        P.op("dve", (lambda Bd, ra, rb: lambda e: e.tensor_tensor(
                            out=Bd[:, ra:rb], in0=Bd[:, ra:rb], in1=TMP[:, 0:rb - ra], op=ALU.mult))(Bd, ra, rb),
                            reads=[t_Bd, t_TMP], writes=[t_Bd])
                    f_ = (lambda ap: ap) if d == 0 else rev
                    c0, c1 = CT0, CT0 + 256
                    l0, l1 = LT0, LT0 + 4096
                    P.op("dve", (lambda Bd, f_: lambda e: e.tensor_tensor_scan(
                        out=f_(Bd[:, c0:c1]), data0=f_(RX[:, c0:c1]), data1=f_(Bd[:, c0:c1]), initial=0.0,
                        op0=ALU.mult, op1=ALU.add))(Bd, f_), reads=[t_RX, t_Bd], writes=[t_Bd])
                    ini = (c1 - 1) if d == 0 else c0
                    P.op("dve", (lambda Bd, f_, ini: lambda e: e.tensor_tensor_scan(
                        out=f_(Bd[:, l0:l1]), data0=f_(RX[:, l0:l1]), data1=f_(Bd[:, l0:l1]), initial=Bd[:, ini:ini + 1],
                        op0=ALU.mult, op1=ALU.add))(Bd, f_, ini), reads=[t_RX, t_Bd], writes=[t_Bd])
                ys, t_ys = YS.next()
                for b in range(8):
                    pj, t_pj = PJ.next()
                    gl, t_gl = GL.next()
                    ts, t_ts = TS.next()
                    for kc in range(8):
                        P.op("pe", (lambda pj, wrg, kc, b: lambda e: e.matmul(
                            pj[:], lhsT=wrg[:, kc, :], rhs=uT[:, kc, 256 + b * 512:256 + (b + 1) * 512], start=(kc == 0), stop=(kc == 7)))(
                            pj, wrg, kc, b), reads=[t_wrg] + t_uT[4:68], writes=[t_pj])
                    P.op("act", (lambda pj, gl: lambda e: e.activation(out=gl[:], in_=pj[:], func=AF.Gelu))(pj, gl),
                         reads=[t_pj], writes=[t_gl])
                    P.op("dve", (lambda ts, b: lambda e: e.tensor_tensor(
                        out=ts[:].rearrange("p (j r) -> p j r", r=64),
                        in0=BB[0][:, LT0:LT0 + 4096].rearrange("p (r j) -> p j r", j=64)[:, 8 * b:8 * b + 8, :],
                        in1=BB[1][:, LT0:LT0 + 4096].rearrange("p (r j) -> p j r", j=64)[:, 8 * b:8 * b + 8, :], op=ALU.add))(ts, b),
                        reads=t_BB, writes=[t_ts])
                    P.op("dve", (lambda ys, ts, gl, b: lambda e: e.tensor_tensor(
                        out=ys[:, b * 512:(b + 1) * 512], in0=ts[:], in1=gl[:], op=ALU.mult))(ys, ts, gl, b),
                        reads=[t_ts, t_gl], writes=[t_ys])
                P.dma("sp", (lambda ys, c: lambda e: e.dma_start(out=YRG_d[c], in_=ys[:]))(ys, c), reads=[t_ys], writes=[t_YRG[c]],
                      semtrk=t_ys)
                if c == 3:
                    if "hrg" in debug:
                        dump("hrg", HD[:], t_HD, [128, NLAT])
                    dump("yrg", ys[:], t_ys, [128, NLAT], BF16)
            P.barrier()
            P.emit()
        if stop_after == "rg":
            P.wait_all("sp", P.out_toks)
            P.barrier()
            P.emit()
            ust.close()
            return nc, dbg_d

        t_HF = [Trk("HF%d" % i) for i in range(32)]
        t_YML = [Trk("YML%d" % i) for i in range(16)]
        t_SPQ = [[Trk("SPQ%d_%d" % (g, i)) for i in range(4)] for g in range(17)]
        t_SPX = [Trk("SPX%d" % g) for g in range(16)]
        t_spd = [Trk("spd%d" % i) for i in range(5)]
        with ExitStack() as st:
            sb = lambda name, shape, dt: st.enter_context(nc.sbuf_tensor(name, list(shape), dt))
            ps = lambda name, shape, dt: st.enter_context(nc.psum_tensor(name, list(shape), dt))
            MLBD = sb("MLBD", [128, 24, 128], BF16); t_MLBD = Trk("MLBD")
            WGT = sb("WGT", [128, 24, 16], BF16); t_WGT = Trk("WGT")
            GBI = sb("GBI", [128, 2, 16], F32); t_GBI = Trk("GBI")
            TRI = sb("TRI", [128, 2, 128], F32); t_TRI = Trk("TRI")
            ONES = sb("ONES", [128, 128], F32); t_ONES = Trk("ONES")
            P.dma("pool", lambda e: e.dma_start(out=MLBD[:], in_=mlbd_d, max_dma_last_dim=4096), writes=[t_MLBD])
            P.dma("pool", lambda e: e.dma_start(out=WGT[:], in_=wgt_d), writes=[t_WGT])
            P.dma("sp", lambda e: e.dma_start(out=GBI[:], in_=gbias_d), writes=[t_GBI])
            P.dma("sp", lambda e: e.dma_start(out=TRI[:], in_=tri_d), writes=[t_TRI])
            P.op("dve", lambda e: e.memset(ONES[:], 1.0), writes=[t_ONES])
            B_PM = ps("B_PM", [128, 512], F32); t_PM = Trk("PS_PM")
            B_QK = ps("B_QK", [128, 512], F32); t_PQ = Trk("PS_PQ")
            B_VO = ps("B_VO", [128, 512], F32); t_PV = Trk("PS_PV")
            B_GP = ps("B_GP", [128, 512], F32); t_GP = Trk("PS_GP")
            B_N = ps("B_N", [128, 4, 512], F32); t_N = [Trk("PS_N%d" % i) for i in range(4)]
            BU = [B_PM, B_QK, B_VO, B_GP]; t_BU = [t_PM, t_PQ, t_PV, t_GP]
            VT = Rot(sb, "VT", 2, [128, 256], BF16)
            XM = sb("XM", [128, 8, 256], F32); t_XM = [Trk("XM%d" % c) for c in range(8)]
            XMB = sb("XMB", [128, 8, 256], BF16); t_XMB = [Trk("XMB%d" % c) for c in range(8)]
            UMB = sb("UMB", [128, 8, 256], BF16); t_UMB = [Trk("UMB%d" % c) for c in range(8)]
            QT = sb("QT", [128, 8, 256], BF16); t_QT = [Trk("QT%d" % c) for c in range(8)]
            KT = sb("KT", [128, 8, 256], BF16); t_KT = [Trk("KT%d" % c) for c in range(8)]
            GF = sb("GF", [8, 256], F32); t_GF = Trk("GF")
            GG = sb("GG", [128, 16], F32); t_GG = Trk("GG")
            GE = sb("GE", [128, 8], F32); t_GE = Trk("GE")
            GLn = sb("GLn", [128, 8], F32); t_GLn = Trk("GLn")
            GT2 = sb("GT2", [128, 8], F32); t_GT2 = Trk("GT2")
            EB = sb("EB", [128, 8], F32); t_EB = Trk("EB")
            WS = sb("WS", [128, 8], F32); t_WS = Trk("WS")
            EBL = sb("EBL", [128, 8], F32); t_EBL = Trk("EBL")
            EBLP = sb("EBLP", [128, 4], F32); t_EBLP = Trk("EBLP")
            DS = sb("DS", [128, 4, 2, 257], F32); t_DS = [Trk("DS%d" % h) for h in range(4)]
            DB = sb("DB", [128, 4, 2, 258], BF16); t_DB = [Trk("DB%d" % h) for h in range(4)]
            VX = sb("VX", [128, 2, 4, 258], BF16); t_VX = [Trk("VX%d" % i) for i in range(2)]
            KTM = sb("KTM", [128, 2, 1024], BF16); t_KTM = [Trk("KTM%d" % i) for i in range(2)]
            STt = sb("STt", [128, 2, 4, 128], BF16); t_STt = [Trk("STt%d" % i) for i in range(2)]
            E1 = sb("E1", [128, 8, 4], F32); t_E1 = Trk("E1")
            HH = Rot(sb, "HH", 1, [128, 1024], F32)

            def grp_rhs(kc, g):
                if g == 0:
                    return uT[:, kc, 0:256]
                gi = g - 1
                return uT[:, kc, 256 + gi * 256:256 + (gi + 1) * 256]

            def grp_trks(g):
                return t_uT[0:4] if g == 0 else t_uT[4:68]

            def gates_post(d):
                P.op("act", lambda e: e.activation(out=GF[:], in_=B_GP[0:8, 0:256], func=AF.Copy), reads=[t_GP], writes=[t_GF])
                for ch in range(2):
                    P.op("pe", (lambda ch: lambda e: e.transpose(
                        out=B_GP[:, 256 + ch * 8:256 + ch * 8 + 8], in_=GF[0:8, ch * 128:(ch + 1) * 128], identity=IDF[0:8, 0:8]))(ch),
                        reads=[t_GF, t_IDF], writes=[t_GP])
                P.op("dve", (lambda d: lambda e: e.tensor_tensor(
                    out=GG[:], in0=B_GP[:, 256:272], in1=GBI[:, d, :], op=ALU.add))(d), reads=[t_GP, t_GBI], writes=[t_GG])
                GGv = GG[:].rearrange("t (c k) -> t c k", k=8)
                P.op("act", lambda e: e.activation(
                    out=GE[:].rearrange("t (c h) -> t c h", h=4), in_=GGv[:, :, 4:8], func=AF.Exp, scale=-1.0),
                    reads=[t_GG], writes=[t_GE])
                P.op("act", lambda e: e.activation(out=GLn[:], in_=GE[:], func=AF.Ln, bias=1.0), reads=[t_GE], writes=[t_GLn])
                P.op("pe", (lambda d: lambda e: e.matmul(
                    B_GP[:, 288:296], lhsT=TRI[:, d, :], rhs=GLn[:], start=True, stop=True))(d),
                    reads=[t_TRI, t_GLn], writes=[t_GP])
                P.op("pe", lambda e: e.matmul(B_GP[:, 304:312], lhsT=ONES[:], rhs=GLn[:], start=True, stop=True),
                     reads=[t_ONES, t_GLn], writes=[t_GP])
                P.op("act", lambda e: e.activation(out=EB[:], in_=B_GP[:, 288:296], func=AF.Exp, scale=-1.0),
                     reads=[t_GP], writes=[t_EB])
                P.op("dve", lambda e: e.tensor_tensor(
                    out=GT2[:].rearrange("t (c h) -> t c h", h=4), in0=B_GP[:, 288:296].rearrange("t (c h) -> t c h", h=4),
                    in1=GGv[:, :, 0:4], op=ALU.add), reads=[t_GP, t_GG], writes=[t_GT2])
                P.op("act", lambda e: e.activation(out=WS[:], in_=GT2[:], func=AF.Exp), reads=[t_GT2], writes=[t_WS])
                P.op("act", lambda e: e.activation(out=EBL[:], in_=B_GP[:, 304:312], func=AF.Exp, scale=-1.0),
                     reads=[t_GP], writes=[t_EBL])
            def rec_group(g, d, QT, KT, XMB, UMB, t_QT, t_KT, t_XMB, t_UMB, XM, t_XM, hs):
                lat = g > 0
                gi = g - 1
                have_state = hs[0]
                for ch in range(2):
                    cols = slice(ch * 128, (ch + 1) * 128)
                    for c in range(8):
                        h, half = divmod(c, 2)
                        bk = h // 2
                        off = (h % 2) * 256 + half * 128
                        P.op("pe", (lambda c, bk, off, cols: lambda e: e.matmul(
                            B_N[:, bk, off:off + 128], lhsT=UMB[:, c, cols], rhs=MLBD[:, 16 + c, :], start=True, stop=True))(c, bk, off, cols),
                            reads=[t_UMB[c], t_MLBD], writes=[t_N[bk]])
                        P.op("pe", (lambda c, bk, off, cols: lambda e: e.matmul(
                            B_N[:, 2 + bk, off:off + 128], lhsT=XMB[:, c, cols], rhs=MLBD[:, 8 + c, :], start=True, stop=True))(c, bk, off, cols),
                            reads=[t_XMB[c], t_MLBD], writes=[t_N[2 + bk]])
                    for h in range(4):
                        bk = h // 2
                        off = (h % 2) * 256
                        P.op("dve", (lambda ch, h, bk, off: lambda e: e.tensor_scalar(
                            out=VX[:, ch, h, 0:256], in0=B_N[:, bk, off:off + 256], scalar1=WS[:, ch * 4 + h:ch * 4 + h + 1],
                            scalar2=None, op0=ALU.mult))(ch, h, bk, off), reads=[t_N[bk], t_WS], writes=[t_VX[ch]])
                    P.op("act", (lambda ch: lambda e: e.activation(
                        out=VX[:, ch, :, 256:257], in_=WS[:, ch * 4:ch * 4 + 4].unsqueeze(2), func=AF.Copy))(ch), reads=[t_WS], writes=[t_VX[ch]])
                    for bk in range(2):
                        P.op("act", (lambda ch, bk: lambda e: e.activation(
                            out=KTM[:, ch, bk * 512:(bk + 1) * 512], in_=B_N[:, 2 + bk, :], func=AF.Copy, scale=1.0 / 16.0))(ch, bk),
                            reads=[t_N[2 + bk]], writes=[t_KTM[ch]])
                    for h in range(4):
                        for half in range(2):
                            c = 2 * h + half
                            P.op("pe", (lambda c, h, half, cols: lambda e: e.matmul(
                                B_GP[:, h * 128:(h + 1) * 128], lhsT=KT[:, c, cols], rhs=QT[:, c, cols], start=(half == 0), stop=(half == 1)))(c, h, half, cols),
                                reads=[t_KT[c], t_QT[c]], writes=[t_GP])
                    P.op("dve", (lambda ch, d: lambda e: e.tensor_tensor(
                        out=STt[:, ch], in0=B_GP[:].rearrange("p (h t) -> p h t", t=128),
                        in1=TRI[:, d, :].unsqueeze(1).to_broadcast([128, 4, 128]), op=ALU.mult))(ch, d),
                        reads=[t_GP, t_TRI], writes=[t_STt[ch]])
                chs = (0, 1) if d == 0 else (1, 0)
                if d == 1 and lat:
                    yg, t_yg = YG.next()
                for ch in chs:
                    cols = slice(ch * 128, (ch + 1) * 128)
                    if d == 1 and lat:
                        for cq in (gi * 2 + ch, gi * 2 + ch - 1):
                            if cq >= 0 and cq not in hft_map:
                                hb, t_hb = HFt.next()
                                P.dma("sp", (lambda hb, cq: lambda e: e.dma_start(out=hb[:], in_=HF_d[cq]))(hb, cq),
                                      reads=[t_HF[cq]], writes=[t_hb])
                                hft_map[cq] = (hb, t_hb)
                    last_chunk = (d == 0 and g == 16 and ch == 1) or (d == 1 and g == 1 and ch == 0)
                    if not last_chunk:
                        for r in range(2):
                            for hh2 in range(2):
                                h = 2 * r + hh2
                                for half in range(2):
                                    bi_ = hh2 * 2 + half
                                    P.op("pe", (lambda ch, h, half, bi_: lambda e: e.matmul(
                                        BU[bi_][:, 0:257], lhsT=KTM[:, ch, h * 256 + half * 128:h * 256 + (half + 1) * 128],
                                        rhs=VX[:, ch, h, 0:257], start=True, stop=True))(ch, h, half, bi_),
                                        reads=[t_KTM[ch], t_VX[ch]], writes=[t_BU[bi_]])
                            for hh2 in range(2):
                                h = 2 * r + hh2
                                for half in range(2):
                                    bi_ = hh2 * 2 + half
                                    if have_state:
                                        P.op("dve", (lambda h, half, bi_: lambda e: e.scalar_tensor_tensor(
                                            out=DS[:, h, half, :], in0=DS[:, h, half, :], scalar=EBLP[:, h:h + 1], in1=BU[bi_][:, 0:257],
                                            op0=ALU.mult, op1=ALU.add))(h, half, bi_),
                                            reads=[t_DS[h], t_BU[bi_], t_EBLP], writes=[t_DS[h]])
                                    else:
                                        P.op("dve", (lambda h, half, bi_: lambda e: e.tensor_copy(
                                            out=DS[:, h, half, :], in_=BU[bi_][:, 0:257]))(h, half, bi_),
                                            reads=[t_BU[bi_]], writes=[t_DS[h]])
                    if lat:
                        hh, t_hh = HH.next()
                        for h in range(4):
                            P.op("pe", (lambda ch, h, hs: lambda e: e.matmul(
                                B_N[:, h, 0:257], lhsT=STt[:, ch, h, :], rhs=VX[:, ch, h, 0:257], start=True, stop=(not hs)))(ch, h, have_state),
                                reads=[t_STt[ch], t_VX[ch]], writes=[t_N[h]])
                            if have_state:
                                for half in range(2):
                                    c = 2 * h + half
                                    P.op("pe", (lambda c, h, half, cols: lambda e: e.matmul(
                                        B_N[:, h, 0:257], lhsT=QT[:, c, cols], rhs=DB[:, h, half, 0:257], start=False, stop=(half == 1)))(c, h, half, cols),
                                        reads=[t_QT[c], t_DB[h]], writes=[t_N[h]])
                    if not last_chunk:
                        for h in range(4):
                            idx = ch * 4 + h
                            P.op("act", (lambda h, idx: lambda e: e.activation(
                                out=DB[:, h, :, 0:257], in_=DS[:, h, :, :], func=AF.Copy, scale=EBL[:, idx:idx + 1]))(h, idx),
                                reads=[t_DS[h], t_EBL], writes=[t_DB[h]])
                        P.op("dve", (lambda ch: lambda e: e.tensor_copy(out=EBLP[:], in_=EBL[:, ch * 4:ch * 4 + 4]))(ch),
                             reads=[t_EBL], writes=[t_EBLP])
                    if lat:
                        e0 = ch * 4
                        P.op("dve", (lambda ch: lambda e: e.tensor_tensor(
                            out=E1[:, 0:4, 0], in0=B_N[:, :, 256], in1=EB[:, ch * 4:ch * 4 + 4], op=ALU.mult))(ch),
                            reads=t_N + [t_EB], writes=[t_E1])
                        P.op("dve", lambda e: e.tensor_scalar(
                            out=E1[:, 0:4, 1], in0=E1[:, 0:4, 0], scalar1=-1.0, scalar2=1.0, op0=ALU.mult, op1=ALU.max),
                            reads=[t_E1], writes=[t_E1])
                        P.op("dve", lambda e: e.scalar_tensor_tensor(
                            out=E1[:, 0:4, 2], in0=E1[:, 0:4, 0], scalar=1.0, in1=E1[:, 0:4, 1], op0=ALU.max, op1=ALU.max),
                            reads=[t_E1], writes=[t_E1])
                        P.op("dve", lambda e: e.reciprocal(out=E1[:, 0:4, 3], in_=E1[:, 0:4, 2]), reads=[t_E1], writes=[t_E1])
                        P.op("dve", (lambda ch: lambda e: e.tensor_tensor(
                            out=E1[:, 4:8, 0], in0=E1[:, 0:4, 3], in1=EB[:, ch * 4:ch * 4 + 4], op=ALU.mult))(ch),
                            reads=[t_E1, t_EB], writes=[t_E1])
                        for h in range(4):
                            P.op("act", (lambda hh, h: lambda e: e.activation(
                                out=hh[:, h * 256:(h + 1) * 256], in_=B_N[:, h, 0:256], func=AF.Copy, scale=E1[:, 4 + h, 0:1]))(hh, h),
                                reads=[t_N[h], t_E1], writes=[t_hh])
                    have_state = True; hs[0] = True
                    if not lat:
                        continue
                    cg = gi * 2 + ch
                    if d == 0:
                        P.dma("sp", (lambda hh, cg: lambda e: e.dma_start(out=HF_d[cg], in_=hh[:]))(hh, cg),
                              reads=[t_hh], writes=[t_HF[cg]], semtrk=t_hh)
                        continue
                    hft, t_hft = hft_map.pop(cg)
                    P.op("dve", (lambda hh, hft: lambda e: e.tensor_tensor(out=hh[:], in0=hh[:], in1=hft[:], op=ALU.add))(hh, hft),
                         reads=[t_hh, t_hft], writes=[t_hh])
                    for h in range(4):
                        P.op("dve", (lambda hh, h: lambda e: e.bn_stats(out=BS[:, h, :], in_=hh[:, h * 256:(h + 1) * 256]))(hh, h),
                             reads=[t_hh], writes=[t_BS])
                        P.op("dve", (lambda h: lambda e: e.bn_aggr(out=MV[:, h, :], in_=BS[:, h, :]))(h), reads=[t_BS], writes=[t_MV])
                    P.op("act", lambda e: e.activation(out=SD[:, 0:4], in_=MV[:, :, 1], func=AF.Sqrt, bias=EPS), reads=[t_MV], writes=[t_SD])
                    P.op("dve", lambda e: e.reciprocal(out=SD[:, 4:8], in_=SD[:, 0:4]), reads=[t_SD], writes=[t_SD])
                    for h in range(4):
                        P.op("dve", (lambda hh, h: lambda e: e.tensor_scalar(
                            out=HN[:, h * 256:(h + 1) * 256], in0=hh[:, h * 256:(h + 1) * 256], scalar1=MV[:, h, 0:1],
                            scalar2=SD[:, 4 + h:5 + h], op0=ALU.subtract, op1=ALU.mult))(hh, h),
                            reads=[t_hh, t_MV, t_SD], writes=[t_HN])
                    for c in range(8):
                        P.op("pe", (lambda c: lambda e: e.transpose(
                            out=B_N[:, c // 4, (c % 4) * 128:(c % 4 + 1) * 128], in_=HN[:, c * 128:(c + 1) * 128], identity=IDF[:]))(c),
                            reads=[t_HN, t_IDF], writes=[t_N[c // 4]])
                    o_, w_ = FVCOLS["mlng"]
                    for b2 in range(2):
                        P.op("dve", (lambda b2: lambda e: e.tensor_tensor(
                            out=Y1[:, 4 * b2:4 * b2 + 4, :], in0=B_N[:, b2, :].rearrange("p (c t) -> p c t", t=128),
                            in1=FV[:, o_ + 4 * b2:o_ + 4 * b2 + 4].unsqueeze(2).to_broadcast([128, 4, 128]), op=ALU.mult))(b2),
                            reads=[t_N[b2], t_FV], writes=[t_Y1])
                    P.op("dve", (lambda cols: lambda e: e.tensor_tensor(out=Y1[:], in0=Y1[:], in1=XM[:, :, cols], op=ALU.add))(cols),
                         reads=[t_Y1] + t_XM, writes=[t_Y1])
                    P.op("dve", (lambda yg, cols: lambda e: e.tensor_tensor(out=yg[:, :, cols], in0=Y1[:], in1=SIG[:, :, cols], op=ALU.mult))(yg, cols),
                         reads=[t_Y1] + t_SIG, writes=[t_yg])
                if d == 1 and lat:
                    P.dma("sp", (lambda yg, gi: lambda e: e.dma_start(
                        out=YML_d[:, :, gi * 256:(gi + 1) * 256].rearrange("c p t -> p c t"), in_=yg[:]))(yg, gi),
                        reads=[t_yg], writes=[t_YML[gi]], semtrk=t_yg)
            for d in range(2):
                with ExitStack() as st2:
                    sb2 = lambda name, shape, dt: st2.enter_context(nc.sbuf_tensor(name, list(shape), dt))
                    if d == 0:
                        WMX = sb2("WMX", [128, 8, 8, 128], BF16); t_WMX = [Trk("WMX%d" % c) for c in range(8)]
                        for c in range(8):
                            P.dma("pool", (lambda c: lambda e: e.dma_start(
                                out=WMX[:, c], in_=win_d[16 + c].rearrange("p (k j) -> p k j", j=128)))(c), writes=[t_WMX[c]])
                        UMF = Rot(sb2, "UMF", 2, [128, 260], F32)
                        HALO = sb2("HALO", [128, 8, 2], F32); t_HALO = [Trk("HALO%d" % c) for c in range(8)]
                        XCV = Rot(sb2, "XCV", 2, [128, 256], F32)
                    if d == 1:
                        WMO = sb2("WMO", [128, 8, 8, 128], BF16); t_WMO = [Trk("WMO%d" % c) for c in range(8)]
                        for c in range(8):
                            P.dma("pool", (lambda c: lambda e: e.dma_start(
                                out=WMO[:, c], in_=win_d[24 + c].rearrange("p (k j) -> p k j", j=128)))(c), writes=[t_WMO[c]])
                        SIG = sb2("SIG", [128, 8, 256], F32); t_SIG = [Trk("SIG%d" % c) for c in range(8)]
                        HFt = Rot(sb2, "HFt", 2, [128, 1024], F32)
                        hft_map = {}
                        HN = sb2("HN", [128, 1024], F32); t_HN = Trk("HN")
                        BS = sb2("BS", [128, 4, 6], F32); t_BS = Trk("BS")
                        MV = sb2("MV", [128, 4, 2], F32); t_MV = Trk("MV")
                        SD = sb2("SD", [128, 8], F32); t_SD = Trk("SD")
                        Y1 = sb2("Y1", [128, 8, 128], F32); t_Y1 = Trk("Y1")
                        YG = Rot(sb2, "YG", 2, [128, 8, 256], BF16)
                        GS1 = [sb2("QT1", [128, 8, 256], BF16), sb2("KT1", [128, 8, 256], BF16),
                               sb2("XMB1", [128, 8, 256], BF16), sb2("UMB1", [128, 8, 256], BF16)]
                        GSETS = [((QT, KT, XMB, UMB), [Trk("gs0_%d" % i) for i in range(4)]),
                                 (tuple(GS1), [Trk("gs1_%d" % i) for i in range(4)])]
                        t_XMl = Trk("XMl")
                    hs = [False]
                    order = [0] + (list(range(1, 17)) if d == 0 else list(range(16, 0, -1)))
                    for gpos, g in enumerate(order):
                        lat = g > 0
                        gi = g - 1
                        if d == 0:
                            import os as _os2
                            PB = [B_N[:, 0, :], B_N[:, 1, :]]
                            t_PB = [t_N[0], t_N[1]]
                            if _os2.environ.get('PBPM'):
                                PB = [B_PM[:], B_PM[:]]; t_PB = [t_PM, t_PM]
                            hi = 259 if (lat and gi <= 14) else 258
                            lo = 0 if (lat and gi >= 1) else 2

                            def emit_proj(c):
                                pb, t_pb = PB[c % 2], t_PB[c % 2]
                                for kc in range(8):
                                    P.op("pe", (lambda pb, c, kc, g: lambda e: e.matmul(
                                        pb[:, 2:258], lhsT=WMX[:, c, kc, :], rhs=grp_rhs(kc, g), start=(kc == 0), stop=(kc == 7)))(pb, c, kc, g),
                                        reads=[t_WMX[c]] + grp_trks(g), writes=[t_pb])
                                if hi == 259:
                                    b0 = 256 + (gi + 1) * 256
                                    for kc in range(8):
                                        P.op("pe", (lambda pb, c, kc, b0: lambda e: e.matmul(
                                            pb[:, 258:259], lhsT=WMX[:, c, kc, :], rhs=uT[:, kc, b0:b0 + 1], start=(kc == 0), stop=(kc == 7)))(pb, c, kc, b0),
                                            reads=[t_WMX[c]] + grp_trks(g), writes=[t_pb])

                            bufs_c = {}

                            def emit_evac_act(c):
                                pb, t_pb = PB[c % 2], t_PB[c % 2]
                                umf, t_umf = UMF.next()
                                xcv, t_xcv = XCV.next()
                                bufs_c[c] = (umf, t_umf, xcv, t_xcv)
                                P.op("act", (lambda umf, pb, hi: lambda e: e.activation(
                                    out=umf[:, 2:hi], in_=pb[:, 2:hi], func=AF.Copy))(umf, pb, hi), reads=[t_pb], writes=[t_umf])
                                P.op("act", (lambda c, pb: lambda e: e.activation(out=UMB[:, c, :], in_=pb[:, 2:258], func=AF.Copy))(c, pb),
                                     reads=[t_pb], writes=[t_UMB[c]])

                            def emit_conv(c):
                                umf, t_umf, xcv, t_xcv = bufs_c[c]
                                if lo == 0:
                                    P.op("dve", (lambda umf, c: lambda e: e.tensor_copy(out=umf[:, 0:2], in_=HALO[:, c, :]))(umf, c),
                                         reads=[t_HALO[c]], writes=[t_umf])
                                else:
                                    P.op("dve", (lambda umf: lambda e: e.memset(umf[:, 0:2], 0.0))(umf), writes=[t_umf])
                                if hi == 258:
                                    P.op("dve", (lambda umf: lambda e: e.memset(umf[:, 258:259], 0.0))(umf), writes=[t_umf])
                                if lat and gi <= 14:
                                    P.op("dve", (lambda umf, c: lambda e: e.tensor_copy(out=HALO[:, c, :], in_=umf[:, 256:258]))(umf, c),
                                         reads=[t_umf], writes=[t_HALO[c]])
                                P.op("dve", (lambda umf, xcv, c: lambda e: e.tensor_scalar(
                                    out=xcv[:], in0=umf[:, 0:256], scalar1=fvc("mlcw", c * 4), scalar2=fvc("mlcb", c),
                                    op0=ALU.mult, op1=ALU.add))(umf, xcv, c), reads=[t_umf, t_FV], writes=[t_xcv])
                                for j in range(1, 4):
                                    P.op("dve", (lambda umf, xcv, c, j: lambda e: e.scalar_tensor_tensor(
                                        out=xcv[:], in0=umf[:, j:j + 256], scalar=fvc("mlcw", c * 4 + j), in1=xcv[:],
                                        op0=ALU.mult, op1=ALU.add))(umf, xcv, c, j), reads=[t_umf, t_FV, t_xcv], writes=[t_xcv])
                                P.op("act", (lambda xcv, c: lambda e: e.activation(out=XM[:, c, :], in_=xcv[:], func=AF.Silu))(xcv, c),
                                     reads=[t_xcv], writes=[t_XM[c]])
                                P.op("act", (lambda xcv, c: lambda e: e.activation(out=XMB[:, c, :], in_=xcv[:], func=AF.Silu))(xcv, c),
                                     reads=[t_xcv], writes=[t_XMB[c]])

                            vts = {}

                            def emit_qkv(c):
                                vt, t_vt = VT.next()
                                vts[c] = (vt, t_vt)
                                P.op("pe", (lambda c: lambda e: e.matmul(
                                    B_QK[:, 0:256], lhsT=MLBD[:, c, :], rhs=XMB[:, c, :], start=True, stop=True))(c),
                                    reads=[t_MLBD, t_XMB[c]], writes=[t_PQ])
                                P.op("pe", (lambda c: lambda e: e.matmul(
                                    B_QK[:, 256:512], lhsT=MLBD[:, 8 + c, :], rhs=XMB[:, c, :], start=True, stop=True))(c),
                                    reads=[t_MLBD, t_XMB[c]], writes=[t_PQ])
                                P.op("pe", (lambda c: lambda e: e.matmul(
                                    B_VO[:, 0:256], lhsT=MLBD[:, 16 + c, :], rhs=UMB[:, c, :], start=True, stop=True))(c),
                                    reads=[t_MLBD, t_UMB[c]], writes=[t_PV])
                                P.op("act", (lambda c: lambda e: e.activation(out=QT[:, c, :], in_=B_QK[:, 0:256], func=AF.Copy))(c),
                                     reads=[t_PQ], writes=[t_QT[c]])
                                P.op("act", (lambda c: lambda e: e.activation(
                                    out=KT[:, c, :], in_=B_QK[:, 256:512], func=AF.Copy, scale=1.0 / 16.0))(c),
                                    reads=[t_PQ], writes=[t_KT[c]])
                                P.op("dve", (lambda vt: lambda e: e.tensor_copy(out=vt[:], in_=B_VO[:, 0:256]))(vt),
                                     reads=[t_PV], writes=[t_vt])

                            def emit_gates(c):
                                vt, t_vt = vts[c]
                                for ti, (src, t_src) in enumerate(((QT[:, c, :], t_QT[c]), (KT[:, c, :], t_KT[c]), (vt[:], t_vt))):
                                    P.op("pe", (lambda c, ti, src, d: lambda e: e.matmul(
                                        B_GP[0:8, 0:256], lhsT=WGT[:, ti * 8 + c, d * 8:(d + 1) * 8], rhs=src,
                                        start=(c == 0 and ti == 0), stop=(c == 7 and ti == 2)))(c, ti, src, d),
                                        reads=[t_WGT, t_src], writes=[t_GP])

                            if _os2.environ.get("NOPIPE"):
                                for c in range(8):
                                    emit_proj(c)
                                    emit_evac_act(c)
                                    emit_conv(c)
                                    emit_qkv(c)
                                    emit_gates(c)
                            else:
                                emit_proj(0)
                                emit_proj(1)
                                emit_evac_act(0)
                                for c in range(8):
                                    if c + 2 < 8:
                                        emit_proj(c + 2)
                                    if c + 1 < 8:
                                        emit_evac_act(c + 1)
                                    emit_conv(c)
                                    emit_qkv(c)
                                    if c > 0:
                                        emit_gates(c - 1)
                                emit_gates(7)
                            for wi_, (arr, trs) in enumerate(((QT, t_QT), (KT, t_KT), (XMB, t_XMB), (UMB, t_UMB))):
                                P.dma("sp", (lambda arr, g, wi_: lambda e: e.dma_start(out=SPQ_d[g, wi_], in_=arr[:]))(arr, g, wi_),
                                      reads=trs, writes=[t_SPQ[g][wi_]], semtrk=t_spd[wi_])
                            if lat:
                                P.dma("sp", (lambda gi: lambda e: e.dma_start(out=SPX_d[gi], in_=XM[:]))(gi),
                                      reads=t_XM, writes=[t_SPX[gi]], semtrk=t_spd[4])
                            gates_post(d)
                            rec_group(g, d, QT, KT, XMB, UMB, t_QT, t_KT, t_XMB, t_UMB, XM, t_XM, hs)
                            continue
                        def load_set(gq, si):
                            arrs, trs = GSETS[si]
                            for wi_ in range(4):
                                P.dma("sp", (lambda arrs, gq, wi_: lambda e: e.dma_start(out=arrs[wi_][:], in_=SPQ_d[gq, wi_]))(arrs, gq, wi_),
                                      reads=[t_SPQ[gq][wi_]], writes=[trs[wi_]])
                        if gpos == 0:
                            load_set(g, 0)
                        if gpos + 1 < len(order):
                            load_set(order[gpos + 1], (gpos + 1) % 2)
                        (QTg, KTg, XMBg, UMBg), trs = GSETS[gpos % 2]
                        if lat:
                            P.dma("sp", (lambda gi: lambda e: e.dma_start(out=XM[:], in_=SPX_d[gi]))(gi), reads=[t_SPX[gi]], writes=[t_XMl])
                        for c in range(8):
                            vt, t_vt = VT.next()
                            P.op("pe", (lambda c, UMBg: lambda e: e.matmul(
                                B_VO[:, 0:256], lhsT=MLBD[:, 16 + c, :], rhs=UMBg[:, c, :], start=True, stop=True))(c, UMBg),
                                reads=[t_MLBD, trs[3]], writes=[t_PV])
                            P.op("dve", (lambda vt: lambda e: e.tensor_copy(out=vt[:], in_=B_VO[:, 0:256]))(vt),
                                 reads=[t_PV], writes=[t_vt])
                            if lat:
                                for kc in range(8):
                                    P.op("pe", (lambda c, kc, g: lambda e: e.matmul(
                                        B_PM[:, 0:256], lhsT=WMO[:, c, kc, :], rhs=grp_rhs(kc, g), start=(kc == 0), stop=(kc == 7)))(c, kc, g),
                                        reads=[t_WMO[c]] + grp_trks(g), writes=[t_PM])
                                P.op("act", (lambda c: lambda e: e.activation(out=SIG[:, c, :], in_=B_PM[:, 0:256], func=AF.Sigmoid))(c),
                                     reads=[t_PM], writes=[t_SIG[c]])
                            for ti, (src, t_src) in enumerate(((QTg[:, c, :], trs[0]), (KTg[:, c, :], trs[1]), (vt[:], t_vt))):
                                P.op("pe", (lambda c, ti, src, d: lambda e: e.matmul(
                                    B_GP[0:8, 0:256], lhsT=WGT[:, ti * 8 + c, d * 8:(d + 1) * 8], rhs=src,
                                    start=(c == 0 and ti == 0), stop=(c == 7 and ti == 2)))(c, ti, src, d),
                                    reads=[t_WGT, t_src], writes=[t_GP])
                        if lat:
                            o2_, w2_ = FVCOLS["mlsk"]
                            P.op("dve", lambda e: e.tensor_tensor(
                                out=XM[:], in0=XM[:], in1=FV[:, o2_:o2_ + 8].unsqueeze(2).to_broadcast([128, 8, 256]), op=ALU.mult),
                                reads=[t_XMl, t_FV], writes=[t_XMl])
                        gates_post(d)
                        rec_group(g, d, QTg, KTg, XMBg, UMBg, [trs[0]] * 8, [trs[1]] * 8, [trs[2]] * 8, [trs[3]] * 8, XM, [t_XMl] * 8, hs)
                    P.barrier()
                    P.emit()
        if "yml" in debug:
            dbg_d["yml"] = YML_d
        if stop_after == "ml":
            P.wait_all("sp", P.out_toks)
            P.barrier()
            P.emit()
            ust.close()
            return nc, dbg_d

        x_cm = x_d.rearrange("(r j) d -> j r d", j=64)
        out_cm = out_d.rearrange("(r j) d -> j r d", j=64)
        t_X1 = [Trk("X1_%d" % i) for i in range(32)]
        t_H2 = [Trk("H2_%d" % i) for i in range(16)]
        with ExitStack() as st:
            sb = lambda name, shape, dt: st.enter_context(nc.sbuf_tensor(name, list(shape), dt))
            ps = lambda name, shape, dt: st.enter_context(nc.psum_tensor(name, list(shape), dt))
            WGR = sb("WGR", [128, 8, 8, 128], BF16); WGM = sb("WGM", [128, 8, 8, 128], BF16)
            WBR = sb("WBR", [128, 8, 8, 128], BF16); WBM = sb("WBM", [128, 8, 8, 128], BF16)
            WOUT = sb("WOUT", [128, 8, 1024], BF16)
            t_WGR = [Trk("WGR%d" % i) for i in range(8)]; t_WGM = [Trk("WGM%d" % i) for i in range(8)]
            t_WBR = [Trk("WBR%d" % i) for i in range(8)]; t_WBM = [Trk("WBM%d" % i) for i in range(8)]
            t_WOUT = [Trk("WOUT%d" % i) for i in range(8)]
            for oc in range(8):
                for (W, t_W, src) in ((WGR, t_WGR, win_d[32 + oc]), (WGM, t_WGM, win_d[40 + oc]),
                                      (WBR, t_WBR, wbrg_d[oc]), (WBM, t_WBM, wbml_d[oc])):
                    P.dma("pool", (lambda W, oc, src: lambda e: e.dma_start(
                        out=W[:, oc], in_=src.rearrange("p (k j) -> p k j", j=128)))(W, oc, src), writes=[t_W[oc]])
            for kc in range(8):
                P.dma("pool", (lambda kc: lambda e: e.dma_start(out=WOUT[:, kc, :], in_=wout_d[:, kc, :]))(kc), writes=[t_WOUT[kc]])
            YRt = Rot(sb, "YRt", 1, [128, 8, 512], BF16)
            YMt = Rot(sb, "YMt", 1, [128, 8, 512], BF16)
            SG = Rot(sb, "SG", 1, [128, 1024], F32)
            MIX = Rot(sb, "MIX", 1, [128, 8, 512], BF16)
            XT = Rot(sb, "XTc", 2, [128, 1024], F32)
            X1t = Rot(sb, "X1t", 1, [128, 1024], F32)
            XN = Rot(sb, "XNc", 1, [128, 1024], BF16)
            H2s = Rot(sb, "H2s", 1, [128, 8, 256], BF16)
            STc = sb("STc", [128, 32, 4], F32)
            BA0 = ps("BA0", [128, 512], F32); t_BA0 = Trk("PS_BA0")
            BA1 = ps("BA1", [128, 512], F32); t_BA1 = Trk("PS_BA1")
            BB0 = ps("BB0", [128, 512], F32); t_BB0 = Trk("PS_BB0")
            BB1 = ps("BB1", [128, 512], F32); t_BB1 = Trk("PS_BB1")
            BY = ps("BY", [128, 1024], F32); t_BY = Trk("PS_BY")
            BT = ps("BT", [128, 8, 128], BF16); t_BT = Trk("PS_BT")
            BT2 = ps("BT2", [128, 8, 128], BF16); t_BT2 = Trk("PS_BT2")
            def load_y(T):
                yr, t_yr = YRt.next()
                ym, t_ym = YMt.next()
                P.dma("sp", (lambda yr, T: lambda e: e.dma_start(
                    out=yr[:], in_=YRG_d[:, :, T * 512:(T + 1) * 512].rearrange("c p t -> p c t")))(yr, T), reads=t_YRG, writes=[t_yr])
                P.dma("sp", (lambda ym, T: lambda e: e.dma_start(
                    out=ym[:], in_=YML_d[:, :, T * 512:(T + 1) * 512].rearrange("c p t -> p c t")))(ym, T),
                    reads=t_YML[2 * T:2 * T + 2], writes=[t_ym])
                return yr, t_yr, ym, t_ym

            def load_x(ti):
                xt, t_xt = XT.next()
                for jj in range(2):
                    P.dma("sp", (lambda xt, jj, ti: lambda e: e.dma_start(
                        out=xt[jj * 64:(jj + 1) * 64, :], in_=x_cm[2 * ti + jj]))(xt, jj, ti), writes=[t_xt])
                return xt, t_xt

            h2s_trk2 = {}
            ynext = load_y(0)
            xnext = load_x(0)
            for T in range(8):
                yr, t_yr, ym, t_ym = ynext
                mix, t_mix = MIX.next()
                for oc in range(8):
                    sg, t_sg = SG.next()
                    for (W, t_W, bank, t_bank) in ((WGR, t_WGR, BA0, t_BA0), (WGM, t_WGM, BA1, t_BA1)):
                        for kc in range(8):
                            P.op("pe", (lambda W, bank, oc, kc, T: lambda e: e.matmul(
                                bank[:], lhsT=W[:, oc, kc, :],
                                rhs=uT[:, kc, 256 + T * 512:256 + (T + 1) * 512],
                                start=(kc == 0), stop=(kc == 7)))(W, bank, oc, kc, T), reads=[t_W[oc]] + t_uT[4:68], writes=[t_bank])
                    for (W, t_W, src, t_src, bank, t_bank) in ((WBR, t_WBR, yr, t_yr, BB0, t_BB0), (WBM, t_WBM, ym, t_ym, BB1, t_BB1)):
                        for kc in range(8):
                            P.op("pe", (lambda W, bank, oc, kc, src: lambda e: e.matmul(
                                bank[:], lhsT=W[:, oc, kc, :], rhs=src[:, kc, :],
                                start=(kc == 0), stop=(kc == 7)))(W, bank, oc, kc, src), reads=[t_W[oc], t_src], writes=[t_bank])
                    for (i_, bank, t_bank) in ((0, BA0, t_BA0), (1, BA1, t_BA1)):
                        P.op("act", (lambda sg, bank, i_: lambda e: e.activation(
                            out=sg[:, i_ * 512:(i_ + 1) * 512], in_=bank[:], func=AF.Sigmoid))(sg, bank, i_), reads=[t_bank], writes=[t_sg])
                    for (i_, bank, t_bank) in ((0, BB0, t_BB0), (1, BB1, t_BB1)):
                        P.op("dve", (lambda sg, bank, i_: lambda e: e.tensor_tensor(
                            out=sg[:, i_ * 512:(i_ + 1) * 512], in0=bank[:], in1=sg[:, i_ * 512:(i_ + 1) * 512], op=ALU.mult))(sg, bank, i_),
                            reads=[t_bank, t_sg], writes=[t_sg])
                    P.op("dve", (lambda mix, sg, oc: lambda e: e.tensor_tensor(
                        out=mix[:, oc, :], in0=sg[:, 0:512], in1=sg[:, 512:1024], op=ALU.add))(mix, sg, oc),
                        reads=[t_sg], writes=[t_mix])
                if T + 1 < 8:
                    ynext = load_y(T + 1)
                for s_ in range(4):
                    ti = T * 4 + s_
                    if s_ % 2 == 0:
                        h2s, t_h2s = H2s.next()
                        t_h2s2 = h2s_trk2.setdefault(id(t_h2s), Trk("h2s_b"))
                    xt, t_xt = xnext
                    if ti + 1 < 32:
                        xnext = load_x(ti + 1)
                    x1, t_x1 = X1t.next()
                    xn, t_xn = XN.next()
                    t_st = Trk("stc%d" % ti)
                    for half in range(2):
                        for kc in range(8):
                            P.op("pe", (lambda mix, half, kc, s_: lambda e: e.matmul(
                                BY[:, half * 512:(half + 1) * 512], lhsT=mix[:, kc, s_ * 128:(s_ + 1) * 128],
                                rhs=WOUT[:, kc, half * 512:(half + 1) * 512], start=(kc == 0), stop=(kc == 7)))(mix, half, kc, s_),
                                reads=[t_mix, t_WOUT[kc]], writes=[t_BY])
                    P.op("dve", (lambda x1: lambda e: e.tensor_tensor(out=x1[:], in0=BY[:], in1=GROW[:, 0, :], op=ALU.mult))(x1),
                         reads=[t_BY, t_GROW], writes=[t_x1])
                    P.op("dve", (lambda x1, xt: lambda e: e.tensor_tensor(out=x1[:], in0=x1[:], in1=xt[:], op=ALU.add))(x1, xt),
                         reads=[t_x1, t_xt], writes=[t_x1])
                    P.dma("sp", (lambda x1, ti: lambda e: e.dma_start(out=X1_d[ti * 128:(ti + 1) * 128, :], in_=x1[:]))(x1, ti),
                          reads=[t_x1], writes=[t_X1[ti]], semtrk=t_x1)
                    P.op("act", (lambda x1, xn, ti: lambda e: e.activation(
                        out=xn[:], in_=x1[:], func=AF.Square, accum_out=STc[:, ti, 0:1]))(x1, xn, ti), reads=[t_x1], writes=[t_xn, t_st])
                    P.op("act", (lambda ti: lambda e: e.activation(
                        out=STc[:, ti, 1:2], in_=STc[:, ti, 0:1], func=AF.Sqrt, scale=1.0 / 1024.0, bias=EPS))(ti), reads=[t_st], writes=[t_st])
                    P.op("dve", (lambda ti: lambda e: e.reciprocal(out=STc[:, ti, 2:3], in_=STc[:, ti, 1:2]))(ti), reads=[t_st], writes=[t_st])
                    P.op("act", (lambda x1, xn, ti: lambda e: e.activation(
                        out=xn[:], in_=x1[:], func=AF.Copy, scale=STc[:, ti, 2:3]))(x1, xn, ti), reads=[t_x1, t_st], writes=[t_xn])
                    for c in range(8):
                        btc, t_btc = (BT, t_BT) if c < 4 else (BT2, t_BT2)
                        P.op("pe", (lambda xn, c, btc: lambda e: e.transpose(
                            out=btc[:, c, :], in_=xn[:, c * 128:(c + 1) * 128], identity=IDB[:]))(xn, c, btc), reads=[t_xn, t_IDB], writes=[t_btc])
                    for c in (0, 4, 1, 5, 2, 6, 3, 7):
                        dst = h2s[:, c, (s_ % 2) * 128:(s_ % 2 + 1) * 128]
                        if c < 4:
                            P.op("dve", (lambda c, dst: lambda e: e.tensor_scalar(
                                out=dst, in0=BT[:, c, :], scalar1=A2[:, c:c + 1], scalar2=MODT[:, 48 + 2 * c:49 + 2 * c],
                                op0=ALU.mult, op1=ALU.add))(c, dst), reads=[t_BT, t_A2, t_MODT], writes=[t_h2s])
                        else:
                            P.op("act", (lambda c, dst: lambda e: e.activation(
                                out=dst, in_=BT2[:, c, :], func=AF.Identity, scale=A2[:, c:c + 1], bias=MODT[:, 48 + 2 * c:49 + 2 * c]))(c, dst),
                                reads=[t_BT2, t_A2, t_MODT], writes=[t_h2s2])
                    if s_ % 2 == 1:
                        Tq = ti // 2
                        P.dma("sp", (lambda h2s, Tq: lambda e: e.dma_start(out=H2_d[Tq], in_=h2s[:]))(h2s, Tq),
                              reads=[t_h2s, t_h2s2], writes=[t_H2[Tq]], semtrk=t_h2s)
            P.barrier()
            P.emit()
        ust.close()
        if "x1" in debug:
            dbg_d["x1"] = X1_d

        with ExitStack() as st:
            sb = lambda name, shape, dt: st.enter_context(nc.sbuf_tensor(name, list(shape), dt))
            ps = lambda name, shape, dt: st.enter_context(nc.psum_tensor(name, list(shape), dt))
            WFI = sb("WFI", [128, 44, 8, 128], BF16); t_WFI = [Trk("WFI%d" % i) for i in range(44)]
            WFO = sb("WFO", [128, 22, 1024], BF16); t_WFO = [Trk("WFO%d" % i) for i in range(22)]
            FGR = sb("FGR", [128, 1024], F32); t_FGR = Trk("FGR")
            P.dma("sp", lambda e: e.dma_start(out=FGR[:], in_=fgrow_d), writes=[t_FGR])
            for f_ in range(22):
                for ci in (f_, 22 + f_):
                    P.dma("pool", (lambda ci: lambda e: e.dma_start(
                        out=WFI[:, ci], in_=wffi_d[ci].rearrange("p (k j) -> p k j", j=128)))(ci), writes=[t_WFI[ci]])
                P.dma("pool", (lambda f_: lambda e: e.dma_start(out=WFO[:, f_, :], in_=wffo_d[:, f_, :]))(f_), writes=[t_WFO[f_]])
            H2t = Rot(sb, "H2t", 2, [128, 8, 512], BF16)
            SGt = Rot(sb, "SGt", 2, [128, 512], F32)
            HID = Rot(sb, "HID", 1, [128, 22, 512], BF16)
            X1r = Rot(sb, "X1r", 2, [128, 1024], F32)
            T2 = Rot(sb, "T2", 2, [128, 1024], F32)
            JK = sb("JK", [128, 1024], BF16); t_JK = Trk("JK")
            STf = sb("STf", [128, 32, 4], F32)
            BG0 = Rot(ps, "PS_BG0", 2, [128, 512], F32)
            BG1 = Rot(ps, "PS_BG1", 2, [128, 512], F32)
            BO = Rot(ps, "PS_BO", 2, [128, 1024], F32)
            h2_map = {}
            x1_map = {}
            for T in range(8):
                for Tq in (T, T + 1):
                    if Tq < 8 and Tq not in h2_map:
                        hb, t_hb = H2t.next()
                        for q_ in range(2):
                            P.dma("sp", (lambda hb, Tq, q_: lambda e: e.dma_start(out=hb[:, :, q_ * 256:(q_ + 1) * 256], in_=H2_d[2 * Tq + q_]))(hb, Tq, q_),
                                  reads=[t_H2[2 * Tq + q_]], writes=[t_hb])
                        h2_map[Tq] = (hb, t_hb)
                h2, t_h2 = h2_map.pop(T)
                hid, t_hid = HID.next()
                for f_ in range(22):
                    g0, t_g0 = BG0.next()
                    g1, t_g1 = BG1.next()
                    sg, t_sg = SGt.next()
                    for (ci, bank, t_bank) in ((f_, g0, t_g0), (22 + f_, g1, t_g1)):
                        for kc in range(8):
                            P.op("pe", (lambda bank, ci, kc, h2: lambda e: e.matmul(
                                bank[:], lhsT=WFI[:, ci, kc, :], rhs=h2[:, kc, :], start=(kc == 0), stop=(kc == 7)))(bank, ci, kc, h2),
                                reads=[t_WFI[ci], t_h2], writes=[t_bank])
                    P.op("act", (lambda sg, g0: lambda e: e.activation(out=sg[:], in_=g0[:], func=AF.Silu))(sg, g0),
                         reads=[t_g0], writes=[t_sg])
                    P.op("dve", (lambda hid, f_, sg, g1: lambda e: e.tensor_tensor(
                        out=hid[:, f_, :], in0=g1[:], in1=sg[:], op=ALU.mult))(hid, f_, sg, g1), reads=[t_g1, t_sg], writes=[t_hid])
                for s_ in range(4):
                    ti = T * 4 + s_
                    bo, t_bo = BO.next()
                    for tq in (ti, ti + 1):
                        if tq < 32 and tq not in x1_map:
                            xb, t_xb = X1r.next()
                            P.dma("sp", (lambda xb, tq: lambda e: e.dma_start(out=xb[:], in_=X1_d[tq * 128:(tq + 1) * 128, :]))(xb, tq),
                                  reads=[t_X1[tq]], writes=[t_xb])
                            x1_map[tq] = (xb, t_xb)
                    x1, t_x1 = x1_map.pop(ti)
                    t2, t_t2 = T2.next()
                    t_st = Trk("stf%d" % ti)
                    for half in range(2):
                        for f_ in range(22):
                            P.op("pe", (lambda bo, hid, half, f_, s_: lambda e: e.matmul(
                                bo[:, half * 512:(half + 1) * 512], lhsT=hid[:, f_, s_ * 128:(s_ + 1) * 128],
                                rhs=WFO[:, f_, half * 512:(half + 1) * 512], start=(f_ == 0), stop=(f_ == 21)))(bo, hid, half, f_, s_),
                                reads=[t_hid, t_WFO[f_]], writes=[t_bo])
                    P.op("dve", (lambda t2, bo: lambda e: e.tensor_tensor(out=t2[:], in0=bo[:], in1=GROW[:, 1, :], op=ALU.mult))(t2, bo),
                         reads=[t_bo, t_GROW], writes=[t_t2])
                    P.op("dve", (lambda t2, x1: lambda e: e.tensor_tensor(out=t2[:], in0=t2[:], in1=x1[:], op=ALU.add))(t2, x1),
                         reads=[t_t2, t_x1], writes=[t_t2])
                    P.op("act", (lambda t2, ti: lambda e: e.activation(
                        out=JK[:], in_=t2[:], func=AF.Square, accum_out=STf[:, ti, 0:1]))(t2, ti), reads=[t_t2], writes=[t_JK, t_st])
                    P.op("act", (lambda ti: lambda e: e.activation(
                        out=STf[:, ti, 1:2], in_=STf[:, ti, 0:1], func=AF.Sqrt, scale=1.0 / 1024.0, bias=EPS))(ti), reads=[t_st], writes=[t_st])
                    P.op("dve", (lambda ti: lambda e: e.reciprocal(out=STf[:, ti, 2:3], in_=STf[:, ti, 1:2]))(ti), reads=[t_st], writes=[t_st])
                    P.op("act", (lambda t2, ti: lambda e: e.activation(
                        out=t2[:], in_=t2[:], func=AF.Copy, scale=STf[:, ti, 2:3]))(t2, ti), reads=[t_t2, t_st], writes=[t_t2])
                    P.op("dve", (lambda t2: lambda e: e.tensor_tensor(out=t2[:], in0=t2[:], in1=FGR[:], op=ALU.mult))(t2),
                         reads=[t_t2, t_FGR], writes=[t_t2])
                    for jj in range(2):
                        P.out_toks.append(P.dma("sp", (lambda t2, jj, ti: lambda e: e.dma_start(
                            out=out_cm[2 * ti + jj], in_=t2[jj * 64:(jj + 1) * 64, :]))(t2, jj, ti), reads=[t_t2], semtrk=t_t2))
            P.barrier()
            P.emit()

        P.wait_all("sp", P.out_toks)
        P.emit()
    return nc, dbg_d


def make_in_maps(inputs):
    sh = prep_shared(inputs)
    x = np.asarray(inputs["x"], np.float32)
    c = np.asarray(inputs["c"], np.float32)
    ctx = np.asarray(inputs["ctx"], np.float32)
    c_ctx = np.asarray(inputs["c_ctx"], np.float32)
    maps = []
    for b in range(8):
        m = dict(sh)
        m["x"] = np.ascontiguousarray(x[b])
        m["ctx"] = np.ascontiguousarray(ctx[b])
        m["cv"] = np.ascontiguousarray(np.stack([fm(c[b]), fm(c_ctx)], 2).reshape(128, 16))
        maps.append(m)
    return maps


def kernel(**inputs):
    nc, _ = build()
    maps = make_in_maps(inputs)
    res = run_bass_kernel_spmd(nc, maps, core_ids=list(range(8)))
    return np.stack([r["out"] for r in res.results], 0)
```
